# Optimizing a Trainium2 kernel written in Bass

```python
import jax, jax.numpy as jnp
from jax import lax
import numpy as np

D_MODEL = 1024
BATCH = 2
SEQ = 8192
DEPTH = 2

CTX_LEN = 256
GRID_W = 64
D_MIX = D_MODEL
W_ATTN = D_MIX // 4
W_FOURIER = D_MIX // 4
W_CONV = D_MIX // 4
W_POOL = D_MIX // 4
HEAD_DIM = 64
N_Q_HEADS = W_ATTN // HEAD_DIM
N_KV_HEADS = 2
Q_PER_KV = N_Q_HEADS // N_KV_HEADS
N_FOURIER_HEADS = 4
FOURIER_HEAD_DIM = W_FOURIER // N_FOURIER_HEADS
CONV_WIDTH = 31
POOL_WINDOWS = (2, 4, 8, 16)
POOL_GROUP = W_POOL // len(POOL_WINDOWS)
D_FF = -(-(8 * D_MODEL) // (3 * 256)) * 256
ROPE_THETA = 10000.0
Q_BLOCK = 128
EPS = 1e-6
ATTN_SCALE = HEAD_DIM ** -0.5

OFF_Q = 0
OFF_K = OFF_Q + N_Q_HEADS * HEAD_DIM
OFF_V = OFF_K + N_KV_HEADS * HEAD_DIM
OFF_F = OFF_V + N_KV_HEADS * HEAD_DIM
OFF_C = OFF_F + W_FOURIER
OFF_P = OFF_C + 2 * W_CONV
D_IN = OFF_P + W_POOL

kernel_name = 'hymba_style_fourier_conv_pool_gqa_dit_block'


def rms_norm(x, g):
    xf = x.astype(jnp.float32)
    y = xf * lax.rsqrt(jnp.mean(xf * xf, axis=-1, keepdims=True) + EPS)
    return (y * g.astype(jnp.float32)).astype(x.dtype)


def layer_norm(x, g, b):
    xf = x.astype(jnp.float32)
    xc = xf - jnp.mean(xf, axis=-1, keepdims=True)
    y = xc * lax.rsqrt(jnp.mean(xc * xc, axis=-1, keepdims=True) + EPS)
    return (y * g.astype(jnp.float32) + b.astype(jnp.float32)).astype(x.dtype)


def axial_rope_tables(n, dtype):
    rows = n // GRID_W
    row = jnp.repeat(jnp.arange(rows, dtype=jnp.float32), GRID_W)
    col = jnp.tile(jnp.arange(GRID_W, dtype=jnp.float32), rows)
    n_freq = HEAD_DIM // 4
    inv_freq = ROPE_THETA ** (-jnp.arange(n_freq, dtype=jnp.float32) / n_freq)
    ang = jnp.concatenate([row[:, None] * inv_freq, col[:, None] * inv_freq], axis=-1)
    return jnp.cos(ang).astype(dtype), jnp.sin(ang).astype(dtype)


def apply_rope(x, cos, sin):
    xp = x.reshape(x.shape[:-1] + (HEAD_DIM // 2, 2))
    x0, x1 = xp[..., 0], xp[..., 1]
    cs = cos[None, :, None, :]
    sn = sin[None, :, None, :]
    return jnp.stack([x0 * cs - x1 * sn, x0 * sn + x1 * cs], axis=-1).reshape(x.shape)


def q_heads(pq, q_g):
    b, n, _ = pq.shape
    return rms_norm(pq.reshape(b, n, N_Q_HEADS, HEAD_DIM), q_g)


def kv_heads(pkv, k_g):
    b, n, _ = pkv.shape
    kv = pkv.reshape(b, n, 2, N_KV_HEADS, HEAD_DIM)
    return rms_norm(kv[:, :, 0], k_g), kv[:, :, 1]


def attend(q, k, v):
    b, nq = q.shape[:2]
    qg = q.reshape(b, nq, N_KV_HEADS, Q_PER_KV, HEAD_DIM)
    s = jnp.einsum('bqhgd,bkhd->bhgqk', qg, k).astype(jnp.float32) * ATTN_SCALE
    p = jax.nn.softmax(s, axis=-1).astype(v.dtype)
    o = jnp.einsum('bhgqk,bkhd->bqhgd', p, v)
    return o.reshape(b, nq, N_Q_HEADS * HEAD_DIM)


def attend_query_blocks(q, k, v):
    b, n = q.shape[:2]
    nb = n // Q_BLOCK
    qb = jnp.swapaxes(q.reshape(b, nb, Q_BLOCK, N_Q_HEADS, HEAD_DIM), 0, 1)
    ob = lax.map(lambda qi: attend(qi, k, v), qb)
    return jnp.swapaxes(ob, 0, 1).reshape(b, n, N_Q_HEADS * HEAD_DIM)


def fourier_mixer(u, w_f):
    b, n, _ = u.shape
    uf = u.astype(jnp.float32).reshape(b, n, N_FOURIER_HEADS, FOURIER_HEAD_DIM)
    y = jnp.fft.fft2(uf, axes=(1, 3), norm='ortho').real
    return y.reshape(b, n, W_FOURIER).astype(u.dtype) @ w_f


def conv_module(a, dw_w, dw_b, ln_g, ln_b, w_pw):
    glu = a[..., :W_CONV] * jax.nn.sigmoid(a[..., W_CONV:])
    y = lax.conv_general_dilated(
        glu, dw_w[:, None, :], window_strides=(1,),
        padding=[(CONV_WIDTH // 2, CONV_WIDTH // 2)],
        dimension_numbers=('NWC', 'WIO', 'NWC'),
        feature_group_count=W_CONV) + dw_b
    y = jax.nn.silu(layer_norm(y, ln_g, ln_b))
    return y @ w_pw


def pool_mixer(u, w_pool, scale):
    b, n, _ = u.shape
    uf = u.astype(jnp.float32)
    csum = jnp.concatenate([jnp.zeros((b, 1, W_POOL), jnp.float32), jnp.cumsum(uf, axis=1)], axis=1)
    t = jnp.arange(n)
    groups = []
    for gi, win in enumerate(POOL_WINDOWS):
        sl = slice(gi * POOL_GROUP, (gi + 1) * POOL_GROUP)
        lo = jnp.clip(t - win // 2, 0, n)
        hi = jnp.clip(t - win // 2 + win, 0, n)
        cs = csum[..., sl]
        cnt = (hi - lo).astype(jnp.float32)[None, :, None]
        mean = (jnp.take(cs, hi, axis=1) - jnp.take(cs, lo, axis=1)) / cnt
        groups.append(mean - uf[..., sl])
    y = jnp.stack(groups, axis=2).astype(u.dtype)
    y = jnp.einsum('blgc,gcd->blgd', y, w_pool).reshape(b, n, W_POOL)
    return y * scale


def swiglu(x, w_in, w_out):
    a, g = jnp.split(x @ w_in, 2, axis=-1)
    return (jax.nn.silu(a) * g) @ w_out


def mixers_and_ffn(x, p, attn, gate1, shift2, scale2, gate2, g2, w_f, dw_w, dw_b,
                   ln_g, ln_b, w_pw, w_pl, pl_scale, w_o, w_fi, w_fo):
    four = fourier_mixer(p[..., OFF_F:OFF_C], w_f)
    conv = conv_module(p[..., OFF_C:OFF_P], dw_w, dw_b, ln_g, ln_b, w_pw)
    pool = pool_mixer(p[..., OFF_P:], w_pl, pl_scale)
    mix = jnp.concatenate([attn, four, conv, pool], axis=-1) @ w_o
    x = x + gate1 * mix
    xn = rms_norm(x, g2) * (1 + scale2) + shift2
    return x + gate2 * swiglu(xn, w_fi, w_fo)


def setup_inputs(seed: int = 0) -> dict:
    key = jax.random.key(seed)
    ks = jax.random.split(key, 24)
    f32 = jnp.float32
    L = DEPTH
    D = D_MODEL

    def nrm(k, shape, scale):
        return jax.random.normal(k, shape, f32) * scale

    return {
        'x': nrm(ks[0], (BATCH, SEQ, D), 1.0),
        'c': nrm(ks[1], (BATCH, D), 1.0),
        'ctx': nrm(ks[2], (BATCH, CTX_LEN, D), 1.0),
        'c_ctx': nrm(ks[3], (D,), 1.0),
        'w_mod': nrm(ks[4], (L, D, 6 * D), D ** -0.5),
        'b_mod': nrm(ks[5], (L, 6 * D), 0.02),
        'g_norm1': 1.0 + nrm(ks[6], (L, D), 0.02),
        'g_norm2': 1.0 + nrm(ks[7], (L, D), 0.02),
        'w_in': nrm(ks[8], (L, D, D_IN), D ** -0.5),
        'q_norm_g': 1.0 + nrm(ks[9], (L, HEAD_DIM), 0.02),
        'k_norm_g': 1.0 + nrm(ks[10], (L, HEAD_DIM), 0.02),
        'w_fourier': nrm(ks[11], (L, W_FOURIER, W_FOURIER), W_FOURIER ** -0.5),
        'conv_dw_w': nrm(ks[12], (L, CONV_WIDTH, W_CONV), CONV_WIDTH ** -0.5),
        'conv_dw_b': nrm(ks[13], (L, W_CONV), 0.02),
        'conv_ln_g': 1.0 + nrm(ks[14], (L, W_CONV), 0.02),
        'conv_ln_b': nrm(ks[15], (L, W_CONV), 0.02),
        'w_conv_pw': nrm(ks[16], (L, W_CONV, W_CONV), W_CONV ** -0.5),
        'w_pool': nrm(ks[17], (L, len(POOL_WINDOWS), POOL_GROUP, POOL_GROUP), POOL_GROUP ** -0.5),
        'pool_scale': 1.0 + nrm(ks[18], (L, W_POOL), 0.02),
        'w_out': nrm(ks[19], (L, D_MIX, D), D_MIX ** -0.5),
        'w_ffn_in': nrm(ks[20], (L, D, 2 * D_FF), D ** -0.5),
        'w_ffn_out': nrm(ks[21], (L, D_FF, D), D_FF ** -0.5),
    }


def reference(x, c, ctx, c_ctx, w_mod, b_mod, g_norm1, g_norm2, w_in, q_norm_g, k_norm_g,
              w_fourier, conv_dw_w, conv_dw_b, conv_ln_g, conv_ln_b, w_conv_pw, w_pool,
              pool_scale, w_out, w_ffn_in, w_ffn_out):
    n = x.shape[1]
    cos, sin = axial_rope_tables(n, x.dtype)
    h = ctx
    sc = jax.nn.silu(c)
    scc = jax.nn.silu(c_ctx)
    for l in range(DEPTH):
        last = l == DEPTH - 1
        mod = jnp.split(sc @ w_mod[l] + b_mod[l], 6, axis=-1)
        shift1, scale1, gate1, shift2, scale2, gate2 = [m[:, None, :] for m in mod]
        n_ctx_mod = 2 if last else 6
        mod_c = jnp.split(scc @ w_mod[l][:, :n_ctx_mod * D_MODEL] + b_mod[l][:n_ctx_mod * D_MODEL], n_ctx_mod)

        hn = rms_norm(h, g_norm1[l]) * (1 + mod_c[1]) + mod_c[0]
        if last:
            kc, vc = kv_heads(hn @ w_in[l][:, OFF_K:OFF_F], k_norm_g[l])
        else:
            pc = hn @ w_in[l]
            qc = q_heads(pc[..., OFF_Q:OFF_K], q_norm_g[l])
            kc, vc = kv_heads(pc[..., OFF_K:OFF_F], k_norm_g[l])

        xn = rms_norm(x, g_norm1[l]) * (1 + scale1) + shift1
        p = xn @ w_in[l]
        q = apply_rope(q_heads(p[..., OFF_Q:OFF_K], q_norm_g[l]), cos, sin)
        k, v = kv_heads(p[..., OFF_K:OFF_F], k_norm_g[l])
        k = apply_rope(k, cos, sin)
        attn = attend_query_blocks(q, jnp.concatenate([kc, k], axis=1), jnp.concatenate([vc, v], axis=1))
        x_new = mixers_and_ffn(x, p, attn, gate1, shift2, scale2, gate2, g_norm2[l],
                               w_fourier[l], conv_dw_w[l], conv_dw_b[l], conv_ln_g[l], conv_ln_b[l],
                               w_conv_pw[l], w_pool[l], pool_scale[l], w_out[l], w_ffn_in[l], w_ffn_out[l])
        if not last:
            attn_c = attend(qc, kc, vc)
            h = mixers_and_ffn(h, pc, attn_c, mod_c[2], mod_c[3], mod_c[4], mod_c[5], g_norm2[l],
                               w_fourier[l], conv_dw_w[l], conv_dw_b[l], conv_ln_g[l], conv_ln_b[l],
                               w_conv_pw[l], w_pool[l], pool_scale[l], w_out[l], w_ffn_in[l], w_ffn_out[l])
        x = x_new
    return x
```

```python
import contextlib
import numpy as np
import concourse.bass as bass
import concourse.mybir as mybir
from concourse.bass_utils import run_bass_kernel_spmd

F32 = mybir.dt.float32
BF16 = mybir.dt.bfloat16
AF = mybir.ActivationFunctionType
ALU = mybir.AluOpType

D = 1024
SEQ = 8192
TL = 2048
CTX = 256
DEPTH = 2
D_IN = 1536
D_FF = 2816
NHC = D_FF // 128
EPS = 1e-6
NKT = (CTX + SEQ) // 128
NP_COLS = 138
R1 = 832


class Dep:
    __slots__ = ("name", "w", "r", "sem_in", "cnt_in", "sem_out", "cnt_out")

    def __init__(self, name=""):
        self.name = name
        self.w = None
        self.r = []
        self.sem_in = None
        self.cnt_in = 0
        self.sem_out = None
        self.cnt_out = 0


class Op:
    __slots__ = ("eng", "fn", "deps", "alldeps", "signaled", "sigidx", "dsem", "dval", "name", "dinc", "seq", "cost", "lat", "seg", "eidx", "pend", "tmin")

    def __init__(self, eng, fn, name=""):
        self.eng = eng
        self.fn = fn
        self.deps = []
        self.signaled = False
        self.sigidx = None
        self.dsem = None
        self.dval = 0
        self.dinc = 16
        self.seq = 0
        self.cost = 0.0
        self.lat = 0.0
        self.seg = 0
        self.eidx = 0
        self.alldeps = []
        self.pend = None
        self.tmin = 0.0
        self.name = name


ENGS = ["pe", "act", "dve", "pool", "sp"]


class Sched:
    def __init__(self, nc, stack):
        self.nc = nc
        self.stack = stack
        self.ops = {e: [] for e in ENGS}
        self.esem = {e: stack.enter_context(nc.semaphore("es_" + e)) for e in ENGS}
        self.nsem = len(ENGS)
        self.pending_dma = []
        self.cc_sems = {}
        self.seg = 0
        self.ecount = 0
        self.reorder = True

    def new_sem(self, name):
        self.nsem += 1
        return self.stack.enter_context(self.nc.semaphore("%s_%d" % (name.replace(".", "_"), self.nsem)))

    def _collect(self, o, reads, writes, extra):
        deps = []
        seen = set()

        def add(d):
            if d is None or d is o or id(d) in seen:
                return
            seen.add(id(d))
            deps.append(d)

        for t in reads:
            add(t.w)
        for t in writes:
            add(t.w)
            for r in t.r:
                add(r)
        for d in extra:
            add(d)
        o.alldeps = deps
        o.deps = deps
        for t in reads:
            t.r.append(o)
        for t in writes:
            t.w = o
            t.r = []

    DEFCOST = {"pe": 0.25, "act": 0.6, "dve": 0.6, "pool": 1.2, "sp": 0.1}

    def _register(self, o):
        o.seg = self.seg
        o.eidx = self.ecount
        self.ecount += 1
        self.ops[o.eng].append(o)

    def op(self, eng, fn, reads=(), writes=(), extra=(), name="", cost=None):
        o = Op(eng, fn, name)
        o.cost = self.DEFCOST[eng] if cost is None else cost
        o.lat = o.cost
        self._collect(o, reads, writes, extra)
        self._register(o)
        return o

    def dma(self, q, out_ap, in_ap, reads=(), writes=(), sem_dep=None, out_side=False, extra=(), name="", tmin=0.0):
        if sem_dep is None:
            sem_dep = (reads[0] if out_side else writes[0])
        if out_side:
            if sem_dep.sem_out is None:
                sem_dep.sem_out = self.new_sem("do_" + sem_dep.name)
            sem_dep.cnt_out += 16
            dsem, dval = sem_dep.sem_out, sem_dep.cnt_out
        else:
            if sem_dep.sem_in is None:
                sem_dep.sem_in = self.new_sem("di_" + sem_dep.name)
            sem_dep.cnt_in += 16
            dsem, dval = sem_dep.sem_in, sem_dep.cnt_in

        def fn(eng, out_ap=out_ap, in_ap=in_ap):
            return eng.dma_start(out=out_ap, in_=in_ap)

        o = Op(q, fn, name)
        o.dsem, o.dval = dsem, dval
        o.cost, o.lat = 0.1, 6.0
        o.tmin = tmin
        self._collect(o, reads, writes, extra)
        o.alldeps = [d for d in o.alldeps if not (d.dsem is not None and d.dsem is dsem)]
        self._register(o)
        self.pending_dma.append(o)
        return o

    def collective(self, fn, reads=(), writes=(), extra=(), name="cc"):
        o = Op("pool", fn, name)
        if name not in self.cc_sems:
            self.cc_sems[name] = [self.new_sem("cc_" + name), 0]
        self.cc_sems[name][1] += 1
        o.dsem = self.cc_sems[name][0]
        o.dval = self.cc_sems[name][1]
        o.dinc = 1
        o.cost, o.lat = 0.5, 40.0
        self._collect(o, reads, writes, extra)
        self._register(o)
        return o

    def barrier(self):
        pend = list(self.pending_dma)
        self.pending_dma = []
        for e in ENGS:
            o = Op(e, None, "barrier")
            o.pend = pend
            o.seg = self.seg
            o.eidx = self.ecount
            self.ops[e].append(o)
        self.ecount += 1
        self.seg += 1

    def schedule(self):
        W = 64
        nseg = self.seg + 1
        segs = [{e: [] for e in ENGS} for _ in range(nseg)]
        bars = [{e: None for e in ENGS} for _ in range(nseg)]
        for e in ENGS:
            for o in self.ops[e]:
                if o.fn is None:
                    bars[o.seg][e] = o
                else:
                    segs[o.seg][e].append(o)
        new_ops = {e: [] for e in ENGS}
        for si in range(nseg):
            lists = segs[si]
            if self.reorder:
                finish = {}
                done = set()
                etime = {e: 0.0 for e in ENGS}
                remaining = {e: list(lists[e]) for e in ENGS}
                out = {e: [] for e in ENGS}
                total = sum(len(v) for v in remaining.values())
                while total:
                    best = None
                    for e in ENGS:
                        rem = remaining[e]
                        if not rem:
                            continue
                        seen_dma = False
                        for idx in range(min(W, len(rem))):
                            o = rem[idx]
                            if o.dsem is not None:
                                if seen_dma:
                                    continue
                                seen_dma = True
                            ready = o.tmin
                            ok = True
                            for d in o.alldeps:
                                if d.seg != si or d.fn is None:
                                    continue
                                if id(d) not in done:
                                    ok = False
                                    break
                                f = finish[id(d)] + (0.0 if d.eng == e else 0.15)
                                if f > ready:
                                    ready = f
                            if not ok:
                                continue
                            start = max(etime[e], ready)
                            key = (start, o.eidx)
                            if best is None or key < best[0]:
                                best = (key, e, idx, o)
                    if best is None:
                        raise RuntimeError("scheduler stuck")
                    (start, _), e, idx, o = best
                    remaining[e].pop(idx)
                    out[e].append(o)
                    done.add(id(o))
                    finish[id(o)] = start + o.lat
                    etime[e] = start + o.cost
                    total -= 1
                lists = out
            for e in ENGS:
                new_ops[e].extend(lists[e])
            if bars[si][ENGS[0]] is not None:
                lasts = [lists[e][-1] for e in ENGS if lists[e] and lists[e][-1].dsem is None]
                for e in ENGS:
                    b = bars[si][e]
                    b.alldeps = [d for d in lasts + b.pend if not (d.eng == e == "pe") or d.dsem is not None]
                    new_ops[e].append(b)
        self.ops = new_ops
        for e in ENGS:
            for i, o in enumerate(self.ops[e]):
                o.seq = i
        for e in ENGS:
            for o in self.ops[e]:
                bestd = {}
                for d in o.alldeps:
                    if d.dsem is not None:
                        key = ("s", id(d.dsem))
                        if key not in bestd or bestd[key].dval < d.dval:
                            bestd[key] = d
                    else:
                        key = ("e", d.eng)
                        if key not in bestd or bestd[key].seq < d.seq:
                            bestd[key] = d
                o.deps = list(bestd.values())

    def finalize(self, final_waits):
        for o in final_waits:
            if o.dsem is None:
                o.signaled = True
        for e in ENGS:
            for o in self.ops[e]:
                for d in o.deps:
                    if d.dsem is None:
                        if d.eng == "pe" and o.eng == "pe":
                            continue
                        d.signaled = True
        for e in ENGS:
            c = 0
            for o in self.ops[e]:
                if o.dsem is None and o.signaled:
                    c += 1
                    o.sigidx = c

    def emit(self, block, final_waits=()):
        self.schedule()
        self.finalize(final_waits)
        esem = self.esem

        def run(ename, eng):
            waited = {}
            for o in self.ops[ename]:
                for d in o.deps:
                    if d.dsem is not None:
                        key, val, sem = id(d.dsem), d.dval, d.dsem
                    else:
                        if d.eng == "pe" and ename == "pe":
                            continue
                        key, val, sem = d.eng, d.sigidx, esem[d.eng]
                    if waited.get(key, 0) >= val:
                        continue
                    waited[key] = val
                    eng.wait_ge(sem, val)
                if o.fn is None:
                    continue
                ins = o.fn(eng)
                if o.dsem is not None:
                    ins.then_inc(o.dsem, o.dinc)
                elif o.sigidx:
                    ins.then_inc(esem[ename], 1)
            if ename == "sp":
                fin = {}
                for o in final_waits:
                    if o.dsem is not None:
                        key, sem, val = id(o.dsem), o.dsem, o.dval
                    else:
                        key, sem, val = o.eng, esem[o.eng], o.sigidx
                    if key not in fin or fin[key][1] < val:
                        fin[key] = (sem, val)
                for sem, val in fin.values():
                    eng.wait_ge(sem, val)

        block.tensor(lambda eng: run("pe", eng))
        block.scalar(lambda eng: run("act", eng))
        block.vector(lambda eng: run("dve", eng))
        block.gpsimd(lambda eng: run("pool", eng))
        block.sync(lambda eng: run("sp", eng))


class Ring:
    def __init__(self, items):
        self.items = items
        self.i = 0

    def next(self):
        it = self.items[self.i % len(self.items)]
        self.i += 1
        return it


ARENA_WORDS = 52736


class Arena:
    def __init__(self, base_ap):
        self.base = base_ap
        self.free_list = [(0, ARENA_WORDS)]
        self.live = {}
        self.peak = 0

    def alloc(self, name, shape, dtype, top=False):
        elems = 1
        for d in shape[1:]:
            elems *= d
        esz = 4 if dtype == F32 else 2
        words = (elems * esz + 3) // 4
        words = (words + 15) // 16 * 16
        order = range(len(self.free_list) - 1, -1, -1) if top else range(len(self.free_list))
        for i in order:
            o, n = self.free_list[i]
            if n >= words:
                if top:
                    off = o + n - words
                    if n == words:
                        self.free_list.pop(i)
                    else:
                        self.free_list[i] = (o, n - words)
                else:
                    off = o
                    if n == words:
                        self.free_list.pop(i)
                    else:
                        self.free_list[i] = (o + words, n - words)
                break
        else:
            raise RuntimeError("arena full allocating %s (%d words); free=%s" % (name, words, self.free_list))
        v = self.base[0:shape[0], off:off + words]
        if dtype != F32:
            v = v.bitcast(dtype)
        v = v[:, 0:elems]
        if len(shape) == 3:
            v = v.rearrange("p (a b) -> p a b", b=shape[2])
        elif len(shape) == 4:
            v = v.rearrange("p (a b c) -> p a b c", b=shape[2], c=shape[3])
        elif len(shape) == 5:
            v = v.rearrange("p (a b c d) -> p a b c d", b=shape[2], c=shape[3], d=shape[4])
        self.live[name] = (off, words)
        used = ARENA_WORDS - sum(n for _, n in self.free_list)
        self.peak = max(self.peak, used)
        return v

    def free(self, *names):
        for name in names:
            off, words = self.live.pop(name)
            self.free_list.append((off, words))
        self.free_list.sort()
        merged = []
        for o, n in self.free_list:
            if merged and merged[-1][0] + merged[-1][1] == o:
                merged[-1] = (merged[-1][0], merged[-1][1] + n)
            else:
                merged.append((o, n))
        self.free_list = merged


def build_program(taps=(), stop_after=None):
    nc = bass.Bass("TRN2", target_bir_lowering=False)
    dt = nc.dram_tensor

    def ein(name, shape, dtype=F32):
        return dt(name, list(shape), dtype, kind="ExternalInput")

    xT_d = ein("xT", [D, TL])
    ctxT_d = ein("ctxT", [D, CTX])
    cc_d = ein("cc", [128, 16])
    ropeC_d = ein("ropeC", [128, TL])
    ropeS_d = ein("ropeS", [128, TL])
    percore_d = ein("percore", [128, 66])
    params_d = ein("params", [DEPTH, 128, NP_COLS])
    w_mod_d = ein("w_mod", [DEPTH, D, 6 * D])
    w_in_d = ein("w_in", [DEPTH, D, D_IN])
    w_f_d = ein("w_fourier", [DEPTH, 256, 256])
    w_pw_d = ein("w_conv_pw", [DEPTH, 256, 256])
    w_pool_d = ein("w_pool", [DEPTH, 4, 64, 64])
    w_out_d = ein("w_out", [DEPTH, D, D])
    w_fi_d = ein("w_ffn_in", [DEPTH, D, 2 * D_FF])
    w_fo_d = ein("w_ffn_out", [DEPTH, D_FF, D])
    cmat_d = ein("cmat", [128, 9, 128])
    csblk_d = ein("csblk", [128, 2, 512])
    tw_d = ein("tw", [128, 2, 512])
    c256_d = ein("cs256", [128, 2, 2, 256])
    outT_d = dt("outT", [D, TL], F32, kind="ExternalOutput")

    RC = [R1, R1, R1, R1]
    snde = dt("snde", [512, 32], BF16)
    rcve = dt("rcve", [4 * 512, 32], BF16)
    sndc = [dt("sndc%d" % c, [RC[c], 512], BF16) for c in range(4)]
    rcvc = [dt("rcvc%d" % c, [4 * RC[c], 512], BF16) for c in range(4)]
    snd2 = dt("snd2", [64, SEQ], BF16)
    rcv2 = dt("rcv2", [4 * 64, SEQ], BF16)
    RG = [[0, 1, 2, 3], [4, 5, 6, 7]]

    tap_out = {}
    final_ops = []
    stopped = [False]
    out_done = [False]

    with contextlib.ExitStack() as st:
        S = Sched(nc, st)
        pid = nc.partition_id()
        jr = pid % 4
        arena_t = st.enter_context(nc.sbuf_tensor("arena", [128, ARENA_WORDS], F32))
        A = Arena(arena_t[:, :])
        al = A.alloc

        psb = [st.enter_context(nc.psum_tensor("ps%d" % i, [128, 512], F32)) for i in range(8)]
        psd = [Dep("ps%d" % i) for i in range(8)]
        ps_main = Ring([(psb[i], psd[i]) for i in range(0, 4)])
        ps_aux = Ring([(psb[i], psd[i]) for i in range(4, 6)])
        ps_acc = Ring([(psb[i], psd[i]) for i in range(6, 8)])

        xT = al("xT", [128, 8, TL], F32)
        hT = al("hT", [128, 8, CTX], F32)
        d_x = [[Dep("x%d_%d" % (k, c)) for c in range(4)] for k in range(8)]
        d_h = [Dep("h%d" % k) for k in range(8)]
        cmat = al("cmat", [128, 9, 128], BF16)
        onesrow = al("onesrow", [128, 128], F32)
        csblk = al("csblk", [128, 2, 512], BF16)
        cs256 = al("cs256", [128, 2, 2, 256], BF16)
        percore = al("percore", [128, 66], F32)
        ccs = al("ccs", [128, 16], F32)
        scb = al("scb", [128, 8, 2], BF16)
        d_const = Dep("const")
        d_pc = Dep("percore")
        d_sc = Dep("sc")
        params = al("params", [128, NP_COLS], F32)
        d_params = Dep("params")
        modT2 = al("modT", [128, 2, 48, 2], F32)
        d_mod2 = [(Dep("modT0e"), Dep("modT0l")), (Dep("modT1e"), Dep("modT1l"))]
        gm2 = al("gm", [128, 2, 2, 8, 2], F32)
        d_gm2 = [(Dep("gm0e"), Dep("gm0l")), (Dep("gm1e"), Dep("gm1l"))]
        d_snde, d_rcve = Dep("snde"), Dep("rcve")
        d_sndc = [Dep("sndc%d" % i) for i in range(4)]
        d_rcvc = [Dep("rcvc%d" % i) for i in range(4)]
        d_snd2, d_rcv2 = Dep("snd2"), Dep("rcv2")

        ONES_MEAN, BLK64, SWAPP, ONESLN, R1M, C128S, S128S = range(7)

        xsrc = xT_d.ap().rearrange("(k p) t -> p k t", p=128)
        d_xload = [Dep("xload%d" % c) for c in range(4)]
        for c in range(4):
            S.dma("sp", xT[:, :, c * 512:(c + 1) * 512], xsrc[:, :, c * 512:(c + 1) * 512], writes=[d_x[k][c] for k in range(8)],
                  sem_dep=d_xload[c])
        hsrc = ctxT_d.ap().rearrange("(k p) t -> p k t", p=128)
        S.dma("sp", hT, hsrc, writes=d_h, sem_dep=Dep("hload"))
        S.dma("pool", cmat, cmat_d.ap(), writes=[d_const])
        S.dma("pool", csblk, csblk_d.ap(), writes=[d_const])
        S.dma("pool", cs256, c256_d.ap(), writes=[d_const])
        S.dma("sp", percore, percore_d.ap(), writes=[d_pc])
        S.dma("sp", ccs, cc_d.ap(), writes=[d_sc])
        cvec = al("cvec", [128, 4], F32)
        S.op("pool", lambda e: e.memset(cvec, EPS), writes=[d_const])
        S.op("pool", lambda e: e.memset(onesrow, 1.0), writes=[d_const])
        S.op("act", lambda e: e.activation(out=scb.rearrange("p k s -> p (k s)"), in_=ccs, func=AF.Silu),
             reads=[d_sc], writes=[d_sc])

        DEPS = {}

        def gdep(name):
            if name not in DEPS:
                DEPS[name] = Dep(name)
            return DEPS[name]

        def tap(name, ap_sb, shape, dtype, reads):
            if name not in taps:
                return
            t = dt("tap_" + name, list(shape), dtype, kind="ExternalOutput")
            o = S.dma("sp", t.ap(), ap_sb, reads=reads, out_side=True, sem_dep=Dep("tap" + name))
            final_ops.append(o)
            tap_out[name] = t

        def phase_end(names):
            S.barrier()
            A.free(*names)

        lat_chunks = [(0, xT, [d_x[k][c] for k in range(8)], c * 512, 512, c) for c in range(4)]
        ctx_chunk = (1, hT, d_h, 0, CTX, 4)

        def norm_mod(chunk, which, xn_ap, d_xn, tmps, mods):
            s, X, dX, T0, n, ci = chunk
            sqb, d_sq, rstd, d_rstd, tmpr = tmps
            modT, gm, d_mod, d_gm = mods
            pb, pd = ps_aux.next()
            for k in range(8):
                S.op("act", lambda e, k=k: e.activation(out=sqb[:, k % 4, :n], in_=X[:, k, T0:T0 + n], func=AF.Square),
                     reads=[dX[k]], writes=[d_sq[k % 4]])
                S.op("pe", lambda e, k=k: e.matmul(pb[:, :n], cmat[:, ONES_MEAN, :], sqb[:, k % 4, :n], start=(k == 0), stop=(k == 7)),
                     reads=[d_sq[k % 4], d_const], writes=[pd])
            S.op("act", lambda e: e.activation(out=rstd[:, :n], in_=pb[:, :n], func=AF.Ln, bias=cvec[:, 0:1]), reads=[pd, d_const], writes=[d_rstd])
            S.op("act", lambda e: e.activation(out=rstd[:, :n], in_=rstd[:, :n], func=AF.Exp, scale=-0.5), reads=[d_rstd], writes=[d_rstd])
            shift_chunk = 0 if which == 0 else 3
            for k in range(8):
                tb, td = tmpr.next()
                S.op("dve", lambda e, k=k, tb=tb: e.tensor_tensor(out=tb[:, :n], in0=X[:, k, T0:T0 + n], in1=rstd[:, :n], op=ALU.mult),
                     reads=[dX[k], d_rstd], writes=[td])
                S.op("act", lambda e, k=k, tb=tb: e.activation(out=xn_ap[:, k, :n], in_=tb[:, :n], func=AF.Identity,
                                                             bias=modT[:, shift_chunk * 8 + k, s:s + 1], scale=gm[:, which, k, s:s + 1]),
                     reads=[td, d_gm[which], d_mod[which]], writes=[d_xn[k]])

        def stage_M(l, part, narrow=None):
            modT, gm, d_mod, d_gm = modT2[:, l % 2], gm2[:, l % 2], d_mod2[l % 2], d_gm2[l % 2]
            if part == "begin":
                S.dma("sp", params, params_d.ap()[l], writes=[d_params])
                return
            if part in ("end1", "end2"):
                which, sc_chunk, gcol = (0, 1, 0) if part == "end1" else (1, 4, 8)
                for s_ in range(2):
                    S.op("dve", lambda e, which=which, sc_chunk=sc_chunk, gcol=gcol, s_=s_: e.scalar_tensor_tensor(
                        out=gm[:, which, :, s_], in0=modT[:, sc_chunk * 8:(sc_chunk + 1) * 8, s_], scalar=1.0,
                        in1=params[:, gcol:gcol + 8], op0=ALU.add, op1=ALU.mult),
                        reads=[d_mod[which], d_params], writes=[d_gm[which]], cost=0.2)
                if part == "end2":
                    tap("modT%d" % l, modT, [128, 48, 2], F32, list(d_mod))
                    tap("gm%d" % l, gm, [128, 2, 8, 2], F32, list(d_gm))
                return
            wsrc = w_mod_d.ap()[l].rearrange("(k p) o -> p k o", p=128)
            if isinstance(part, tuple):
                o = part[1]
                sl = o % 2
                wmv, dwm = A_views["wms%d" % sl], gdep("wms%d" % sl)
                S.dma("pool", wmv, wsrc[:, :, o * 128:(o + 1) * 128], writes=[dwm])
                tiles = [(o, 0)]
            else:
                oc = part
                sl = oc % 2
                wmv, dwm = A_views["wm%d" % sl], gdep("wm%d" % sl)
                S.dma("pool", wmv, wsrc[:, :, oc * 512:(oc + 1) * 512], writes=[dwm])
                tiles = [(oc * 4 + o4, o4) for o4 in range(4)]
            for (o, o4) in tiles:
                pb, pd = ps_aux.next()
                for k in range(8):
                    S.op("pe", lambda e, k=k, o4=o4, pb=pb, wmv=wmv: e.matmul(pb[:, 0:2], wmv[:, k, o4 * 128:(o4 + 1) * 128], scb[:, k, :],
                                                                            start=(k == 0), stop=(k == 7)),
                         reads=[dwm, d_sc], writes=[pd], cost=0.06)
                S.op("dve", lambda e, o=o, pb=pb: e.tensor_scalar(out=modT[:, o, :], in0=pb[:, 0:2], scalar1=params[:, 16 + o:17 + o], scalar2=None,
                                                                 op0=ALU.add),
                     reads=[pd, d_params], writes=[d_mod[0] if o < 16 else d_mod[1]], cost=0.2)

        A_views = {}
        A_views["wm0"] = al("wm0", [128, 8, 512], BF16)
        A_views["wm1"] = al("wm1", [128, 8, 512], BF16)
        for part in ["begin", 0, 1, 2, 3, "end1"]:
            stage_M(0, part)
        phase_end(["wm0", "wm1"])
        deferred_M0 = [("n", o) for o in range(16, 48)]

        def do_layer(l):
            last = (l == DEPTH - 1)
            L = "%d" % l
            modT, gm, d_mod, d_gm = modT2[:, l % 2], gm2[:, l % 2], d_mod2[l % 2], d_gm2[l % 2]
            mods = (modT, gm, d_mod, d_gm)
            ycT = al("ycT", [128, 2, CTX], BF16, top=True)
            d_ycT = gdep("ycT")
            qT = al("qT", [128, 4, TL], BF16, top=True)
            qC = al("qC", [128, 4, CTX], BF16, top=True)
            d_q = [[gdep("q%d_%d" % (t, c)) for c in range(5)] for t in range(2)]
            KTc = al("KTc", [128, CTX], BF16, top=True)
            d_KTc = gdep("KTc")
            Vxc = al("Vxc", [128, 2, 192], BF16, top=True)
            d_Vxc = gdep("Vxc")
            zc_tm = al("zctm", [128, 2, 512], BF16, top=True)
            d_zc = gdep("zc")
            glu = al("glu", [128, 2, TL + 30], BF16, top=True)
            gluC = al("gluC", [128, 2, CTX + 30], BF16, top=True)
            pu = al("pu", [128, 2, TL + 16], BF16, top=True)
            puC = al("puC", [128, 2, CTX + 16], BF16, top=True)
            d_glu = [gdep("glu%d" % i) for i in range(2)]
            d_gluC = [gdep("gluC%d" % i) for i in range(2)]
            d_pu = [gdep("pu%d" % i) for i in range(2)]
            d_puC = [gdep("puC%d" % i) for i in range(2)]
            S.op("pool", lambda e: e.memset(qT, 0.0), writes=[gdep("q%d_%d" % (t, c)) for t in range(2) for c in range(4)])
            if not last:
                S.op("pool", lambda e: e.memset(qC, 0.0), writes=[gdep("q%d_4" % t) for t in range(2)])
            S.op("pool", lambda e: e.memset(Vxc, 1.0), writes=[d_Vxc])
            if not last:
                S.op("pool", lambda e: e.memset(gluC, 0.0), writes=d_gluC)
                S.op("pool", lambda e: e.memset(puC, 0.0), writes=d_puC)

            diag = al("diag", [128, 62, 128], BF16, top=True)
            d_diag = gdep("diag")
            for idx in range(62):
                S.op("dve", lambda e, idx=idx: e.tensor_scalar(out=diag[:, idx, :], in0=cmat[:, 7, :], scalar1=params[:, 66 + idx:67 + idx], scalar2=None,
                                                               op0=ALU.mult),
                     reads=[d_const, d_params], writes=[d_diag])
            wpw = al("wpw", [128, 2, 256], BF16, top=True)
            wpool = al("wpool", [128, 2, 128], BF16, top=True)
            d_wcp = gdep("wcp")
            S.dma("pool", wpw, w_pw_d.ap()[l].rearrange("(k p) o -> p k o", p=128), writes=[d_wcp])
            S.op("pool", lambda e: e.memset(wpool, 0.0), writes=[d_wcp])
            for g in range(4):
                tile_, half = g // 2, g % 2
                S.dma("pool", wpool[half * 64:(half + 1) * 64, tile_, half * 64:(half + 1) * 64], w_pool_d.ap()[l, g], writes=[d_wcp],
                      reads=[])
            w_in = al("w_in", [128, 8, D_IN], BF16)
            d_win = gdep("w_in")
            wisrc = w_in_d.ap()[l].rearrange("(k p) o -> p k o", p=128)
            for c3 in range(3):
                S.dma("pool", w_in[:, :, c3 * 512:(c3 + 1) * 512], wisrc[:, :, c3 * 512:(c3 + 1) * 512], writes=[d_win])
            xn = al("xnA", [128, 8, 512], BF16)
            d_xn = [gdep("xnA%d" % k) for k in range(8)]
            sqb = al("sqbA", [128, 4, 512], BF16)
            d_sq = [gdep("sqA%d" % k) for k in range(8)]
            rstd = al("rstdA", [128, 512], F32)
            d_rstd = gdep("rstdA")
            tmpr = Ring([(al("tmpA%d" % i, [128, 512], F32), gdep("tmpA%d" % i)) for i in range(3)])
            ntmps = (sqb, d_sq, rstd, d_rstd, tmpr)
            ropeC = al("ropeC", [128, 512], F32)
            ropeS = al("ropeS", [128, 512], F32)
            d_rope = gdep("rope")
            sqq = al("sqq", [128, 512], BF16)
            d_sqq = gdep("sqq")
            rs2 = al("rs2", [128, 512], F32)
            d_rs2 = gdep("rs2")
            qg = al("qg", [128, 512], BF16)
            d_qg = gdep("qg")
            kloc = al("kloc", [128, 512], BF16)
            d_kloc = gdep("kloc")
            vsb = al("vsb", [128, 4, 192], BF16)
            d_vsb = gdep("vsb")
            S.op("pool", lambda e: e.memset(vsb, 1.0), writes=[d_vsb])
            ub = al("ub", [128, 2, 512], BF16)
            d_ub = [gdep("ub0"), gdep("ub1")]
            zsb = al("zsb", [128, 4, 512], BF16)
            d_zsb = gdep("zsb")
            a_names = ["w_in", "xnA", "sqbA", "rstdA", "tmpA0", "tmpA1", "tmpA2", "ropeC", "ropeS", "sqq", "rs2", "qg", "kloc", "vsb",
                       "ub", "zsb"]

            def qk_tile(chunk, ctile, dest_ap, d_dest, gcol, rope):
                s, X, dX, T0, n, ci = chunk
                pb, pd = ps_main.next()
                for k in range(8):
                    S.op("pe", lambda e, k=k: e.matmul(pb[:, :n], w_in[:, k, ctile * 128:(ctile + 1) * 128], xn[:, k, :n],
                                                       start=(k == 0), stop=(k == 7)),
                         reads=[d_win, d_xn[k]], writes=[pd])
                S.op("act", lambda e: e.activation(out=sqq[:, :n], in_=pb[:, :n], func=AF.Square), reads=[pd], writes=[d_sqq])
                p2, pd2 = ps_aux.next()
                S.op("pe", lambda e: e.matmul(p2[:, :n], cmat[:, BLK64, :], sqq[:, :n], start=True, stop=True),
                     reads=[d_sqq, d_const], writes=[pd2])
                S.op("act", lambda e: e.activation(out=rs2[:, :n], in_=p2[:, :n], func=AF.Ln, bias=cvec[:, 0:1]), reads=[pd2, d_const], writes=[d_rs2])
                S.op("act", lambda e: e.activation(out=rs2[:, :n], in_=rs2[:, :n], func=AF.Exp, scale=-0.5), reads=[d_rs2], writes=[d_rs2])
                if not rope:
                    for (psl, dap) in dest_ap:
                        S.op("dve", lambda e, psl=psl, dap=dap: e.scalar_tensor_tensor(out=dap, in0=pb[psl, :n], scalar=params[psl, gcol:gcol + 1],
                                                                                       in1=rs2[psl, :n], op0=ALU.mult, op1=ALU.mult),
                             reads=[pd, d_rs2, d_params], writes=[d_dest])
                    return
                S.op("dve", lambda e: e.scalar_tensor_tensor(out=qg[:, :n], in0=pb[:, :n], scalar=params[:, gcol:gcol + 1], in1=rs2[:, :n],
                                                             op0=ALU.mult, op1=ALU.mult),
                     reads=[pd, d_rs2, d_params], writes=[d_qg])
                p3, pd3 = ps_aux.next()
                S.op("pe", lambda e: e.matmul(p3[:, :n], cmat[:, SWAPP, :], qg[:, :n], start=True, stop=True),
                     reads=[d_qg, d_const], writes=[pd3])
                t1, td1 = tmpr.next()
                t2, td2 = tmpr.next()
                S.op("dve", lambda e: e.tensor_tensor(out=t1[:, :n], in0=qg[:, :n], in1=ropeC[:, :n], op=ALU.mult),
                     reads=[d_qg, d_rope], writes=[td1])
                S.op("dve", lambda e: e.tensor_tensor(out=t2[:, :n], in0=p3[:, :n], in1=ropeS[:, :n], op=ALU.mult),
                     reads=[pd3, d_rope], writes=[td2])
                for (psl, dap) in dest_ap:
                    S.op("dve", lambda e, psl=psl, dap=dap: e.tensor_tensor(out=dap, in0=t1[psl, :n], in1=t2[psl, :n], op=ALU.add),
                         reads=[td1, td2], writes=[d_dest])

            def v_tiles(chunk, dest_fn, d_dest):
                s, X, dX, T0, n, ci = chunk
                for tt in range(n // 128):
                    pb, pd = ps_main.next()
                    for k in range(8):
                        S.op("pe", lambda e, k=k, tt=tt, pb=pb: e.matmul(pb[:, 0:128], xn[:, k, tt * 128:(tt + 1) * 128], w_in[:, k, 384:512],
                                                                          start=(k == 0), stop=(k == 7)),
                             reads=[d_win, d_xn[k]], writes=[pd])
                    S.op("act", lambda e, tt=tt, pb=pb: e.copy(out=dest_fn(tt)[:, 0:64], in_=pb[:, 0:64]), reads=[pd], writes=[d_dest])
                    S.op("act", lambda e, tt=tt, pb=pb: e.copy(out=dest_fn(tt)[:, 128:192], in_=pb[:, 64:128]), reads=[pd], writes=[d_dest])

            def dest_in(pb):
                return pb[:, 0:128]

            def proj_tile(chunk, ctile):
                s, X, dX, T0, n, ci = chunk
                pb, pd = ps_main.next()
                for k in range(8):
                    S.op("pe", lambda e, k=k: e.matmul(pb[:, :n], w_in[:, k, ctile * 128:(ctile + 1) * 128], xn[:, k, :n],
                                                       start=(k == 0), stop=(k == 7)),
                         reads=[d_win, d_xn[k]], writes=[pd])
                return pb, pd

            chunks = [ctx_chunk] + lat_chunks
            def do_chunk_A(chunk):
                s, X, dX, T0, n, ci = chunk
                isctx = (s == 1)
                so_box = []
                norm_mod(chunk, 0, xn, d_xn, ntmps, mods)
                if not isctx:
                    S.dma("sp", ropeC[:, :n], ropeC_d.ap()[:, T0:T0 + n], writes=[d_rope])
                    S.dma("sp", ropeS[:, :n], ropeS_d.ap()[:, T0:T0 + n], writes=[d_rope])
                full = not (isctx and last)
                if full:
                    H0, H1, ALLP = slice(0, 64), slice(64, 128), slice(0, 128)
                    for tq in range(2):
                        if isctx:
                            dest = [(H0, qC[0:64, tq, :]), (H1, qC[64:128, 2 + tq, :])]
                        else:
                            dest = [(H0, qT[0:64, tq, T0:T0 + n]), (H1, qT[64:128, 2 + tq, T0:T0 + n])]
                        qk_tile(chunk, tq, dest, d_q[tq][ci], 64, rope=not isctx)
                ALLP = slice(0, 128)
                if isctx:
                    qk_tile(chunk, 2, [(ALLP, KTc[:, :])], d_KTc, 65, rope=False)
                else:
                    qk_tile(chunk, 2, [(ALLP, kloc[:, :n])], d_kloc, 65, rope=True)
                if isctx:
                    for tt in range(2):
                        pb, pd = ps_main.next()
                        for k in range(8):
                            S.op("pe", lambda e, k=k, tt=tt, pb=pb: e.matmul(pb[:, 0:128], xn[:, k, tt * 128:(tt + 1) * 128], w_in[:, k, 384:512],
                                                                              start=(k == 0), stop=(k == 7)),
                                 reads=[d_win, d_xn[k]], writes=[pd])
                        S.op("act", lambda e, tt=tt, pb=pb: e.copy(out=Vxc[:, tt, 0:64], in_=pb[:, 0:64]), reads=[pd], writes=[d_Vxc])
                        S.op("act", lambda e, tt=tt, pb=pb: e.copy(out=Vxc[:, tt, 128:192], in_=pb[:, 64:128]), reads=[pd], writes=[d_Vxc])
                else:
                    v_tiles(chunk, lambda tt: vsb[:, tt, :], d_vsb)
                if not full:
                    return
                for ut in range(2):
                    pb, pd = proj_tile(chunk, 4 + ut)
                    S.op("act", lambda e, ut=ut, pb=pb: e.copy(out=ub[:, ut, :n], in_=pb[:, :n]), reads=[pd], writes=[d_ub[ut]])
                if isctx:
                    for tt in range(2):
                        pb, pd = ps_main.next()
                        for k2 in range(2):
                            S.op("pe", lambda e, k2=k2, tt=tt, pb=pb: e.matmul(pb[:, :], ub[:, k2, tt * 128:(tt + 1) * 128], csblk[:, k2, :],
                                                                                start=(k2 == 0), stop=(k2 == 1)),
                                 reads=[d_ub[k2], d_const], writes=[pd])
                        S.op("act", lambda e, tt=tt, pb=pb: e.copy(out=zc_tm[:, tt, :], in_=pb[:, :]), reads=[pd], writes=[d_zc])
                else:
                    for zt in range(4):
                        pb, pd = ps_main.next()
                        for k2 in range(2):
                            S.op("pe", lambda e, k2=k2, zt=zt, pb=pb: e.matmul(pb[:, :n], csblk[:, k2, zt * 128:(zt + 1) * 128], ub[:, k2, :n],
                                                                                start=(k2 == 0), stop=(k2 == 1)),
                                 reads=[d_ub[k2], d_const], writes=[pd])
                        S.op("dve", lambda e, zt=zt, pb=pb: e.tensor_copy(out=zsb[:, zt, :n], in_=pb[:, :n]), reads=[pd], writes=[d_zsb])
                    sc_ = sndc[ci].ap()
                    so = [S.dma("sp", sc_[320:832, :].rearrange("(z p) t -> p z t", p=128), zsb[:, :, :n], reads=[d_zsb], writes=[d_sndc[ci]],
                                out_side=True),
                          S.dma("sp", sc_[0:128, :], kloc[:, :n], reads=[d_kloc], writes=[d_sndc[ci]], out_side=True),
                          S.dma("sp", sc_[128:320, :].rearrange("r c -> (r c)").rearrange("(tt p c) -> p tt c", p=128, c=192), vsb,
                                reads=[d_vsb], writes=[d_sndc[ci]], out_side=True)]
                    so_box.append(so)
                    if ci == 0:
                        tap("zsb" + L, zsb, [128, 4, 512], BF16, [d_zsb])
                for vt in range(2):
                    pv, pdv = proj_tile(chunk, 6 + vt)
                    pg, pdg = proj_tile(chunk, 8 + vt)
                    sig, d_sig = tmpr.next()
                    S.op("act", lambda e, pg=pg, sig=sig: e.activation(out=sig[:, :n], in_=pg[:, :n], func=AF.Sigmoid), reads=[pdg], writes=[d_sig])
                    gdst = (gluC[:, vt, 15:15 + n] if isctx else glu[:, vt, 15 + T0:15 + T0 + n])
                    S.op("dve", lambda e, pv=pv, gdst=gdst, sig=sig: e.tensor_tensor(out=gdst, in0=pv[:, :n], in1=sig[:, :n], op=ALU.mult),
                         reads=[pdv, d_sig], writes=[(d_gluC if isctx else d_glu)[vt]])
                for pt in range(2):
                    pb, pd = proj_tile(chunk, 10 + pt)
                    pdst = (puC[:, pt, 8:8 + n] if isctx else pu[:, pt, 8 + T0:8 + T0 + n])
                    S.op("act", lambda e, pb=pb, pdst=pdst: e.copy(out=pdst, in_=pb[:, :n]), reads=[pd],
                         writes=[(d_puC if isctx else d_pu)[pt]])

                if so_box:
                    so = so_box[0]
                    S.collective(lambda e: e.collective_compute("AllGather", ALU.bypass, replica_groups=RG, ins=[sndc[ci].ap()],
                                                                outs=[rcvc[ci].ap()]),
                                 reads=[d_sndc[ci]], writes=[d_rcvc[ci]], extra=so, name="gc%d" % ci)

            if l == 0:
                A_views["wms0"] = al("wms0", [128, 8, 128], BF16)
                A_views["wms1"] = al("wms1", [128, 8, 128], BF16)
                a_names = a_names + ["wms0", "wms1"]
            for chunk in chunks:
                do_chunk_A(chunk)
                if l == 0:
                    for _ in range(7):
                        if deferred_M0:
                            stage_M(0, deferred_M0.pop(0))
            if l == 0:
                while deferred_M0:
                    stage_M(0, deferred_M0.pop(0))
                stage_M(0, "end2")
            sev = snde.ap().rearrange("(a p) c -> p a c", p=128)
            e_ops = []
            e_ops.append(S.dma("sp", sev[:, 0:2, 0:16], glu[:, :, 15:31], reads=d_glu, writes=[d_snde], out_side=True, sem_dep=d_glu[0]))
            e_ops.append(S.dma("sp", sev[:, 0:2, 16:32], glu[:, :, TL - 1:TL + 15], reads=d_glu, writes=[d_snde], out_side=True, sem_dep=d_glu[0]))
            e_ops.append(S.dma("sp", sev[:, 2:4, 0:16], pu[:, :, 8:24], reads=d_pu, writes=[d_snde], out_side=True, sem_dep=d_pu[0]))
            e_ops.append(S.dma("sp", sev[:, 2:4, 16:32], pu[:, :, TL - 8:TL + 8], reads=d_pu, writes=[d_snde], out_side=True, sem_dep=d_pu[0]))
            S.collective(lambda e: e.collective_compute("AllGather", ALU.bypass, replica_groups=RG, ins=[snde.ap()], outs=[rcve.ap()]),
                         reads=[d_snde], writes=[d_rcve], extra=e_ops, name="ge")

            tap("q" + L, qT, [128, 4, TL], BF16, [d_q[0][3], d_q[1][3], d_q[0][0], d_q[1][0]])
            tap("glu" + L, glu, [128, 2, TL + 30], BF16, d_glu)
            tap("pu" + L, pu, [128, 2, TL + 16], BF16, d_pu)
            tap("KTc" + L, KTc, [128, CTX], BF16, [d_KTc])
            tap("Vxc" + L, Vxc, [128, 2, 192], BF16, [d_Vxc])
            phase_end(a_names)
            if stop_after == "A" + L:
                return True

            catT = al("catT", [128, 6, TL], BF16)
            catC = al("catC", [128, 6, CTX], BF16)
            d_cat = [[gdep("cat%d_%d" % (r, c)) for c in range(5)] for r in range(6)]
            halL = al("halL", [128, 4, 32], BF16)
            halR = al("halR", [128, 4, 32], BF16)
            d_hal = gdep("hal")
            d_haloG, d_haloP = gdep("haloG"), gdep("haloP")
            def halo_fill():
                jl = (jr + 3) % 4
                jrr = (jr + 1) % 4
                rv = rcve.ap()
                S.dma("sp", halL, rv[bass.ds(jl * 512, 512), :].rearrange("(a p) c -> p a c", p=128), reads=[d_rcve], writes=[d_hal], tmin=150.0)
                S.dma("sp", halR, rv[bass.ds(jrr * 512, 512), :].rearrange("(a p) c -> p a c", p=128), reads=[d_rcve], writes=[d_hal], tmin=150.0)
                S.op("dve", lambda e: e.tensor_scalar(out=glu[:, :, 0:15], in0=halL[:, 0:2, 17:32], scalar1=percore[:, 0:1], scalar2=None, op0=ALU.mult),
                     reads=[d_hal, d_pc], writes=[d_haloG])
                S.op("dve", lambda e: e.tensor_scalar(out=glu[:, :, TL + 15:TL + 30], in0=halR[:, 0:2, 0:15], scalar1=percore[:, 1:2], scalar2=None,
                                                      op0=ALU.mult),
                     reads=[d_hal, d_pc], writes=[d_haloG])
                S.op("dve", lambda e: e.tensor_scalar(out=pu[:, :, 0:8], in0=halL[:, 2:4, 24:32], scalar1=percore[:, 0:1], scalar2=None, op0=ALU.mult),
                     reads=[d_hal, d_pc], writes=[d_haloP])
                S.op("dve", lambda e: e.tensor_scalar(out=pu[:, :, TL + 8:TL + 16], in0=halR[:, 2:4, 0:8], scalar1=percore[:, 1:2], scalar2=None,
                                                      op0=ALU.mult),
                     reads=[d_hal, d_pc], writes=[d_haloP])

            accr = Ring([(al("acc%d" % i, [128, 2, 512], F32), [gdep("acc%d_0" % i), gdep("acc%d_1" % i)]) for i in range(1)])
            ybr = Ring([(al("yb%d" % i, [128, 2, 512], BF16), [gdep("yb%d_0" % i), gdep("yb%d_1" % i)]) for i in range(2)])
            ysqr = Ring([(al("ysq%d" % i, [128, 2, 512], BF16), [gdep("ysq%d_0" % i), gdep("ysq%d_1" % i)]) for i in range(2)])
            meanr = Ring([(al("mean%d" % i, [128, 512], F32), gdep("mean%d" % i)) for i in range(2)])
            msqr = Ring([(al("msq%d" % i, [128, 512], F32), gdep("msq%d" % i)) for i in range(1)])
            rstdr = Ring([(al("rstdc%d" % i, [128, 512], F32), gdep("rstdc%d" % i)) for i in range(2)])
            actr = Ring([(al("actb%d" % i, [128, 2, 512], BF16), [gdep("actb%d_0" % i), gdep("actb%d_1" % i)]) for i in range(2)])
            p1e = al("p1e", [128, 528], F32)
            w4e = al("w4e", [128, 528], F32)
            w8e = al("w8e", [128, 528], F32)
            d_p1e, d_w4e, d_w8e = gdep("p1e"), gdep("w4e"), gdep("w8e")
            WS = al("WS", [128, 2, 512], F32)
            d_WS = [gdep("WS0"), gdep("WS1")]
            ybp = al("ybp", [128, 2, 512], BF16)
            d_ybp = [gdep("ybp0"), gdep("ybp1")]
            tmp8 = al("tmp8", [128, 8], F32)
            d_tmp8 = gdep("tmp8")
            b1_names = ["halL", "halR", "wpw", "wpool", "diag", "acc0", "yb0", "yb1", "ysq0", "ysq1", "mean0", "mean1", "msq0",
                        "rstdc0", "rstdc1", "actb0", "actb1", "p1e", "w4e", "w8e", "WS", "ybp", "tmp8", "glu", "gluC", "pu", "puC"]

            def conv_pool_chunk(G, dG, U, dU, cat, ci, T0, n, T, fixcol):
                acc, d_acc = accr.next()
                yb, d_yb = ybr.next()
                ysq, d_ysq = ysqr.next()
                mean_sb, d_mean = meanr.next()
                msq, d_msq = msqr.next()
                rstdc, d_rstdc = rstdr.next()
                actb, d_actb = actr.next()
                pcs = []
                for vt in range(2):
                    pc, pdc = ps_main.next()
                    pcs.append((pc, pdc))
                    for j in range(31):
                        S.op("pe", lambda e, vt=vt, j=j, pc=pc: e.matmul(pc[:, :n], diag[:, vt * 31 + j, :], G[:, vt, T0 + j:T0 + j + n],
                                                                         start=(j == 0), stop=(j == 30)),
                             reads=dG[vt] + [d_diag], writes=[pdc])
                    S.op("act", lambda e, vt=vt, pc=pc: e.activation(out=yb[:, vt, :n], in_=pc[:, :n], func=AF.Identity, bias=params[:, 128 + vt:129 + vt]),
                         reads=[pdc, d_params], writes=[d_yb[vt]])
                    S.op("act", lambda e, vt=vt, pc=pc: e.activation(out=ysq[:, vt, :n], in_=pc[:, :n], func=AF.Square, bias=params[:, 128 + vt:129 + vt]),
                         reads=[pdc, d_params], writes=[d_ysq[vt]])
                pm, pdm = ps_aux.next()
                pq, pdq = ps_aux.next()
                for vt in range(2):
                    S.op("pe", lambda e, vt=vt: e.matmul(pm[:, :n], cmat[:, ONESLN, :], yb[:, vt, :n], start=(vt == 0), stop=(vt == 1)),
                         reads=[d_yb[vt], d_const], writes=[pdm])
                for vt in range(2):
                    S.op("pe", lambda e, vt=vt: e.matmul(pq[:, :n], cmat[:, ONESLN, :], ysq[:, vt, :n], start=(vt == 0), stop=(vt == 1)),
                         reads=[d_ysq[vt], d_const], writes=[pdq])
                S.op("act", lambda e: e.copy(out=mean_sb[:, :n], in_=pm[:, :n]), reads=[pdm], writes=[d_mean])
                S.op("dve", lambda e: e.tensor_tensor(out=msq[:, :n], in0=mean_sb[:, :n], in1=mean_sb[:, :n], op=ALU.mult),
                     reads=[d_mean], writes=[d_msq])
                S.op("dve", lambda e: e.tensor_tensor(out=rstdc[:, :n], in0=pq[:, :n], in1=msq[:, :n], op=ALU.subtract),
                     reads=[pdq, d_msq], writes=[d_rstdc])
                S.op("act", lambda e: e.activation(out=rstdc[:, :n], in_=rstdc[:, :n], func=AF.Ln, bias=cvec[:, 0:1]), reads=[d_rstdc, d_const],
                     writes=[d_rstdc])
                S.op("act", lambda e: e.activation(out=rstdc[:, :n], in_=rstdc[:, :n], func=AF.Exp, scale=-0.5), reads=[d_rstdc], writes=[d_rstdc])
                for vt in range(2):
                    pc, pdc = pcs[vt]
                    S.op("dve", lambda e, vt=vt, pc=pc: e.scalar_tensor_tensor(out=acc[:, vt, :n], in0=pc[:, :n], scalar=params[:, 128 + vt:129 + vt],
                                                                               in1=mean_sb[:, :n], op0=ALU.add, op1=ALU.subtract),
                         reads=[pdc, d_mean, d_params], writes=[d_acc[vt]])
                    S.op("dve", lambda e, vt=vt: e.tensor_tensor(out=acc[:, vt, :n], in0=acc[:, vt, :n], in1=rstdc[:, :n], op=ALU.mult),
                         reads=[d_rstdc, d_acc[vt]], writes=[d_acc[vt]])
                    S.op("act", lambda e, vt=vt: e.activation(out=actb[:, vt, :n], in_=acc[:, vt, :n], func=AF.Silu,
                                                              bias=params[:, 132 + vt:133 + vt], scale=params[:, 130 + vt:131 + vt]),
                         reads=[d_acc[vt], d_params], writes=[d_actb[vt]])
                for ot in range(2):
                    pb, pd = ps_main.next()
                    for vt in range(2):
                        S.op("pe", lambda e, vt=vt, ot=ot, pb=pb: e.matmul(pb[:, :n], wpw[:, vt, ot * 128:(ot + 1) * 128], actb[:, vt, :n],
                                                                            start=(vt == 0), stop=(vt == 1)),
                             reads=[d_actb[vt], d_wcp], writes=[pd])
                    S.op("act", lambda e, ot=ot, pb=pb: e.copy(out=cat[:, 2 + ot, T0:T0 + n], in_=pb[:, :n]), reads=[pd], writes=[d_cat[2 + ot][ci]])
                e_ = n + 16
                c0 = 8
                S.op("dve", lambda e: e.tensor_tensor(out=WS[0:64, 0, :n], in0=U[0:64, 0, T0 + c0 - 1:T0 + c0 - 1 + n], in1=U[0:64, 0, T0 + c0:T0 + c0 + n],
                                                       op=ALU.add),
                     reads=dU[0], writes=[d_WS[0]])
                S.op("dve", lambda e: e.tensor_tensor(out=p1e[64:128, 1:e_], in0=U[64:128, 0, T0:T0 + e_ - 1], in1=U[64:128, 0, T0 + 1:T0 + e_], op=ALU.add),
                     reads=dU[0], writes=[d_p1e])
                S.op("dve", lambda e: e.tensor_tensor(out=WS[64:128, 0, :n], in0=p1e[64:128, c0 - 1:c0 - 1 + n], in1=p1e[64:128, c0 + 1:c0 + 1 + n],
                                                       op=ALU.add),
                     reads=[d_p1e], writes=[d_WS[0]])
                S.op("dve", lambda e: e.tensor_tensor(out=p1e[:, 1:e_], in0=U[:, 1, T0:T0 + e_ - 1], in1=U[:, 1, T0 + 1:T0 + e_], op=ALU.add),
                     reads=dU[1], writes=[d_p1e])
                S.op("dve", lambda e: e.tensor_tensor(out=w4e[:, 2:e_ - 1], in0=p1e[:, 1:e_ - 2], in1=p1e[:, 3:e_], op=ALU.add),
                     reads=[d_p1e], writes=[d_w4e])
                S.op("dve", lambda e: e.tensor_tensor(out=WS[0:64, 1, :n], in0=w4e[0:64, c0 - 2:c0 - 2 + n], in1=w4e[0:64, c0 + 2:c0 + 2 + n], op=ALU.add),
                     reads=[d_w4e], writes=[d_WS[1]])
                S.op("dve", lambda e: e.tensor_tensor(out=w8e[64:128, 4:e_ - 3], in0=w4e[64:128, 2:e_ - 5], in1=w4e[64:128, 6:e_ - 1], op=ALU.add),
                     reads=[d_w4e], writes=[d_w8e])
                S.op("dve", lambda e: e.tensor_tensor(out=WS[64:128, 1, :n], in0=w8e[64:128, c0 - 4:c0 - 4 + n], in1=w8e[64:128, c0 + 4:c0 + 4 + n],
                                                       op=ALU.add),
                     reads=[d_w8e], writes=[d_WS[1]])
                for tl_ in range(2):
                    S.op("dve", lambda e, tl_=tl_: e.scalar_tensor_tensor(out=ybp[:, tl_, :n], in0=WS[:, tl_, :n], scalar=params[:, 136 + tl_:137 + tl_],
                                                                          in1=U[:, tl_, T0 + c0:T0 + c0 + n], op0=ALU.mult, op1=ALU.subtract),
                         reads=[d_WS[tl_], d_params] + dU[tl_], writes=[d_ybp[tl_]])
                    edges = []
                    if T0 == 0:
                        edges.append((0, fixcol + tl_ * 16))
                    if T0 + n == T:
                        edges.append((n - 8, fixcol + tl_ * 16 + 8))
                    for (e0, fc) in edges:
                        S.op("dve", lambda e, tl_=tl_, e0=e0, fc=fc: e.tensor_tensor(out=tmp8[:, :], in0=WS[:, tl_, e0:e0 + 8], in1=percore[:, fc:fc + 8],
                                                                                      op=ALU.mult),
                             reads=[d_WS[tl_], d_pc], writes=[d_tmp8])
                        S.op("dve", lambda e, tl_=tl_, e0=e0: e.tensor_tensor(out=ybp[:, tl_, e0:e0 + 8], in0=tmp8[:, :],
                                                                               in1=U[:, tl_, T0 + c0 + e0:T0 + c0 + e0 + 8], op=ALU.subtract),
                             reads=[d_tmp8] + dU[tl_], writes=[d_ybp[tl_]])
                    pb, pd = ps_main.next()
                    S.op("pe", lambda e, tl_=tl_, pb=pb: e.matmul(pb[:, :n], wpool[:, tl_, :], ybp[:, tl_, :n], start=True, stop=True),
                         reads=[d_ybp[tl_], d_wcp], writes=[pd])
                    S.op("act", lambda e, tl_=tl_, pb=pb: e.activation(out=cat[:, 4 + tl_, T0:T0 + n], in_=pb[:, :n], func=AF.Identity,
                                                                        scale=params[:, 134 + tl_:135 + tl_]),
                         reads=[pd, d_params], writes=[d_cat[4 + tl_][ci]])

            if not last:
                conv_pool_chunk(gluC, [[d] for d in d_gluC], puC, [[d] for d in d_puC], catC, 4, 0, CTX, CTX, 34)
            for c in (1, 2, 0, 3):
                edge = c in (0, 3)
                if c == 0:
                    halo_fill()
                conv_pool_chunk(glu, [[d] + ([d_haloG] if edge else []) for d in d_glu], pu, [[d] + ([d_haloP] if edge else []) for d in d_pu],
                                catT, c, c * 512, 512, TL, 2)
            tap("catconv" + L, catT[:, 2:6, :], [128, 4, TL], BF16, [d_cat[r][c] for r in range(2, 6) for c in range(4)])
            tap("catCconv" + L, catC[:, 2:6, :], [128, 4, CTX], BF16, [d_cat[r][4] for r in range(2, 6)])
            phase_end(b1_names)
            if stop_after == "B1" + L:
                return True

            KT = al("KT", [128, SEQ], BF16)
            d_KT = [gdep("KT%d" % r) for r in range(4)]
            X1 = al("X1", [128, 64, 128], BF16)
            d_X1 = gdep("X1")
            Gr = al("Gr", [128, 64, 64], BF16)
            Gi = al("Gi", [128, 64, 64], BF16)
            d_Gr, d_Gi = gdep("Gr"), gdep("Gi")
            Yout = al("Yout", [128, 64, 64], BF16)
            d_Yout = gdep("Yout")
            tw = al("tw", [128, 2, 512], F32)
            d_tw = gdep("tw")
            S.dma("sp", tw, tw_d.ap(), writes=[d_tw])
            ta = Ring([(al("twa%d" % i, [128, 512], F32), gdep("twa%d" % i)) for i in range(2)])
            tb_ = Ring([(al("twb%d" % i, [128, 512], F32), gdep("twb%d" % i)) for i in range(2)])
            b2_names = ["X1", "Gr", "Gi", "Yout", "tw", "twa0", "twa1", "twb0", "twb1"]
            for c in range(4):
                for ri in range(2):
                    for r in range(4):
                        src = rcvc[c].ap()[bass.ds(jr * 64 + (r * RC[c] + 320 + ri * 256), 64), :].rearrange("m (a t) -> a m t", t=128)
                        p0 = ri * 64 + r * 16 + c * 4
                        S.dma("sp", X1[p0:p0 + 4, :, :], src, reads=[d_rcvc[c]], writes=[d_X1])
            for r in range(4):
                for c in range(4):
                    S.dma("sp", KT[:, r * TL + c * 512:r * TL + (c + 1) * 512], rcvc[c].ap()[r * RC[c]:r * RC[c] + 128, :], reads=[d_rcvc[c]],
                          writes=[d_KT[r]])
            for mg in range(16):
                pb, pd = ps_main.next()
                for mi in range(4):
                    S.op("pe", lambda e, mi=mi, mg=mg, pb=pb: e.matmul(pb[:, mi * 128:(mi + 1) * 128], X1[:, mg * 4 + mi, :], cmat[:, R1M, :],
                                                                        start=True, stop=True),
                         reads=[d_X1, d_const], writes=[pd])
                a_, da = ta.next()
                b_, db = tb_.next()
                S.op("dve", lambda e, pb=pb, a_=a_: e.tensor_tensor(out=a_[:, :], in0=pb[:, :], in1=tw[:, 0, :], op=ALU.mult), reads=[pd, d_tw], writes=[da])
                S.op("dve", lambda e, pb=pb, b_=b_: e.tensor_tensor(out=b_[:, :], in0=pb[:, :], in1=tw[:, 1, :], op=ALU.mult), reads=[pd, d_tw], writes=[db])
                av = a_.rearrange("p (m r k) -> p m r k", r=2, k=64)
                bv = b_.rearrange("p (m r k) -> p m r k", r=2, k=64)
                S.op("dve", lambda e, av=av, bv=bv, mg=mg: e.tensor_tensor(out=Gr[:, mg * 4:(mg + 1) * 4, :], in0=av[:, :, 0, :], in1=bv[:, :, 1, :], op=ALU.add),
                     reads=[da, db], writes=[d_Gr])
                S.op("dve", lambda e, av=av, bv=bv, mg=mg: e.tensor_tensor(out=Gi[:, mg * 4:(mg + 1) * 4, :], in0=av[:, :, 1, :], in1=bv[:, :, 0, :],
                                                                             op=ALU.subtract),
                     reads=[da, db], writes=[d_Gi])
            Grf = Gr.rearrange("p m k -> p (m k)")
            Gif = Gi.rearrange("p m k -> p (m k)")
            Yf = Yout.rearrange("p m k -> p (m k)")
            for ch in range(8):
                pb, pd = ps_main.next()
                S.op("pe", lambda e, ch=ch, pb=pb: e.matmul(pb[:, :], cmat[:, C128S, :], Grf[:, ch * 512:(ch + 1) * 512], start=True, stop=False),
                     reads=[d_Gr, d_const], writes=[pd])
                S.op("pe", lambda e, ch=ch, pb=pb: e.matmul(pb[:, :], cmat[:, S128S, :], Gif[:, ch * 512:(ch + 1) * 512], start=False, stop=True),
                     reads=[d_Gi, d_const], writes=[pd])
                S.op("act", lambda e, ch=ch, pb=pb: e.copy(out=Yf[:, ch * 512:(ch + 1) * 512], in_=pb[:, :]), reads=[pd], writes=[d_Yout])
            o2 = S.dma("sp", snd2.ap().rearrange("m (k2 k1) -> k2 m k1", k1=64), Yout, reads=[d_Yout], writes=[d_snd2], out_side=True)
            tap("Yout" + L, Yout, [128, 64, 64], BF16, [d_Yout])
            S.collective(lambda e: e.collective_compute("AllGather", ALU.bypass, replica_groups=RG, ins=[snd2.ap()], outs=[rcv2.ap()]),
                         reads=[d_snd2], writes=[d_rcv2], extra=[o2], name="g2")
            if not last:
                for mt in range(2):
                    pb, pd = ps_main.next()
                    for tt in range(2):
                        S.op("pe", lambda e, tt=tt, mt=mt, pb=pb: e.matmul(pb[:, 0:CTX], zc_tm[:, tt, mt * 128:(mt + 1) * 128], cs256[:, 0, tt, :],
                                                                            start=(tt == 0), stop=False),
                             reads=[d_zc, d_const], writes=[pd])
                        S.op("pe", lambda e, tt=tt, mt=mt, pb=pb: e.matmul(pb[:, 0:CTX], zc_tm[:, tt, 256 + mt * 128:256 + (mt + 1) * 128], cs256[:, 1, tt, :],
                                                                            start=False, stop=(tt == 1)),
                             reads=[d_zc, d_const], writes=[pd])
                    S.op("act", lambda e, mt=mt, pb=pb: e.copy(out=ycT[:, mt, :], in_=pb[:, 0:CTX]), reads=[pd], writes=[d_ycT])
                tap("ycT" + L, ycT, [128, 2, CTX], BF16, [d_ycT])
            phase_end(b2_names)
            if stop_after == "B2" + L:
                return True

            attnT = al("attnT", [128, 2, TL], BF16, top=True)
            attnC = al("attnC", [128, 2, CTX], BF16, top=True)
            d_attn = [[gdep("attn%d_%d" % (h, c)) for c in range(5)] for h in range(4)]
            wo_att = al("wo_att", [128, 2, D], BF16)
            wo_rest = al("wo_rest", [128, 6, D], BF16)
            d_wo = gdep("wo")
            for tq in range(2):
                S.dma("pool", wo_att[0:64, tq, :], w_out_d.ap()[l, tq * 64:(tq + 1) * 64, :], writes=[d_wo])
                S.dma("pool", wo_att[64:128, tq, :], w_out_d.ap()[l, (2 + tq) * 64:(3 + tq) * 64, :], writes=[d_wo])
            S.dma("pool", wo_rest, w_out_d.ap()[l, 256:1024, :].rearrange("(r p) o -> p r o", p=128), writes=[d_wo])
            wf = al("wf", [128, 2, 256], BF16)
            d_wf = gdep("wf")
            S.dma("pool", wf, w_f_d.ap()[l].rearrange("(k p) o -> p k o", p=128), writes=[d_wf])
            Vx = al("Vx", [128, 64, 192], BF16)
            d_Vx = [gdep("Vx%d" % r) for r in range(4)]
            for r in range(4):
                for c in range(4):
                    rcv_ = rcvc[c].ap()
                    vsrc = rcv_[r * RC[c] + 128:r * RC[c] + 320, :].rearrange("r c -> (r c)").rearrange("(tt p c) -> p tt c", p=128, c=192)
                    S.dma("sp", Vx[:, r * 16 + c * 4:r * 16 + (c + 1) * 4, :], vsrc, reads=[d_rcvc[c]], writes=[d_Vx[r]])
            ering = Ring([(al("E%d" % i, [128, 512], BF16), gdep("E%d" % i)) for i in range(3)])
            b3_names = ["KT", "Vx", "E0", "E1", "E2", "ob0", "ob1", "rs0", "qT", "qC", "KTc", "Vxc", "zctm"]

            obr = Ring([(al("ob%d" % i, [128, 512], F32), gdep("ob%d" % i)) for i in range(2)])
            rsr = Ring([(al("rs%d" % i, [128, 512], F32), gdep("rsr%d" % i)) for i in range(1)])
            pending_fin = []

            def flush_fin():
                while pending_fin:
                    pending_fin.pop(0)()

            def attention(Q, dQ, ci, T0, n, key_tiles, dest):
                for tq in range(2):
                    for hf in range(2):
                        head = hf * 2 + tq
                        ps_ = slice(hf * 64, (hf + 1) * 64)
                        pO, pdO = ps_acc.next()
                        nk = len(key_tiles)
                        sbank = {}

                        def issue_S(kt, head=head, tq=tq, sbank=sbank):
                            Ksrc, dK, Vsrc, dV = key_tiles[kt]
                            pS, pdS = ps_main.next()
                            sbank[kt] = (pS, pdS)
                            S.op("pe", lambda e, pS=pS, Ksrc=Ksrc, head=head: e.matmul(pS[:, :n], Ksrc[:, :], Q[:, head, T0:T0 + n],
                                                                                      start=True, stop=True),
                                 reads=[dK, dQ[tq][ci]], writes=[pdS])

                        LA = 2
                        for k0 in range(min(LA, nk)):
                            issue_S(k0)
                        for kt in range(nk):
                            Ksrc, dK, Vsrc, dV = key_tiles[kt]
                            pS, pdS = sbank.pop(kt)
                            Eb, dE = ering.next()
                            S.op("act", lambda e, pS=pS, Eb=Eb: e.activation(out=Eb[:, :n], in_=pS[:, :n], func=AF.Exp, scale=0.125),
                                 reads=[pdS], writes=[dE])
                            S.op("pe", lambda e, Eb=Eb, Vsrc=Vsrc, kt=kt, pO=pO, hf=hf, nk=nk: e.matmul(pO[:, :n], Vsrc[:, hf * 64:hf * 64 + 128], Eb[:, :n],
                                                                                   start=(kt == 0), stop=(kt == nk - 1)),
                                 reads=[dE, dV], writes=[pdO])
                            if kt + LA < nk:
                                issue_S(kt + LA)
                            if kt == min(3, nk - 1):
                                flush_fin()
                        sr = (64 if hf == 0 else 0)
                        ob, dob = obr.next()
                        rs_, drs = rsr.next()
                        S.op("act", lambda e, pO=pO, ob=ob, ps_=ps_: e.copy(out=ob[ps_, :n], in_=pO[ps_, :n]), reads=[pdO], writes=[dob])
                        S.op("act", lambda e, pO=pO, rs_=rs_, sr=sr: e.activation(out=rs_[sr:sr + 1, :n], in_=pO[sr:sr + 1, :n], func=AF.Ln),
                             reads=[pdO], writes=[drs])
                        S.op("act", lambda e, rs_=rs_, sr=sr: e.activation(out=rs_[sr:sr + 1, :n], in_=rs_[sr:sr + 1, :n], func=AF.Exp, scale=-1.0),
                             reads=[drs], writes=[drs])

                        def fin(ob=ob, dob=dob, rs_=rs_, drs=drs, sr=sr, ps_=ps_, tq=tq, head=head):
                            pbc, pdbc = ps_aux.next()
                            S.op("pe", lambda e: e.matmul(pbc[:, :n], onesrow[sr:sr + 1, :], rs_[sr:sr + 1, :n], start=True, stop=True),
                                 reads=[drs, d_const], writes=[pdbc])
                            S.op("dve", lambda e: e.tensor_tensor(out=dest[ps_, tq, T0:T0 + n], in0=ob[ps_, :n], in1=pbc[ps_, :n], op=ALU.mult),
                                 reads=[dob, pdbc], writes=[d_attn[head][ci]])
                        pending_fin.append(fin)

            ctx_keys = [(KTc[:, tt * 128:(tt + 1) * 128], d_KTc, Vxc[:, tt, :], d_Vxc) for tt in range(2)]
            lat_keys = [(KT[:, tt * 128:(tt + 1) * 128], d_KT[tt // 16], Vx[:, tt, :], d_Vx[tt // 16]) for tt in range(64)]
            if not last:
                attention(qC, d_q, 4, 0, CTX, ctx_keys, attnC)
            for c in range(4):
                attention(qT, d_q, c, c * 512, 512, ctx_keys + lat_keys, attnT)
            flush_fin()
            tap("attnT" + L, attnT, [128, 2, TL], BF16, [d_attn[h][c] for h in range(4) for c in range(4)])
            tap("attnC" + L, attnC, [128, 2, CTX], BF16, [d_attn[h][4] for h in range(4)])
            phase_end(b3_names)
            if stop_after == "B3" + L:
                return True

            XN2 = al("XN2", [128, 8, TL], BF16, top=True)
            xnC = al("xnC", [128, 8, CTX], BF16, top=True)
            d_xn2 = [[gdep("xn2_%d_%d" % (c, k)) for k in range(8)] for c in range(5)]
            A_views["wfi0"] = al("wfi0", [128, 8, 1024], BF16, top=True)
            fisrc0 = w_fi_d.ap()[l].rearrange("(k p) o -> p k o", p=128)
            S.dma("pool", A_views["wfi0"][:, :, 0:512], fisrc0[:, :, 0:512], writes=[gdep("wfi0")])
            S.dma("pool", A_views["wfi0"][:, :, 512:1024], fisrc0[:, :, D_FF:D_FF + 512], writes=[gdep("wfi0")])
            yT = al("yT", [128, 2, 512], BF16)
            d_yT = gdep("yT")
            sqbC = al("sqbC", [128, 4, 512], BF16)
            d_sqC = [gdep("sqC%d" % k) for k in range(8)]
            rstdC = al("rstdC", [128, 512], F32)
            d_rstdC = gdep("rstdC")
            tmprC = Ring([(al("tmpC%d" % i, [128, 512], F32), gdep("tmpC%d" % i)) for i in range(3)])
            ntmpsC = (sqbC, d_sqC, rstdC, d_rstdC, tmprC)
            c1_names = ["wo_att", "wo_rest", "wf", "yT", "sqbC", "rstdC", "tmpC0", "tmpC1", "tmpC2", "catT", "catC", "attnT", "attnC", "ycT"]
            r2v = rcv2.ap()
            chunks = ([ctx_chunk] if not last else []) + lat_chunks
            def do_chunk_C1(chunk):
                s, X, dX, T0, n, ci = chunk
                isctx = (s == 1)
                cat = catC if isctx else catT
                att = attnC if isctx else attnT
                if isctx:
                    ysrc, dys = ycT, d_ycT
                else:
                    S.dma("sp", yT[:, :, :n], r2v[:, bass.ds(jr * TL + T0, n)].rearrange("(k p) t -> p k t", p=128), reads=[d_rcv2], writes=[d_yT])
                    ysrc, dys = yT, d_yT
                for ot in range(2):
                    pb, pd = ps_main.next()
                    for k2 in range(2):
                        S.op("pe", lambda e, k2=k2, ot=ot, pb=pb, ysrc=ysrc: e.matmul(pb[:, :n], wf[:, k2, ot * 128:(ot + 1) * 128], ysrc[:, k2, :n],
                                                                                       start=(k2 == 0), stop=(k2 == 1)),
                             reads=[dys, d_wf], writes=[pd])
                    S.op("act", lambda e, ot=ot, pb=pb, cat=cat: e.copy(out=cat[:, ot, T0:T0 + n], in_=pb[:, :n]), reads=[pd], writes=[d_cat[ot][ci]])
                for ot in range(8):
                    pb, pd = ps_main.next()
                    for h in range(2):
                        S.op("pe", lambda e, h=h, ot=ot, pb=pb, att=att: e.matmul(pb[:, :n], wo_att[:, h, ot * 128:(ot + 1) * 128], att[:, h, T0:T0 + n],
                                                                                   start=(h == 0), stop=False),
                             reads=[d_attn[h][ci], d_attn[2 + h][ci], d_wo], writes=[pd])
                    for r in range(6):
                        S.op("pe", lambda e, r=r, ot=ot, pb=pb, cat=cat: e.matmul(pb[:, :n], wo_rest[:, r, ot * 128:(ot + 1) * 128], cat[:, r, T0:T0 + n],
                                                                                   start=False, stop=(r == 5)),
                             reads=[d_cat[r][ci], d_wo], writes=[pd])
                    S.op("dve", lambda e, ot=ot, pb=pb: e.scalar_tensor_tensor(out=X[:, ot, T0:T0 + n], in0=pb[:, :n], scalar=modT[:, 16 + ot, s:s + 1],
                                                                               in1=X[:, ot, T0:T0 + n], op0=ALU.mult, op1=ALU.add),
                         reads=[pd, d_mod[1], dX[ot]], writes=[dX[ot]])
                xdst = xnC if isctx else XN2[:, :, T0:T0 + n]
                norm_mod(chunk, 1, xdst, d_xn2[ci], ntmpsC, mods)

            for chunk in chunks:
                do_chunk_C1(chunk)
            tap("x1_" + L, xT, [128, 8, TL], F32, [d_x[k][c] for k in range(8) for c in range(4)])
            tap("h1_" + L, hT, [128, 8, CTX], F32, d_h)
            tap("cat" + L, catT, [128, 6, TL], BF16, [d_cat[r][c] for r in range(6) for c in range(4)])
            phase_end(c1_names)
            if stop_after == "C1" + L:
                return True

            wfi = [A_views["wfi0"], al("wfi1", [128, 8, 1024], BF16)]
            wfo = [al("wfo%d" % i, [128, 4, D], BF16) for i in range(2)]
            d_wfi = [gdep("wfi%d" % i) for i in range(2)]
            d_wfo = [gdep("wfo%d" % i) for i in range(2)]
            hbr = Ring([(al("hb%d" % i, [128, 4, 512], BF16), gdep("hb%d" % i)) for i in range(2)])
            sar = Ring([(al("sa%d" % i, [128, 512], F32), gdep("sa%d" % i)) for i in range(2)])
            c2_names = ["wfi0", "wfi1", "wfo0", "wfo1", "hb0", "hb1", "sa0", "sa1", "XN2", "xnC"]
            if not last:
                A_views["wm0"] = al("wm0", [128, 8, 512], BF16)
                A_views["wm1"] = al("wm1", [128, 8, 512], BF16)
                c2_names = c2_names + ["wm0", "wm1"]
                stage_M(l + 1, "begin")
            fisrc = w_fi_d.ap()[l].rearrange("(k p) o -> p k o", p=128)
            groups = [(0, 4), (4, 4), (8, 4), (12, 4), (16, 4), (20, 2)]
            for gi, (h0, gw) in enumerate(groups):
                sl = gi % 2
                if gi > 0:
                    S.dma("pool", wfi[sl][:, :, 0:gw * 128], fisrc[:, :, h0 * 128:(h0 + gw) * 128], writes=[d_wfi[sl]])
                    S.dma("pool", wfi[sl][:, :, 512:512 + gw * 128], fisrc[:, :, D_FF + h0 * 128:D_FF + (h0 + gw) * 128], writes=[d_wfi[sl]])
                S.dma("pool", wfo[sl][:, 0:gw, :], w_fo_d.ap()[l, h0 * 128:(h0 + gw) * 128, :].rearrange("(c p) o -> p c o", p=128), writes=[d_wfo[sl]])
                def do_chunk_C2(chunk, sl=sl, gw=gw):
                    s, X, dX, T0, n, ci = chunk
                    isctx = (s == 1)
                    xsrc_ = xnC if isctx else XN2[:, :, T0:T0 + n]
                    hb, dhb = hbr.next()
                    for hc in range(gw):
                        pa, pda = ps_main.next()
                        pg, pdg = ps_main.next()
                        for k in range(8):
                            S.op("pe", lambda e, k=k, hc=hc, pa=pa, sl=sl, xsrc_=xsrc_: e.matmul(pa[:, :n], wfi[sl][:, k, hc * 128:(hc + 1) * 128], xsrc_[:, k, :n],
                                                                                                  start=(k == 0), stop=(k == 7)),
                                 reads=[d_wfi[sl], d_xn2[ci][k]], writes=[pda])
                        for k in range(8):
                            S.op("pe", lambda e, k=k, hc=hc, pg=pg, sl=sl, xsrc_=xsrc_: e.matmul(pg[:, :n], wfi[sl][:, k, 512 + hc * 128:512 + (hc + 1) * 128],
                                                                                                  xsrc_[:, k, :n], start=(k == 0), stop=(k == 7)),
                                 reads=[d_wfi[sl], d_xn2[ci][k]], writes=[pdg])
                        sa, dsa = sar.next()
                        S.op("act", lambda e, pa=pa, sa=sa: e.activation(out=sa[:, :n], in_=pa[:, :n], func=AF.Silu), reads=[pda], writes=[dsa])
                        S.op("dve", lambda e, pg=pg, sa=sa, hb=hb, hc=hc: e.tensor_tensor(out=hb[:, hc, :n], in0=pg[:, :n], in1=sa[:, :n], op=ALU.mult),
                             reads=[pdg, dsa], writes=[dhb])
                    for ot in range(8):
                        po, pdo = ps_acc.next()
                        for hc in range(gw):
                            S.op("pe", lambda e, hc=hc, ot=ot, po=po, sl=sl, hb=hb: e.matmul(po[:, :n], wfo[sl][:, hc, ot * 128:(ot + 1) * 128], hb[:, hc, :n],
                                                                                              start=(hc == 0), stop=(hc == gw - 1)),
                                 reads=[d_wfo[sl], dhb], writes=[pdo])
                        S.op("dve", lambda e, ot=ot, po=po, X=X, T0=T0, s=s: e.scalar_tensor_tensor(out=X[:, ot, T0:T0 + n], in0=po[:, :n],
                                                                                                    scalar=modT[:, 40 + ot, s:s + 1], in1=X[:, ot, T0:T0 + n],
                                                                                                    op0=ALU.mult, op1=ALU.add),
                             reads=[pdo, d_mod[1], dX[ot]], writes=[dX[ot]])

                for chunk in chunks:
                    do_chunk_C2(chunk)
                if not last:
                    stage_M(l + 1, 2 * gi)
                    stage_M(l + 1, 2 * gi + 1)
            if not last:
                stage_M(l + 1, "end1")
                stage_M(l + 1, "end2")
            tap("x2_" + L, xT, [128, 8, TL], F32, [d_x[k][c] for k in range(8) for c in range(4)])
            if last and stop_after is None:
                osrc = outT_d.ap().rearrange("(k p) t -> p k t", p=128)
                d_osem = Dep("osem")
                for c in range(4):
                    o = S.dma("sp", osrc[:, :, c * 512:(c + 1) * 512], xT[:, :, c * 512:(c + 1) * 512], reads=[d_x[k][c] for k in range(8)],
                              out_side=True, sem_dep=d_osem)
                    final_ops.append(o)
                out_done[0] = True
            phase_end(c2_names)
            if stop_after == "C2" + L:
                return True
            return False

        for l in range(DEPTH):
            if do_layer(l):
                break

        if not out_done[0]:
            osrc = outT_d.ap().rearrange("(k p) t -> p k t", p=128)
            d_osem = Dep("osem")
            for k in range(8):
                o = S.dma("sp", osrc[:, k, :], xT[:, k, :], reads=d_x[k], out_side=True, sem_dep=d_osem)
                final_ops.append(o)
        block = st.enter_context(nc.Block())
        S.emit(block, final_waits=final_ops)
        build_program.peak_words = A.peak
    return nc, tap_out


def _consts():
    f = np.float32
    cm = np.zeros((128, 9, 128), f)
    cm[:, 0, :] = 1.0 / 1024
    for b in range(2):
        cm[b * 64:(b + 1) * 64, 1, b * 64:(b + 1) * 64] = 1.0 / 64
    for k in range(128):
        cm[k, 2, k ^ 1] = 1.0
    cm[:, 3, :] = 1.0 / 256
    cm[:, 7, :] = np.eye(128)
    t1 = np.arange(64)[:, None].astype(np.float64)
    k1 = np.arange(64)[None, :].astype(np.float64)
    C = np.cos(2 * np.pi * t1 * k1 / 64)
    Sn = np.sin(2 * np.pi * t1 * k1 / 64)
    R1m = np.zeros((128, 128))
    R1m[0:64, 0:64] = C
    R1m[64:128, 0:64] = Sn
    R1m[0:64, 64:128] = -Sn
    R1m[64:128, 64:128] = C
    cm[:, 4, :] = R1m
    t2 = np.arange(128)[:, None].astype(np.float64)
    k2 = np.arange(128)[None, :].astype(np.float64)
    nrm = 1.0 / np.sqrt(8192.0 * 64.0)
    cm[:, 5, :] = np.cos(2 * np.pi * t2 * k2 / 128) * nrm
    cm[:, 6, :] = np.sin(2 * np.pi * t2 * k2 / 128) * nrm
    cs = np.zeros((256, 512))
    cc = np.arange(64)[:, None].astype(np.float64)
    m = np.arange(64)[None, :].astype(np.float64)
    for h in range(4):
        cs[h * 64:(h + 1) * 64, h * 64:(h + 1) * 64] = np.cos(2 * np.pi * cc * m / 64)
        cs[h * 64:(h + 1) * 64, 256 + h * 64:256 + (h + 1) * 64] = -np.sin(2 * np.pi * cc * m / 64)
    csblk = np.ascontiguousarray(cs.reshape(2, 128, 512).transpose(1, 0, 2)).astype(f)
    k1r = np.arange(64)[None, :].astype(np.float64)
    twr = np.cos(2 * np.pi * t2 * k1r / 8192)
    twi = np.sin(2 * np.pi * t2 * k1r / 8192)
    tw = np.stack([np.tile(twr, (1, 8)), np.tile(twi, (1, 8))], axis=1).astype(f)
    t = np.arange(256)[:, None].astype(np.float64)
    k = np.arange(256)[None, :].astype(np.float64)
    n2 = 1.0 / np.sqrt(256.0 * 64.0)
    c256 = np.cos(2 * np.pi * t * k / 256) * n2
    s256 = np.sin(2 * np.pi * t * k / 256) * n2
    cs256 = np.stack([c256.reshape(2, 128, 256).transpose(1, 0, 2), s256.reshape(2, 128, 256).transpose(1, 0, 2)], axis=1).astype(f)
    return dict(cmat=cm, csblk=csblk, tw=tw, cs256=np.ascontiguousarray(cs256))


def _rope_tables(t0):
    tpos = np.arange(t0, t0 + TL)
    row = (tpos // 64).astype(np.float32)
    col = (tpos % 64).astype(np.float32)
    inv_freq = (np.float32(10000.0) ** (-np.arange(16, dtype=np.float32) / np.float32(16))).astype(np.float32)
    ang = np.concatenate([row[:, None] * inv_freq, col[:, None] * inv_freq], axis=-1).astype(np.float32)
    cos = np.cos(ang).astype(np.float32)
    sin = np.sin(ang).astype(np.float32)
    p = np.arange(128)
    d = p % 64
    i = d // 2
    sign = np.where(d % 2 == 0, -1.0, 1.0).astype(np.float32)
    rc = np.ascontiguousarray(cos[:, i].T)
    rs = np.ascontiguousarray((sin[:, i] * sign[None, :]).T)
    return rc, rs


def _invcnt(tglob, n, win):
    lo = np.clip(tglob - win // 2, 0, n)
    hi = np.clip(tglob - win // 2 + win, 0, n)
    return (1.0 / (hi - lo)).astype(np.float32)


def _percore(j):
    pc = np.zeros((128, 66), np.float32)
    pc[:, 0] = 0.0 if j == 0 else 1.0
    pc[:, 1] = 0.0 if j == 3 else 1.0
    wins = {(0, 0): 2, (0, 1): 4, (1, 0): 8, (1, 1): 16}
    for tile in range(2):
        for half in range(2):
            win = wins[(tile, half)]
            ps = slice(half * 64, (half + 1) * 64)
            tl = np.concatenate([np.arange(j * TL, j * TL + 8), np.arange((j + 1) * TL - 8, (j + 1) * TL)])
            pc[ps, 2 + tile * 16:2 + (tile + 1) * 16] = _invcnt(tl, SEQ, win)[None, :]
            tc = np.concatenate([np.arange(0, 8), np.arange(CTX - 8, CTX)])
            pc[ps, 34 + tile * 16:34 + (tile + 1) * 16] = _invcnt(tc, CTX, win)[None, :]
    return pc


def _params(inp):
    P = np.zeros((DEPTH, 128, NP_COLS), np.float32)
    for l in range(DEPTH):
        P[l, :, 0:8] = inp["g_norm1"][l].reshape(8, 128).T
        P[l, :, 8:16] = inp["g_norm2"][l].reshape(8, 128).T
        P[l, :, 16:64] = inp["b_mod"][l].reshape(48, 128).T
        P[l, :, 64] = np.tile(inp["q_norm_g"][l], 2)
        P[l, :, 65] = np.tile(inp["k_norm_g"][l], 2)
        cw = inp["conv_dw_w"][l]
        for vt in range(2):
            P[l, :, 66 + vt * 31:66 + (vt + 1) * 31] = cw[:, vt * 128:(vt + 1) * 128].T
        P[l, :, 128:130] = inp["conv_dw_b"][l].reshape(2, 128).T
        P[l, :, 130:132] = inp["conv_ln_g"][l].reshape(2, 128).T
        P[l, :, 132:134] = inp["conv_ln_b"][l].reshape(2, 128).T
        P[l, :, 134:136] = inp["pool_scale"][l].reshape(2, 128).T
        P[l, 0:64, 136] = 1.0 / 2
        P[l, 64:128, 136] = 1.0 / 4
        P[l, 0:64, 137] = 1.0 / 8
        P[l, 64:128, 137] = 1.0 / 16
    return P


_QPERM = np.concatenate([np.arange(0, 64), np.arange(128, 192), np.arange(64, 128), np.arange(192, 256), np.arange(256, D_IN)])


def prep_inputs(inp):
    inp = {k: np.asarray(v) for k, v in inp.items()}
    cst = _consts()
    params = _params(inp)
    w_in_p = np.ascontiguousarray(inp["w_in"][:, :, _QPERM])
    shared = dict(params=params, w_mod=inp["w_mod"], w_in=w_in_p, w_fourier=inp["w_fourier"], w_conv_pw=inp["w_conv_pw"],
                  w_pool=inp["w_pool"], w_out=inp["w_out"], w_ffn_in=inp["w_ffn_in"], w_ffn_out=inp["w_ffn_out"], **cst)
    maps = []
    for i in range(8):
        b, j = i // 4, i % 4
        m = dict(shared)
        m["xT"] = np.ascontiguousarray(inp["x"][b, j * TL:(j + 1) * TL, :].T)
        m["ctxT"] = np.ascontiguousarray(inp["ctx"][b].T)
        ccv = np.zeros((128, 16), np.float32)
        ccv[:, 0::2] = inp["c"][b].reshape(8, 128).T
        ccv[:, 1::2] = inp["c_ctx"].reshape(8, 128).T
        m["cc"] = ccv
        rc, rs = _rope_tables(j * TL)
        m["ropeC"] = rc
        m["ropeS"] = rs
        m["percore"] = _percore(j)
        maps.append(m)
    return maps


_NC_CACHE = {}


def kernel(**inputs):
    maps = prep_inputs(inputs)
    if "nc" not in _NC_CACHE:
        _NC_CACHE["nc"] = build_program()[0]
    nc = _NC_CACHE["nc"]
    res = run_bass_kernel_spmd(nc, maps, core_ids=list(range(8)))
    out = np.zeros((2, SEQ, D), np.float32)
    for i in range(8):
        b, j = i // 4, i % 4
        out[b, j * TL:(j + 1) * TL, :] = res.results[i]["outT"].T
    return out
```

```python
import contextlib
import numpy as np
import concourse.bass as bass
import concourse.mybir as mybir
from concourse.bass_utils import run_bass_kernel_spmd

F32 = mybir.dt.float32
BF16 = mybir.dt.bfloat16
AF = mybir.ActivationFunctionType
ALU = mybir.AluOpType

D = 1024
SEQ = 8192
TL = 2048
CTX = 256
DEPTH = 2
D_IN = 1536
D_FF = 2816
NHC = D_FF // 128
EPS = 1e-6
NKT = (CTX + SEQ) // 128
NP_COLS = 138
R1 = 832


class Dep:
    __slots__ = ("name", "w", "r", "sem_in", "cnt_in", "sem_out", "cnt_out")

    def __init__(self, name=""):
        self.name = name
        self.w = None
        self.r = []
        self.sem_in = None
        self.cnt_in = 0
        self.sem_out = None
        self.cnt_out = 0


class Op:
    __slots__ = ("eng", "fn", "deps", "alldeps", "signaled", "sigidx", "dsem", "dval", "name", "dinc", "seq", "cost", "lat", "seg", "eidx", "pend", "tmin")

    def __init__(self, eng, fn, name=""):
        self.eng = eng
        self.fn = fn
        self.deps = []
        self.signaled = False
        self.sigidx = None
        self.dsem = None
        self.dval = 0
        self.dinc = 16
        self.seq = 0
        self.cost = 0.0
        self.lat = 0.0
        self.seg = 0
        self.eidx = 0
        self.alldeps = []
        self.pend = None
        self.tmin = 0.0
        self.name = name


ENGS = ["pe", "act", "dve", "pool", "sp"]


class Sched:
    def __init__(self, nc, stack):
        self.nc = nc
        self.stack = stack
        self.ops = {e: [] for e in ENGS}
        self.esem = {e: stack.enter_context(nc.semaphore("es_" + e)) for e in ENGS}
        self.nsem = len(ENGS)
        self.pending_dma = []
        self.cc_sems = {}
        self.seg = 0
        self.ecount = 0
        self.reorder = True

    def new_sem(self, name):
        self.nsem += 1
        return self.stack.enter_context(self.nc.semaphore("%s_%d" % (name.replace(".", "_"), self.nsem)))

    def _collect(self, o, reads, writes, extra):
        deps = []
        seen = set()

        def add(d):
            if d is None or d is o or id(d) in seen:
                return
            seen.add(id(d))
            deps.append(d)

        for t in reads:
            add(t.w)
        for t in writes:
            add(t.w)
            for r in t.r:
                add(r)
        for d in extra:
            add(d)
        o.alldeps = deps
        o.deps = deps
        for t in reads:
            t.r.append(o)
        for t in writes:
            t.w = o
            t.r = []

    DEFCOST = {"pe": 0.25, "act": 0.6, "dve": 0.6, "pool": 1.2, "sp": 0.1}

    def _register(self, o):
        o.seg = self.seg
        o.eidx = self.ecount
        self.ecount += 1
        self.ops[o.eng].append(o)

    def op(self, eng, fn, reads=(), writes=(), extra=(), name="", cost=None):
        o = Op(eng, fn, name)
        o.cost = self.DEFCOST[eng] if cost is None else cost
        o.lat = o.cost
        self._collect(o, reads, writes, extra)
        self._register(o)
        return o

    def dma(self, q, out_ap, in_ap, reads=(), writes=(), sem_dep=None, out_side=False, extra=(), name="", tmin=0.0):
        if sem_dep is None:
            sem_dep = (reads[0] if out_side else writes[0])
        if out_side:
            if sem_dep.sem_out is None:
                sem_dep.sem_out = self.new_sem("do_" + sem_dep.name)
            sem_dep.cnt_out += 16
            dsem, dval = sem_dep.sem_out, sem_dep.cnt_out
        else:
            if sem_dep.sem_in is None:
                sem_dep.sem_in = self.new_sem("di_" + sem_dep.name)
            sem_dep.cnt_in += 16
            dsem, dval = sem_dep.sem_in, sem_dep.cnt_in

        def fn(eng, out_ap=out_ap, in_ap=in_ap):
            return eng.dma_start(out=out_ap, in_=in_ap)

        o = Op(q, fn, name)
        o.dsem, o.dval = dsem, dval
        o.cost, o.lat = 0.1, 6.0
        o.tmin = tmin
        self._collect(o, reads, writes, extra)
        o.alldeps = [d for d in o.alldeps if not (d.dsem is not None and d.dsem is dsem)]
        self._register(o)
        self.pending_dma.append(o)
        return o

    def collective(self, fn, reads=(), writes=(), extra=(), name="cc"):
        o = Op("pool", fn, name)
        if name not in self.cc_sems:
            self.cc_sems[name] = [self.new_sem("cc_" + name), 0]
        self.cc_sems[name][1] += 1
        o.dsem = self.cc_sems[name][0]
        o.dval = self.cc_sems[name][1]
        o.dinc = 1
        o.cost, o.lat = 0.5, 40.0
        self._collect(o, reads, writes, extra)
        self._register(o)
        return o

    def barrier(self):
        pend = list(self.pending_dma)
        self.pending_dma = []
        for e in ENGS:
            o = Op(e, None, "barrier")
            o.pend = pend
            o.seg = self.seg
            o.eidx = self.ecount
            self.ops[e].append(o)
        self.ecount += 1
        self.seg += 1

    def schedule(self):
        W = 64
        nseg = self.seg + 1
        segs = [{e: [] for e in ENGS} for _ in range(nseg)]
        bars = [{e: None for e in ENGS} for _ in range(nseg)]
        for e in ENGS:
            for o in self.ops[e]:
                if o.fn is None:
                    bars[o.seg][e] = o
                else:
                    segs[o.seg][e].append(o)
        new_ops = {e: [] for e in ENGS}
        for si in range(nseg):
            lists = segs[si]
            if self.reorder:
                finish = {}
                done = set()
                etime = {e: 0.0 for e in ENGS}
                remaining = {e: list(lists[e]) for e in ENGS}
                out = {e: [] for e in ENGS}
                total = sum(len(v) for v in remaining.values())
                while total:
                    best = None
                    for e in ENGS:
                        rem = remaining[e]
                        if not rem:
                            continue
                        seen_dma = False
                        for idx in range(min(W, len(rem))):
                            o = rem[idx]
                            if o.dsem is not None:
                                if seen_dma:
                                    continue
                                seen_dma = True
                            ready = o.tmin
                            ok = True
                            for d in o.alldeps:
                                if d.seg != si or d.fn is None:
                                    continue
                                if id(d) not in done:
                                    ok = False
                                    break
                                f = finish[id(d)] + (0.0 if d.eng == e else 0.15)
                                if f > ready:
                                    ready = f
                            if not ok:
                                continue
                            start = max(etime[e], ready)
                            key = (start, o.eidx)
                            if best is None or key < best[0]:
                                best = (key, e, idx, o)
                    if best is None:
                        raise RuntimeError("scheduler stuck")
                    (start, _), e, idx, o = best
                    remaining[e].pop(idx)
                    out[e].append(o)
                    done.add(id(o))
                    finish[id(o)] = start + o.lat
                    etime[e] = start + o.cost
                    total -= 1
                lists = out
            for e in ENGS:
                new_ops[e].extend(lists[e])
            if bars[si][ENGS[0]] is not None:
                lasts = [lists[e][-1] for e in ENGS if lists[e] and lists[e][-1].dsem is None]
                for e in ENGS:
                    b = bars[si][e]
                    b.alldeps = [d for d in lasts + b.pend if not (d.eng == e == "pe") or d.dsem is not None]
                    new_ops[e].append(b)
        self.ops = new_ops
        for e in ENGS:
            for i, o in enumerate(self.ops[e]):
                o.seq = i
        for e in ENGS:
            for o in self.ops[e]:
                bestd = {}
                for d in o.alldeps:
                    if d.dsem is not None:
                        key = ("s", id(d.dsem))
                        if key not in bestd or bestd[key].dval < d.dval:
                            bestd[key] = d
                    else:
                        key = ("e", d.eng)
                        if key not in bestd or bestd[key].seq < d.seq:
                            bestd[key] = d
                o.deps = list(bestd.values())

    def finalize(self, final_waits):
        for o in final_waits:
            if o.dsem is None:
                o.signaled = True
        for e in ENGS:
            for o in self.ops[e]:
                for d in o.deps:
                    if d.dsem is None:
                        if d.eng == "pe" and o.eng == "pe":
                            continue
                        d.signaled = True
        for e in ENGS:
            c = 0
            for o in self.ops[e]:
                if o.dsem is None and o.signaled:
                    c += 1
                    o.sigidx = c

    def emit(self, block, final_waits=()):
        self.schedule()
        self.finalize(final_waits)
        esem = self.esem

        def run(ename, eng):
            waited = {}
            for o in self.ops[ename]:
                for d in o.deps:
                    if d.dsem is not None:
                        key, val, sem = id(d.dsem), d.dval, d.dsem
                    else:
                        if d.eng == "pe" and ename == "pe":
                            continue
                        key, val, sem = d.eng, d.sigidx, esem[d.eng]
                    if waited.get(key, 0) >= val:
                        continue
                    waited[key] = val
                    eng.wait_ge(sem, val)
                if o.fn is None:
                    continue
                ins = o.fn(eng)
                if o.dsem is not None:
                    ins.then_inc(o.dsem, o.dinc)
                elif o.sigidx:
                    ins.then_inc(esem[ename], 1)
            if ename == "sp":
                fin = {}
                for o in final_waits:
                    if o.dsem is not None:
                        key, sem, val = id(o.dsem), o.dsem, o.dval
                    else:
                        key, sem, val = o.eng, esem[o.eng], o.sigidx
                    if key not in fin or fin[key][1] < val:
                        fin[key] = (sem, val)
                for sem, val in fin.values():
                    eng.wait_ge(sem, val)

        block.tensor(lambda eng: run("pe", eng))
        block.scalar(lambda eng: run("act", eng))
        block.vector(lambda eng: run("dve", eng))
        block.gpsimd(lambda eng: run("pool", eng))
        block.sync(lambda eng: run("sp", eng))


class Ring:
    def __init__(self, items):
        self.items = items
        self.i = 0

    def next(self):
        it = self.items[self.i % len(self.items)]
        self.i += 1
        return it


ARENA_WORDS = 52736


class Arena:
    def __init__(self, base_ap):
        self.base = base_ap
        self.free_list = [(0, ARENA_WORDS)]
        self.live = {}
        self.peak = 0

    def alloc(self, name, shape, dtype, top=False):
        elems = 1
        for d in shape[1:]:
            elems *= d
        esz = 4 if dtype == F32 else 2
        words = (elems * esz + 3) // 4
        words = (words + 15) // 16 * 16
        order = range(len(self.free_list) - 1, -1, -1) if top else range(len(self.free_list))
        for i in order:
            o, n = self.free_list[i]
            if n >= words:
                if top:
                    off = o + n - words
                    if n == words:
                        self.free_list.pop(i)
                    else:
                        self.free_list[i] = (o, n - words)
                else:
                    off = o
                    if n == words:
                        self.free_list.pop(i)
                    else:
                        self.free_list[i] = (o + words, n - words)
                break
        else:
            raise RuntimeError("arena full allocating %s (%d words); free=%s" % (name, words, self.free_list))
        v = self.base[0:shape[0], off:off + words]
        if dtype != F32:
            v = v.bitcast(dtype)
        v = v[:, 0:elems]
        if len(shape) == 3:
            v = v.rearrange("p (a b) -> p a b", b=shape[2])
        elif len(shape) == 4:
            v = v.rearrange("p (a b c) -> p a b c", b=shape[2], c=shape[3])
        elif len(shape) == 5:
            v = v.rearrange("p (a b c d) -> p a b c d", b=shape[2], c=shape[3], d=shape[4])
        self.live[name] = (off, words)
        used = ARENA_WORDS - sum(n for _, n in self.free_list)
        self.peak = max(self.peak, used)
        return v

    def free(self, *names):
        for name in names:
            off, words = self.live.pop(name)
            self.free_list.append((off, words))
        self.free_list.sort()
        merged = []
        for o, n in self.free_list:
            if merged and merged[-1][0] + merged[-1][1] == o:
                merged[-1] = (merged[-1][0], merged[-1][1] + n)
            else:
                merged.append((o, n))
        self.free_list = merged


def build_program(taps=(), stop_after=None):
    nc = bass.Bass("TRN2", target_bir_lowering=False)
    dt = nc.dram_tensor

    def ein(name, shape, dtype=F32):
        return dt(name, list(shape), dtype, kind="ExternalInput")

    xT_d = ein("xT", [D, TL])
    ctxT_d = ein("ctxT", [D, CTX])
    cc_d = ein("cc", [128, 16])
    ropeC_d = ein("ropeC", [128, TL])
    ropeS_d = ein("ropeS", [128, TL])
    percore_d = ein("percore", [128, 66])
    params_d = ein("params", [DEPTH, 128, NP_COLS])
    w_mod_d = ein("w_mod", [DEPTH, D, 6 * D])
    w_in_d = ein("w_in", [DEPTH, D, D_IN])
    w_f_d = ein("w_fourier", [DEPTH, 256, 256])
    w_pw_d = ein("w_conv_pw", [DEPTH, 256, 256])
    w_pool_d = ein("w_pool", [DEPTH, 4, 64, 64])
    w_out_d = ein("w_out", [DEPTH, D, D])
    w_fi_d = ein("w_ffn_in", [DEPTH, D, 2 * D_FF])
    w_fo_d = ein("w_ffn_out", [DEPTH, D_FF, D])
    cmat_d = ein("cmat", [128, 9, 128])
    csblk_d = ein("csblk", [128, 2, 512])
    tw_d = ein("tw", [128, 2, 512])
    c256_d = ein("cs256", [128, 2, 2, 256])
    outT_d = dt("outT", [D, TL], F32, kind="ExternalOutput")

    RC = [R1, R1, R1, R1]
    snde = dt("snde", [512, 32], BF16)
    rcve = dt("rcve", [4 * 512, 32], BF16)
    sndc = [dt("sndc%d" % c, [RC[c], 512], BF16) for c in range(4)]
    rcvc = [dt("rcvc%d" % c, [4 * RC[c], 512], BF16) for c in range(4)]
    snd2 = dt("snd2", [64, SEQ], BF16)
    rcv2 = dt("rcv2", [4 * 64, SEQ], BF16)
    RG = [[0, 1, 2, 3], [4, 5, 6, 7]]

    tap_out = {}
    final_ops = []
    stopped = [False]
    out_done = [False]

    with contextlib.ExitStack() as st:
        S = Sched(nc, st)
        pid = nc.partition_id()
        jr = pid % 4
        arena_t = st.enter_context(nc.sbuf_tensor("arena", [128, ARENA_WORDS], F32))
        A = Arena(arena_t[:, :])
        al = A.alloc

        psb = [st.enter_context(nc.psum_tensor("ps%d" % i, [128, 512], F32)) for i in range(8)]
        psd = [Dep("ps%d" % i) for i in range(8)]
        ps_main = Ring([(psb[i], psd[i]) for i in range(0, 4)])
        ps_aux = Ring([(psb[i], psd[i]) for i in range(4, 6)])
        ps_acc = Ring([(psb[i], psd[i]) for i in range(6, 8)])

        xT = al("xT", [128, 8, TL], F32)
        hT = al("hT", [128, 8, CTX], F32)
        d_x = [[Dep("x%d_%d" % (k, c)) for c in range(4)] for k in range(8)]
        d_h = [Dep("h%d" % k) for k in range(8)]
        cmat = al("cmat", [128, 9, 128], BF16)
        onesrow = al("onesrow", [128, 128], F32)
        csblk = al("csblk", [128, 2, 512], BF16)
        cs256 = al("cs256", [128, 2, 2, 256], BF16)
        percore = al("percore", [128, 66], F32)
        ccs = al("ccs", [128, 16], F32)
        scb = al("scb", [128, 8, 2], BF16)
        d_const = Dep("const")
        d_pc = Dep("percore")
        d_sc = Dep("sc")
        params = al("params", [128, NP_COLS], F32)
        d_params = Dep("params")
        modT2 = al("modT", [128, 2, 48, 2], F32)
        d_mod2 = [Dep("modT0"), Dep("modT1")]
        gm2 = al("gm", [128, 2, 2, 8, 2], F32)
        d_gm2 = [Dep("gm0"), Dep("gm1")]
        d_snde, d_rcve = Dep("snde"), Dep("rcve")
        d_sndc = [Dep("sndc%d" % i) for i in range(4)]
        d_rcvc = [Dep("rcvc%d" % i) for i in range(4)]
        d_snd2, d_rcv2 = Dep("snd2"), Dep("rcv2")

        ONES_MEAN, BLK64, SWAPP, ONESLN, R1M, C128S, S128S = range(7)

        xsrc = xT_d.ap().rearrange("(k p) t -> p k t", p=128)
        d_xload = [Dep("xload%d" % c) for c in range(4)]
        for c in range(4):
            S.dma("sp", xT[:, :, c * 512:(c + 1) * 512], xsrc[:, :, c * 512:(c + 1) * 512], writes=[d_x[k][c] for k in range(8)],
                  sem_dep=d_xload[c])
        hsrc = ctxT_d.ap().rearrange("(k p) t -> p k t", p=128)
        S.dma("sp", hT, hsrc, writes=d_h, sem_dep=Dep("hload"))
        S.dma("pool", cmat, cmat_d.ap(), writes=[d_const])
        S.dma("pool", csblk, csblk_d.ap(), writes=[d_const])
        S.dma("pool", cs256, c256_d.ap(), writes=[d_const])
        S.dma("sp", percore, percore_d.ap(), writes=[d_pc])
        S.dma("sp", ccs, cc_d.ap(), writes=[d_sc])
        cvec = al("cvec", [128, 4], F32)
        S.op("pool", lambda e: e.memset(cvec, EPS), writes=[d_const])
        S.op("pool", lambda e: e.memset(onesrow, 1.0), writes=[d_const])
        S.op("act", lambda e: e.activation(out=scb.rearrange("p k s -> p (k s)"), in_=ccs, func=AF.Silu),
             reads=[d_sc], writes=[d_sc])

        DEPS = {}

        def gdep(name):
            if name not in DEPS:
                DEPS[name] = Dep(name)
            return DEPS[name]

        def tap(name, ap_sb, shape, dtype, reads):
            if name not in taps:
                return
            t = dt("tap_" + name, list(shape), dtype, kind="ExternalOutput")
            o = S.dma("sp", t.ap(), ap_sb, reads=reads, out_side=True, sem_dep=Dep("tap" + name))
            final_ops.append(o)
            tap_out[name] = t

        def phase_end(names):
            S.barrier()
            A.free(*names)

        lat_chunks = [(0, xT, [d_x[k][c] for k in range(8)], c * 512, 512, c) for c in range(4)]
        ctx_chunk = (1, hT, d_h, 0, CTX, 4)

        def norm_mod(chunk, which, xn_ap, d_xn, tmps, mods):
            s, X, dX, T0, n, ci = chunk
            sqb, d_sq, rstd, d_rstd, tmpr = tmps
            modT, gm, d_mod, d_gm = mods
            pb, pd = ps_aux.next()
            for k in range(8):
                S.op("act", lambda e, k=k: e.activation(out=sqb[:, k % 4, :n], in_=X[:, k, T0:T0 + n], func=AF.Square),
                     reads=[dX[k]], writes=[d_sq[k % 4]])
                S.op("pe", lambda e, k=k: e.matmul(pb[:, :n], cmat[:, ONES_MEAN, :], sqb[:, k % 4, :n], start=(k == 0), stop=(k == 7)),
                     reads=[d_sq[k % 4], d_const], writes=[pd])
            S.op("act", lambda e: e.activation(out=rstd[:, :n], in_=pb[:, :n], func=AF.Ln, bias=cvec[:, 0:1]), reads=[pd, d_const], writes=[d_rstd])
            S.op("act", lambda e: e.activation(out=rstd[:, :n], in_=rstd[:, :n], func=AF.Exp, scale=-0.5), reads=[d_rstd], writes=[d_rstd])
            shift_chunk = 0 if which == 0 else 3
            for k in range(8):
                tb, td = tmpr.next()
                S.op("dve", lambda e, k=k, tb=tb: e.tensor_tensor(out=tb[:, :n], in0=X[:, k, T0:T0 + n], in1=rstd[:, :n], op=ALU.mult),
                     reads=[dX[k], d_rstd], writes=[td])
                S.op("act", lambda e, k=k, tb=tb: e.activation(out=xn_ap[:, k, :n], in_=tb[:, :n], func=AF.Identity,
                                                             bias=modT[:, shift_chunk * 8 + k, s:s + 1], scale=gm[:, which, k, s:s + 1]),
                     reads=[td, d_gm, d_mod], writes=[d_xn[k]])

        def stage_M(l, part):
            modT, gm, d_mod, d_gm = modT2[:, l % 2], gm2[:, l % 2], d_mod2[l % 2], d_gm2[l % 2]
            if part == "begin":
                S.dma("sp", params, params_d.ap()[l], writes=[d_params])
                return
            if part == "end":
                for which, sc_chunk, gcol in ((0, 1, 0), (1, 4, 8)):
                    for s_ in range(2):
                        S.op("dve", lambda e, which=which, sc_chunk=sc_chunk, gcol=gcol, s_=s_: e.scalar_tensor_tensor(
                            out=gm[:, which, :, s_], in0=modT[:, sc_chunk * 8:(sc_chunk + 1) * 8, s_], scalar=1.0,
                            in1=params[:, gcol:gcol + 8], op0=ALU.add, op1=ALU.mult),
                            reads=[d_mod, d_params], writes=[d_gm], cost=0.2)
                tap("modT%d" % l, modT, [128, 48, 2], F32, [d_mod])
                tap("gm%d" % l, gm, [128, 2, 8, 2], F32, [d_gm])
                return
            oc = part
            wm = [A_views["wm0"], A_views["wm1"]]
            d_wm = [gdep("wm%d" % i) for i in range(2)]
            wsrc = w_mod_d.ap()[l].rearrange("(k p) o -> p k o", p=128)
            sl = oc % 2
            S.dma("pool", wm[sl], wsrc[:, :, oc * 512:(oc + 1) * 512], writes=[d_wm[sl]])
            for o4 in range(4):
                o = oc * 4 + o4
                pb, pd = ps_aux.next()
                for k in range(8):
                    S.op("pe", lambda e, k=k, sl=sl, o4=o4, pb=pb: e.matmul(pb[:, 0:2], wm[sl][:, k, o4 * 128:(o4 + 1) * 128], scb[:, k, :],
                                                                          start=(k == 0), stop=(k == 7)),
                         reads=[d_wm[sl], d_sc], writes=[pd], cost=0.06)
                S.op("dve", lambda e, o=o, pb=pb: e.tensor_scalar(out=modT[:, o, :], in0=pb[:, 0:2], scalar1=params[:, 16 + o:17 + o], scalar2=None,
                                                                 op0=ALU.add),
                     reads=[pd, d_params], writes=[d_mod], cost=0.2)

        A_views = {}
        A_views["wm0"] = al("wm0", [128, 8, 512], BF16)
        A_views["wm1"] = al("wm1", [128, 8, 512], BF16)
        for part in ["begin"] + list(range(12)) + ["end"]:
            stage_M(0, part)
        phase_end(["wm0", "wm1"])

        def do_layer(l):
            last = (l == DEPTH - 1)
            L = "%d" % l
            modT, gm, d_mod, d_gm = modT2[:, l % 2], gm2[:, l % 2], d_mod2[l % 2], d_gm2[l % 2]
            mods = (modT, gm, d_mod, d_gm)
            ycT = al("ycT", [128, 2, CTX], BF16, top=True)
            d_ycT = gdep("ycT")
            qT = al("qT", [128, 4, TL], BF16, top=True)
            qC = al("qC", [128, 4, CTX], BF16, top=True)
            d_q = [[gdep("q%d_%d" % (t, c)) for c in range(5)] for t in range(2)]
            KTc = al("KTc", [128, CTX], BF16, top=True)
            d_KTc = gdep("KTc")
            Vxc = al("Vxc", [128, 2, 192], BF16, top=True)
            d_Vxc = gdep("Vxc")
            zc_tm = al("zctm", [128, 2, 512], BF16, top=True)
            d_zc = gdep("zc")
            glu = al("glu", [128, 2, TL + 30], BF16, top=True)
            gluC = al("gluC", [128, 2, CTX + 30], BF16, top=True)
            pu = al("pu", [128, 2, TL + 16], BF16, top=True)
            puC = al("puC", [128, 2, CTX + 16], BF16, top=True)
            d_glu = [gdep("glu%d" % i) for i in range(2)]
            d_gluC = [gdep("gluC%d" % i) for i in range(2)]
            d_pu = [gdep("pu%d" % i) for i in range(2)]
            d_puC = [gdep("puC%d" % i) for i in range(2)]
            S.op("pool", lambda e: e.memset(qT, 0.0), writes=[gdep("q%d_%d" % (t, c)) for t in range(2) for c in range(4)])
            if not last:
                S.op("pool", lambda e: e.memset(qC, 0.0), writes=[gdep("q%d_4" % t) for t in range(2)])
            S.op("pool", lambda e: e.memset(Vxc, 1.0), writes=[d_Vxc])
            if not last:
                S.op("pool", lambda e: e.memset(gluC, 0.0), writes=d_gluC)
                S.op("pool", lambda e: e.memset(puC, 0.0), writes=d_puC)

            diag = al("diag", [128, 62, 128], BF16, top=True)
            d_diag = gdep("diag")
            for idx in range(62):
                S.op("dve", lambda e, idx=idx: e.tensor_scalar(out=diag[:, idx, :], in0=cmat[:, 7, :], scalar1=params[:, 66 + idx:67 + idx], scalar2=None,
                                                               op0=ALU.mult),
                     reads=[d_const, d_params], writes=[d_diag])
            wpw = al("wpw", [128, 2, 256], BF16, top=True)
            wpool = al("wpool", [128, 2, 128], BF16, top=True)
            d_wcp = gdep("wcp")
            S.dma("pool", wpw, w_pw_d.ap()[l].rearrange("(k p) o -> p k o", p=128), writes=[d_wcp])
            S.op("pool", lambda e: e.memset(wpool, 0.0), writes=[d_wcp])
            for g in range(4):
                tile_, half = g // 2, g % 2
                S.dma("pool", wpool[half * 64:(half + 1) * 64, tile_, half * 64:(half + 1) * 64], w_pool_d.ap()[l, g], writes=[d_wcp],
                      reads=[])
            w_in = al("w_in", [128, 8, D_IN], BF16)
            d_win = gdep("w_in")
            wisrc = w_in_d.ap()[l].rearrange("(k p) o -> p k o", p=128)
            for c3 in range(3):
                S.dma("pool", w_in[:, :, c3 * 512:(c3 + 1) * 512], wisrc[:, :, c3 * 512:(c3 + 1) * 512], writes=[d_win])
            xn = al("xnA", [128, 8, 512], BF16)
            d_xn = [gdep("xnA%d" % k) for k in range(8)]
            sqb = al("sqbA", [128, 4, 512], BF16)
            d_sq = [gdep("sqA%d" % k) for k in range(8)]
            rstd = al("rstdA", [128, 512], F32)
            d_rstd = gdep("rstdA")
            tmpr = Ring([(al("tmpA%d" % i, [128, 512], F32), gdep("tmpA%d" % i)) for i in range(3)])
            ntmps = (sqb, d_sq, rstd, d_rstd, tmpr)
            ropeC = al("ropeC", [128, 512], F32)
            ropeS = al("ropeS", [128, 512], F32)
            d_rope = gdep("rope")
            sqq = al("sqq", [128, 512], BF16)
            d_sqq = gdep("sqq")
            rs2 = al("rs2", [128, 512], F32)
            d_rs2 = gdep("rs2")
            qg = al("qg", [128, 512], BF16)
            d_qg = gdep("qg")
            kloc = al("kloc", [128, 512], BF16)
            d_kloc = gdep("kloc")
            vsb = al("vsb", [128, 4, 192], BF16)
            d_vsb = gdep("vsb")
            S.op("pool", lambda e: e.memset(vsb, 1.0), writes=[d_vsb])
            ub = al("ub", [128, 2, 512], BF16)
            d_ub = [gdep("ub0"), gdep("ub1")]
            zsb = al("zsb", [128, 4, 512], BF16)
            d_zsb = gdep("zsb")
            a_names = ["w_in", "xnA", "sqbA", "rstdA", "tmpA0", "tmpA1", "tmpA2", "ropeC", "ropeS", "sqq", "rs2", "qg", "kloc", "vsb",
                       "ub", "zsb"]

            def qk_tile(chunk, ctile, dest_ap, d_dest, gcol, rope):
                s, X, dX, T0, n, ci = chunk
                pb, pd = ps_main.next()
                for k in range(8):
                    S.op("pe", lambda e, k=k: e.matmul(pb[:, :n], w_in[:, k, ctile * 128:(ctile + 1) * 128], xn[:, k, :n],
                                                       start=(k == 0), stop=(k == 7)),
                         reads=[d_win, d_xn[k]], writes=[pd])
                S.op("act", lambda e: e.activation(out=sqq[:, :n], in_=pb[:, :n], func=AF.Square), reads=[pd], writes=[d_sqq])
                p2, pd2 = ps_aux.next()
                S.op("pe", lambda e: e.matmul(p2[:, :n], cmat[:, BLK64, :], sqq[:, :n], start=True, stop=True),
                     reads=[d_sqq, d_const], writes=[pd2])
                S.op("act", lambda e: e.activation(out=rs2[:, :n], in_=p2[:, :n], func=AF.Ln, bias=cvec[:, 0:1]), reads=[pd2, d_const], writes=[d_rs2])
                S.op("act", lambda e: e.activation(out=rs2[:, :n], in_=rs2[:, :n], func=AF.Exp, scale=-0.5), reads=[d_rs2], writes=[d_rs2])
                if not rope:
                    for (psl, dap) in dest_ap:
                        S.op("dve", lambda e, psl=psl, dap=dap: e.scalar_tensor_tensor(out=dap, in0=pb[psl, :n], scalar=params[psl, gcol:gcol + 1],
                                                                                       in1=rs2[psl, :n], op0=ALU.mult, op1=ALU.mult),
                             reads=[pd, d_rs2, d_params], writes=[d_dest])
                    return
                S.op("dve", lambda e: e.scalar_tensor_tensor(out=qg[:, :n], in0=pb[:, :n], scalar=params[:, gcol:gcol + 1], in1=rs2[:, :n],
                                                             op0=ALU.mult, op1=ALU.mult),
                     reads=[pd, d_rs2, d_params], writes=[d_qg])
                p3, pd3 = ps_aux.next()
                S.op("pe", lambda e: e.matmul(p3[:, :n], cmat[:, SWAPP, :], qg[:, :n], start=True, stop=True),
                     reads=[d_qg, d_const], writes=[pd3])
                t1, td1 = tmpr.next()
                t2, td2 = tmpr.next()
                S.op("dve", lambda e: e.tensor_tensor(out=t1[:, :n], in0=qg[:, :n], in1=ropeC[:, :n], op=ALU.mult),
                     reads=[d_qg, d_rope], writes=[td1])
                S.op("dve", lambda e: e.tensor_tensor(out=t2[:, :n], in0=p3[:, :n], in1=ropeS[:, :n], op=ALU.mult),
                     reads=[pd3, d_rope], writes=[td2])
                for (psl, dap) in dest_ap:
                    S.op("dve", lambda e, psl=psl, dap=dap: e.tensor_tensor(out=dap, in0=t1[psl, :n], in1=t2[psl, :n], op=ALU.add),
                         reads=[td1, td2], writes=[d_dest])

            def v_tiles(chunk, dest_fn, d_dest):
                s, X, dX, T0, n, ci = chunk
                for tt in range(n // 128):
                    pb, pd = ps_main.next()
                    for k in range(8):
                        S.op("pe", lambda e, k=k, tt=tt, pb=pb: e.matmul(pb[:, 0:128], xn[:, k, tt * 128:(tt + 1) * 128], w_in[:, k, 384:512],
                                                                          start=(k == 0), stop=(k == 7)),
                             reads=[d_win, d_xn[k]], writes=[pd])
                    S.op("act", lambda e, tt=tt, pb=pb: e.copy(out=dest_fn(tt)[:, 0:64], in_=pb[:, 0:64]), reads=[pd], writes=[d_dest])
                    S.op("act", lambda e, tt=tt, pb=pb: e.copy(out=dest_fn(tt)[:, 128:192], in_=pb[:, 64:128]), reads=[pd], writes=[d_dest])

            def dest_in(pb):
                return pb[:, 0:128]

            def proj_tile(chunk, ctile):
                s, X, dX, T0, n, ci = chunk
                pb, pd = ps_main.next()
                for k in range(8):
                    S.op("pe", lambda e, k=k: e.matmul(pb[:, :n], w_in[:, k, ctile * 128:(ctile + 1) * 128], xn[:, k, :n],
                                                       start=(k == 0), stop=(k == 7)),
                         reads=[d_win, d_xn[k]], writes=[pd])
                return pb, pd

            chunks = [ctx_chunk] + lat_chunks
            def do_chunk_A(chunk):
                s, X, dX, T0, n, ci = chunk
                isctx = (s == 1)
                so_box = []
                norm_mod(chunk, 0, xn, d_xn, ntmps, mods)
                if not isctx:
                    S.dma("sp", ropeC[:, :n], ropeC_d.ap()[:, T0:T0 + n], writes=[d_rope])
                    S.dma("sp", ropeS[:, :n], ropeS_d.ap()[:, T0:T0 + n], writes=[d_rope])
                full = not (isctx and last)
                if full:
                    H0, H1, ALLP = slice(0, 64), slice(64, 128), slice(0, 128)
                    for tq in range(2):
                        if isctx:
                            dest = [(H0, qC[0:64, tq, :]), (H1, qC[64:128, 2 + tq, :])]
                        else:
                            dest = [(H0, qT[0:64, tq, T0:T0 + n]), (H1, qT[64:128, 2 + tq, T0:T0 + n])]
                        qk_tile(chunk, tq, dest, d_q[tq][ci], 64, rope=not isctx)
                ALLP = slice(0, 128)
                if isctx:
                    qk_tile(chunk, 2, [(ALLP, KTc[:, :])], d_KTc, 65, rope=False)
                else:
                    qk_tile(chunk, 2, [(ALLP, kloc[:, :n])], d_kloc, 65, rope=True)
                if isctx:
                    for tt in range(2):
                        pb, pd = ps_main.next()
                        for k in range(8):
                            S.op("pe", lambda e, k=k, tt=tt, pb=pb: e.matmul(pb[:, 0:128], xn[:, k, tt * 128:(tt + 1) * 128], w_in[:, k, 384:512],
                                                                              start=(k == 0), stop=(k == 7)),
                                 reads=[d_win, d_xn[k]], writes=[pd])
                        S.op("act", lambda e, tt=tt, pb=pb: e.copy(out=Vxc[:, tt, 0:64], in_=pb[:, 0:64]), reads=[pd], writes=[d_Vxc])
                        S.op("act", lambda e, tt=tt, pb=pb: e.copy(out=Vxc[:, tt, 128:192], in_=pb[:, 64:128]), reads=[pd], writes=[d_Vxc])
                else:
                    v_tiles(chunk, lambda tt: vsb[:, tt, :], d_vsb)
                if not full:
                    return
                for ut in range(2):
                    pb, pd = proj_tile(chunk, 4 + ut)
                    S.op("act", lambda e, ut=ut, pb=pb: e.copy(out=ub[:, ut, :n], in_=pb[:, :n]), reads=[pd], writes=[d_ub[ut]])
                if isctx:
                    for tt in range(2):
                        pb, pd = ps_main.next()
                        for k2 in range(2):
                            S.op("pe", lambda e, k2=k2, tt=tt, pb=pb: e.matmul(pb[:, :], ub[:, k2, tt * 128:(tt + 1) * 128], csblk[:, k2, :],
                                                                                start=(k2 == 0), stop=(k2 == 1)),
                                 reads=[d_ub[k2], d_const], writes=[pd])
                        S.op("act", lambda e, tt=tt, pb=pb: e.copy(out=zc_tm[:, tt, :], in_=pb[:, :]), reads=[pd], writes=[d_zc])
                else:
                    for zt in range(4):
                        pb, pd = ps_main.next()
                        for k2 in range(2):
                            S.op("pe", lambda e, k2=k2, zt=zt, pb=pb: e.matmul(pb[:, :n], csblk[:, k2, zt * 128:(zt + 1) * 128], ub[:, k2, :n],
                                                                                start=(k2 == 0), stop=(k2 == 1)),
                                 reads=[d_ub[k2], d_const], writes=[pd])
                        S.op("dve", lambda e, zt=zt, pb=pb: e.tensor_copy(out=zsb[:, zt, :n], in_=pb[:, :n]), reads=[pd], writes=[d_zsb])
                    sc_ = sndc[ci].ap()
                    so = [S.dma("sp", sc_[320:832, :].rearrange("(z p) t -> p z t", p=128), zsb[:, :, :n], reads=[d_zsb], writes=[d_sndc[ci]],
                                out_side=True),
                          S.dma("sp", sc_[0:128, :], kloc[:, :n], reads=[d_kloc], writes=[d_sndc[ci]], out_side=True),
                          S.dma("sp", sc_[128:320, :].rearrange("r c -> (r c)").rearrange("(tt p c) -> p tt c", p=128, c=192), vsb,
                                reads=[d_vsb], writes=[d_sndc[ci]], out_side=True)]
                    so_box.append(so)
                    if ci == 0:
                        tap("zsb" + L, zsb, [128, 4, 512], BF16, [d_zsb])
                for vt in range(2):
                    pv, pdv = proj_tile(chunk, 6 + vt)
                    pg, pdg = proj_tile(chunk, 8 + vt)
                    sig, d_sig = tmpr.next()
                    S.op("act", lambda e, pg=pg, sig=sig: e.activation(out=sig[:, :n], in_=pg[:, :n], func=AF.Sigmoid), reads=[pdg], writes=[d_sig])
                    gdst = (gluC[:, vt, 15:15 + n] if isctx else glu[:, vt, 15 + T0:15 + T0 + n])
                    S.op("dve", lambda e, pv=pv, gdst=gdst, sig=sig: e.tensor_tensor(out=gdst, in0=pv[:, :n], in1=sig[:, :n], op=ALU.mult),
                         reads=[pdv, d_sig], writes=[(d_gluC if isctx else d_glu)[vt]])
                for pt in range(2):
                    pb, pd = proj_tile(chunk, 10 + pt)
                    pdst = (puC[:, pt, 8:8 + n] if isctx else pu[:, pt, 8 + T0:8 + T0 + n])
                    S.op("act", lambda e, pb=pb, pdst=pdst: e.copy(out=pdst, in_=pb[:, :n]), reads=[pd],
                         writes=[(d_puC if isctx else d_pu)[pt]])

                if so_box:
                    so = so_box[0]
                    S.collective(lambda e: e.collective_compute("AllGather", ALU.bypass, replica_groups=RG, ins=[sndc[ci].ap()],
                                                                outs=[rcvc[ci].ap()]),
                                 reads=[d_sndc[ci]], writes=[d_rcvc[ci]], extra=so, name="gc%d" % ci)

            for chunk in chunks:
                do_chunk_A(chunk)
            sev = snde.ap().rearrange("(a p) c -> p a c", p=128)
            e_ops = []
            e_ops.append(S.dma("sp", sev[:, 0:2, 0:16], glu[:, :, 15:31], reads=d_glu, writes=[d_snde], out_side=True, sem_dep=d_glu[0]))
            e_ops.append(S.dma("sp", sev[:, 0:2, 16:32], glu[:, :, TL - 1:TL + 15], reads=d_glu, writes=[d_snde], out_side=True, sem_dep=d_glu[0]))
            e_ops.append(S.dma("sp", sev[:, 2:4, 0:16], pu[:, :, 8:24], reads=d_pu, writes=[d_snde], out_side=True, sem_dep=d_pu[0]))
            e_ops.append(S.dma("sp", sev[:, 2:4, 16:32], pu[:, :, TL - 8:TL + 8], reads=d_pu, writes=[d_snde], out_side=True, sem_dep=d_pu[0]))
            S.collective(lambda e: e.collective_compute("AllGather", ALU.bypass, replica_groups=RG, ins=[snde.ap()], outs=[rcve.ap()]),
                         reads=[d_snde], writes=[d_rcve], extra=e_ops, name="ge")

            tap("q" + L, qT, [128, 4, TL], BF16, [d_q[0][3], d_q[1][3], d_q[0][0], d_q[1][0]])
            tap("glu" + L, glu, [128, 2, TL + 30], BF16, d_glu)
            tap("pu" + L, pu, [128, 2, TL + 16], BF16, d_pu)
            tap("KTc" + L, KTc, [128, CTX], BF16, [d_KTc])
            tap("Vxc" + L, Vxc, [128, 2, 192], BF16, [d_Vxc])
            phase_end(a_names)
            if stop_after == "A" + L:
                return True

            catT = al("catT", [128, 6, TL], BF16)
            catC = al("catC", [128, 6, CTX], BF16)
            d_cat = [[gdep("cat%d_%d" % (r, c)) for c in range(5)] for r in range(6)]
            halL = al("halL", [128, 4, 32], BF16)
            halR = al("halR", [128, 4, 32], BF16)
            d_hal = gdep("hal")
            d_haloG, d_haloP = gdep("haloG"), gdep("haloP")
            def halo_fill():
                jl = (jr + 3) % 4
                jrr = (jr + 1) % 4
                rv = rcve.ap()
                S.dma("sp", halL, rv[bass.ds(jl * 512, 512), :].rearrange("(a p) c -> p a c", p=128), reads=[d_rcve], writes=[d_hal], tmin=150.0)
                S.dma("sp", halR, rv[bass.ds(jrr * 512, 512), :].rearrange("(a p) c -> p a c", p=128), reads=[d_rcve], writes=[d_hal], tmin=150.0)
                S.op("dve", lambda e: e.tensor_scalar(out=glu[:, :, 0:15], in0=halL[:, 0:2, 17:32], scalar1=percore[:, 0:1], scalar2=None, op0=ALU.mult),
                     reads=[d_hal, d_pc], writes=[d_haloG])
                S.op("dve", lambda e: e.tensor_scalar(out=glu[:, :, TL + 15:TL + 30], in0=halR[:, 0:2, 0:15], scalar1=percore[:, 1:2], scalar2=None,
                                                      op0=ALU.mult),
                     reads=[d_hal, d_pc], writes=[d_haloG])
                S.op("dve", lambda e: e.tensor_scalar(out=pu[:, :, 0:8], in0=halL[:, 2:4, 24:32], scalar1=percore[:, 0:1], scalar2=None, op0=ALU.mult),
                     reads=[d_hal, d_pc], writes=[d_haloP])
                S.op("dve", lambda e: e.tensor_scalar(out=pu[:, :, TL + 8:TL + 16], in0=halR[:, 2:4, 0:8], scalar1=percore[:, 1:2], scalar2=None,
                                                      op0=ALU.mult),
                     reads=[d_hal, d_pc], writes=[d_haloP])

            accr = Ring([(al("acc%d" % i, [128, 2, 512], F32), [gdep("acc%d_0" % i), gdep("acc%d_1" % i)]) for i in range(1)])
            ybr = Ring([(al("yb%d" % i, [128, 2, 512], BF16), [gdep("yb%d_0" % i), gdep("yb%d_1" % i)]) for i in range(2)])
            ysqr = Ring([(al("ysq%d" % i, [128, 2, 512], BF16), [gdep("ysq%d_0" % i), gdep("ysq%d_1" % i)]) for i in range(2)])
            meanr = Ring([(al("mean%d" % i, [128, 512], F32), gdep("mean%d" % i)) for i in range(2)])
            msqr = Ring([(al("msq%d" % i, [128, 512], F32), gdep("msq%d" % i)) for i in range(1)])
            rstdr = Ring([(al("rstdc%d" % i, [128, 512], F32), gdep("rstdc%d" % i)) for i in range(2)])
            actr = Ring([(al("actb%d" % i, [128, 2, 512], BF16), [gdep("actb%d_0" % i), gdep("actb%d_1" % i)]) for i in range(2)])
            p1e = al("p1e", [128, 528], F32)
            w4e = al("w4e", [128, 528], F32)
            w8e = al("w8e", [128, 528], F32)
            d_p1e, d_w4e, d_w8e = gdep("p1e"), gdep("w4e"), gdep("w8e")
            WS = al("WS", [128, 2, 512], F32)
            d_WS = [gdep("WS0"), gdep("WS1")]
            ybp = al("ybp", [128, 2, 512], BF16)
            d_ybp = [gdep("ybp0"), gdep("ybp1")]
            tmp8 = al("tmp8", [128, 8], F32)
            d_tmp8 = gdep("tmp8")
            b1_names = ["halL", "halR", "wpw", "wpool", "diag", "acc0", "yb0", "yb1", "ysq0", "ysq1", "mean0", "mean1", "msq0",
                        "rstdc0", "rstdc1", "actb0", "actb1", "p1e", "w4e", "w8e", "WS", "ybp", "tmp8", "glu", "gluC", "pu", "puC"]

            def conv_pool_chunk(G, dG, U, dU, cat, ci, T0, n, T, fixcol):
                acc, d_acc = accr.next()
                yb, d_yb = ybr.next()
                ysq, d_ysq = ysqr.next()
                mean_sb, d_mean = meanr.next()
                msq, d_msq = msqr.next()
                rstdc, d_rstdc = rstdr.next()
                actb, d_actb = actr.next()
                pcs = []
                for vt in range(2):
                    pc, pdc = ps_main.next()
                    pcs.append((pc, pdc))
                    for j in range(31):
                        S.op("pe", lambda e, vt=vt, j=j, pc=pc: e.matmul(pc[:, :n], diag[:, vt * 31 + j, :], G[:, vt, T0 + j:T0 + j + n],
                                                                         start=(j == 0), stop=(j == 30)),
                             reads=dG[vt] + [d_diag], writes=[pdc])
                    S.op("act", lambda e, vt=vt, pc=pc: e.activation(out=yb[:, vt, :n], in_=pc[:, :n], func=AF.Identity, bias=params[:, 128 + vt:129 + vt]),
                         reads=[pdc, d_params], writes=[d_yb[vt]])
                    S.op("act", lambda e, vt=vt, pc=pc: e.activation(out=ysq[:, vt, :n], in_=pc[:, :n], func=AF.Square, bias=params[:, 128 + vt:129 + vt]),
                         reads=[pdc, d_params], writes=[d_ysq[vt]])
                pm, pdm = ps_aux.next()
                pq, pdq = ps_aux.next()
                for vt in range(2):
                    S.op("pe", lambda e, vt=vt: e.matmul(pm[:, :n], cmat[:, ONESLN, :], yb[:, vt, :n], start=(vt == 0), stop=(vt == 1)),
                         reads=[d_yb[vt], d_const], writes=[pdm])
                for vt in range(2):
                    S.op("pe", lambda e, vt=vt: e.matmul(pq[:, :n], cmat[:, ONESLN, :], ysq[:, vt, :n], start=(vt == 0), stop=(vt == 1)),
                         reads=[d_ysq[vt], d_const], writes=[pdq])
                S.op("act", lambda e: e.copy(out=mean_sb[:, :n], in_=pm[:, :n]), reads=[pdm], writes=[d_mean])
                S.op("dve", lambda e: e.tensor_tensor(out=msq[:, :n], in0=mean_sb[:, :n], in1=mean_sb[:, :n], op=ALU.mult),
                     reads=[d_mean], writes=[d_msq])
                S.op("dve", lambda e: e.tensor_tensor(out=rstdc[:, :n], in0=pq[:, :n], in1=msq[:, :n], op=ALU.subtract),
                     reads=[pdq, d_msq], writes=[d_rstdc])
                S.op("act", lambda e: e.activation(out=rstdc[:, :n], in_=rstdc[:, :n], func=AF.Ln, bias=cvec[:, 0:1]), reads=[d_rstdc, d_const],
                     writes=[d_rstdc])
                S.op("act", lambda e: e.activation(out=rstdc[:, :n], in_=rstdc[:, :n], func=AF.Exp, scale=-0.5), reads=[d_rstdc], writes=[d_rstdc])
                for vt in range(2):
                    pc, pdc = pcs[vt]
                    S.op("dve", lambda e, vt=vt, pc=pc: e.scalar_tensor_tensor(out=acc[:, vt, :n], in0=pc[:, :n], scalar=params[:, 128 + vt:129 + vt],
                                                                               in1=mean_sb[:, :n], op0=ALU.add, op1=ALU.subtract),
                         reads=[pdc, d_mean, d_params], writes=[d_acc[vt]])
                    S.op("dve", lambda e, vt=vt: e.tensor_tensor(out=acc[:, vt, :n], in0=acc[:, vt, :n], in1=rstdc[:, :n], op=ALU.mult),
                         reads=[d_rstdc, d_acc[vt]], writes=[d_acc[vt]])
                    S.op("act", lambda e, vt=vt: e.activation(out=actb[:, vt, :n], in_=acc[:, vt, :n], func=AF.Silu,
                                                              bias=params[:, 132 + vt:133 + vt], scale=params[:, 130 + vt:131 + vt]),
                         reads=[d_acc[vt], d_params], writes=[d_actb[vt]])
                for ot in range(2):
                    pb, pd = ps_main.next()
                    for vt in range(2):
                        S.op("pe", lambda e, vt=vt, ot=ot, pb=pb: e.matmul(pb[:, :n], wpw[:, vt, ot * 128:(ot + 1) * 128], actb[:, vt, :n],
                                                                            start=(vt == 0), stop=(vt == 1)),
                             reads=[d_actb[vt], d_wcp], writes=[pd])
                    S.op("act", lambda e, ot=ot, pb=pb: e.copy(out=cat[:, 2 + ot, T0:T0 + n], in_=pb[:, :n]), reads=[pd], writes=[d_cat[2 + ot][ci]])
                e_ = n + 16
                c0 = 8
                S.op("dve", lambda e: e.tensor_tensor(out=WS[0:64, 0, :n], in0=U[0:64, 0, T0 + c0 - 1:T0 + c0 - 1 + n], in1=U[0:64, 0, T0 + c0:T0 + c0 + n],
                                                       op=ALU.add),
                     reads=dU[0], writes=[d_WS[0]])
                S.op("dve", lambda e: e.tensor_tensor(out=p1e[64:128, 1:e_], in0=U[64:128, 0, T0:T0 + e_ - 1], in1=U[64:128, 0, T0 + 1:T0 + e_], op=ALU.add),
                     reads=dU[0], writes=[d_p1e])
                S.op("dve", lambda e: e.tensor_tensor(out=WS[64:128, 0, :n], in0=p1e[64:128, c0 - 1:c0 - 1 + n], in1=p1e[64:128, c0 + 1:c0 + 1 + n],
                                                       op=ALU.add),
                     reads=[d_p1e], writes=[d_WS[0]])
                S.op("dve", lambda e: e.tensor_tensor(out=p1e[:, 1:e_], in0=U[:, 1, T0:T0 + e_ - 1], in1=U[:, 1, T0 + 1:T0 + e_], op=ALU.add),
                     reads=dU[1], writes=[d_p1e])
                S.op("dve", lambda e: e.tensor_tensor(out=w4e[:, 2:e_ - 1], in0=p1e[:, 1:e_ - 2], in1=p1e[:, 3:e_], op=ALU.add),
                     reads=[d_p1e], writes=[d_w4e])
                S.op("dve", lambda e: e.tensor_tensor(out=WS[0:64, 1, :n], in0=w4e[0:64, c0 - 2:c0 - 2 + n], in1=w4e[0:64, c0 + 2:c0 + 2 + n], op=ALU.add),
                     reads=[d_w4e], writes=[d_WS[1]])
                S.op("dve", lambda e: e.tensor_tensor(out=w8e[64:128, 4:e_ - 3], in0=w4e[64:128, 2:e_ - 5], in1=w4e[64:128, 6:e_ - 1], op=ALU.add),
                     reads=[d_w4e], writes=[d_w8e])
                S.op("dve", lambda e: e.tensor_tensor(out=WS[64:128, 1, :n], in0=w8e[64:128, c0 - 4:c0 - 4 + n], in1=w8e[64:128, c0 + 4:c0 + 4 + n],
                                                       op=ALU.add),
                     reads=[d_w8e], writes=[d_WS[1]])
                for tl_ in range(2):
                    S.op("dve", lambda e, tl_=tl_: e.scalar_tensor_tensor(out=ybp[:, tl_, :n], in0=WS[:, tl_, :n], scalar=params[:, 136 + tl_:137 + tl_],
                                                                          in1=U[:, tl_, T0 + c0:T0 + c0 + n], op0=ALU.mult, op1=ALU.subtract),
                         reads=[d_WS[tl_], d_params] + dU[tl_], writes=[d_ybp[tl_]])
                    edges = []
                    if T0 == 0:
                        edges.append((0, fixcol + tl_ * 16))
                    if T0 + n == T:
                        edges.append((n - 8, fixcol + tl_ * 16 + 8))
                    for (e0, fc) in edges:
                        S.op("dve", lambda e, tl_=tl_, e0=e0, fc=fc: e.tensor_tensor(out=tmp8[:, :], in0=WS[:, tl_, e0:e0 + 8], in1=percore[:, fc:fc + 8],
                                                                                      op=ALU.mult),
                             reads=[d_WS[tl_], d_pc], writes=[d_tmp8])
                        S.op("dve", lambda e, tl_=tl_, e0=e0: e.tensor_tensor(out=ybp[:, tl_, e0:e0 + 8], in0=tmp8[:, :],
                                                                               in1=U[:, tl_, T0 + c0 + e0:T0 + c0 + e0 + 8], op=ALU.subtract),
                             reads=[d_tmp8] + dU[tl_], writes=[d_ybp[tl_]])
                    pb, pd = ps_main.next()
                    S.op("pe", lambda e, tl_=tl_, pb=pb: e.matmul(pb[:, :n], wpool[:, tl_, :], ybp[:, tl_, :n], start=True, stop=True),
                         reads=[d_ybp[tl_], d_wcp], writes=[pd])
                    S.op("act", lambda e, tl_=tl_, pb=pb: e.activation(out=cat[:, 4 + tl_, T0:T0 + n], in_=pb[:, :n], func=AF.Identity,
                                                                        scale=params[:, 134 + tl_:135 + tl_]),
                         reads=[pd, d_params], writes=[d_cat[4 + tl_][ci]])

            if not last:
                conv_pool_chunk(gluC, [[d] for d in d_gluC], puC, [[d] for d in d_puC], catC, 4, 0, CTX, CTX, 34)
            for c in (1, 2, 0, 3):
                edge = c in (0, 3)
                if c == 0:
                    halo_fill()
                conv_pool_chunk(glu, [[d] + ([d_haloG] if edge else []) for d in d_glu], pu, [[d] + ([d_haloP] if edge else []) for d in d_pu],
                                catT, c, c * 512, 512, TL, 2)
            tap("catconv" + L, catT[:, 2:6, :], [128, 4, TL], BF16, [d_cat[r][c] for r in range(2, 6) for c in range(4)])
            tap("catCconv" + L, catC[:, 2:6, :], [128, 4, CTX], BF16, [d_cat[r][4] for r in range(2, 6)])
            phase_end(b1_names)
            if stop_after == "B1" + L:
                return True

            KT = al("KT", [128, SEQ], BF16)
            d_KT = [gdep("KT%d" % r) for r in range(4)]
            X1 = al("X1", [128, 64, 128], BF16)
            d_X1 = gdep("X1")
            d_X1b = gdep("X1b")
            Gr = al("Gr", [128, 64, 64], BF16)
            Gi = al("Gi", [128, 64, 64], BF16)
            d_Gr, d_Gi = gdep("Gr"), gdep("Gi")
            Yout = al("Yout", [128, 64, 64], BF16)
            d_Yout = gdep("Yout")
            tw = al("tw", [128, 2, 512], F32)
            d_tw = gdep("tw")
            S.dma("sp", tw, tw_d.ap(), writes=[d_tw])
            ta = Ring([(al("twa%d" % i, [128, 512], F32), gdep("twa%d" % i)) for i in range(2)])
            tb_ = Ring([(al("twb%d" % i, [128, 512], F32), gdep("twb%d" % i)) for i in range(2)])
            b2_names = ["X1", "Gr", "Gi", "Yout", "tw", "twa0", "twa1", "twb0", "twb1"]
            for c in range(4):
                for ri in range(2):
                    for r in range(4):
                        src = rcvc[c].ap()[bass.ds(jr * 64 + (r * RC[c] + 320 + ri * 256), 64), :].rearrange("m (a t) -> a m t", t=128)
                        p0 = ri * 64 + r * 16 + c * 4
                        if (ri + r + c) % 2 == 0:
                            S.dma("sp", X1[p0:p0 + 4, :, :], src, reads=[d_rcvc[c]], writes=[d_X1])
                        else:
                            S.dma("pool", X1[p0:p0 + 4, :, :], src, reads=[d_rcvc[c]], writes=[d_X1b])
            for r in range(4):
                for c in range(4):
                    S.dma("sp", KT[:, r * TL + c * 512:r * TL + (c + 1) * 512], rcvc[c].ap()[r * RC[c]:r * RC[c] + 128, :], reads=[d_rcvc[c]],
                          writes=[d_KT[r]])
            for mg in range(16):
                pb, pd = ps_main.next()
                for mi in range(4):
                    S.op("pe", lambda e, mi=mi, mg=mg, pb=pb: e.matmul(pb[:, mi * 128:(mi + 1) * 128], X1[:, mg * 4 + mi, :], cmat[:, R1M, :],
                                                                        start=True, stop=True),
                         reads=[d_X1, d_X1b, d_const], writes=[pd])
                a_, da = ta.next()
                b_, db = tb_.next()
                S.op("dve", lambda e, pb=pb, a_=a_: e.tensor_tensor(out=a_[:, :], in0=pb[:, :], in1=tw[:, 0, :], op=ALU.mult), reads=[pd, d_tw], writes=[da])
                S.op("dve", lambda e, pb=pb, b_=b_: e.tensor_tensor(out=b_[:, :], in0=pb[:, :], in1=tw[:, 1, :], op=ALU.mult), reads=[pd, d_tw], writes=[db])
                av = a_.rearrange("p (m r k) -> p m r k", r=2, k=64)
                bv = b_.rearrange("p (m r k) -> p m r k", r=2, k=64)
                S.op("dve", lambda e, av=av, bv=bv, mg=mg: e.tensor_tensor(out=Gr[:, mg * 4:(mg + 1) * 4, :], in0=av[:, :, 0, :], in1=bv[:, :, 1, :], op=ALU.add),
                     reads=[da, db], writes=[d_Gr])
                S.op("dve", lambda e, av=av, bv=bv, mg=mg: e.tensor_tensor(out=Gi[:, mg * 4:(mg + 1) * 4, :], in0=av[:, :, 1, :], in1=bv[:, :, 0, :],
                                                                             op=ALU.subtract),
                     reads=[da, db], writes=[d_Gi])
            Grf = Gr.rearrange("p m k -> p (m k)")
            Gif = Gi.rearrange("p m k -> p (m k)")
            Yf = Yout.rearrange("p m k -> p (m k)")
            for ch in range(8):
                pb, pd = ps_main.next()
                S.op("pe", lambda e, ch=ch, pb=pb: e.matmul(pb[:, :], cmat[:, C128S, :], Grf[:, ch * 512:(ch + 1) * 512], start=True, stop=False),
                     reads=[d_Gr, d_const], writes=[pd])
                S.op("pe", lambda e, ch=ch, pb=pb: e.matmul(pb[:, :], cmat[:, S128S, :], Gif[:, ch * 512:(ch + 1) * 512], start=False, stop=True),
                     reads=[d_Gi, d_const], writes=[pd])
                S.op("act", lambda e, ch=ch, pb=pb: e.copy(out=Yf[:, ch * 512:(ch + 1) * 512], in_=pb[:, :]), reads=[pd], writes=[d_Yout])
            o2 = S.dma("sp", snd2.ap().rearrange("m (k2 k1) -> k2 m k1", k1=64), Yout, reads=[d_Yout], writes=[d_snd2], out_side=True)
            tap("Yout" + L, Yout, [128, 64, 64], BF16, [d_Yout])
            S.collective(lambda e: e.collective_compute("AllGather", ALU.bypass, replica_groups=RG, ins=[snd2.ap()], outs=[rcv2.ap()]),
                         reads=[d_snd2], writes=[d_rcv2], extra=[o2], name="g2")
            if not last:
                for mt in range(2):
                    pb, pd = ps_main.next()
                    for tt in range(2):
                        S.op("pe", lambda e, tt=tt, mt=mt, pb=pb: e.matmul(pb[:, 0:CTX], zc_tm[:, tt, mt * 128:(mt + 1) * 128], cs256[:, 0, tt, :],
                                                                            start=(tt == 0), stop=False),
                             reads=[d_zc, d_const], writes=[pd])
                        S.op("pe", lambda e, tt=tt, mt=mt, pb=pb: e.matmul(pb[:, 0:CTX], zc_tm[:, tt, 256 + mt * 128:256 + (mt + 1) * 128], cs256[:, 1, tt, :],
                                                                            start=False, stop=(tt == 1)),
                             reads=[d_zc, d_const], writes=[pd])
                    S.op("act", lambda e, mt=mt, pb=pb: e.copy(out=ycT[:, mt, :], in_=pb[:, 0:CTX]), reads=[pd], writes=[d_ycT])
                tap("ycT" + L, ycT, [128, 2, CTX], BF16, [d_ycT])
            phase_end(b2_names)
            if stop_after == "B2" + L:
                return True

            attnT = al("attnT", [128, 2, TL], BF16, top=True)
            attnC = al("attnC", [128, 2, CTX], BF16, top=True)
            d_attn = [[gdep("attn%d_%d" % (h, c)) for c in range(5)] for h in range(4)]
            wo_att = al("wo_att", [128, 2, D], BF16)
            wo_rest = al("wo_rest", [128, 6, D], BF16)
            d_wo = gdep("wo")
            for tq in range(2):
                S.dma("pool", wo_att[0:64, tq, :], w_out_d.ap()[l, tq * 64:(tq + 1) * 64, :], writes=[d_wo])
                S.dma("pool", wo_att[64:128, tq, :], w_out_d.ap()[l, (2 + tq) * 64:(3 + tq) * 64, :], writes=[d_wo])
            S.dma("pool", wo_rest, w_out_d.ap()[l, 256:1024, :].rearrange("(r p) o -> p r o", p=128), writes=[d_wo])
            wf = al("wf", [128, 2, 256], BF16)
            d_wf = gdep("wf")
            S.dma("pool", wf, w_f_d.ap()[l].rearrange("(k p) o -> p k o", p=128), writes=[d_wf])
            Vx = al("Vx", [128, 64, 192], BF16)
            d_Vx = [gdep("Vx%d" % r) for r in range(4)]
            for r in range(4):
                for c in range(4):
                    rcv_ = rcvc[c].ap()
                    vsrc = rcv_[r * RC[c] + 128:r * RC[c] + 320, :].rearrange("r c -> (r c)").rearrange("(tt p c) -> p tt c", p=128, c=192)
                    S.dma("sp", Vx[:, r * 16 + c * 4:r * 16 + (c + 1) * 4, :], vsrc, reads=[d_rcvc[c]], writes=[d_Vx[r]])
            ering = Ring([(al("E%d" % i, [128, 512], BF16), gdep("E%d" % i)) for i in range(3)])
            b3_names = ["KT", "Vx", "E0", "E1", "E2", "ob0", "ob1", "rs0", "qT", "qC", "KTc", "Vxc", "zctm"]

            obr = Ring([(al("ob%d" % i, [128, 512], F32), gdep("ob%d" % i)) for i in range(2)])
            rsr = Ring([(al("rs%d" % i, [128, 512], F32), gdep("rsr%d" % i)) for i in range(1)])
            pending_fin = []

            def flush_fin():
                while pending_fin:
                    pending_fin.pop(0)()

            def attention(Q, dQ, ci, T0, n, key_tiles, dest):
                for tq in range(2):
                    for hf in range(2):
                        head = hf * 2 + tq
                        ps_ = slice(hf * 64, (hf + 1) * 64)
                        pO, pdO = ps_acc.next()
                        nk = len(key_tiles)
                        sbank = {}

                        def issue_S(kt, head=head, tq=tq, sbank=sbank):
                            Ksrc, dK, Vsrc, dV = key_tiles[kt]
                            pS, pdS = ps_main.next()
                            sbank[kt] = (pS, pdS)
                            S.op("pe", lambda e, pS=pS, Ksrc=Ksrc, head=head: e.matmul(pS[:, :n], Ksrc[:, :], Q[:, head, T0:T0 + n],
                                                                                      start=True, stop=True),
                                 reads=[dK, dQ[tq][ci]], writes=[pdS])

                        LA = 2
                        for k0 in range(min(LA, nk)):
                            issue_S(k0)
                        for kt in range(nk):
                            Ksrc, dK, Vsrc, dV = key_tiles[kt]
                            pS, pdS = sbank.pop(kt)
                            Eb, dE = ering.next()
                            S.op("act", lambda e, pS=pS, Eb=Eb: e.activation(out=Eb[:, :n], in_=pS[:, :n], func=AF.Exp, scale=0.125),
                                 reads=[pdS], writes=[dE])
                            S.op("pe", lambda e, Eb=Eb, Vsrc=Vsrc, kt=kt, pO=pO, hf=hf, nk=nk: e.matmul(pO[:, :n], Vsrc[:, hf * 64:hf * 64 + 128], Eb[:, :n],
                                                                                   start=(kt == 0), stop=(kt == nk - 1)),
                                 reads=[dE, dV], writes=[pdO])
                            if kt + LA < nk:
                                issue_S(kt + LA)
                            if kt == min(3, nk - 1):
                                flush_fin()
                        sr = (64 if hf == 0 else 0)
                        ob, dob = obr.next()
                        rs_, drs = rsr.next()
                        S.op("act", lambda e, pO=pO, ob=ob, ps_=ps_: e.copy(out=ob[ps_, :n], in_=pO[ps_, :n]), reads=[pdO], writes=[dob])
                        S.op("act", lambda e, pO=pO, rs_=rs_, sr=sr: e.activation(out=rs_[sr:sr + 1, :n], in_=pO[sr:sr + 1, :n], func=AF.Ln),
                             reads=[pdO], writes=[drs])
                        S.op("act", lambda e, rs_=rs_, sr=sr: e.activation(out=rs_[sr:sr + 1, :n], in_=rs_[sr:sr + 1, :n], func=AF.Exp, scale=-1.0),
                             reads=[drs], writes=[drs])

                        def fin(ob=ob, dob=dob, rs_=rs_, drs=drs, sr=sr, ps_=ps_, tq=tq, head=head):
                            pbc, pdbc = ps_aux.next()
                            S.op("pe", lambda e: e.matmul(pbc[:, :n], onesrow[sr:sr + 1, :], rs_[sr:sr + 1, :n], start=True, stop=True),
                                 reads=[drs, d_const], writes=[pdbc])
                            S.op("dve", lambda e: e.tensor_tensor(out=dest[ps_, tq, T0:T0 + n], in0=ob[ps_, :n], in1=pbc[ps_, :n], op=ALU.mult),
                                 reads=[dob, pdbc], writes=[d_attn[head][ci]])
                        pending_fin.append(fin)

            ctx_keys = [(KTc[:, tt * 128:(tt + 1) * 128], d_KTc, Vxc[:, tt, :], d_Vxc) for tt in range(2)]
            lat_keys = [(KT[:, tt * 128:(tt + 1) * 128], d_KT[tt // 16], Vx[:, tt, :], d_Vx[tt // 16]) for tt in range(64)]
            if not last:
                attention(qC, d_q, 4, 0, CTX, ctx_keys, attnC)
            for c in range(4):
                attention(qT, d_q, c, c * 512, 512, ctx_keys + lat_keys, attnT)
            flush_fin()
            tap("attnT" + L, attnT, [128, 2, TL], BF16, [d_attn[h][c] for h in range(4) for c in range(4)])
            tap("attnC" + L, attnC, [128, 2, CTX], BF16, [d_attn[h][4] for h in range(4)])
            phase_end(b3_names)
            if stop_after == "B3" + L:
                return True

            XN2 = al("XN2", [128, 8, TL], BF16, top=True)
            xnC = al("xnC", [128, 8, CTX], BF16, top=True)
            d_xn2 = [[gdep("xn2_%d_%d" % (c, k)) for k in range(8)] for c in range(5)]
            A_views["wfi0"] = al("wfi0", [128, 8, 1024], BF16, top=True)
            fisrc0 = w_fi_d.ap()[l].rearrange("(k p) o -> p k o", p=128)
            S.dma("pool", A_views["wfi0"][:, :, 0:512], fisrc0[:, :, 0:512], writes=[gdep("wfi0")])
            S.dma("pool", A_views["wfi0"][:, :, 512:1024], fisrc0[:, :, D_FF:D_FF + 512], writes=[gdep("wfi0")])
            yT = al("yT", [128, 2, 512], BF16)
            d_yT = gdep("yT")
            sqbC = al("sqbC", [128, 4, 512], BF16)
            d_sqC = [gdep("sqC%d" % k) for k in range(8)]
            rstdC = al("rstdC", [128, 512], F32)
            d_rstdC = gdep("rstdC")
            tmprC = Ring([(al("tmpC%d" % i, [128, 512], F32), gdep("tmpC%d" % i)) for i in range(3)])
            ntmpsC = (sqbC, d_sqC, rstdC, d_rstdC, tmprC)
            c1_names = ["wo_att", "wo_rest", "wf", "yT", "sqbC", "rstdC", "tmpC0", "tmpC1", "tmpC2", "catT", "catC", "attnT", "attnC", "ycT"]
            r2v = rcv2.ap()
            chunks = ([ctx_chunk] if not last else []) + lat_chunks
            def do_chunk_C1(chunk):
                s, X, dX, T0, n, ci = chunk
                isctx = (s == 1)
                cat = catC if isctx else catT
                att = attnC if isctx else attnT
                if isctx:
                    ysrc, dys = ycT, d_ycT
                else:
                    S.dma("sp", yT[:, :, :n], r2v[:, bass.ds(jr * TL + T0, n)].rearrange("(k p) t -> p k t", p=128), reads=[d_rcv2], writes=[d_yT])
                    ysrc, dys = yT, d_yT
                for ot in range(2):
                    pb, pd = ps_main.next()
                    for k2 in range(2):
                        S.op("pe", lambda e, k2=k2, ot=ot, pb=pb, ysrc=ysrc: e.matmul(pb[:, :n], wf[:, k2, ot * 128:(ot + 1) * 128], ysrc[:, k2, :n],
                                                                                       start=(k2 == 0), stop=(k2 == 1)),
                             reads=[dys, d_wf], writes=[pd])
                    S.op("act", lambda e, ot=ot, pb=pb, cat=cat: e.copy(out=cat[:, ot, T0:T0 + n], in_=pb[:, :n]), reads=[pd], writes=[d_cat[ot][ci]])
                for ot in range(8):
                    pb, pd = ps_main.next()
                    for h in range(2):
                        S.op("pe", lambda e, h=h, ot=ot, pb=pb, att=att: e.matmul(pb[:, :n], wo_att[:, h, ot * 128:(ot + 1) * 128], att[:, h, T0:T0 + n],
                                                                                   start=(h == 0), stop=False),
                             reads=[d_attn[h][ci], d_attn[2 + h][ci], d_wo], writes=[pd])
                    for r in range(6):
                        S.op("pe", lambda e, r=r, ot=ot, pb=pb, cat=cat: e.matmul(pb[:, :n], wo_rest[:, r, ot * 128:(ot + 1) * 128], cat[:, r, T0:T0 + n],
                                                                                   start=False, stop=(r == 5)),
                             reads=[d_cat[r][ci], d_wo], writes=[pd])
                    S.op("dve", lambda e, ot=ot, pb=pb: e.scalar_tensor_tensor(out=X[:, ot, T0:T0 + n], in0=pb[:, :n], scalar=modT[:, 16 + ot, s:s + 1],
                                                                               in1=X[:, ot, T0:T0 + n], op0=ALU.mult, op1=ALU.add),
                         reads=[pd, d_mod, dX[ot]], writes=[dX[ot]])
                xdst = xnC if isctx else XN2[:, :, T0:T0 + n]
                norm_mod(chunk, 1, xdst, d_xn2[ci], ntmpsC, mods)

            for chunk in chunks:
                do_chunk_C1(chunk)
            tap("x1_" + L, xT, [128, 8, TL], F32, [d_x[k][c] for k in range(8) for c in range(4)])
            tap("h1_" + L, hT, [128, 8, CTX], F32, d_h)
            tap("cat" + L, catT, [128, 6, TL], BF16, [d_cat[r][c] for r in range(6) for c in range(4)])
            phase_end(c1_names)
            if stop_after == "C1" + L:
                return True

            wfi = [A_views["wfi0"], al("wfi1", [128, 8, 1024], BF16)]
            wfo = [al("wfo%d" % i, [128, 4, D], BF16) for i in range(2)]
            d_wfi = [gdep("wfi%d" % i) for i in range(2)]
            d_wfo = [gdep("wfo%d" % i) for i in range(2)]
            hbr = Ring([(al("hb%d" % i, [128, 4, 512], BF16), gdep("hb%d" % i)) for i in range(2)])
            sar = Ring([(al("sa%d" % i, [128, 512], F32), gdep("sa%d" % i)) for i in range(2)])
            c2_names = ["wfi0", "wfi1", "wfo0", "wfo1", "hb0", "hb1", "sa0", "sa1", "XN2", "xnC"]
            if not last:
                A_views["wm0"] = al("wm0", [128, 8, 512], BF16)
                A_views["wm1"] = al("wm1", [128, 8, 512], BF16)
                c2_names = c2_names + ["wm0", "wm1"]
                stage_M(l + 1, "begin")
            fisrc = w_fi_d.ap()[l].rearrange("(k p) o -> p k o", p=128)
            groups = [(0, 4), (4, 4), (8, 4), (12, 4), (16, 4), (20, 2)]
            for gi, (h0, gw) in enumerate(groups):
                sl = gi % 2
                if gi > 0:
                    S.dma("pool", wfi[sl][:, :, 0:gw * 128], fisrc[:, :, h0 * 128:(h0 + gw) * 128], writes=[d_wfi[sl]])
                    S.dma("pool", wfi[sl][:, :, 512:512 + gw * 128], fisrc[:, :, D_FF + h0 * 128:D_FF + (h0 + gw) * 128], writes=[d_wfi[sl]])
                S.dma("pool", wfo[sl][:, 0:gw, :], w_fo_d.ap()[l, h0 * 128:(h0 + gw) * 128, :].rearrange("(c p) o -> p c o", p=128), writes=[d_wfo[sl]])
                def do_chunk_C2(chunk, sl=sl, gw=gw):
                    s, X, dX, T0, n, ci = chunk
                    isctx = (s == 1)
                    xsrc_ = xnC if isctx else XN2[:, :, T0:T0 + n]
                    hb, dhb = hbr.next()
                    for hc in range(gw):
                        pa, pda = ps_main.next()
                        pg, pdg = ps_main.next()
                        for k in range(8):
                            S.op("pe", lambda e, k=k, hc=hc, pa=pa, sl=sl, xsrc_=xsrc_: e.matmul(pa[:, :n], wfi[sl][:, k, hc * 128:(hc + 1) * 128], xsrc_[:, k, :n],
                                                                                                  start=(k == 0), stop=(k == 7)),
                                 reads=[d_wfi[sl], d_xn2[ci][k]], writes=[pda])
                        for k in range(8):
                            S.op("pe", lambda e, k=k, hc=hc, pg=pg, sl=sl, xsrc_=xsrc_: e.matmul(pg[:, :n], wfi[sl][:, k, 512 + hc * 128:512 + (hc + 1) * 128],
                                                                                                  xsrc_[:, k, :n], start=(k == 0), stop=(k == 7)),
                                 reads=[d_wfi[sl], d_xn2[ci][k]], writes=[pdg])
                        sa, dsa = sar.next()
                        S.op("act", lambda e, pa=pa, sa=sa: e.activation(out=sa[:, :n], in_=pa[:, :n], func=AF.Silu), reads=[pda], writes=[dsa])
                        S.op("dve", lambda e, pg=pg, sa=sa, hb=hb, hc=hc: e.tensor_tensor(out=hb[:, hc, :n], in0=pg[:, :n], in1=sa[:, :n], op=ALU.mult),
                             reads=[pdg, dsa], writes=[dhb])
                    for ot in range(8):
                        po, pdo = ps_acc.next()
                        for hc in range(gw):
                            S.op("pe", lambda e, hc=hc, ot=ot, po=po, sl=sl, hb=hb: e.matmul(po[:, :n], wfo[sl][:, hc, ot * 128:(ot + 1) * 128], hb[:, hc, :n],
                                                                                              start=(hc == 0), stop=(hc == gw - 1)),
                                 reads=[d_wfo[sl], dhb], writes=[pdo])
                        S.op("dve", lambda e, ot=ot, po=po, X=X, T0=T0, s=s: e.scalar_tensor_tensor(out=X[:, ot, T0:T0 + n], in0=po[:, :n],
                                                                                                    scalar=modT[:, 40 + ot, s:s + 1], in1=X[:, ot, T0:T0 + n],
                                                                                                    op0=ALU.mult, op1=ALU.add),
                             reads=[pdo, d_mod, dX[ot]], writes=[dX[ot]])

                for chunk in chunks:
                    do_chunk_C2(chunk)
                if not last:
                    stage_M(l + 1, 2 * gi)
                    stage_M(l + 1, 2 * gi + 1)
            if not last:
                stage_M(l + 1, "end")
            tap("x2_" + L, xT, [128, 8, TL], F32, [d_x[k][c] for k in range(8) for c in range(4)])
            if last and stop_after is None:
                osrc = outT_d.ap().rearrange("(k p) t -> p k t", p=128)
                d_osem = Dep("osem")
                for c in range(4):
                    o = S.dma("sp", osrc[:, :, c * 512:(c + 1) * 512], xT[:, :, c * 512:(c + 1) * 512], reads=[d_x[k][c] for k in range(8)],
                              out_side=True, sem_dep=d_osem)
                    final_ops.append(o)
                out_done[0] = True
            phase_end(c2_names)
            if stop_after == "C2" + L:
                return True
            return False

        for l in range(DEPTH):
            if do_layer(l):
                break

        if not out_done[0]:
            osrc = outT_d.ap().rearrange("(k p) t -> p k t", p=128)
            d_osem = Dep("osem")
            for k in range(8):
                o = S.dma("sp", osrc[:, k, :], xT[:, k, :], reads=d_x[k], out_side=True, sem_dep=d_osem)
                final_ops.append(o)
        block = st.enter_context(nc.Block())
        S.emit(block, final_waits=final_ops)
        build_program.peak_words = A.peak
    return nc, tap_out


def _consts():
    f = np.float32
    cm = np.zeros((128, 9, 128), f)
    cm[:, 0, :] = 1.0 / 1024
    for b in range(2):
        cm[b * 64:(b + 1) * 64, 1, b * 64:(b + 1) * 64] = 1.0 / 64
    for k in range(128):
        cm[k, 2, k ^ 1] = 1.0
    cm[:, 3, :] = 1.0 / 256
    cm[:, 7, :] = np.eye(128)
    t1 = np.arange(64)[:, None].astype(np.float64)
    k1 = np.arange(64)[None, :].astype(np.float64)
    C = np.cos(2 * np.pi * t1 * k1 / 64)
    Sn = np.sin(2 * np.pi * t1 * k1 / 64)
    R1m = np.zeros((128, 128))
    R1m[0:64, 0:64] = C
    R1m[64:128, 0:64] = Sn
    R1m[0:64, 64:128] = -Sn
    R1m[64:128, 64:128] = C
    cm[:, 4, :] = R1m
    t2 = np.arange(128)[:, None].astype(np.float64)
    k2 = np.arange(128)[None, :].astype(np.float64)
    nrm = 1.0 / np.sqrt(8192.0 * 64.0)
    cm[:, 5, :] = np.cos(2 * np.pi * t2 * k2 / 128) * nrm
    cm[:, 6, :] = np.sin(2 * np.pi * t2 * k2 / 128) * nrm
    cs = np.zeros((256, 512))
    cc = np.arange(64)[:, None].astype(np.float64)
    m = np.arange(64)[None, :].astype(np.float64)
    for h in range(4):
        cs[h * 64:(h + 1) * 64, h * 64:(h + 1) * 64] = np.cos(2 * np.pi * cc * m / 64)
        cs[h * 64:(h + 1) * 64, 256 + h * 64:256 + (h + 1) * 64] = -np.sin(2 * np.pi * cc * m / 64)
    csblk = np.ascontiguousarray(cs.reshape(2, 128, 512).transpose(1, 0, 2)).astype(f)
    k1r = np.arange(64)[None, :].astype(np.float64)
    twr = np.cos(2 * np.pi * t2 * k1r / 8192)
    twi = np.sin(2 * np.pi * t2 * k1r / 8192)
    tw = np.stack([np.tile(twr, (1, 8)), np.tile(twi, (1, 8))], axis=1).astype(f)
    t = np.arange(256)[:, None].astype(np.float64)
    k = np.arange(256)[None, :].astype(np.float64)
    n2 = 1.0 / np.sqrt(256.0 * 64.0)
    c256 = np.cos(2 * np.pi * t * k / 256) * n2
    s256 = np.sin(2 * np.pi * t * k / 256) * n2
    cs256 = np.stack([c256.reshape(2, 128, 256).transpose(1, 0, 2), s256.reshape(2, 128, 256).transpose(1, 0, 2)], axis=1).astype(f)
    return dict(cmat=cm, csblk=csblk, tw=tw, cs256=np.ascontiguousarray(cs256))


def _rope_tables(t0):
    tpos = np.arange(t0, t0 + TL)
    row = (tpos // 64).astype(np.float32)
    col = (tpos % 64).astype(np.float32)
    inv_freq = (np.float32(10000.0) ** (-np.arange(16, dtype=np.float32) / np.float32(16))).astype(np.float32)
    ang = np.concatenate([row[:, None] * inv_freq, col[:, None] * inv_freq], axis=-1).astype(np.float32)
    cos = np.cos(ang).astype(np.float32)
    sin = np.sin(ang).astype(np.float32)
    p = np.arange(128)
    d = p % 64
    i = d // 2
    sign = np.where(d % 2 == 0, -1.0, 1.0).astype(np.float32)
    rc = np.ascontiguousarray(cos[:, i].T)
    rs = np.ascontiguousarray((sin[:, i] * sign[None, :]).T)
    return rc, rs


def _invcnt(tglob, n, win):
    lo = np.clip(tglob - win // 2, 0, n)
    hi = np.clip(tglob - win // 2 + win, 0, n)
    return (1.0 / (hi - lo)).astype(np.float32)


def _percore(j):
    pc = np.zeros((128, 66), np.float32)
    pc[:, 0] = 0.0 if j == 0 else 1.0
    pc[:, 1] = 0.0 if j == 3 else 1.0
    wins = {(0, 0): 2, (0, 1): 4, (1, 0): 8, (1, 1): 16}
    for tile in range(2):
        for half in range(2):
            win = wins[(tile, half)]
            ps = slice(half * 64, (half + 1) * 64)
            tl = np.concatenate([np.arange(j * TL, j * TL + 8), np.arange((j + 1) * TL - 8, (j + 1) * TL)])
            pc[ps, 2 + tile * 16:2 + (tile + 1) * 16] = _invcnt(tl, SEQ, win)[None, :]
            tc = np.concatenate([np.arange(0, 8), np.arange(CTX - 8, CTX)])
            pc[ps, 34 + tile * 16:34 + (tile + 1) * 16] = _invcnt(tc, CTX, win)[None, :]
    return pc


def _params(inp):
    P = np.zeros((DEPTH, 128, NP_COLS), np.float32)
    for l in range(DEPTH):
        P[l, :, 0:8] = inp["g_norm1"][l].reshape(8, 128).T
        P[l, :, 8:16] = inp["g_norm2"][l].reshape(8, 128).T
        P[l, :, 16:64] = inp["b_mod"][l].reshape(48, 128).T
        P[l, :, 64] = np.tile(inp["q_norm_g"][l], 2)
        P[l, :, 65] = np.tile(inp["k_norm_g"][l], 2)
        cw = inp["conv_dw_w"][l]
        for vt in range(2):
            P[l, :, 66 + vt * 31:66 + (vt + 1) * 31] = cw[:, vt * 128:(vt + 1) * 128].T
        P[l, :, 128:130] = inp["conv_dw_b"][l].reshape(2, 128).T
        P[l, :, 130:132] = inp["conv_ln_g"][l].reshape(2, 128).T
        P[l, :, 132:134] = inp["conv_ln_b"][l].reshape(2, 128).T
        P[l, :, 134:136] = inp["pool_scale"][l].reshape(2, 128).T
        P[l, 0:64, 136] = 1.0 / 2
        P[l, 64:128, 136] = 1.0 / 4
        P[l, 0:64, 137] = 1.0 / 8
        P[l, 64:128, 137] = 1.0 / 16
    return P


_QPERM = np.concatenate([np.arange(0, 64), np.arange(128, 192), np.arange(64, 128), np.arange(192, 256), np.arange(256, D_IN)])


def prep_inputs(inp):
    inp = {k: np.asarray(v) for k, v in inp.items()}
    cst = _consts()
    params = _params(inp)
    w_in_p = np.ascontiguousarray(inp["w_in"][:, :, _QPERM])
    shared = dict(params=params, w_mod=inp["w_mod"], w_in=w_in_p, w_fourier=inp["w_fourier"], w_conv_pw=inp["w_conv_pw"],
                  w_pool=inp["w_pool"], w_out=inp["w_out"], w_ffn_in=inp["w_ffn_in"], w_ffn_out=inp["w_ffn_out"], **cst)
    maps = []
    for i in range(8):
        b, j = i // 4, i % 4
        m = dict(shared)
        m["xT"] = np.ascontiguousarray(inp["x"][b, j * TL:(j + 1) * TL, :].T)
        m["ctxT"] = np.ascontiguousarray(inp["ctx"][b].T)
        ccv = np.zeros((128, 16), np.float32)
        ccv[:, 0::2] = inp["c"][b].reshape(8, 128).T
        ccv[:, 1::2] = inp["c_ctx"].reshape(8, 128).T
        m["cc"] = ccv
        rc, rs = _rope_tables(j * TL)
        m["ropeC"] = rc
        m["ropeS"] = rs
        m["percore"] = _percore(j)
        maps.append(m)
    return maps


_NC_CACHE = {}


def kernel(**inputs):
    maps = prep_inputs(inputs)
    if "nc" not in _NC_CACHE:
        _NC_CACHE["nc"] = build_program()[0]
    nc = _NC_CACHE["nc"]
    res = run_bass_kernel_spmd(nc, maps, core_ids=list(range(8)))
    out = np.zeros((2, SEQ, D), np.float32)
    for i in range(8):
        b, j = i // 4, i % 4
        out[b, j * TL:(j + 1) * TL, :] = res.results[i]["outT"].T
    return out
```

```python
import contextlib
import numpy as np
import concourse.bass as bass
import concourse.mybir as mybir
from concourse.bass_utils import run_bass_kernel_spmd

F32 = mybir.dt.float32
BF16 = mybir.dt.bfloat16
AF = mybir.ActivationFunctionType
ALU = mybir.AluOpType

D = 1024
SEQ = 8192
TL = 2048
CTX = 256
DEPTH = 2
D_IN = 1536
D_FF = 2816
NHC = D_FF // 128
EPS = 1e-6
NKT = (CTX + SEQ) // 128
NP_COLS = 138
R1 = 832


class Dep:
    __slots__ = ("name", "w", "r", "sem_in", "cnt_in", "sem_out", "cnt_out")

    def __init__(self, name=""):
        self.name = name
        self.w = None
        self.r = []
        self.sem_in = None
        self.cnt_in = 0
        self.sem_out = None
        self.cnt_out = 0


class Op:
    __slots__ = ("eng", "fn", "deps", "alldeps", "signaled", "sigidx", "dsem", "dval", "name", "dinc", "seq", "cost", "lat", "seg", "eidx", "pend", "tmin")

    def __init__(self, eng, fn, name=""):
        self.eng = eng
        self.fn = fn
        self.deps = []
        self.signaled = False
        self.sigidx = None
        self.dsem = None
        self.dval = 0
        self.dinc = 16
        self.seq = 0
        self.cost = 0.0
        self.lat = 0.0
        self.seg = 0
        self.eidx = 0
        self.alldeps = []
        self.pend = None
        self.tmin = 0.0
        self.name = name


ENGS = ["pe", "act", "dve", "pool", "sp"]


class Sched:
    def __init__(self, nc, stack):
        self.nc = nc
        self.stack = stack
        self.ops = {e: [] for e in ENGS}
        self.esem = {e: stack.enter_context(nc.semaphore("es_" + e)) for e in ENGS}
        self.nsem = len(ENGS)
        self.pending_dma = []
        self.cc_sems = {}
        self.seg = 0
        self.ecount = 0
        self.reorder = True

    def new_sem(self, name):
        self.nsem += 1
        return self.stack.enter_context(self.nc.semaphore("%s_%d" % (name.replace(".", "_"), self.nsem)))

    def _collect(self, o, reads, writes, extra):
        deps = []
        seen = set()

        def add(d):
            if d is None or d is o or id(d) in seen:
                return
            seen.add(id(d))
            deps.append(d)

        for t in reads:
            add(t.w)
        for t in writes:
            add(t.w)
            for r in t.r:
                add(r)
        for d in extra:
            add(d)
        o.alldeps = deps
        o.deps = deps
        for t in reads:
            t.r.append(o)
        for t in writes:
            t.w = o
            t.r = []

    DEFCOST = {"pe": 0.25, "act": 0.6, "dve": 0.6, "pool": 1.2, "sp": 0.1}

    def _register(self, o):
        o.seg = self.seg
        o.eidx = self.ecount
        self.ecount += 1
        self.ops[o.eng].append(o)

    def op(self, eng, fn, reads=(), writes=(), extra=(), name="", cost=None):
        o = Op(eng, fn, name)
        o.cost = self.DEFCOST[eng] if cost is None else cost
        o.lat = o.cost
        self._collect(o, reads, writes, extra)
        self._register(o)
        return o

    def dma(self, q, out_ap, in_ap, reads=(), writes=(), sem_dep=None, out_side=False, extra=(), name="", tmin=0.0):
        if sem_dep is None:
            sem_dep = (reads[0] if out_side else writes[0])
        if out_side:
            if sem_dep.sem_out is None:
                sem_dep.sem_out = self.new_sem("do_" + sem_dep.name)
            sem_dep.cnt_out += 16
            dsem, dval = sem_dep.sem_out, sem_dep.cnt_out
        else:
            if sem_dep.sem_in is None:
                sem_dep.sem_in = self.new_sem("di_" + sem_dep.name)
            sem_dep.cnt_in += 16
            dsem, dval = sem_dep.sem_in, sem_dep.cnt_in

        def fn(eng, out_ap=out_ap, in_ap=in_ap):
            return eng.dma_start(out=out_ap, in_=in_ap)

        o = Op(q, fn, name)
        o.dsem, o.dval = dsem, dval
        o.cost, o.lat = 0.1, 6.0
        o.tmin = tmin
        self._collect(o, reads, writes, extra)
        o.alldeps = [d for d in o.alldeps if not (d.dsem is not None and d.dsem is dsem)]
        self._register(o)
        self.pending_dma.append(o)
        return o

    def collective(self, fn, reads=(), writes=(), extra=(), name="cc"):
        o = Op("pool", fn, name)
        if name not in self.cc_sems:
            self.cc_sems[name] = [self.new_sem("cc_" + name), 0]
        self.cc_sems[name][1] += 1
        o.dsem = self.cc_sems[name][0]
        o.dval = self.cc_sems[name][1]
        o.dinc = 1
        o.cost, o.lat = 0.5, 40.0
        self._collect(o, reads, writes, extra)
        self._register(o)
        return o

    def barrier(self):
        pend = list(self.pending_dma)
        self.pending_dma = []
        for e in ENGS:
            o = Op(e, None, "barrier")
            o.pend = pend
            o.seg = self.seg
            o.eidx = self.ecount
            self.ops[e].append(o)
        self.ecount += 1
        self.seg += 1

    def schedule(self):
        W = 64
        nseg = self.seg + 1
        segs = [{e: [] for e in ENGS} for _ in range(nseg)]
        bars = [{e: None for e in ENGS} for _ in range(nseg)]
        for e in ENGS:
            for o in self.ops[e]:
                if o.fn is None:
                    bars[o.seg][e] = o
                else:
                    segs[o.seg][e].append(o)
        new_ops = {e: [] for e in ENGS}
        for si in range(nseg):
            lists = segs[si]
            if self.reorder:
                finish = {}
                done = set()
                etime = {e: 0.0 for e in ENGS}
                remaining = {e: list(lists[e]) for e in ENGS}
                out = {e: [] for e in ENGS}
                total = sum(len(v) for v in remaining.values())
                while total:
                    best = None
                    for e in ENGS:
                        rem = remaining[e]
                        if not rem:
                            continue
                        seen_dma = False
                        for idx in range(min(W, len(rem))):
                            o = rem[idx]
                            if o.dsem is not None:
                                if seen_dma:
                                    continue
                                seen_dma = True
                            ready = o.tmin
                            ok = True
                            for d in o.alldeps:
                                if d.seg != si or d.fn is None:
                                    continue
                                if id(d) not in done:
                                    ok = False
                                    break
                                f = finish[id(d)] + (0.0 if d.eng == e else 0.15)
                                if f > ready:
                                    ready = f
                            if not ok:
                                continue
                            start = max(etime[e], ready)
                            key = (start, o.eidx)
                            if best is None or key < best[0]:
                                best = (key, e, idx, o)
                    if best is None:
                        raise RuntimeError("scheduler stuck")
                    (start, _), e, idx, o = best
                    remaining[e].pop(idx)
                    out[e].append(o)
                    done.add(id(o))
                    finish[id(o)] = start + o.lat
                    etime[e] = start + o.cost
                    total -= 1
                lists = out
            for e in ENGS:
                new_ops[e].extend(lists[e])
            if bars[si][ENGS[0]] is not None:
                lasts = [lists[e][-1] for e in ENGS if lists[e] and lists[e][-1].dsem is None]
                for e in ENGS:
                    b = bars[si][e]
                    b.alldeps = [d for d in lasts + b.pend if not (d.eng == e == "pe") or d.dsem is not None]
                    new_ops[e].append(b)
        self.ops = new_ops
        for e in ENGS:
            for i, o in enumerate(self.ops[e]):
                o.seq = i
        for e in ENGS:
            for o in self.ops[e]:
                bestd = {}
                for d in o.alldeps:
                    if d.dsem is not None:
                        key = ("s", id(d.dsem))
                        if key not in bestd or bestd[key].dval < d.dval:
                            bestd[key] = d
                    else:
                        key = ("e", d.eng)
                        if key not in bestd or bestd[key].seq < d.seq:
                            bestd[key] = d
                o.deps = list(bestd.values())

    def finalize(self, final_waits):
        for o in final_waits:
            if o.dsem is None:
                o.signaled = True
        for e in ENGS:
            for o in self.ops[e]:
                for d in o.deps:
                    if d.dsem is None:
                        if d.eng == "pe" and o.eng == "pe":
                            continue
                        d.signaled = True
        for e in ENGS:
            c = 0
            for o in self.ops[e]:
                if o.dsem is None and o.signaled:
                    c += 1
                    o.sigidx = c

    def emit(self, block, final_waits=()):
        self.schedule()
        self.finalize(final_waits)
        esem = self.esem

        def run(ename, eng):
            waited = {}
            for o in self.ops[ename]:
                for d in o.deps:
                    if d.dsem is not None:
                        key, val, sem = id(d.dsem), d.dval, d.dsem
                    else:
                        if d.eng == "pe" and ename == "pe":
                            continue
                        key, val, sem = d.eng, d.sigidx, esem[d.eng]
                    if waited.get(key, 0) >= val:
                        continue
                    waited[key] = val
                    eng.wait_ge(sem, val)
                if o.fn is None:
                    continue
                ins = o.fn(eng)
                if o.dsem is not None:
                    ins.then_inc(o.dsem, o.dinc)
                elif o.sigidx:
                    ins.then_inc(esem[ename], 1)
            if ename == "sp":
                fin = {}
                for o in final_waits:
                    if o.dsem is not None:
                        key, sem, val = id(o.dsem), o.dsem, o.dval
                    else:
                        key, sem, val = o.eng, esem[o.eng], o.sigidx
                    if key not in fin or fin[key][1] < val:
                        fin[key] = (sem, val)
                for sem, val in fin.values():
                    eng.wait_ge(sem, val)

        block.tensor(lambda eng: run("pe", eng))
        block.scalar(lambda eng: run("act", eng))
        block.vector(lambda eng: run("dve", eng))
        block.gpsimd(lambda eng: run("pool", eng))
        block.sync(lambda eng: run("sp", eng))


class Ring:
    def __init__(self, items):
        self.items = items
        self.i = 0

    def next(self):
        it = self.items[self.i % len(self.items)]
        self.i += 1
        return it


ARENA_WORDS = 52736


class Arena:
    def __init__(self, base_ap):
        self.base = base_ap
        self.free_list = [(0, ARENA_WORDS)]
        self.live = {}
        self.peak = 0

    def alloc(self, name, shape, dtype, top=False):
        elems = 1
        for d in shape[1:]:
            elems *= d
        esz = 4 if dtype == F32 else 2
        words = (elems * esz + 3) // 4
        words = (words + 15) // 16 * 16
        order = range(len(self.free_list) - 1, -1, -1) if top else range(len(self.free_list))
        for i in order:
            o, n = self.free_list[i]
            if n >= words:
                if top:
                    off = o + n - words
                    if n == words:
                        self.free_list.pop(i)
                    else:
                        self.free_list[i] = (o, n - words)
                else:
                    off = o
                    if n == words:
                        self.free_list.pop(i)
                    else:
                        self.free_list[i] = (o + words, n - words)
                break
        else:
            raise RuntimeError("arena full allocating %s (%d words); free=%s" % (name, words, self.free_list))
        v = self.base[0:shape[0], off:off + words]
        if dtype != F32:
            v = v.bitcast(dtype)
        v = v[:, 0:elems]
        if len(shape) == 3:
            v = v.rearrange("p (a b) -> p a b", b=shape[2])
        elif len(shape) == 4:
            v = v.rearrange("p (a b c) -> p a b c", b=shape[2], c=shape[3])
        elif len(shape) == 5:
            v = v.rearrange("p (a b c d) -> p a b c d", b=shape[2], c=shape[3], d=shape[4])
        self.live[name] = (off, words)
        used = ARENA_WORDS - sum(n for _, n in self.free_list)
        self.peak = max(self.peak, used)
        return v

    def free(self, *names):
        for name in names:
            off, words = self.live.pop(name)
            self.free_list.append((off, words))
        self.free_list.sort()
        merged = []
        for o, n in self.free_list:
            if merged and merged[-1][0] + merged[-1][1] == o:
                merged[-1] = (merged[-1][0], merged[-1][1] + n)
            else:
                merged.append((o, n))
        self.free_list = merged


def build_program(taps=(), stop_after=None):
    nc = bass.Bass("TRN2", target_bir_lowering=False)
    dt = nc.dram_tensor

    def ein(name, shape, dtype=F32):
        return dt(name, list(shape), dtype, kind="ExternalInput")

    xT_d = ein("xT", [D, TL])
    ctxT_d = ein("ctxT", [D, CTX])
    cc_d = ein("cc", [128, 16])
    ropeC_d = ein("ropeC", [128, TL])
    ropeS_d = ein("ropeS", [128, TL])
    percore_d = ein("percore", [128, 66])
    params_d = ein("params", [DEPTH, 128, NP_COLS])
    w_mod_d = ein("w_mod", [DEPTH, D, 6 * D])
    w_in_d = ein("w_in", [DEPTH, D, D_IN])
    w_f_d = ein("w_fourier", [DEPTH, 256, 256])
    w_pw_d = ein("w_conv_pw", [DEPTH, 256, 256])
    w_pool_d = ein("w_pool", [DEPTH, 4, 64, 64])
    w_out_d = ein("w_out", [DEPTH, D, D])
    w_fi_d = ein("w_ffn_in", [DEPTH, D, 2 * D_FF])
    w_fo_d = ein("w_ffn_out", [DEPTH, D_FF, D])
    cmat_d = ein("cmat", [128, 9, 128])
    csblk_d = ein("csblk", [128, 2, 512])
    tw_d = ein("tw", [128, 2, 512])
    c256_d = ein("cs256", [128, 2, 2, 256])
    outT_d = dt("outT", [D, TL], F32, kind="ExternalOutput")

    RC = [R1, R1, R1, R1]
    snde = dt("snde", [512, 32], BF16)
    rcve = dt("rcve", [4 * 512, 32], BF16)
    sndc = [dt("sndc%d" % c, [RC[c], 512], BF16) for c in range(4)]
    rcvc = [dt("rcvc%d" % c, [4 * RC[c], 512], BF16) for c in range(4)]
    snd2 = dt("snd2", [64, SEQ], BF16)
    rcv2 = dt("rcv2", [4 * 64, SEQ], BF16)
    RG = [[0, 1, 2, 3], [4, 5, 6, 7]]

    tap_out = {}
    final_ops = []
    stopped = [False]
    out_done = [False]

    with contextlib.ExitStack() as st:
        S = Sched(nc, st)
        pid = nc.partition_id()
        jr = pid % 4
        arena_t = st.enter_context(nc.sbuf_tensor("arena", [128, ARENA_WORDS], F32))
        A = Arena(arena_t[:, :])
        al = A.alloc

        psb = [st.enter_context(nc.psum_tensor("ps%d" % i, [128, 512], F32)) for i in range(8)]
        psd = [Dep("ps%d" % i) for i in range(8)]
        ps_main = Ring([(psb[i], psd[i]) for i in range(0, 4)])
        ps_aux = Ring([(psb[i], psd[i]) for i in range(4, 6)])
        ps_acc = Ring([(psb[i], psd[i]) for i in range(6, 8)])

        xT = al("xT", [128, 8, TL], F32)
        hT = al("hT", [128, 8, CTX], F32)
        d_x = [[Dep("x%d_%d" % (k, c)) for c in range(4)] for k in range(8)]
        d_h = [Dep("h%d" % k) for k in range(8)]
        cmat = al("cmat", [128, 9, 128], BF16)
        onesrow = al("onesrow", [128, 128], F32)
        csblk = al("csblk", [128, 2, 512], BF16)
        cs256 = al("cs256", [128, 2, 2, 256], BF16)
        percore = al("percore", [128, 66], F32)
        ccs = al("ccs", [128, 16], F32)
        scb = al("scb", [128, 8, 2], BF16)
        d_const = Dep("const")
        d_pc = Dep("percore")
        d_sc = Dep("sc")
        params = al("params", [128, NP_COLS], F32)
        d_params = Dep("params")
        modT2 = al("modT", [128, 2, 48, 2], F32)
        d_mod2 = [Dep("modT0"), Dep("modT1")]
        gm2 = al("gm", [128, 2, 2, 8, 2], F32)
        d_gm2 = [Dep("gm0"), Dep("gm1")]
        d_snde, d_rcve = Dep("snde"), Dep("rcve")
        d_sndc = [Dep("sndc%d" % i) for i in range(4)]
        d_rcvc = [Dep("rcvc%d" % i) for i in range(4)]
        d_snd2, d_rcv2 = Dep("snd2"), Dep("rcv2")

        ONES_MEAN, BLK64, SWAPP, ONESLN, R1M, C128S, S128S = range(7)

        xsrc = xT_d.ap().rearrange("(k p) t -> p k t", p=128)
        d_xload = [Dep("xload%d" % c) for c in range(4)]
        for c in range(4):
            S.dma("sp", xT[:, :, c * 512:(c + 1) * 512], xsrc[:, :, c * 512:(c + 1) * 512], writes=[d_x[k][c] for k in range(8)],
                  sem_dep=d_xload[c])
        hsrc = ctxT_d.ap().rearrange("(k p) t -> p k t", p=128)
        S.dma("sp", hT, hsrc, writes=d_h, sem_dep=Dep("hload"))
        S.dma("pool", cmat, cmat_d.ap(), writes=[d_const])
        S.dma("pool", csblk, csblk_d.ap(), writes=[d_const])
        S.dma("pool", cs256, c256_d.ap(), writes=[d_const])
        S.dma("sp", percore, percore_d.ap(), writes=[d_pc])
        S.dma("sp", ccs, cc_d.ap(), writes=[d_sc])
        cvec = al("cvec", [128, 4], F32)
        S.op("pool", lambda e: e.memset(cvec, EPS), writes=[d_const])
        S.op("pool", lambda e: e.memset(onesrow, 1.0), writes=[d_const])
        S.op("act", lambda e: e.activation(out=scb.rearrange("p k s -> p (k s)"), in_=ccs, func=AF.Silu),
             reads=[d_sc], writes=[d_sc])

        DEPS = {}

        def gdep(name):
            if name not in DEPS:
                DEPS[name] = Dep(name)
            return DEPS[name]

        def tap(name, ap_sb, shape, dtype, reads):
            if name not in taps:
                return
            t = dt("tap_" + name, list(shape), dtype, kind="ExternalOutput")
            o = S.dma("sp", t.ap(), ap_sb, reads=reads, out_side=True, sem_dep=Dep("tap" + name))
            final_ops.append(o)
            tap_out[name] = t

        def phase_end(names):
            S.barrier()
            A.free(*names)

        lat_chunks = [(0, xT, [d_x[k][c] for k in range(8)], c * 512, 512, c) for c in range(4)]
        ctx_chunk = (1, hT, d_h, 0, CTX, 4)

        def norm_mod(chunk, which, xn_ap, d_xn, tmps, mods):
            s, X, dX, T0, n, ci = chunk
            sqb, d_sq, rstd, d_rstd, tmpr = tmps
            modT, gm, d_mod, d_gm = mods
            pb, pd = ps_aux.next()
            for k in range(8):
                S.op("act", lambda e, k=k: e.activation(out=sqb[:, k % 4, :n], in_=X[:, k, T0:T0 + n], func=AF.Square),
                     reads=[dX[k]], writes=[d_sq[k % 4]])
                S.op("pe", lambda e, k=k: e.matmul(pb[:, :n], cmat[:, ONES_MEAN, :], sqb[:, k % 4, :n], start=(k == 0), stop=(k == 7)),
                     reads=[d_sq[k % 4], d_const], writes=[pd])
            S.op("act", lambda e: e.activation(out=rstd[:, :n], in_=pb[:, :n], func=AF.Ln, bias=cvec[:, 0:1]), reads=[pd, d_const], writes=[d_rstd])
            S.op("act", lambda e: e.activation(out=rstd[:, :n], in_=rstd[:, :n], func=AF.Exp, scale=-0.5), reads=[d_rstd], writes=[d_rstd])
            shift_chunk = 0 if which == 0 else 3
            for k in range(8):
                tb, td = tmpr.next()
                S.op("dve", lambda e, k=k, tb=tb: e.tensor_tensor(out=tb[:, :n], in0=X[:, k, T0:T0 + n], in1=rstd[:, :n], op=ALU.mult),
                     reads=[dX[k], d_rstd], writes=[td])
                S.op("act", lambda e, k=k, tb=tb: e.activation(out=xn_ap[:, k, :n], in_=tb[:, :n], func=AF.Identity,
                                                             bias=modT[:, shift_chunk * 8 + k, s:s + 1], scale=gm[:, which, k, s:s + 1]),
                     reads=[td, d_gm, d_mod], writes=[d_xn[k]])

        def stage_M(l, part):
            modT, gm, d_mod, d_gm = modT2[:, l % 2], gm2[:, l % 2], d_mod2[l % 2], d_gm2[l % 2]
            if part == "begin":
                S.dma("sp", params, params_d.ap()[l], writes=[d_params])
                return
            if part == "end":
                for which, sc_chunk, gcol in ((0, 1, 0), (1, 4, 8)):
                    for s_ in range(2):
                        S.op("dve", lambda e, which=which, sc_chunk=sc_chunk, gcol=gcol, s_=s_: e.scalar_tensor_tensor(
                            out=gm[:, which, :, s_], in0=modT[:, sc_chunk * 8:(sc_chunk + 1) * 8, s_], scalar=1.0,
                            in1=params[:, gcol:gcol + 8], op0=ALU.add, op1=ALU.mult),
                            reads=[d_mod, d_params], writes=[d_gm], cost=0.2)
                tap("modT%d" % l, modT, [128, 48, 2], F32, [d_mod])
                tap("gm%d" % l, gm, [128, 2, 8, 2], F32, [d_gm])
                return
            oc = part
            wm = [A_views["wm0"], A_views["wm1"]]
            d_wm = [gdep("wm%d" % i) for i in range(2)]
            wsrc = w_mod_d.ap()[l].rearrange("(k p) o -> p k o", p=128)
            sl = oc % 2
            S.dma("pool", wm[sl], wsrc[:, :, oc * 512:(oc + 1) * 512], writes=[d_wm[sl]])
            for o4 in range(4):
                o = oc * 4 + o4
                pb, pd = ps_aux.next()
                for k in range(8):
                    S.op("pe", lambda e, k=k, sl=sl, o4=o4, pb=pb: e.matmul(pb[:, 0:2], wm[sl][:, k, o4 * 128:(o4 + 1) * 128], scb[:, k, :],
                                                                          start=(k == 0), stop=(k == 7)),
                         reads=[d_wm[sl], d_sc], writes=[pd], cost=0.06)
                S.op("dve", lambda e, o=o, pb=pb: e.tensor_scalar(out=modT[:, o, :], in0=pb[:, 0:2], scalar1=params[:, 16 + o:17 + o], scalar2=None,
                                                                 op0=ALU.add),
                     reads=[pd, d_params], writes=[d_mod], cost=0.2)

        A_views = {}
        A_views["wm0"] = al("wm0", [128, 8, 512], BF16)
        A_views["wm1"] = al("wm1", [128, 8, 512], BF16)
        for part in ["begin"] + list(range(12)) + ["end"]:
            stage_M(0, part)
        phase_end(["wm0", "wm1"])

        def do_layer(l):
            last = (l == DEPTH - 1)
            L = "%d" % l
            modT, gm, d_mod, d_gm = modT2[:, l % 2], gm2[:, l % 2], d_mod2[l % 2], d_gm2[l % 2]
            mods = (modT, gm, d_mod, d_gm)
            ycT = al("ycT", [128, 2, CTX], BF16, top=True)
            d_ycT = gdep("ycT")
            qT = al("qT", [128, 4, TL], BF16, top=True)
            qC = al("qC", [128, 4, CTX], BF16, top=True)
            d_q = [[gdep("q%d_%d" % (t, c)) for c in range(5)] for t in range(2)]
            KTc = al("KTc", [128, CTX], BF16, top=True)
            d_KTc = gdep("KTc")
            Vxc = al("Vxc", [128, 2, 192], BF16, top=True)
            d_Vxc = gdep("Vxc")
            zc_tm = al("zctm", [128, 2, 512], BF16, top=True)
            d_zc = gdep("zc")
            glu = al("glu", [128, 2, TL + 30], BF16, top=True)
            gluC = al("gluC", [128, 2, CTX + 30], BF16, top=True)
            pu = al("pu", [128, 2, TL + 16], BF16, top=True)
            puC = al("puC", [128, 2, CTX + 16], BF16, top=True)
            d_glu = [gdep("glu%d" % i) for i in range(2)]
            d_gluC = [gdep("gluC%d" % i) for i in range(2)]
            d_pu = [gdep("pu%d" % i) for i in range(2)]
            d_puC = [gdep("puC%d" % i) for i in range(2)]
            S.op("pool", lambda e: e.memset(qT, 0.0), writes=[gdep("q%d_%d" % (t, c)) for t in range(2) for c in range(4)])
            if not last:
                S.op("pool", lambda e: e.memset(qC, 0.0), writes=[gdep("q%d_4" % t) for t in range(2)])
            S.op("pool", lambda e: e.memset(Vxc, 1.0), writes=[d_Vxc])
            if not last:
                S.op("pool", lambda e: e.memset(gluC, 0.0), writes=d_gluC)
                S.op("pool", lambda e: e.memset(puC, 0.0), writes=d_puC)

            diag = al("diag", [128, 62, 128], BF16, top=True)
            d_diag = gdep("diag")
            for idx in range(62):
                S.op("dve", lambda e, idx=idx: e.tensor_scalar(out=diag[:, idx, :], in0=cmat[:, 7, :], scalar1=params[:, 66 + idx:67 + idx], scalar2=None,
                                                               op0=ALU.mult),
                     reads=[d_const, d_params], writes=[d_diag])
            wpw = al("wpw", [128, 2, 256], BF16, top=True)
            wpool = al("wpool", [128, 2, 128], BF16, top=True)
            d_wcp = gdep("wcp")
            S.dma("pool", wpw, w_pw_d.ap()[l].rearrange("(k p) o -> p k o", p=128), writes=[d_wcp])
            S.op("pool", lambda e: e.memset(wpool, 0.0), writes=[d_wcp])
            for g in range(4):
                tile_, half = g // 2, g % 2
                S.dma("pool", wpool[half * 64:(half + 1) * 64, tile_, half * 64:(half + 1) * 64], w_pool_d.ap()[l, g], writes=[d_wcp],
                      reads=[])
            w_in = al("w_in", [128, 8, D_IN], BF16)
            d_win = gdep("w_in")
            wisrc = w_in_d.ap()[l].rearrange("(k p) o -> p k o", p=128)
            for c3 in range(3):
                S.dma("pool", w_in[:, :, c3 * 512:(c3 + 1) * 512], wisrc[:, :, c3 * 512:(c3 + 1) * 512], writes=[d_win])
            xn_r = Ring([(al("xnA%d" % i, [128, 8, 512], BF16), [gdep("xnA%d_%d" % (i, k)) for k in range(8)]) for i in range(2)])
            sqb = al("sqbA", [128, 4, 512], BF16)
            d_sq = [gdep("sqA%d" % k) for k in range(8)]
            rstd = al("rstdA", [128, 512], F32)
            d_rstd = gdep("rstdA")
            tmpr = Ring([(al("tmpA%d" % i, [128, 512], F32), gdep("tmpA%d" % i)) for i in range(2)])
            ntmps = (sqb, d_sq, rstd, d_rstd, tmpr)
            ropeC = al("ropeC", [128, 512], F32)
            ropeS = al("ropeS", [128, 512], F32)
            d_rope = gdep("rope")
            sqq = al("sqq", [128, 512], BF16)
            d_sqq = gdep("sqq")
            rs2 = al("rs2", [128, 512], F32)
            d_rs2 = gdep("rs2")
            qg = al("qg", [128, 512], BF16)
            d_qg = gdep("qg")
            kloc = al("kloc", [128, 512], BF16)
            d_kloc = gdep("kloc")
            vsb = al("vsb", [128, 4, 192], BF16)
            d_vsb = gdep("vsb")
            S.op("pool", lambda e: e.memset(vsb, 1.0), writes=[d_vsb])
            ub = al("ub", [128, 2, 512], BF16)
            d_ub = [gdep("ub0"), gdep("ub1")]
            zsb = al("zsb", [128, 4, 512], BF16)
            d_zsb = gdep("zsb")
            a_names = ["w_in", "xnA0", "xnA1", "sqbA", "rstdA", "tmpA0", "tmpA1", "ropeC", "ropeS", "sqq", "rs2", "qg", "kloc", "vsb",
                       "ub", "zsb"]

            def make_helpers(xn, d_xn):
                def qk_tile(chunk, ctile, dest_ap, d_dest, gcol, rope):
                    s, X, dX, T0, n, ci = chunk
                    pb, pd = ps_main.next()
                    for k in range(8):
                        S.op("pe", lambda e, k=k: e.matmul(pb[:, :n], w_in[:, k, ctile * 128:(ctile + 1) * 128], xn[:, k, :n],
                                                           start=(k == 0), stop=(k == 7)),
                             reads=[d_win, d_xn[k]], writes=[pd])
                    S.op("act", lambda e: e.activation(out=sqq[:, :n], in_=pb[:, :n], func=AF.Square), reads=[pd], writes=[d_sqq])
                    p2, pd2 = ps_aux.next()
                    S.op("pe", lambda e: e.matmul(p2[:, :n], cmat[:, BLK64, :], sqq[:, :n], start=True, stop=True),
                         reads=[d_sqq, d_const], writes=[pd2])
                    S.op("act", lambda e: e.activation(out=rs2[:, :n], in_=p2[:, :n], func=AF.Ln, bias=cvec[:, 0:1]), reads=[pd2, d_const], writes=[d_rs2])
                    S.op("act", lambda e: e.activation(out=rs2[:, :n], in_=rs2[:, :n], func=AF.Exp, scale=-0.5), reads=[d_rs2], writes=[d_rs2])
                    if not rope:
                        for (psl, dap) in dest_ap:
                            S.op("dve", lambda e, psl=psl, dap=dap: e.scalar_tensor_tensor(out=dap, in0=pb[psl, :n], scalar=params[psl, gcol:gcol + 1],
                                                                                           in1=rs2[psl, :n], op0=ALU.mult, op1=ALU.mult),
                                 reads=[pd, d_rs2, d_params], writes=[d_dest])
                        return
                    S.op("dve", lambda e: e.scalar_tensor_tensor(out=qg[:, :n], in0=pb[:, :n], scalar=params[:, gcol:gcol + 1], in1=rs2[:, :n],
                                                                 op0=ALU.mult, op1=ALU.mult),
                         reads=[pd, d_rs2, d_params], writes=[d_qg])
                    p3, pd3 = ps_aux.next()
                    S.op("pe", lambda e: e.matmul(p3[:, :n], cmat[:, SWAPP, :], qg[:, :n], start=True, stop=True),
                         reads=[d_qg, d_const], writes=[pd3])
                    t1, td1 = tmpr.next()
                    t2, td2 = tmpr.next()
                    S.op("dve", lambda e: e.tensor_tensor(out=t1[:, :n], in0=qg[:, :n], in1=ropeC[:, :n], op=ALU.mult),
                         reads=[d_qg, d_rope], writes=[td1])
                    S.op("dve", lambda e: e.tensor_tensor(out=t2[:, :n], in0=p3[:, :n], in1=ropeS[:, :n], op=ALU.mult),
                         reads=[pd3, d_rope], writes=[td2])
                    for (psl, dap) in dest_ap:
                        S.op("dve", lambda e, psl=psl, dap=dap: e.tensor_tensor(out=dap, in0=t1[psl, :n], in1=t2[psl, :n], op=ALU.add),
                             reads=[td1, td2], writes=[d_dest])

                def v_tiles(chunk, dest_fn, d_dest):
                    s, X, dX, T0, n, ci = chunk
                    for tt in range(n // 128):
                        pb, pd = ps_main.next()
                        for k in range(8):
                            S.op("pe", lambda e, k=k, tt=tt, pb=pb: e.matmul(pb[:, 0:128], xn[:, k, tt * 128:(tt + 1) * 128], w_in[:, k, 384:512],
                                                                              start=(k == 0), stop=(k == 7)),
                                 reads=[d_win, d_xn[k]], writes=[pd])
                        S.op("act", lambda e, tt=tt, pb=pb: e.copy(out=dest_fn(tt)[:, 0:64], in_=pb[:, 0:64]), reads=[pd], writes=[d_dest])
                        S.op("act", lambda e, tt=tt, pb=pb: e.copy(out=dest_fn(tt)[:, 128:192], in_=pb[:, 64:128]), reads=[pd], writes=[d_dest])

                def dest_in(pb):
                    return pb[:, 0:128]

                def proj_tile(chunk, ctile):
                    s, X, dX, T0, n, ci = chunk
                    pb, pd = ps_main.next()
                    for k in range(8):
                        S.op("pe", lambda e, k=k: e.matmul(pb[:, :n], w_in[:, k, ctile * 128:(ctile + 1) * 128], xn[:, k, :n],
                                                           start=(k == 0), stop=(k == 7)),
                             reads=[d_win, d_xn[k]], writes=[pd])
                    return pb, pd
                return qk_tile, v_tiles, proj_tile

            chunks = [ctx_chunk] + lat_chunks
            def do_chunk_A(chunk):
                s, X, dX, T0, n, ci = chunk
                isctx = (s == 1)
                so_box = []
                xn, d_xn = xn_r.next()
                qk_tile, v_tiles, proj_tile = make_helpers(xn, d_xn)
                norm_mod(chunk, 0, xn, d_xn, ntmps, mods)
                if not isctx:
                    S.dma("sp", ropeC[:, :n], ropeC_d.ap()[:, T0:T0 + n], writes=[d_rope])
                    S.dma("sp", ropeS[:, :n], ropeS_d.ap()[:, T0:T0 + n], writes=[d_rope])
                full = not (isctx and last)
                if full:
                    H0, H1, ALLP = slice(0, 64), slice(64, 128), slice(0, 128)
                    for tq in range(2):
                        if isctx:
                            dest = [(H0, qC[0:64, tq, :]), (H1, qC[64:128, 2 + tq, :])]
                        else:
                            dest = [(H0, qT[0:64, tq, T0:T0 + n]), (H1, qT[64:128, 2 + tq, T0:T0 + n])]
                        qk_tile(chunk, tq, dest, d_q[tq][ci], 64, rope=not isctx)
                ALLP = slice(0, 128)
                if isctx:
                    qk_tile(chunk, 2, [(ALLP, KTc[:, :])], d_KTc, 65, rope=False)
                else:
                    qk_tile(chunk, 2, [(ALLP, kloc[:, :n])], d_kloc, 65, rope=True)
                if isctx:
                    for tt in range(2):
                        pb, pd = ps_main.next()
                        for k in range(8):
                            S.op("pe", lambda e, k=k, tt=tt, pb=pb: e.matmul(pb[:, 0:128], xn[:, k, tt * 128:(tt + 1) * 128], w_in[:, k, 384:512],
                                                                              start=(k == 0), stop=(k == 7)),
                                 reads=[d_win, d_xn[k]], writes=[pd])
                        S.op("act", lambda e, tt=tt, pb=pb: e.copy(out=Vxc[:, tt, 0:64], in_=pb[:, 0:64]), reads=[pd], writes=[d_Vxc])
                        S.op("act", lambda e, tt=tt, pb=pb: e.copy(out=Vxc[:, tt, 128:192], in_=pb[:, 64:128]), reads=[pd], writes=[d_Vxc])
                else:
                    v_tiles(chunk, lambda tt: vsb[:, tt, :], d_vsb)
                if not full:
                    return
                for ut in range(2):
                    pb, pd = proj_tile(chunk, 4 + ut)
                    S.op("act", lambda e, ut=ut, pb=pb: e.copy(out=ub[:, ut, :n], in_=pb[:, :n]), reads=[pd], writes=[d_ub[ut]])
                if isctx:
                    for tt in range(2):
                        pb, pd = ps_main.next()
                        for k2 in range(2):
                            S.op("pe", lambda e, k2=k2, tt=tt, pb=pb: e.matmul(pb[:, :], ub[:, k2, tt * 128:(tt + 1) * 128], csblk[:, k2, :],
                                                                                start=(k2 == 0), stop=(k2 == 1)),
                                 reads=[d_ub[k2], d_const], writes=[pd])
                        S.op("act", lambda e, tt=tt, pb=pb: e.copy(out=zc_tm[:, tt, :], in_=pb[:, :]), reads=[pd], writes=[d_zc])
                else:
                    for zt in range(4):
                        pb, pd = ps_main.next()
                        for k2 in range(2):
                            S.op("pe", lambda e, k2=k2, zt=zt, pb=pb: e.matmul(pb[:, :n], csblk[:, k2, zt * 128:(zt + 1) * 128], ub[:, k2, :n],
                                                                                start=(k2 == 0), stop=(k2 == 1)),
                                 reads=[d_ub[k2], d_const], writes=[pd])
                        S.op("dve", lambda e, zt=zt, pb=pb: e.tensor_copy(out=zsb[:, zt, :n], in_=pb[:, :n]), reads=[pd], writes=[d_zsb])
                    sc_ = sndc[ci].ap()
                    so = [S.dma("sp", sc_[320:832, :].rearrange("(z p) t -> p z t", p=128), zsb[:, :, :n], reads=[d_zsb], writes=[d_sndc[ci]],
                                out_side=True),
                          S.dma("sp", sc_[0:128, :], kloc[:, :n], reads=[d_kloc], writes=[d_sndc[ci]], out_side=True),
                          S.dma("sp", sc_[128:320, :].rearrange("r c -> (r c)").rearrange("(tt p c) -> p tt c", p=128, c=192), vsb,
                                reads=[d_vsb], writes=[d_sndc[ci]], out_side=True)]
                    so_box.append(so)
                    if ci == 0:
                        tap("zsb" + L, zsb, [128, 4, 512], BF16, [d_zsb])
                for vt in range(2):
                    pv, pdv = proj_tile(chunk, 6 + vt)
                    pg, pdg = proj_tile(chunk, 8 + vt)
                    sig, d_sig = tmpr.next()
                    S.op("act", lambda e, pg=pg, sig=sig: e.activation(out=sig[:, :n], in_=pg[:, :n], func=AF.Sigmoid), reads=[pdg], writes=[d_sig])
                    gdst = (gluC[:, vt, 15:15 + n] if isctx else glu[:, vt, 15 + T0:15 + T0 + n])
                    S.op("dve", lambda e, pv=pv, gdst=gdst, sig=sig: e.tensor_tensor(out=gdst, in0=pv[:, :n], in1=sig[:, :n], op=ALU.mult),
                         reads=[pdv, d_sig], writes=[(d_gluC if isctx else d_glu)[vt]])
                for pt in range(2):
                    pb, pd = proj_tile(chunk, 10 + pt)
                    pdst = (puC[:, pt, 8:8 + n] if isctx else pu[:, pt, 8 + T0:8 + T0 + n])
                    S.op("act", lambda e, pb=pb, pdst=pdst: e.copy(out=pdst, in_=pb[:, :n]), reads=[pd],
                         writes=[(d_puC if isctx else d_pu)[pt]])

                if so_box:
                    so = so_box[0]
                    S.collective(lambda e: e.collective_compute("AllGather", ALU.bypass, replica_groups=RG, ins=[sndc[ci].ap()],
                                                                outs=[rcvc[ci].ap()]),
                                 reads=[d_sndc[ci]], writes=[d_rcvc[ci]], extra=so, name="gc%d" % ci)

            for chunk in chunks:
                do_chunk_A(chunk)
            sev = snde.ap().rearrange("(a p) c -> p a c", p=128)
            e_ops = []
            e_ops.append(S.dma("sp", sev[:, 0:2, 0:16], glu[:, :, 15:31], reads=d_glu, writes=[d_snde], out_side=True, sem_dep=d_glu[0]))
            e_ops.append(S.dma("sp", sev[:, 0:2, 16:32], glu[:, :, TL - 1:TL + 15], reads=d_glu, writes=[d_snde], out_side=True, sem_dep=d_glu[0]))
            e_ops.append(S.dma("sp", sev[:, 2:4, 0:16], pu[:, :, 8:24], reads=d_pu, writes=[d_snde], out_side=True, sem_dep=d_pu[0]))
            e_ops.append(S.dma("sp", sev[:, 2:4, 16:32], pu[:, :, TL - 8:TL + 8], reads=d_pu, writes=[d_snde], out_side=True, sem_dep=d_pu[0]))
            S.collective(lambda e: e.collective_compute("AllGather", ALU.bypass, replica_groups=RG, ins=[snde.ap()], outs=[rcve.ap()]),
                         reads=[d_snde], writes=[d_rcve], extra=e_ops, name="ge")

            tap("q" + L, qT, [128, 4, TL], BF16, [d_q[0][3], d_q[1][3], d_q[0][0], d_q[1][0]])
            tap("glu" + L, glu, [128, 2, TL + 30], BF16, d_glu)
            tap("pu" + L, pu, [128, 2, TL + 16], BF16, d_pu)
            tap("KTc" + L, KTc, [128, CTX], BF16, [d_KTc])
            tap("Vxc" + L, Vxc, [128, 2, 192], BF16, [d_Vxc])
            phase_end(a_names)
            if stop_after == "A" + L:
                return True

            catT = al("catT", [128, 6, TL], BF16)
            catC = al("catC", [128, 6, CTX], BF16)
            d_cat = [[gdep("cat%d_%d" % (r, c)) for c in range(5)] for r in range(6)]
            halL = al("halL", [128, 4, 32], BF16)
            halR = al("halR", [128, 4, 32], BF16)
            d_hal = gdep("hal")
            d_haloG, d_haloP = gdep("haloG"), gdep("haloP")
            def halo_fill():
                jl = (jr + 3) % 4
                jrr = (jr + 1) % 4
                rv = rcve.ap()
                S.dma("sp", halL, rv[bass.ds(jl * 512, 512), :].rearrange("(a p) c -> p a c", p=128), reads=[d_rcve], writes=[d_hal], tmin=150.0)
                S.dma("sp", halR, rv[bass.ds(jrr * 512, 512), :].rearrange("(a p) c -> p a c", p=128), reads=[d_rcve], writes=[d_hal], tmin=150.0)
                S.op("dve", lambda e: e.tensor_scalar(out=glu[:, :, 0:15], in0=halL[:, 0:2, 17:32], scalar1=percore[:, 0:1], scalar2=None, op0=ALU.mult),
                     reads=[d_hal, d_pc], writes=[d_haloG])
                S.op("dve", lambda e: e.tensor_scalar(out=glu[:, :, TL + 15:TL + 30], in0=halR[:, 0:2, 0:15], scalar1=percore[:, 1:2], scalar2=None,
                                                      op0=ALU.mult),
                     reads=[d_hal, d_pc], writes=[d_haloG])
                S.op("dve", lambda e: e.tensor_scalar(out=pu[:, :, 0:8], in0=halL[:, 2:4, 24:32], scalar1=percore[:, 0:1], scalar2=None, op0=ALU.mult),
                     reads=[d_hal, d_pc], writes=[d_haloP])
                S.op("dve", lambda e: e.tensor_scalar(out=pu[:, :, TL + 8:TL + 16], in0=halR[:, 2:4, 0:8], scalar1=percore[:, 1:2], scalar2=None,
                                                      op0=ALU.mult),
                     reads=[d_hal, d_pc], writes=[d_haloP])

            accr = Ring([(al("acc%d" % i, [128, 2, 512], F32), [gdep("acc%d_0" % i), gdep("acc%d_1" % i)]) for i in range(1)])
            ybr = Ring([(al("yb%d" % i, [128, 2, 512], BF16), [gdep("yb%d_0" % i), gdep("yb%d_1" % i)]) for i in range(2)])
            ysqr = Ring([(al("ysq%d" % i, [128, 2, 512], BF16), [gdep("ysq%d_0" % i), gdep("ysq%d_1" % i)]) for i in range(2)])
            meanr = Ring([(al("mean%d" % i, [128, 512], F32), gdep("mean%d" % i)) for i in range(2)])
            msqr = Ring([(al("msq%d" % i, [128, 512], F32), gdep("msq%d" % i)) for i in range(1)])
            rstdr = Ring([(al("rstdc%d" % i, [128, 512], F32), gdep("rstdc%d" % i)) for i in range(2)])
            actr = Ring([(al("actb%d" % i, [128, 2, 512], BF16), [gdep("actb%d_0" % i), gdep("actb%d_1" % i)]) for i in range(2)])
            p1e = al("p1e", [128, 528], F32)
            w4e = al("w4e", [128, 528], F32)
            w8e = al("w8e", [128, 528], F32)
            d_p1e, d_w4e, d_w8e = gdep("p1e"), gdep("w4e"), gdep("w8e")
            WS = al("WS", [128, 2, 512], F32)
            d_WS = [gdep("WS0"), gdep("WS1")]
            ybp = al("ybp", [128, 2, 512], BF16)
            d_ybp = [gdep("ybp0"), gdep("ybp1")]
            tmp8 = al("tmp8", [128, 8], F32)
            d_tmp8 = gdep("tmp8")
            b1_names = ["halL", "halR", "wpw", "wpool", "diag", "acc0", "yb0", "yb1", "ysq0", "ysq1", "mean0", "mean1", "msq0",
                        "rstdc0", "rstdc1", "actb0", "actb1", "p1e", "w4e", "w8e", "WS", "ybp", "tmp8", "glu", "gluC", "pu", "puC"]

            def conv_pool_chunk(G, dG, U, dU, cat, ci, T0, n, T, fixcol):
                acc, d_acc = accr.next()
                yb, d_yb = ybr.next()
                ysq, d_ysq = ysqr.next()
                mean_sb, d_mean = meanr.next()
                msq, d_msq = msqr.next()
                rstdc, d_rstdc = rstdr.next()
                actb, d_actb = actr.next()
                pcs = []
                for vt in range(2):
                    pc, pdc = ps_main.next()
                    pcs.append((pc, pdc))
                    for j in range(31):
                        S.op("pe", lambda e, vt=vt, j=j, pc=pc: e.matmul(pc[:, :n], diag[:, vt * 31 + j, :], G[:, vt, T0 + j:T0 + j + n],
                                                                         start=(j == 0), stop=(j == 30)),
                             reads=dG[vt] + [d_diag], writes=[pdc])
                    S.op("act", lambda e, vt=vt, pc=pc: e.activation(out=yb[:, vt, :n], in_=pc[:, :n], func=AF.Identity, bias=params[:, 128 + vt:129 + vt]),
                         reads=[pdc, d_params], writes=[d_yb[vt]])
                    S.op("act", lambda e, vt=vt, pc=pc: e.activation(out=ysq[:, vt, :n], in_=pc[:, :n], func=AF.Square, bias=params[:, 128 + vt:129 + vt]),
                         reads=[pdc, d_params], writes=[d_ysq[vt]])
                pm, pdm = ps_aux.next()
                pq, pdq = ps_aux.next()
                for vt in range(2):
                    S.op("pe", lambda e, vt=vt: e.matmul(pm[:, :n], cmat[:, ONESLN, :], yb[:, vt, :n], start=(vt == 0), stop=(vt == 1)),
                         reads=[d_yb[vt], d_const], writes=[pdm])
                for vt in range(2):
                    S.op("pe", lambda e, vt=vt: e.matmul(pq[:, :n], cmat[:, ONESLN, :], ysq[:, vt, :n], start=(vt == 0), stop=(vt == 1)),
                         reads=[d_ysq[vt], d_const], writes=[pdq])
                S.op("act", lambda e: e.copy(out=mean_sb[:, :n], in_=pm[:, :n]), reads=[pdm], writes=[d_mean])
                S.op("dve", lambda e: e.tensor_tensor(out=msq[:, :n], in0=mean_sb[:, :n], in1=mean_sb[:, :n], op=ALU.mult),
                     reads=[d_mean], writes=[d_msq])
                S.op("dve", lambda e: e.tensor_tensor(out=rstdc[:, :n], in0=pq[:, :n], in1=msq[:, :n], op=ALU.subtract),
                     reads=[pdq, d_msq], writes=[d_rstdc])
                S.op("act", lambda e: e.activation(out=rstdc[:, :n], in_=rstdc[:, :n], func=AF.Ln, bias=cvec[:, 0:1]), reads=[d_rstdc, d_const],
                     writes=[d_rstdc])
                S.op("act", lambda e: e.activation(out=rstdc[:, :n], in_=rstdc[:, :n], func=AF.Exp, scale=-0.5), reads=[d_rstdc], writes=[d_rstdc])
                for vt in range(2):
                    pc, pdc = pcs[vt]
                    S.op("dve", lambda e, vt=vt, pc=pc: e.scalar_tensor_tensor(out=acc[:, vt, :n], in0=pc[:, :n], scalar=params[:, 128 + vt:129 + vt],
                                                                               in1=mean_sb[:, :n], op0=ALU.add, op1=ALU.subtract),
                         reads=[pdc, d_mean, d_params], writes=[d_acc[vt]])
                    S.op("dve", lambda e, vt=vt: e.tensor_tensor(out=acc[:, vt, :n], in0=acc[:, vt, :n], in1=rstdc[:, :n], op=ALU.mult),
                         reads=[d_rstdc, d_acc[vt]], writes=[d_acc[vt]])
                    S.op("act", lambda e, vt=vt: e.activation(out=actb[:, vt, :n], in_=acc[:, vt, :n], func=AF.Silu,
                                                              bias=params[:, 132 + vt:133 + vt], scale=params[:, 130 + vt:131 + vt]),
                         reads=[d_acc[vt], d_params], writes=[d_actb[vt]])
                for ot in range(2):
                    pb, pd = ps_main.next()
                    for vt in range(2):
                        S.op("pe", lambda e, vt=vt, ot=ot, pb=pb: e.matmul(pb[:, :n], wpw[:, vt, ot * 128:(ot + 1) * 128], actb[:, vt, :n],
                                                                            start=(vt == 0), stop=(vt == 1)),
                             reads=[d_actb[vt], d_wcp], writes=[pd])
                    S.op("act", lambda e, ot=ot, pb=pb: e.copy(out=cat[:, 2 + ot, T0:T0 + n], in_=pb[:, :n]), reads=[pd], writes=[d_cat[2 + ot][ci]])
                e_ = n + 16
                c0 = 8
                S.op("dve", lambda e: e.tensor_tensor(out=WS[0:64, 0, :n], in0=U[0:64, 0, T0 + c0 - 1:T0 + c0 - 1 + n], in1=U[0:64, 0, T0 + c0:T0 + c0 + n],
                                                       op=ALU.add),
                     reads=dU[0], writes=[d_WS[0]])
                S.op("dve", lambda e: e.tensor_tensor(out=p1e[64:128, 1:e_], in0=U[64:128, 0, T0:T0 + e_ - 1], in1=U[64:128, 0, T0 + 1:T0 + e_], op=ALU.add),
                     reads=dU[0], writes=[d_p1e])
                S.op("dve", lambda e: e.tensor_tensor(out=WS[64:128, 0, :n], in0=p1e[64:128, c0 - 1:c0 - 1 + n], in1=p1e[64:128, c0 + 1:c0 + 1 + n],
                                                       op=ALU.add),
                     reads=[d_p1e], writes=[d_WS[0]])
                S.op("dve", lambda e: e.tensor_tensor(out=p1e[:, 1:e_], in0=U[:, 1, T0:T0 + e_ - 1], in1=U[:, 1, T0 + 1:T0 + e_], op=ALU.add),
                     reads=dU[1], writes=[d_p1e])
                S.op("dve", lambda e: e.tensor_tensor(out=w4e[:, 2:e_ - 1], in0=p1e[:, 1:e_ - 2], in1=p1e[:, 3:e_], op=ALU.add),
                     reads=[d_p1e], writes=[d_w4e])
                S.op("dve", lambda e: e.tensor_tensor(out=WS[0:64, 1, :n], in0=w4e[0:64, c0 - 2:c0 - 2 + n], in1=w4e[0:64, c0 + 2:c0 + 2 + n], op=ALU.add),
                     reads=[d_w4e], writes=[d_WS[1]])
                S.op("dve", lambda e: e.tensor_tensor(out=w8e[64:128, 4:e_ - 3], in0=w4e[64:128, 2:e_ - 5], in1=w4e[64:128, 6:e_ - 1], op=ALU.add),
                     reads=[d_w4e], writes=[d_w8e])
                S.op("dve", lambda e: e.tensor_tensor(out=WS[64:128, 1, :n], in0=w8e[64:128, c0 - 4:c0 - 4 + n], in1=w8e[64:128, c0 + 4:c0 + 4 + n],
                                                       op=ALU.add),
                     reads=[d_w8e], writes=[d_WS[1]])
                for tl_ in range(2):
                    S.op("dve", lambda e, tl_=tl_: e.scalar_tensor_tensor(out=ybp[:, tl_, :n], in0=WS[:, tl_, :n], scalar=params[:, 136 + tl_:137 + tl_],
                                                                          in1=U[:, tl_, T0 + c0:T0 + c0 + n], op0=ALU.mult, op1=ALU.subtract),
                         reads=[d_WS[tl_], d_params] + dU[tl_], writes=[d_ybp[tl_]])
                    edges = []
                    if T0 == 0:
                        edges.append((0, fixcol + tl_ * 16))
                    if T0 + n == T:
                        edges.append((n - 8, fixcol + tl_ * 16 + 8))
                    for (e0, fc) in edges:
                        S.op("dve", lambda e, tl_=tl_, e0=e0, fc=fc: e.tensor_tensor(out=tmp8[:, :], in0=WS[:, tl_, e0:e0 + 8], in1=percore[:, fc:fc + 8],
                                                                                      op=ALU.mult),
                             reads=[d_WS[tl_], d_pc], writes=[d_tmp8])
                        S.op("dve", lambda e, tl_=tl_, e0=e0: e.tensor_tensor(out=ybp[:, tl_, e0:e0 + 8], in0=tmp8[:, :],
                                                                               in1=U[:, tl_, T0 + c0 + e0:T0 + c0 + e0 + 8], op=ALU.subtract),
                             reads=[d_tmp8] + dU[tl_], writes=[d_ybp[tl_]])
                    pb, pd = ps_main.next()
                    S.op("pe", lambda e, tl_=tl_, pb=pb: e.matmul(pb[:, :n], wpool[:, tl_, :], ybp[:, tl_, :n], start=True, stop=True),
                         reads=[d_ybp[tl_], d_wcp], writes=[pd])
                    S.op("act", lambda e, tl_=tl_, pb=pb: e.activation(out=cat[:, 4 + tl_, T0:T0 + n], in_=pb[:, :n], func=AF.Identity,
                                                                        scale=params[:, 134 + tl_:135 + tl_]),
                         reads=[pd, d_params], writes=[d_cat[4 + tl_][ci]])

            if not last:
                conv_pool_chunk(gluC, [[d] for d in d_gluC], puC, [[d] for d in d_puC], catC, 4, 0, CTX, CTX, 34)
            for c in (1, 2, 0, 3):
                edge = c in (0, 3)
                if c == 0:
                    halo_fill()
                conv_pool_chunk(glu, [[d] + ([d_haloG] if edge else []) for d in d_glu], pu, [[d] + ([d_haloP] if edge else []) for d in d_pu],
                                catT, c, c * 512, 512, TL, 2)
            tap("catconv" + L, catT[:, 2:6, :], [128, 4, TL], BF16, [d_cat[r][c] for r in range(2, 6) for c in range(4)])
            tap("catCconv" + L, catC[:, 2:6, :], [128, 4, CTX], BF16, [d_cat[r][4] for r in range(2, 6)])
            phase_end(b1_names)
            if stop_after == "B1" + L:
                return True

            KT = al("KT", [128, SEQ], BF16)
            d_KT = [gdep("KT%d" % r) for r in range(4)]
            X1 = al("X1", [128, 64, 128], BF16)
            d_X1 = gdep("X1")
            Gr = al("Gr", [128, 64, 64], BF16)
            Gi = al("Gi", [128, 64, 64], BF16)
            d_Gr, d_Gi = gdep("Gr"), gdep("Gi")
            Yout = al("Yout", [128, 64, 64], BF16)
            d_Yout = gdep("Yout")
            tw = al("tw", [128, 2, 512], F32)
            d_tw = gdep("tw")
            S.dma("sp", tw, tw_d.ap(), writes=[d_tw])
            ta = Ring([(al("twa%d" % i, [128, 512], F32), gdep("twa%d" % i)) for i in range(2)])
            tb_ = Ring([(al("twb%d" % i, [128, 512], F32), gdep("twb%d" % i)) for i in range(2)])
            b2_names = ["X1", "Gr", "Gi", "Yout", "tw", "twa0", "twa1", "twb0", "twb1"]
            for c in range(4):
                for ri in range(2):
                    for r in range(4):
                        src = rcvc[c].ap()[bass.ds(jr * 64 + (r * RC[c] + 320 + ri * 256), 64), :].rearrange("m (a t) -> a m t", t=128)
                        p0 = ri * 64 + r * 16 + c * 4
                        S.dma("sp", X1[p0:p0 + 4, :, :], src, reads=[d_rcvc[c]], writes=[d_X1])
            for r in range(4):
                for c in range(4):
                    S.dma("sp", KT[:, r * TL + c * 512:r * TL + (c + 1) * 512], rcvc[c].ap()[r * RC[c]:r * RC[c] + 128, :], reads=[d_rcvc[c]],
                          writes=[d_KT[r]])
            for mg in range(16):
                pb, pd = ps_main.next()
                for mi in range(4):
                    S.op("pe", lambda e, mi=mi, mg=mg, pb=pb: e.matmul(pb[:, mi * 128:(mi + 1) * 128], X1[:, mg * 4 + mi, :], cmat[:, R1M, :],
                                                                        start=True, stop=True),
                         reads=[d_X1, d_const], writes=[pd])
                a_, da = ta.next()
                b_, db = tb_.next()
                S.op("dve", lambda e, pb=pb, a_=a_: e.tensor_tensor(out=a_[:, :], in0=pb[:, :], in1=tw[:, 0, :], op=ALU.mult), reads=[pd, d_tw], writes=[da])
                S.op("dve", lambda e, pb=pb, b_=b_: e.tensor_tensor(out=b_[:, :], in0=pb[:, :], in1=tw[:, 1, :], op=ALU.mult), reads=[pd, d_tw], writes=[db])
                av = a_.rearrange("p (m r k) -> p m r k", r=2, k=64)
                bv = b_.rearrange("p (m r k) -> p m r k", r=2, k=64)
                S.op("dve", lambda e, av=av, bv=bv, mg=mg: e.tensor_tensor(out=Gr[:, mg * 4:(mg + 1) * 4, :], in0=av[:, :, 0, :], in1=bv[:, :, 1, :], op=ALU.add),
                     reads=[da, db], writes=[d_Gr])
                S.op("dve", lambda e, av=av, bv=bv, mg=mg: e.tensor_tensor(out=Gi[:, mg * 4:(mg + 1) * 4, :], in0=av[:, :, 1, :], in1=bv[:, :, 0, :],
                                                                             op=ALU.subtract),
                     reads=[da, db], writes=[d_Gi])
            Grf = Gr.rearrange("p m k -> p (m k)")
            Gif = Gi.rearrange("p m k -> p (m k)")
            Yf = Yout.rearrange("p m k -> p (m k)")
            for ch in range(8):
                pb, pd = ps_main.next()
                S.op("pe", lambda e, ch=ch, pb=pb: e.matmul(pb[:, :], cmat[:, C128S, :], Grf[:, ch * 512:(ch + 1) * 512], start=True, stop=False),
                     reads=[d_Gr, d_const], writes=[pd])
                S.op("pe", lambda e, ch=ch, pb=pb: e.matmul(pb[:, :], cmat[:, S128S, :], Gif[:, ch * 512:(ch + 1) * 512], start=False, stop=True),
                     reads=[d_Gi, d_const], writes=[pd])
                S.op("act", lambda e, ch=ch, pb=pb: e.copy(out=Yf[:, ch * 512:(ch + 1) * 512], in_=pb[:, :]), reads=[pd], writes=[d_Yout])
            o2 = S.dma("sp", snd2.ap().rearrange("m (k2 k1) -> k2 m k1", k1=64), Yout, reads=[d_Yout], writes=[d_snd2], out_side=True)
            tap("Yout" + L, Yout, [128, 64, 64], BF16, [d_Yout])
            S.collective(lambda e: e.collective_compute("AllGather", ALU.bypass, replica_groups=RG, ins=[snd2.ap()], outs=[rcv2.ap()]),
                         reads=[d_snd2], writes=[d_rcv2], extra=[o2], name="g2")
            if not last:
                for mt in range(2):
                    pb, pd = ps_main.next()
                    for tt in range(2):
                        S.op("pe", lambda e, tt=tt, mt=mt, pb=pb: e.matmul(pb[:, 0:CTX], zc_tm[:, tt, mt * 128:(mt + 1) * 128], cs256[:, 0, tt, :],
                                                                            start=(tt == 0), stop=False),
                             reads=[d_zc, d_const], writes=[pd])
                        S.op("pe", lambda e, tt=tt, mt=mt, pb=pb: e.matmul(pb[:, 0:CTX], zc_tm[:, tt, 256 + mt * 128:256 + (mt + 1) * 128], cs256[:, 1, tt, :],
                                                                            start=False, stop=(tt == 1)),
                             reads=[d_zc, d_const], writes=[pd])
                    S.op("act", lambda e, mt=mt, pb=pb: e.copy(out=ycT[:, mt, :], in_=pb[:, 0:CTX]), reads=[pd], writes=[d_ycT])
                tap("ycT" + L, ycT, [128, 2, CTX], BF16, [d_ycT])
            phase_end(b2_names)
            if stop_after == "B2" + L:
                return True

            attnT = al("attnT", [128, 2, TL], BF16, top=True)
            attnC = al("attnC", [128, 2, CTX], BF16, top=True)
            d_attn = [[gdep("attn%d_%d" % (h, c)) for c in range(5)] for h in range(4)]
            wo_att = al("wo_att", [128, 2, D], BF16)
            wo_rest = al("wo_rest", [128, 6, D], BF16)
            d_wo = gdep("wo")
            for tq in range(2):
                S.dma("pool", wo_att[0:64, tq, :], w_out_d.ap()[l, tq * 64:(tq + 1) * 64, :], writes=[d_wo])
                S.dma("pool", wo_att[64:128, tq, :], w_out_d.ap()[l, (2 + tq) * 64:(3 + tq) * 64, :], writes=[d_wo])
            S.dma("pool", wo_rest, w_out_d.ap()[l, 256:1024, :].rearrange("(r p) o -> p r o", p=128), writes=[d_wo])
            wf = al("wf", [128, 2, 256], BF16)
            d_wf = gdep("wf")
            S.dma("pool", wf, w_f_d.ap()[l].rearrange("(k p) o -> p k o", p=128), writes=[d_wf])
            Vx = al("Vx", [128, 64, 192], BF16)
            d_Vx = [gdep("Vx%d" % r) for r in range(4)]
            for r in range(4):
                for c in range(4):
                    rcv_ = rcvc[c].ap()
                    vsrc = rcv_[r * RC[c] + 128:r * RC[c] + 320, :].rearrange("r c -> (r c)").rearrange("(tt p c) -> p tt c", p=128, c=192)
                    S.dma("sp", Vx[:, r * 16 + c * 4:r * 16 + (c + 1) * 4, :], vsrc, reads=[d_rcvc[c]], writes=[d_Vx[r]])
            ering = Ring([(al("E%d" % i, [128, 512], BF16), gdep("E%d" % i)) for i in range(3)])
            b3_names = ["KT", "Vx", "E0", "E1", "E2", "ob0", "ob1", "rs0", "qT", "qC", "KTc", "Vxc", "zctm"]

            obr = Ring([(al("ob%d" % i, [128, 512], F32), gdep("ob%d" % i)) for i in range(2)])
            rsr = Ring([(al("rs%d" % i, [128, 512], F32), gdep("rsr%d" % i)) for i in range(1)])
            pending_fin = []

            def flush_fin():
                while pending_fin:
                    pending_fin.pop(0)()

            def attention(Q, dQ, ci, T0, n, key_tiles, dest):
                for tq in range(2):
                    for hf in range(2):
                        head = hf * 2 + tq
                        ps_ = slice(hf * 64, (hf + 1) * 64)
                        pO, pdO = ps_acc.next()
                        nk = len(key_tiles)
                        sbank = {}

                        def issue_S(kt, head=head, tq=tq, sbank=sbank):
                            Ksrc, dK, Vsrc, dV = key_tiles[kt]
                            pS, pdS = ps_main.next()
                            sbank[kt] = (pS, pdS)
                            S.op("pe", lambda e, pS=pS, Ksrc=Ksrc, head=head: e.matmul(pS[:, :n], Ksrc[:, :], Q[:, head, T0:T0 + n],
                                                                                      start=True, stop=True),
                                 reads=[dK, dQ[tq][ci]], writes=[pdS])

                        LA = 2
                        for k0 in range(min(LA, nk)):
                            issue_S(k0)
                        for kt in range(nk):
                            Ksrc, dK, Vsrc, dV = key_tiles[kt]
                            pS, pdS = sbank.pop(kt)
                            Eb, dE = ering.next()
                            S.op("act", lambda e, pS=pS, Eb=Eb: e.activation(out=Eb[:, :n], in_=pS[:, :n], func=AF.Exp, scale=0.125),
                                 reads=[pdS], writes=[dE])
                            S.op("pe", lambda e, Eb=Eb, Vsrc=Vsrc, kt=kt, pO=pO, hf=hf, nk=nk: e.matmul(pO[:, :n], Vsrc[:, hf * 64:hf * 64 + 128], Eb[:, :n],
                                                                                   start=(kt == 0), stop=(kt == nk - 1)),
                                 reads=[dE, dV], writes=[pdO])
                            if kt + LA < nk:
                                issue_S(kt + LA)
                            if kt == min(3, nk - 1):
                                flush_fin()
                        sr = (64 if hf == 0 else 0)
                        ob, dob = obr.next()
                        rs_, drs = rsr.next()
                        S.op("act", lambda e, pO=pO, ob=ob, ps_=ps_: e.copy(out=ob[ps_, :n], in_=pO[ps_, :n]), reads=[pdO], writes=[dob])
                        S.op("act", lambda e, pO=pO, rs_=rs_, sr=sr: e.activation(out=rs_[sr:sr + 1, :n], in_=pO[sr:sr + 1, :n], func=AF.Ln),
                             reads=[pdO], writes=[drs])
                        S.op("act", lambda e, rs_=rs_, sr=sr: e.activation(out=rs_[sr:sr + 1, :n], in_=rs_[sr:sr + 1, :n], func=AF.Exp, scale=-1.0),
                             reads=[drs], writes=[drs])

                        def fin(ob=ob, dob=dob, rs_=rs_, drs=drs, sr=sr, ps_=ps_, tq=tq, head=head):
                            pbc, pdbc = ps_aux.next()
                            S.op("pe", lambda e: e.matmul(pbc[:, :n], onesrow[sr:sr + 1, :], rs_[sr:sr + 1, :n], start=True, stop=True),
                                 reads=[drs, d_const], writes=[pdbc])
                            S.op("dve", lambda e: e.tensor_tensor(out=dest[ps_, tq, T0:T0 + n], in0=ob[ps_, :n], in1=pbc[ps_, :n], op=ALU.mult),
                                 reads=[dob, pdbc], writes=[d_attn[head][ci]])
                        pending_fin.append(fin)

            ctx_keys = [(KTc[:, tt * 128:(tt + 1) * 128], d_KTc, Vxc[:, tt, :], d_Vxc) for tt in range(2)]
            lat_keys = [(KT[:, tt * 128:(tt + 1) * 128], d_KT[tt // 16], Vx[:, tt, :], d_Vx[tt // 16]) for tt in range(64)]
            if not last:
                attention(qC, d_q, 4, 0, CTX, ctx_keys, attnC)
            for c in range(4):
                attention(qT, d_q, c, c * 512, 512, ctx_keys + lat_keys, attnT)
            flush_fin()
            tap("attnT" + L, attnT, [128, 2, TL], BF16, [d_attn[h][c] for h in range(4) for c in range(4)])
            tap("attnC" + L, attnC, [128, 2, CTX], BF16, [d_attn[h][4] for h in range(4)])
            phase_end(b3_names)
            if stop_after == "B3" + L:
                return True

            XN2 = al("XN2", [128, 8, TL], BF16, top=True)
            xnC = al("xnC", [128, 8, CTX], BF16, top=True)
            d_xn2 = [[gdep("xn2_%d_%d" % (c, k)) for k in range(8)] for c in range(5)]
            A_views["wfi0"] = al("wfi0", [128, 8, 1024], BF16, top=True)
            fisrc0 = w_fi_d.ap()[l].rearrange("(k p) o -> p k o", p=128)
            S.dma("pool", A_views["wfi0"][:, :, 0:512], fisrc0[:, :, 0:512], writes=[gdep("wfi0")])
            S.dma("pool", A_views["wfi0"][:, :, 512:1024], fisrc0[:, :, D_FF:D_FF + 512], writes=[gdep("wfi0")])
            yT = al("yT", [128, 2, 512], BF16)
            d_yT = gdep("yT")
            sqbC = al("sqbC", [128, 4, 512], BF16)
            d_sqC = [gdep("sqC%d" % k) for k in range(8)]
            rstdC = al("rstdC", [128, 512], F32)
            d_rstdC = gdep("rstdC")
            tmprC = Ring([(al("tmpC%d" % i, [128, 512], F32), gdep("tmpC%d" % i)) for i in range(3)])
            ntmpsC = (sqbC, d_sqC, rstdC, d_rstdC, tmprC)
            c1_names = ["wo_att", "wo_rest", "wf", "yT", "sqbC", "rstdC", "tmpC0", "tmpC1", "tmpC2", "catT", "catC", "attnT", "attnC", "ycT"]
            r2v = rcv2.ap()
            chunks = ([ctx_chunk] if not last else []) + lat_chunks
            def do_chunk_C1(chunk):
                s, X, dX, T0, n, ci = chunk
                isctx = (s == 1)
                cat = catC if isctx else catT
                att = attnC if isctx else attnT
                if isctx:
                    ysrc, dys = ycT, d_ycT
                else:
                    S.dma("sp", yT[:, :, :n], r2v[:, bass.ds(jr * TL + T0, n)].rearrange("(k p) t -> p k t", p=128), reads=[d_rcv2], writes=[d_yT])
                    ysrc, dys = yT, d_yT
                for ot in range(2):
                    pb, pd = ps_main.next()
                    for k2 in range(2):
                        S.op("pe", lambda e, k2=k2, ot=ot, pb=pb, ysrc=ysrc: e.matmul(pb[:, :n], wf[:, k2, ot * 128:(ot + 1) * 128], ysrc[:, k2, :n],
                                                                                       start=(k2 == 0), stop=(k2 == 1)),
                             reads=[dys, d_wf], writes=[pd])
                    S.op("act", lambda e, ot=ot, pb=pb, cat=cat: e.copy(out=cat[:, ot, T0:T0 + n], in_=pb[:, :n]), reads=[pd], writes=[d_cat[ot][ci]])
                for ot in range(8):
                    pb, pd = ps_main.next()
                    for h in range(2):
                        S.op("pe", lambda e, h=h, ot=ot, pb=pb, att=att: e.matmul(pb[:, :n], wo_att[:, h, ot * 128:(ot + 1) * 128], att[:, h, T0:T0 + n],
                                                                                   start=(h == 0), stop=False),
                             reads=[d_attn[h][ci], d_attn[2 + h][ci], d_wo], writes=[pd])
                    for r in range(6):
                        S.op("pe", lambda e, r=r, ot=ot, pb=pb, cat=cat: e.matmul(pb[:, :n], wo_rest[:, r, ot * 128:(ot + 1) * 128], cat[:, r, T0:T0 + n],
                                                                                   start=False, stop=(r == 5)),
                             reads=[d_cat[r][ci], d_wo], writes=[pd])
                    S.op("dve", lambda e, ot=ot, pb=pb: e.scalar_tensor_tensor(out=X[:, ot, T0:T0 + n], in0=pb[:, :n], scalar=modT[:, 16 + ot, s:s + 1],
                                                                               in1=X[:, ot, T0:T0 + n], op0=ALU.mult, op1=ALU.add),
                         reads=[pd, d_mod, dX[ot]], writes=[dX[ot]])
                xdst = xnC if isctx else XN2[:, :, T0:T0 + n]
                norm_mod(chunk, 1, xdst, d_xn2[ci], ntmpsC, mods)

            for chunk in chunks:
                do_chunk_C1(chunk)
            tap("x1_" + L, xT, [128, 8, TL], F32, [d_x[k][c] for k in range(8) for c in range(4)])
            tap("h1_" + L, hT, [128, 8, CTX], F32, d_h)
            tap("cat" + L, catT, [128, 6, TL], BF16, [d_cat[r][c] for r in range(6) for c in range(4)])
            phase_end(c1_names)
            if stop_after == "C1" + L:
                return True

            wfi = [A_views["wfi0"], al("wfi1", [128, 8, 1024], BF16)]
            wfo = [al("wfo%d" % i, [128, 4, D], BF16) for i in range(2)]
            d_wfi = [gdep("wfi%d" % i) for i in range(2)]
            d_wfo = [gdep("wfo%d" % i) for i in range(2)]
            hbr = Ring([(al("hb%d" % i, [128, 4, 512], BF16), gdep("hb%d" % i)) for i in range(2)])
            sar = Ring([(al("sa%d" % i, [128, 512], F32), gdep("sa%d" % i)) for i in range(2)])
            c2_names = ["wfi0", "wfi1", "wfo0", "wfo1", "hb0", "hb1", "sa0", "sa1", "XN2", "xnC"]
            if not last:
                A_views["wm0"] = al("wm0", [128, 8, 512], BF16)
                A_views["wm1"] = al("wm1", [128, 8, 512], BF16)
                c2_names = c2_names + ["wm0", "wm1"]
                stage_M(l + 1, "begin")
            fisrc = w_fi_d.ap()[l].rearrange("(k p) o -> p k o", p=128)
            groups = [(0, 4), (4, 4), (8, 4), (12, 4), (16, 4), (20, 2)]
            for gi, (h0, gw) in enumerate(groups):
                sl = gi % 2
                if gi > 0:
                    S.dma("pool", wfi[sl][:, :, 0:gw * 128], fisrc[:, :, h0 * 128:(h0 + gw) * 128], writes=[d_wfi[sl]])
                    S.dma("pool", wfi[sl][:, :, 512:512 + gw * 128], fisrc[:, :, D_FF + h0 * 128:D_FF + (h0 + gw) * 128], writes=[d_wfi[sl]])
                S.dma("pool", wfo[sl][:, 0:gw, :], w_fo_d.ap()[l, h0 * 128:(h0 + gw) * 128, :].rearrange("(c p) o -> p c o", p=128), writes=[d_wfo[sl]])
                def do_chunk_C2(chunk, sl=sl, gw=gw):
                    s, X, dX, T0, n, ci = chunk
                    isctx = (s == 1)
                    xsrc_ = xnC if isctx else XN2[:, :, T0:T0 + n]
                    hb, dhb = hbr.next()
                    for hc in range(gw):
                        pa, pda = ps_main.next()
                        pg, pdg = ps_main.next()
                        for k in range(8):
                            S.op("pe", lambda e, k=k, hc=hc, pa=pa, sl=sl, xsrc_=xsrc_: e.matmul(pa[:, :n], wfi[sl][:, k, hc * 128:(hc + 1) * 128], xsrc_[:, k, :n],
                                                                                                  start=(k == 0), stop=(k == 7)),
                                 reads=[d_wfi[sl], d_xn2[ci][k]], writes=[pda])
                        for k in range(8):
                            S.op("pe", lambda e, k=k, hc=hc, pg=pg, sl=sl, xsrc_=xsrc_: e.matmul(pg[:, :n], wfi[sl][:, k, 512 + hc * 128:512 + (hc + 1) * 128],
                                                                                                  xsrc_[:, k, :n], start=(k == 0), stop=(k == 7)),
                                 reads=[d_wfi[sl], d_xn2[ci][k]], writes=[pdg])
                        sa, dsa = sar.next()
                        S.op("act", lambda e, pa=pa, sa=sa: e.activation(out=sa[:, :n], in_=pa[:, :n], func=AF.Silu), reads=[pda], writes=[dsa])
                        S.op("dve", lambda e, pg=pg, sa=sa, hb=hb, hc=hc: e.tensor_tensor(out=hb[:, hc, :n], in0=pg[:, :n], in1=sa[:, :n], op=ALU.mult),
                             reads=[pdg, dsa], writes=[dhb])
                    for ot in range(8):
                        po, pdo = ps_acc.next()
                        for hc in range(gw):
                            S.op("pe", lambda e, hc=hc, ot=ot, po=po, sl=sl, hb=hb: e.matmul(po[:, :n], wfo[sl][:, hc, ot * 128:(ot + 1) * 128], hb[:, hc, :n],
                                                                                              start=(hc == 0), stop=(hc == gw - 1)),
                                 reads=[d_wfo[sl], dhb], writes=[pdo])
                        S.op("dve", lambda e, ot=ot, po=po, X=X, T0=T0, s=s: e.scalar_tensor_tensor(out=X[:, ot, T0:T0 + n], in0=po[:, :n],
                                                                                                    scalar=modT[:, 40 + ot, s:s + 1], in1=X[:, ot, T0:T0 + n],
                                                                                                    op0=ALU.mult, op1=ALU.add),
                             reads=[pdo, d_mod, dX[ot]], writes=[dX[ot]])

                for chunk in chunks:
                    do_chunk_C2(chunk)
                if not last:
                    stage_M(l + 1, 2 * gi)
                    stage_M(l + 1, 2 * gi + 1)
            if not last:
                stage_M(l + 1, "end")
            tap("x2_" + L, xT, [128, 8, TL], F32, [d_x[k][c] for k in range(8) for c in range(4)])
            if last and stop_after is None:
                osrc = outT_d.ap().rearrange("(k p) t -> p k t", p=128)
                d_osem = Dep("osem")
                for c in range(4):
                    o = S.dma("sp", osrc[:, :, c * 512:(c + 1) * 512], xT[:, :, c * 512:(c + 1) * 512], reads=[d_x[k][c] for k in range(8)],
                              out_side=True, sem_dep=d_osem)
                    final_ops.append(o)
                out_done[0] = True
            phase_end(c2_names)
            if stop_after == "C2" + L:
                return True
            return False

        for l in range(DEPTH):
            if do_layer(l):
                break

        if not out_done[0]:
            osrc = outT_d.ap().rearrange("(k p) t -> p k t", p=128)
            d_osem = Dep("osem")
            for k in range(8):
                o = S.dma("sp", osrc[:, k, :], xT[:, k, :], reads=d_x[k], out_side=True, sem_dep=d_osem)
                final_ops.append(o)
        block = st.enter_context(nc.Block())
        S.emit(block, final_waits=final_ops)
        build_program.peak_words = A.peak
    return nc, tap_out


def _consts():
    f = np.float32
    cm = np.zeros((128, 9, 128), f)
    cm[:, 0, :] = 1.0 / 1024
    for b in range(2):
        cm[b * 64:(b + 1) * 64, 1, b * 64:(b + 1) * 64] = 1.0 / 64
    for k in range(128):
        cm[k, 2, k ^ 1] = 1.0
    cm[:, 3, :] = 1.0 / 256
    cm[:, 7, :] = np.eye(128)
    t1 = np.arange(64)[:, None].astype(np.float64)
    k1 = np.arange(64)[None, :].astype(np.float64)
    C = np.cos(2 * np.pi * t1 * k1 / 64)
    Sn = np.sin(2 * np.pi * t1 * k1 / 64)
    R1m = np.zeros((128, 128))
    R1m[0:64, 0:64] = C
    R1m[64:128, 0:64] = Sn
    R1m[0:64, 64:128] = -Sn
    R1m[64:128, 64:128] = C
    cm[:, 4, :] = R1m
    t2 = np.arange(128)[:, None].astype(np.float64)
    k2 = np.arange(128)[None, :].astype(np.float64)
    nrm = 1.0 / np.sqrt(8192.0 * 64.0)
    cm[:, 5, :] = np.cos(2 * np.pi * t2 * k2 / 128) * nrm
    cm[:, 6, :] = np.sin(2 * np.pi * t2 * k2 / 128) * nrm
    cs = np.zeros((256, 512))
    cc = np.arange(64)[:, None].astype(np.float64)
    m = np.arange(64)[None, :].astype(np.float64)
    for h in range(4):
        cs[h * 64:(h + 1) * 64, h * 64:(h + 1) * 64] = np.cos(2 * np.pi * cc * m / 64)
        cs[h * 64:(h + 1) * 64, 256 + h * 64:256 + (h + 1) * 64] = -np.sin(2 * np.pi * cc * m / 64)
    csblk = np.ascontiguousarray(cs.reshape(2, 128, 512).transpose(1, 0, 2)).astype(f)
    k1r = np.arange(64)[None, :].astype(np.float64)
    twr = np.cos(2 * np.pi * t2 * k1r / 8192)
    twi = np.sin(2 * np.pi * t2 * k1r / 8192)
    tw = np.stack([np.tile(twr, (1, 8)), np.tile(twi, (1, 8))], axis=1).astype(f)
    t = np.arange(256)[:, None].astype(np.float64)
    k = np.arange(256)[None, :].astype(np.float64)
    n2 = 1.0 / np.sqrt(256.0 * 64.0)
    c256 = np.cos(2 * np.pi * t * k / 256) * n2
    s256 = np.sin(2 * np.pi * t * k / 256) * n2
    cs256 = np.stack([c256.reshape(2, 128, 256).transpose(1, 0, 2), s256.reshape(2, 128, 256).transpose(1, 0, 2)], axis=1).astype(f)
    return dict(cmat=cm, csblk=csblk, tw=tw, cs256=np.ascontiguousarray(cs256))


def _rope_tables(t0):
    tpos = np.arange(t0, t0 + TL)
    row = (tpos // 64).astype(np.float32)
    col = (tpos % 64).astype(np.float32)
    inv_freq = (np.float32(10000.0) ** (-np.arange(16, dtype=np.float32) / np.float32(16))).astype(np.float32)
    ang = np.concatenate([row[:, None] * inv_freq, col[:, None] * inv_freq], axis=-1).astype(np.float32)
    cos = np.cos(ang).astype(np.float32)
    sin = np.sin(ang).astype(np.float32)
    p = np.arange(128)
    d = p % 64
    i = d // 2
    sign = np.where(d % 2 == 0, -1.0, 1.0).astype(np.float32)
    rc = np.ascontiguousarray(cos[:, i].T)
    rs = np.ascontiguousarray((sin[:, i] * sign[None, :]).T)
    return rc, rs


def _invcnt(tglob, n, win):
    lo = np.clip(tglob - win // 2, 0, n)
    hi = np.clip(tglob - win // 2 + win, 0, n)
    return (1.0 / (hi - lo)).astype(np.float32)


def _percore(j):
    pc = np.zeros((128, 66), np.float32)
    pc[:, 0] = 0.0 if j == 0 else 1.0
    pc[:, 1] = 0.0 if j == 3 else 1.0
    wins = {(0, 0): 2, (0, 1): 4, (1, 0): 8, (1, 1): 16}
    for tile in range(2):
        for half in range(2):
            win = wins[(tile, half)]
            ps = slice(half * 64, (half + 1) * 64)
            tl = np.concatenate([np.arange(j * TL, j * TL + 8), np.arange((j + 1) * TL - 8, (j + 1) * TL)])
            pc[ps, 2 + tile * 16:2 + (tile + 1) * 16] = _invcnt(tl, SEQ, win)[None, :]
            tc = np.concatenate([np.arange(0, 8), np.arange(CTX - 8, CTX)])
            pc[ps, 34 + tile * 16:34 + (tile + 1) * 16] = _invcnt(tc, CTX, win)[None, :]
    return pc


def _params(inp):
    P = np.zeros((DEPTH, 128, NP_COLS), np.float32)
    for l in range(DEPTH):
        P[l, :, 0:8] = inp["g_norm1"][l].reshape(8, 128).T
        P[l, :, 8:16] = inp["g_norm2"][l].reshape(8, 128).T
        P[l, :, 16:64] = inp["b_mod"][l].reshape(48, 128).T
        P[l, :, 64] = np.tile(inp["q_norm_g"][l], 2)
        P[l, :, 65] = np.tile(inp["k_norm_g"][l], 2)
        cw = inp["conv_dw_w"][l]
        for vt in range(2):
            P[l, :, 66 + vt * 31:66 + (vt + 1) * 31] = cw[:, vt * 128:(vt + 1) * 128].T
        P[l, :, 128:130] = inp["conv_dw_b"][l].reshape(2, 128).T
        P[l, :, 130:132] = inp["conv_ln_g"][l].reshape(2, 128).T
        P[l, :, 132:134] = inp["conv_ln_b"][l].reshape(2, 128).T
        P[l, :, 134:136] = inp["pool_scale"][l].reshape(2, 128).T
        P[l, 0:64, 136] = 1.0 / 2
        P[l, 64:128, 136] = 1.0 / 4
        P[l, 0:64, 137] = 1.0 / 8
        P[l, 64:128, 137] = 1.0 / 16
    return P


_QPERM = np.concatenate([np.arange(0, 64), np.arange(128, 192), np.arange(64, 128), np.arange(192, 256), np.arange(256, D_IN)])


def prep_inputs(inp):
    inp = {k: np.asarray(v) for k, v in inp.items()}
    cst = _consts()
    params = _params(inp)
    w_in_p = np.ascontiguousarray(inp["w_in"][:, :, _QPERM])
    shared = dict(params=params, w_mod=inp["w_mod"], w_in=w_in_p, w_fourier=inp["w_fourier"], w_conv_pw=inp["w_conv_pw"],
                  w_pool=inp["w_pool"], w_out=inp["w_out"], w_ffn_in=inp["w_ffn_in"], w_ffn_out=inp["w_ffn_out"], **cst)
    maps = []
    for i in range(8):
        b, j = i // 4, i % 4
        m = dict(shared)
        m["xT"] = np.ascontiguousarray(inp["x"][b, j * TL:(j + 1) * TL, :].T)
        m["ctxT"] = np.ascontiguousarray(inp["ctx"][b].T)
        ccv = np.zeros((128, 16), np.float32)
        ccv[:, 0::2] = inp["c"][b].reshape(8, 128).T
        ccv[:, 1::2] = inp["c_ctx"].reshape(8, 128).T
        m["cc"] = ccv
        rc, rs = _rope_tables(j * TL)
        m["ropeC"] = rc
        m["ropeS"] = rs
        m["percore"] = _percore(j)
        maps.append(m)
    return maps


_NC_CACHE = {}


def kernel(**inputs):
    maps = prep_inputs(inputs)
    if "nc" not in _NC_CACHE:
        _NC_CACHE["nc"] = build_program()[0]
    nc = _NC_CACHE["nc"]
    res = run_bass_kernel_spmd(nc, maps, core_ids=list(range(8)))
    out = np.zeros((2, SEQ, D), np.float32)
    for i in range(8):
        b, j = i // 4, i % 4
        out[b, j * TL:(j + 1) * TL, :] = res.results[i]["outT"].T
    return out
```

```python
import contextlib
import numpy as np
import concourse.bass as bass
import concourse.mybir as mybir
from concourse.bass_utils import run_bass_kernel_spmd

F32 = mybir.dt.float32
BF16 = mybir.dt.bfloat16
AF = mybir.ActivationFunctionType
ALU = mybir.AluOpType

D = 1024
SEQ = 8192
TL = 2048
CTX = 256
DEPTH = 2
D_IN = 1536
D_FF = 2816
NHC = D_FF // 128
EPS = 1e-6
NKT = (CTX + SEQ) // 128
NP_COLS = 138
R1 = 832


class Dep:
    __slots__ = ("name", "w", "r", "sem_in", "cnt_in", "sem_out", "cnt_out")

    def __init__(self, name=""):
        self.name = name
        self.w = None
        self.r = []
        self.sem_in = None
        self.cnt_in = 0
        self.sem_out = None
        self.cnt_out = 0


class Op:
    __slots__ = ("eng", "fn", "deps", "alldeps", "signaled", "sigidx", "dsem", "dval", "name", "dinc", "seq", "cost", "lat", "seg", "eidx", "pend", "tmin")

    def __init__(self, eng, fn, name=""):
        self.eng = eng
        self.fn = fn
        self.deps = []
        self.signaled = False
        self.sigidx = None
        self.dsem = None
        self.dval = 0
        self.dinc = 16
        self.seq = 0
        self.cost = 0.0
        self.lat = 0.0
        self.seg = 0
        self.eidx = 0
        self.alldeps = []
        self.pend = None
        self.tmin = 0.0
        self.name = name


ENGS = ["pe", "act", "dve", "pool", "sp"]


class Sched:
    def __init__(self, nc, stack):
        self.nc = nc
        self.stack = stack
        self.ops = {e: [] for e in ENGS}
        self.esem = {e: stack.enter_context(nc.semaphore("es_" + e)) for e in ENGS}
        self.nsem = len(ENGS)
        self.pending_dma = []
        self.cc_sems = {}
        self.seg = 0
        self.ecount = 0
        self.reorder = True

    def new_sem(self, name):
        self.nsem += 1
        return self.stack.enter_context(self.nc.semaphore("%s_%d" % (name.replace(".", "_"), self.nsem)))

    def _collect(self, o, reads, writes, extra):
        deps = []
        seen = set()

        def add(d):
            if d is None or d is o or id(d) in seen:
                return
            seen.add(id(d))
            deps.append(d)

        for t in reads:
            add(t.w)
        for t in writes:
            add(t.w)
            for r in t.r:
                add(r)
        for d in extra:
            add(d)
        o.alldeps = deps
        o.deps = deps
        for t in reads:
            t.r.append(o)
        for t in writes:
            t.w = o
            t.r = []

    DEFCOST = {"pe": 0.25, "act": 0.6, "dve": 0.6, "pool": 1.2, "sp": 0.1}

    def _register(self, o):
        o.seg = self.seg
        o.eidx = self.ecount
        self.ecount += 1
        self.ops[o.eng].append(o)

    def op(self, eng, fn, reads=(), writes=(), extra=(), name="", cost=None):
        o = Op(eng, fn, name)
        o.cost = self.DEFCOST[eng] if cost is None else cost
        o.lat = o.cost
        self._collect(o, reads, writes, extra)
        self._register(o)
        return o

    def dma(self, q, out_ap, in_ap, reads=(), writes=(), sem_dep=None, out_side=False, extra=(), name="", tmin=0.0):
        if sem_dep is None:
            sem_dep = (reads[0] if out_side else writes[0])
        if out_side:
            if sem_dep.sem_out is None:
                sem_dep.sem_out = self.new_sem("do_" + sem_dep.name)
            sem_dep.cnt_out += 16
            dsem, dval = sem_dep.sem_out, sem_dep.cnt_out
        else:
            if sem_dep.sem_in is None:
                sem_dep.sem_in = self.new_sem("di_" + sem_dep.name)
            sem_dep.cnt_in += 16
            dsem, dval = sem_dep.sem_in, sem_dep.cnt_in

        def fn(eng, out_ap=out_ap, in_ap=in_ap):
            return eng.dma_start(out=out_ap, in_=in_ap)

        o = Op(q, fn, name)
        o.dsem, o.dval = dsem, dval
        o.cost, o.lat = 0.1, 12.0
        o.tmin = tmin
        self._collect(o, reads, writes, extra)
        o.alldeps = [d for d in o.alldeps if not (d.dsem is not None and d.dsem is dsem)]
        self._register(o)
        self.pending_dma.append(o)
        return o

    def collective(self, fn, reads=(), writes=(), extra=(), name="cc"):
        o = Op("pool", fn, name)
        if name not in self.cc_sems:
            self.cc_sems[name] = [self.new_sem("cc_" + name), 0]
        self.cc_sems[name][1] += 1
        o.dsem = self.cc_sems[name][0]
        o.dval = self.cc_sems[name][1]
        o.dinc = 1
        o.cost, o.lat = 0.5, 40.0
        self._collect(o, reads, writes, extra)
        self._register(o)
        return o

    def barrier(self):
        pend = list(self.pending_dma)
        self.pending_dma = []
        for e in ENGS:
            o = Op(e, None, "barrier")
            o.pend = pend
            o.seg = self.seg
            o.eidx = self.ecount
            self.ops[e].append(o)
        self.ecount += 1
        self.seg += 1

    def schedule(self):
        W = 64
        nseg = self.seg + 1
        segs = [{e: [] for e in ENGS} for _ in range(nseg)]
        bars = [{e: None for e in ENGS} for _ in range(nseg)]
        for e in ENGS:
            for o in self.ops[e]:
                if o.fn is None:
                    bars[o.seg][e] = o
                else:
                    segs[o.seg][e].append(o)
        new_ops = {e: [] for e in ENGS}
        for si in range(nseg):
            lists = segs[si]
            if self.reorder:
                finish = {}
                done = set()
                etime = {e: 0.0 for e in ENGS}
                remaining = {e: list(lists[e]) for e in ENGS}
                out = {e: [] for e in ENGS}
                total = sum(len(v) for v in remaining.values())
                while total:
                    best = None
                    for e in ENGS:
                        rem = remaining[e]
                        if not rem:
                            continue
                        seen_dma = False
                        for idx in range(min(W, len(rem))):
                            o = rem[idx]
                            if o.dsem is not None:
                                if seen_dma:
                                    continue
                                seen_dma = True
                            ready = o.tmin
                            ok = True
                            for d in o.alldeps:
                                if d.seg != si or d.fn is None:
                                    continue
                                if id(d) not in done:
                                    ok = False
                                    break
                                f = finish[id(d)] + (0.0 if d.eng == e else 0.3)
                                if f > ready:
                                    ready = f
                            if not ok:
                                continue
                            start = max(etime[e], ready)
                            key = (start, o.eidx)
                            if best is None or key < best[0]:
                                best = (key, e, idx, o)
                    if best is None:
                        raise RuntimeError("scheduler stuck")
                    (start, _), e, idx, o = best
                    remaining[e].pop(idx)
                    out[e].append(o)
                    done.add(id(o))
                    finish[id(o)] = start + o.lat
                    etime[e] = start + o.cost
                    total -= 1
                lists = out
            for e in ENGS:
                new_ops[e].extend(lists[e])
            if bars[si][ENGS[0]] is not None:
                lasts = [lists[e][-1] for e in ENGS if lists[e] and lists[e][-1].dsem is None]
                for e in ENGS:
                    b = bars[si][e]
                    b.alldeps = [d for d in lasts + b.pend if not (d.eng == e == "pe") or d.dsem is not None]
                    new_ops[e].append(b)
        self.ops = new_ops
        for e in ENGS:
            for i, o in enumerate(self.ops[e]):
                o.seq = i
        for e in ENGS:
            for o in self.ops[e]:
                bestd = {}
                for d in o.alldeps:
                    if d.dsem is not None:
                        key = ("s", id(d.dsem))
                        if key not in bestd or bestd[key].dval < d.dval:
                            bestd[key] = d
                    else:
                        key = ("e", d.eng)
                        if key not in bestd or bestd[key].seq < d.seq:
                            bestd[key] = d
                o.deps = list(bestd.values())

    def finalize(self, final_waits):
        for o in final_waits:
            if o.dsem is None:
                o.signaled = True
        for e in ENGS:
            for o in self.ops[e]:
                for d in o.deps:
                    if d.dsem is None:
                        if d.eng == "pe" and o.eng == "pe":
                            continue
                        d.signaled = True
        for e in ENGS:
            c = 0
            for o in self.ops[e]:
                if o.dsem is None and o.signaled:
                    c += 1
                    o.sigidx = c

    def emit(self, block, final_waits=()):
        self.schedule()
        self.finalize(final_waits)
        esem = self.esem

        def run(ename, eng):
            waited = {}
            for o in self.ops[ename]:
                for d in o.deps:
                    if d.dsem is not None:
                        key, val, sem = id(d.dsem), d.dval, d.dsem
                    else:
                        if d.eng == "pe" and ename == "pe":
                            continue
                        key, val, sem = d.eng, d.sigidx, esem[d.eng]
                    if waited.get(key, 0) >= val:
                        continue
                    waited[key] = val
                    eng.wait_ge(sem, val)
                if o.fn is None:
                    continue
                ins = o.fn(eng)
                if o.dsem is not None:
                    ins.then_inc(o.dsem, o.dinc)
                elif o.sigidx:
                    ins.then_inc(esem[ename], 1)
            if ename == "sp":
                fin = {}
                for o in final_waits:
                    if o.dsem is not None:
                        key, sem, val = id(o.dsem), o.dsem, o.dval
                    else:
                        key, sem, val = o.eng, esem[o.eng], o.sigidx
                    if key not in fin or fin[key][1] < val:
                        fin[key] = (sem, val)
                for sem, val in fin.values():
                    eng.wait_ge(sem, val)

        block.tensor(lambda eng: run("pe", eng))
        block.scalar(lambda eng: run("act", eng))
        block.vector(lambda eng: run("dve", eng))
        block.gpsimd(lambda eng: run("pool", eng))
        block.sync(lambda eng: run("sp", eng))


class Ring:
    def __init__(self, items):
        self.items = items
        self.i = 0

    def next(self):
        it = self.items[self.i % len(self.items)]
        self.i += 1
        return it


ARENA_WORDS = 52736


class Arena:
    def __init__(self, base_ap):
        self.base = base_ap
        self.free_list = [(0, ARENA_WORDS)]
        self.live = {}
        self.peak = 0

    def alloc(self, name, shape, dtype, top=False):
        elems = 1
        for d in shape[1:]:
            elems *= d
        esz = 4 if dtype == F32 else 2
        words = (elems * esz + 3) // 4
        words = (words + 15) // 16 * 16
        order = range(len(self.free_list) - 1, -1, -1) if top else range(len(self.free_list))
        for i in order:
            o, n = self.free_list[i]
            if n >= words:
                if top:
                    off = o + n - words
                    if n == words:
                        self.free_list.pop(i)
                    else:
                        self.free_list[i] = (o, n - words)
                else:
                    off = o
                    if n == words:
                        self.free_list.pop(i)
                    else:
                        self.free_list[i] = (o + words, n - words)
                break
        else:
            raise RuntimeError("arena full allocating %s (%d words); free=%s" % (name, words, self.free_list))
        v = self.base[0:shape[0], off:off + words]
        if dtype != F32:
            v = v.bitcast(dtype)
        v = v[:, 0:elems]
        if len(shape) == 3:
            v = v.rearrange("p (a b) -> p a b", b=shape[2])
        elif len(shape) == 4:
            v = v.rearrange("p (a b c) -> p a b c", b=shape[2], c=shape[3])
        elif len(shape) == 5:
            v = v.rearrange("p (a b c d) -> p a b c d", b=shape[2], c=shape[3], d=shape[4])
        self.live[name] = (off, words)
        used = ARENA_WORDS - sum(n for _, n in self.free_list)
        self.peak = max(self.peak, used)
        return v

    def free(self, *names):
        for name in names:
            off, words = self.live.pop(name)
            self.free_list.append((off, words))
        self.free_list.sort()
        merged = []
        for o, n in self.free_list:
            if merged and merged[-1][0] + merged[-1][1] == o:
                merged[-1] = (merged[-1][0], merged[-1][1] + n)
            else:
                merged.append((o, n))
        self.free_list = merged


def build_program(taps=(), stop_after=None):
    nc = bass.Bass("TRN2", target_bir_lowering=False)
    dt = nc.dram_tensor

    def ein(name, shape, dtype=F32):
        return dt(name, list(shape), dtype, kind="ExternalInput")

    xT_d = ein("xT", [D, TL])
    ctxT_d = ein("ctxT", [D, CTX])
    cc_d = ein("cc", [128, 16])
    ropeC_d = ein("ropeC", [128, TL])
    ropeS_d = ein("ropeS", [128, TL])
    percore_d = ein("percore", [128, 66])
    params_d = ein("params", [DEPTH, 128, NP_COLS])
    w_mod_d = ein("w_mod", [DEPTH, D, 6 * D])
    w_in_d = ein("w_in", [DEPTH, D, D_IN])
    w_f_d = ein("w_fourier", [DEPTH, 256, 256])
    w_pw_d = ein("w_conv_pw", [DEPTH, 256, 256])
    w_pool_d = ein("w_pool", [DEPTH, 4, 64, 64])
    w_out_d = ein("w_out", [DEPTH, D, D])
    w_fi_d = ein("w_ffn_in", [DEPTH, D, 2 * D_FF])
    w_fo_d = ein("w_ffn_out", [DEPTH, D_FF, D])
    cmat_d = ein("cmat", [128, 9, 128])
    csblk_d = ein("csblk", [128, 2, 512])
    tw_d = ein("tw", [128, 2, 512])
    c256_d = ein("cs256", [128, 2, 2, 256])
    outT_d = dt("outT", [D, TL], F32, kind="ExternalOutput")

    RC = [R1, R1, R1, R1]
    snde = dt("snde", [512, 32], BF16)
    rcve = dt("rcve", [4 * 512, 32], BF16)
    sndc = [dt("sndc%d" % c, [RC[c], 512], BF16) for c in range(4)]
    rcvc = [dt("rcvc%d" % c, [4 * RC[c], 512], BF16) for c in range(4)]
    snd2 = dt("snd2", [64, SEQ], BF16)
    rcv2 = dt("rcv2", [4 * 64, SEQ], BF16)
    RG = [[0, 1, 2, 3], [4, 5, 6, 7]]

    tap_out = {}
    final_ops = []
    stopped = [False]
    out_done = [False]

    with contextlib.ExitStack() as st:
        S = Sched(nc, st)
        pid = nc.partition_id()
        jr = pid % 4
        arena_t = st.enter_context(nc.sbuf_tensor("arena", [128, ARENA_WORDS], F32))
        A = Arena(arena_t[:, :])
        al = A.alloc

        psb = [st.enter_context(nc.psum_tensor("ps%d" % i, [128, 512], F32)) for i in range(8)]
        psd = [Dep("ps%d" % i) for i in range(8)]
        ps_main = Ring([(psb[i], psd[i]) for i in range(0, 4)])
        ps_aux = Ring([(psb[i], psd[i]) for i in range(4, 6)])
        ps_acc = Ring([(psb[i], psd[i]) for i in range(6, 8)])

        xT = al("xT", [128, 8, TL], F32)
        hT = al("hT", [128, 8, CTX], F32)
        d_x = [[Dep("x%d_%d" % (k, c)) for c in range(4)] for k in range(8)]
        d_h = [Dep("h%d" % k) for k in range(8)]
        cmat = al("cmat", [128, 9, 128], BF16)
        onesrow = al("onesrow", [128, 128], F32)
        csblk = al("csblk", [128, 2, 512], BF16)
        cs256 = al("cs256", [128, 2, 2, 256], BF16)
        percore = al("percore", [128, 66], F32)
        ccs = al("ccs", [128, 16], F32)
        scb = al("scb", [128, 8, 2], BF16)
        d_const = Dep("const")
        d_pc = Dep("percore")
        d_sc = Dep("sc")
        params = al("params", [128, NP_COLS], F32)
        d_params = Dep("params")
        modT2 = al("modT", [128, 2, 48, 2], F32)
        d_mod2 = [Dep("modT0"), Dep("modT1")]
        gm2 = al("gm", [128, 2, 2, 8, 2], F32)
        d_gm2 = [Dep("gm0"), Dep("gm1")]
        d_snde, d_rcve = Dep("snde"), Dep("rcve")
        d_sndc = [Dep("sndc%d" % i) for i in range(4)]
        d_rcvc = [Dep("rcvc%d" % i) for i in range(4)]
        d_snd2, d_rcv2 = Dep("snd2"), Dep("rcv2")

        ONES_MEAN, BLK64, SWAPP, ONESLN, R1M, C128S, S128S = range(7)

        xsrc = xT_d.ap().rearrange("(k p) t -> p k t", p=128)
        d_xload = [Dep("xload%d" % c) for c in range(4)]
        for c in range(4):
            S.dma("sp", xT[:, :, c * 512:(c + 1) * 512], xsrc[:, :, c * 512:(c + 1) * 512], writes=[d_x[k][c] for k in range(8)],
                  sem_dep=d_xload[c])
        hsrc = ctxT_d.ap().rearrange("(k p) t -> p k t", p=128)
        S.dma("sp", hT, hsrc, writes=d_h, sem_dep=Dep("hload"))
        S.dma("pool", cmat, cmat_d.ap(), writes=[d_const])
        S.dma("pool", csblk, csblk_d.ap(), writes=[d_const])
        S.dma("pool", cs256, c256_d.ap(), writes=[d_const])
        S.dma("sp", percore, percore_d.ap(), writes=[d_pc])
        S.dma("sp", ccs, cc_d.ap(), writes=[d_sc])
        cvec = al("cvec", [128, 4], F32)
        S.op("pool", lambda e: e.memset(cvec, EPS), writes=[d_const])
        S.op("pool", lambda e: e.memset(onesrow, 1.0), writes=[d_const])
        S.op("act", lambda e: e.activation(out=scb.rearrange("p k s -> p (k s)"), in_=ccs, func=AF.Silu),
             reads=[d_sc], writes=[d_sc])

        DEPS = {}

        def gdep(name):
            if name not in DEPS:
                DEPS[name] = Dep(name)
            return DEPS[name]

        def tap(name, ap_sb, shape, dtype, reads):
            if name not in taps:
                return
            t = dt("tap_" + name, list(shape), dtype, kind="ExternalOutput")
            o = S.dma("sp", t.ap(), ap_sb, reads=reads, out_side=True, sem_dep=Dep("tap" + name))
            final_ops.append(o)
            tap_out[name] = t

        def phase_end(names):
            S.barrier()
            A.free(*names)

        lat_chunks = [(0, xT, [d_x[k][c] for k in range(8)], c * 512, 512, c) for c in range(4)]
        ctx_chunk = (1, hT, d_h, 0, CTX, 4)

        def norm_mod(chunk, which, xn_ap, d_xn, tmps, mods):
            s, X, dX, T0, n, ci = chunk
            sqb, d_sq, rstd, d_rstd, tmpr = tmps
            modT, gm, d_mod, d_gm = mods
            pb, pd = ps_aux.next()
            for k in range(8):
                S.op("act", lambda e, k=k: e.activation(out=sqb[:, k % 4, :n], in_=X[:, k, T0:T0 + n], func=AF.Square),
                     reads=[dX[k]], writes=[d_sq[k % 4]])
                S.op("pe", lambda e, k=k: e.matmul(pb[:, :n], cmat[:, ONES_MEAN, :], sqb[:, k % 4, :n], start=(k == 0), stop=(k == 7)),
                     reads=[d_sq[k % 4], d_const], writes=[pd])
            S.op("act", lambda e: e.activation(out=rstd[:, :n], in_=pb[:, :n], func=AF.Ln, bias=cvec[:, 0:1]), reads=[pd, d_const], writes=[d_rstd])
            S.op("act", lambda e: e.activation(out=rstd[:, :n], in_=rstd[:, :n], func=AF.Exp, scale=-0.5), reads=[d_rstd], writes=[d_rstd])
            shift_chunk = 0 if which == 0 else 3
            for k in range(8):
                tb, td = tmpr.next()
                S.op("dve", lambda e, k=k, tb=tb: e.tensor_tensor(out=tb[:, :n], in0=X[:, k, T0:T0 + n], in1=rstd[:, :n], op=ALU.mult),
                     reads=[dX[k], d_rstd], writes=[td])
                S.op("act", lambda e, k=k, tb=tb: e.activation(out=xn_ap[:, k, :n], in_=tb[:, :n], func=AF.Identity,
                                                             bias=modT[:, shift_chunk * 8 + k, s:s + 1], scale=gm[:, which, k, s:s + 1]),
                     reads=[td, d_gm, d_mod], writes=[d_xn[k]])

        def stage_M(l, part):
            modT, gm, d_mod, d_gm = modT2[:, l % 2], gm2[:, l % 2], d_mod2[l % 2], d_gm2[l % 2]
            if part == "begin":
                S.dma("sp", params, params_d.ap()[l], writes=[d_params])
                return
            if part == "end":
                for which, sc_chunk, gcol in ((0, 1, 0), (1, 4, 8)):
                    for s_ in range(2):
                        S.op("dve", lambda e, which=which, sc_chunk=sc_chunk, gcol=gcol, s_=s_: e.scalar_tensor_tensor(
                            out=gm[:, which, :, s_], in0=modT[:, sc_chunk * 8:(sc_chunk + 1) * 8, s_], scalar=1.0,
                            in1=params[:, gcol:gcol + 8], op0=ALU.add, op1=ALU.mult),
                            reads=[d_mod, d_params], writes=[d_gm], cost=0.2)
                tap("modT%d" % l, modT, [128, 48, 2], F32, [d_mod])
                tap("gm%d" % l, gm, [128, 2, 8, 2], F32, [d_gm])
                return
            oc = part
            wm = [A_views["wm0"], A_views["wm1"]]
            d_wm = [gdep("wm%d" % i) for i in range(2)]
            wsrc = w_mod_d.ap()[l].rearrange("(k p) o -> p k o", p=128)
            sl = oc % 2
            S.dma("pool", wm[sl], wsrc[:, :, oc * 512:(oc + 1) * 512], writes=[d_wm[sl]])
            for o4 in range(4):
                o = oc * 4 + o4
                pb, pd = ps_aux.next()
                for k in range(8):
                    S.op("pe", lambda e, k=k, sl=sl, o4=o4, pb=pb: e.matmul(pb[:, 0:2], wm[sl][:, k, o4 * 128:(o4 + 1) * 128], scb[:, k, :],
                                                                          start=(k == 0), stop=(k == 7)),
                         reads=[d_wm[sl], d_sc], writes=[pd], cost=0.06)
                S.op("dve", lambda e, o=o, pb=pb: e.tensor_scalar(out=modT[:, o, :], in0=pb[:, 0:2], scalar1=params[:, 16 + o:17 + o], scalar2=None,
                                                                 op0=ALU.add),
                     reads=[pd, d_params], writes=[d_mod], cost=0.2)

        A_views = {}
        A_views["wm0"] = al("wm0", [128, 8, 512], BF16)
        A_views["wm1"] = al("wm1", [128, 8, 512], BF16)
        for part in ["begin"] + list(range(12)) + ["end"]:
            stage_M(0, part)
        phase_end(["wm0", "wm1"])

        def do_layer(l):
            last = (l == DEPTH - 1)
            L = "%d" % l
            modT, gm, d_mod, d_gm = modT2[:, l % 2], gm2[:, l % 2], d_mod2[l % 2], d_gm2[l % 2]
            mods = (modT, gm, d_mod, d_gm)
            ycT = al("ycT", [128, 2, CTX], BF16, top=True)
            d_ycT = gdep("ycT")
            qT = al("qT", [128, 4, TL], BF16, top=True)
            qC = al("qC", [128, 4, CTX], BF16, top=True)
            d_q = [[gdep("q%d_%d" % (t, c)) for c in range(5)] for t in range(2)]
            KTc = al("KTc", [128, CTX], BF16, top=True)
            d_KTc = gdep("KTc")
            Vxc = al("Vxc", [128, 2, 192], BF16, top=True)
            d_Vxc = gdep("Vxc")
            zc_tm = al("zctm", [128, 2, 512], BF16, top=True)
            d_zc = gdep("zc")
            glu = al("glu", [128, 2, TL + 30], BF16, top=True)
            gluC = al("gluC", [128, 2, CTX + 30], BF16, top=True)
            pu = al("pu", [128, 2, TL + 16], BF16, top=True)
            puC = al("puC", [128, 2, CTX + 16], BF16, top=True)
            d_glu = [gdep("glu%d" % i) for i in range(2)]
            d_gluC = [gdep("gluC%d" % i) for i in range(2)]
            d_pu = [gdep("pu%d" % i) for i in range(2)]
            d_puC = [gdep("puC%d" % i) for i in range(2)]
            S.op("pool", lambda e: e.memset(qT, 0.0), writes=[gdep("q%d_%d" % (t, c)) for t in range(2) for c in range(4)])
            if not last:
                S.op("pool", lambda e: e.memset(qC, 0.0), writes=[gdep("q%d_4" % t) for t in range(2)])
            S.op("pool", lambda e: e.memset(Vxc, 1.0), writes=[d_Vxc])
            if not last:
                S.op("pool", lambda e: e.memset(gluC, 0.0), writes=d_gluC)
                S.op("pool", lambda e: e.memset(puC, 0.0), writes=d_puC)

            diag = al("diag", [128, 62, 128], BF16, top=True)
            d_diag = gdep("diag")
            for idx in range(62):
                S.op("dve", lambda e, idx=idx: e.tensor_scalar(out=diag[:, idx, :], in0=cmat[:, 7, :], scalar1=params[:, 66 + idx:67 + idx], scalar2=None,
                                                               op0=ALU.mult),
                     reads=[d_const, d_params], writes=[d_diag])
            wpw = al("wpw", [128, 2, 256], BF16, top=True)
            wpool = al("wpool", [128, 2, 128], BF16, top=True)
            d_wcp = gdep("wcp")
            S.dma("pool", wpw, w_pw_d.ap()[l].rearrange("(k p) o -> p k o", p=128), writes=[d_wcp])
            S.op("pool", lambda e: e.memset(wpool, 0.0), writes=[d_wcp])
            for g in range(4):
                tile_, half = g // 2, g % 2
                S.dma("pool", wpool[half * 64:(half + 1) * 64, tile_, half * 64:(half + 1) * 64], w_pool_d.ap()[l, g], writes=[d_wcp],
                      reads=[])
            w_in = al("w_in", [128, 8, D_IN], BF16)
            d_win = gdep("w_in")
            wisrc = w_in_d.ap()[l].rearrange("(k p) o -> p k o", p=128)
            for c3 in range(3):
                S.dma("pool", w_in[:, :, c3 * 512:(c3 + 1) * 512], wisrc[:, :, c3 * 512:(c3 + 1) * 512], writes=[d_win])
            xn = al("xnA", [128, 8, 512], BF16)
            d_xn = [gdep("xnA%d" % k) for k in range(8)]
            sqb = al("sqbA", [128, 4, 512], BF16)
            d_sq = [gdep("sqA%d" % k) for k in range(8)]
            rstd = al("rstdA", [128, 512], F32)
            d_rstd = gdep("rstdA")
            tmpr = Ring([(al("tmpA%d" % i, [128, 512], F32), gdep("tmpA%d" % i)) for i in range(3)])
            ntmps = (sqb, d_sq, rstd, d_rstd, tmpr)
            ropeC = al("ropeC", [128, 512], F32)
            ropeS = al("ropeS", [128, 512], F32)
            d_rope = gdep("rope")
            sqq = al("sqq", [128, 512], BF16)
            d_sqq = gdep("sqq")
            rs2 = al("rs2", [128, 512], F32)
            d_rs2 = gdep("rs2")
            qg = al("qg", [128, 512], BF16)
            d_qg = gdep("qg")
            kloc = al("kloc", [128, 512], BF16)
            d_kloc = gdep("kloc")
            vsb = al("vsb", [128, 4, 192], BF16)
            d_vsb = gdep("vsb")
            S.op("pool", lambda e: e.memset(vsb, 1.0), writes=[d_vsb])
            ub = al("ub", [128, 2, 512], BF16)
            d_ub = [gdep("ub0"), gdep("ub1")]
            zsb = al("zsb", [128, 4, 512], BF16)
            d_zsb = gdep("zsb")
            a_names = ["w_in", "xnA", "sqbA", "rstdA", "tmpA0", "tmpA1", "tmpA2", "ropeC", "ropeS", "sqq", "rs2", "qg", "kloc", "vsb",
                       "ub", "zsb"]

            def qk_tile(chunk, ctile, dest_ap, d_dest, gcol, rope):
                s, X, dX, T0, n, ci = chunk
                pb, pd = ps_main.next()
                for k in range(8):
                    S.op("pe", lambda e, k=k: e.matmul(pb[:, :n], w_in[:, k, ctile * 128:(ctile + 1) * 128], xn[:, k, :n],
                                                       start=(k == 0), stop=(k == 7)),
                         reads=[d_win, d_xn[k]], writes=[pd])
                S.op("act", lambda e: e.activation(out=sqq[:, :n], in_=pb[:, :n], func=AF.Square), reads=[pd], writes=[d_sqq])
                p2, pd2 = ps_aux.next()
                S.op("pe", lambda e: e.matmul(p2[:, :n], cmat[:, BLK64, :], sqq[:, :n], start=True, stop=True),
                     reads=[d_sqq, d_const], writes=[pd2])
                S.op("act", lambda e: e.activation(out=rs2[:, :n], in_=p2[:, :n], func=AF.Ln, bias=cvec[:, 0:1]), reads=[pd2, d_const], writes=[d_rs2])
                S.op("act", lambda e: e.activation(out=rs2[:, :n], in_=rs2[:, :n], func=AF.Exp, scale=-0.5), reads=[d_rs2], writes=[d_rs2])
                if not rope:
                    for (psl, dap) in dest_ap:
                        S.op("dve", lambda e, psl=psl, dap=dap: e.scalar_tensor_tensor(out=dap, in0=pb[psl, :n], scalar=params[psl, gcol:gcol + 1],
                                                                                       in1=rs2[psl, :n], op0=ALU.mult, op1=ALU.mult),
                             reads=[pd, d_rs2, d_params], writes=[d_dest])
                    return
                S.op("dve", lambda e: e.scalar_tensor_tensor(out=qg[:, :n], in0=pb[:, :n], scalar=params[:, gcol:gcol + 1], in1=rs2[:, :n],
                                                             op0=ALU.mult, op1=ALU.mult),
                     reads=[pd, d_rs2, d_params], writes=[d_qg])
                p3, pd3 = ps_aux.next()
                S.op("pe", lambda e: e.matmul(p3[:, :n], cmat[:, SWAPP, :], qg[:, :n], start=True, stop=True),
                     reads=[d_qg, d_const], writes=[pd3])
                t1, td1 = tmpr.next()
                t2, td2 = tmpr.next()
                S.op("dve", lambda e: e.tensor_tensor(out=t1[:, :n], in0=qg[:, :n], in1=ropeC[:, :n], op=ALU.mult),
                     reads=[d_qg, d_rope], writes=[td1])
                S.op("dve", lambda e: e.tensor_tensor(out=t2[:, :n], in0=p3[:, :n], in1=ropeS[:, :n], op=ALU.mult),
                     reads=[pd3, d_rope], writes=[td2])
                for (psl, dap) in dest_ap:
                    S.op("dve", lambda e, psl=psl, dap=dap: e.tensor_tensor(out=dap, in0=t1[psl, :n], in1=t2[psl, :n], op=ALU.add),
                         reads=[td1, td2], writes=[d_dest])

            def v_tiles(chunk, dest_fn, d_dest):
                s, X, dX, T0, n, ci = chunk
                for tt in range(n // 128):
                    pb, pd = ps_main.next()
                    for k in range(8):
                        S.op("pe", lambda e, k=k, tt=tt, pb=pb: e.matmul(pb[:, 0:128], xn[:, k, tt * 128:(tt + 1) * 128], w_in[:, k, 384:512],
                                                                          start=(k == 0), stop=(k == 7)),
                             reads=[d_win, d_xn[k]], writes=[pd])
                    S.op("act", lambda e, tt=tt, pb=pb: e.copy(out=dest_fn(tt)[:, 0:64], in_=pb[:, 0:64]), reads=[pd], writes=[d_dest])
                    S.op("act", lambda e, tt=tt, pb=pb: e.copy(out=dest_fn(tt)[:, 128:192], in_=pb[:, 64:128]), reads=[pd], writes=[d_dest])

            def dest_in(pb):
                return pb[:, 0:128]

            def proj_tile(chunk, ctile):
                s, X, dX, T0, n, ci = chunk
                pb, pd = ps_main.next()
                for k in range(8):
                    S.op("pe", lambda e, k=k: e.matmul(pb[:, :n], w_in[:, k, ctile * 128:(ctile + 1) * 128], xn[:, k, :n],
                                                       start=(k == 0), stop=(k == 7)),
                         reads=[d_win, d_xn[k]], writes=[pd])
                return pb, pd

            chunks = [ctx_chunk] + lat_chunks
            def do_chunk_A(chunk):
                s, X, dX, T0, n, ci = chunk
                isctx = (s == 1)
                so_box = []
                norm_mod(chunk, 0, xn, d_xn, ntmps, mods)
                if not isctx:
                    S.dma("sp", ropeC[:, :n], ropeC_d.ap()[:, T0:T0 + n], writes=[d_rope])
                    S.dma("sp", ropeS[:, :n], ropeS_d.ap()[:, T0:T0 + n], writes=[d_rope])
                full = not (isctx and last)
                if full:
                    H0, H1, ALLP = slice(0, 64), slice(64, 128), slice(0, 128)
                    for tq in range(2):
                        if isctx:
                            dest = [(H0, qC[0:64, tq, :]), (H1, qC[64:128, 2 + tq, :])]
                        else:
                            dest = [(H0, qT[0:64, tq, T0:T0 + n]), (H1, qT[64:128, 2 + tq, T0:T0 + n])]
                        qk_tile(chunk, tq, dest, d_q[tq][ci], 64, rope=not isctx)
                ALLP = slice(0, 128)
                if isctx:
                    qk_tile(chunk, 2, [(ALLP, KTc[:, :])], d_KTc, 65, rope=False)
                else:
                    qk_tile(chunk, 2, [(ALLP, kloc[:, :n])], d_kloc, 65, rope=True)
                if isctx:
                    for tt in range(2):
                        pb, pd = ps_main.next()
                        for k in range(8):
                            S.op("pe", lambda e, k=k, tt=tt, pb=pb: e.matmul(pb[:, 0:128], xn[:, k, tt * 128:(tt + 1) * 128], w_in[:, k, 384:512],
                                                                              start=(k == 0), stop=(k == 7)),
                                 reads=[d_win, d_xn[k]], writes=[pd])
                        S.op("act", lambda e, tt=tt, pb=pb: e.copy(out=Vxc[:, tt, 0:64], in_=pb[:, 0:64]), reads=[pd], writes=[d_Vxc])
                        S.op("act", lambda e, tt=tt, pb=pb: e.copy(out=Vxc[:, tt, 128:192], in_=pb[:, 64:128]), reads=[pd], writes=[d_Vxc])
                else:
                    v_tiles(chunk, lambda tt: vsb[:, tt, :], d_vsb)
                if not full:
                    return
                for ut in range(2):
                    pb, pd = proj_tile(chunk, 4 + ut)
                    S.op("act", lambda e, ut=ut, pb=pb: e.copy(out=ub[:, ut, :n], in_=pb[:, :n]), reads=[pd], writes=[d_ub[ut]])
                if isctx:
                    for tt in range(2):
                        pb, pd = ps_main.next()
                        for k2 in range(2):
                            S.op("pe", lambda e, k2=k2, tt=tt, pb=pb: e.matmul(pb[:, :], ub[:, k2, tt * 128:(tt + 1) * 128], csblk[:, k2, :],
                                                                                start=(k2 == 0), stop=(k2 == 1)),
                                 reads=[d_ub[k2], d_const], writes=[pd])
                        S.op("act", lambda e, tt=tt, pb=pb: e.copy(out=zc_tm[:, tt, :], in_=pb[:, :]), reads=[pd], writes=[d_zc])
                else:
                    for zt in range(4):
                        pb, pd = ps_main.next()
                        for k2 in range(2):
                            S.op("pe", lambda e, k2=k2, zt=zt, pb=pb: e.matmul(pb[:, :n], csblk[:, k2, zt * 128:(zt + 1) * 128], ub[:, k2, :n],
                                                                                start=(k2 == 0), stop=(k2 == 1)),
                                 reads=[d_ub[k2], d_const], writes=[pd])
                        S.op("dve", lambda e, zt=zt, pb=pb: e.tensor_copy(out=zsb[:, zt, :n], in_=pb[:, :n]), reads=[pd], writes=[d_zsb])
                    sc_ = sndc[ci].ap()
                    so = [S.dma("sp", sc_[320:832, :].rearrange("(z p) t -> p z t", p=128), zsb[:, :, :n], reads=[d_zsb], writes=[d_sndc[ci]],
                                out_side=True),
                          S.dma("sp", sc_[0:128, :], kloc[:, :n], reads=[d_kloc], writes=[d_sndc[ci]], out_side=True),
                          S.dma("sp", sc_[128:320, :].rearrange("r c -> (r c)").rearrange("(tt p c) -> p tt c", p=128, c=192), vsb,
                                reads=[d_vsb], writes=[d_sndc[ci]], out_side=True)]
                    so_box.append(so)
                    if ci == 0:
                        tap("zsb" + L, zsb, [128, 4, 512], BF16, [d_zsb])
                for vt in range(2):
                    pv, pdv = proj_tile(chunk, 6 + vt)
                    pg, pdg = proj_tile(chunk, 8 + vt)
                    sig, d_sig = tmpr.next()
                    S.op("act", lambda e, pg=pg, sig=sig: e.activation(out=sig[:, :n], in_=pg[:, :n], func=AF.Sigmoid), reads=[pdg], writes=[d_sig])
                    gdst = (gluC[:, vt, 15:15 + n] if isctx else glu[:, vt, 15 + T0:15 + T0 + n])
                    S.op("dve", lambda e, pv=pv, gdst=gdst, sig=sig: e.tensor_tensor(out=gdst, in0=pv[:, :n], in1=sig[:, :n], op=ALU.mult),
                         reads=[pdv, d_sig], writes=[(d_gluC if isctx else d_glu)[vt]])
                for pt in range(2):
                    pb, pd = proj_tile(chunk, 10 + pt)
                    pdst = (puC[:, pt, 8:8 + n] if isctx else pu[:, pt, 8 + T0:8 + T0 + n])
                    S.op("act", lambda e, pb=pb, pdst=pdst: e.copy(out=pdst, in_=pb[:, :n]), reads=[pd],
                         writes=[(d_puC if isctx else d_pu)[pt]])

                if so_box:
                    so = so_box[0]
                    S.collective(lambda e: e.collective_compute("AllGather", ALU.bypass, replica_groups=RG, ins=[sndc[ci].ap()],
                                                                outs=[rcvc[ci].ap()]),
                                 reads=[d_sndc[ci]], writes=[d_rcvc[ci]], extra=so, name="gc%d" % ci)

            for chunk in chunks:
                do_chunk_A(chunk)
            sev = snde.ap().rearrange("(a p) c -> p a c", p=128)
            e_ops = []
            e_ops.append(S.dma("sp", sev[:, 0:2, 0:16], glu[:, :, 15:31], reads=d_glu, writes=[d_snde], out_side=True, sem_dep=d_glu[0]))
            e_ops.append(S.dma("sp", sev[:, 0:2, 16:32], glu[:, :, TL - 1:TL + 15], reads=d_glu, writes=[d_snde], out_side=True, sem_dep=d_glu[0]))
            e_ops.append(S.dma("sp", sev[:, 2:4, 0:16], pu[:, :, 8:24], reads=d_pu, writes=[d_snde], out_side=True, sem_dep=d_pu[0]))
            e_ops.append(S.dma("sp", sev[:, 2:4, 16:32], pu[:, :, TL - 8:TL + 8], reads=d_pu, writes=[d_snde], out_side=True, sem_dep=d_pu[0]))
            S.collective(lambda e: e.collective_compute("AllGather", ALU.bypass, replica_groups=RG, ins=[snde.ap()], outs=[rcve.ap()]),
                         reads=[d_snde], writes=[d_rcve], extra=e_ops, name="ge")

            tap("q" + L, qT, [128, 4, TL], BF16, [d_q[0][3], d_q[1][3], d_q[0][0], d_q[1][0]])
            tap("glu" + L, glu, [128, 2, TL + 30], BF16, d_glu)
            tap("pu" + L, pu, [128, 2, TL + 16], BF16, d_pu)
            tap("KTc" + L, KTc, [128, CTX], BF16, [d_KTc])
            tap("Vxc" + L, Vxc, [128, 2, 192], BF16, [d_Vxc])
            phase_end(a_names)
            if stop_after == "A" + L:
                return True

            catT = al("catT", [128, 6, TL], BF16)
            catC = al("catC", [128, 6, CTX], BF16)
            d_cat = [[gdep("cat%d_%d" % (r, c)) for c in range(5)] for r in range(6)]
            halL = al("halL", [128, 4, 32], BF16)
            halR = al("halR", [128, 4, 32], BF16)
            d_hal = gdep("hal")
            d_haloG, d_haloP = gdep("haloG"), gdep("haloP")
            def halo_fill():
                jl = (jr + 3) % 4
                jrr = (jr + 1) % 4
                rv = rcve.ap()
                S.dma("sp", halL, rv[bass.ds(jl * 512, 512), :].rearrange("(a p) c -> p a c", p=128), reads=[d_rcve], writes=[d_hal], tmin=150.0)
                S.dma("sp", halR, rv[bass.ds(jrr * 512, 512), :].rearrange("(a p) c -> p a c", p=128), reads=[d_rcve], writes=[d_hal], tmin=150.0)
                S.op("dve", lambda e: e.tensor_scalar(out=glu[:, :, 0:15], in0=halL[:, 0:2, 17:32], scalar1=percore[:, 0:1], scalar2=None, op0=ALU.mult),
                     reads=[d_hal, d_pc], writes=[d_haloG])
                S.op("dve", lambda e: e.tensor_scalar(out=glu[:, :, TL + 15:TL + 30], in0=halR[:, 0:2, 0:15], scalar1=percore[:, 1:2], scalar2=None,
                                                      op0=ALU.mult),
                     reads=[d_hal, d_pc], writes=[d_haloG])
                S.op("dve", lambda e: e.tensor_scalar(out=pu[:, :, 0:8], in0=halL[:, 2:4, 24:32], scalar1=percore[:, 0:1], scalar2=None, op0=ALU.mult),
                     reads=[d_hal, d_pc], writes=[d_haloP])
                S.op("dve", lambda e: e.tensor_scalar(out=pu[:, :, TL + 8:TL + 16], in0=halR[:, 2:4, 0:8], scalar1=percore[:, 1:2], scalar2=None,
                                                      op0=ALU.mult),
                     reads=[d_hal, d_pc], writes=[d_haloP])

            accr = Ring([(al("acc%d" % i, [128, 2, 512], F32), [gdep("acc%d_0" % i), gdep("acc%d_1" % i)]) for i in range(1)])
            ybr = Ring([(al("yb%d" % i, [128, 2, 512], BF16), [gdep("yb%d_0" % i), gdep("yb%d_1" % i)]) for i in range(2)])
            ysqr = Ring([(al("ysq%d" % i, [128, 2, 512], BF16), [gdep("ysq%d_0" % i), gdep("ysq%d_1" % i)]) for i in range(2)])
            meanr = Ring([(al("mean%d" % i, [128, 512], F32), gdep("mean%d" % i)) for i in range(2)])
            msqr = Ring([(al("msq%d" % i, [128, 512], F32), gdep("msq%d" % i)) for i in range(1)])
            rstdr = Ring([(al("rstdc%d" % i, [128, 512], F32), gdep("rstdc%d" % i)) for i in range(2)])
            actr = Ring([(al("actb%d" % i, [128, 2, 512], BF16), [gdep("actb%d_0" % i), gdep("actb%d_1" % i)]) for i in range(2)])
            p1e = al("p1e", [128, 528], F32)
            w4e = al("w4e", [128, 528], F32)
            w8e = al("w8e", [128, 528], F32)
            d_p1e, d_w4e, d_w8e = gdep("p1e"), gdep("w4e"), gdep("w8e")
            WS = al("WS", [128, 2, 512], F32)
            d_WS = [gdep("WS0"), gdep("WS1")]
            ybp = al("ybp", [128, 2, 512], BF16)
            d_ybp = [gdep("ybp0"), gdep("ybp1")]
            tmp8 = al("tmp8", [128, 8], F32)
            d_tmp8 = gdep("tmp8")
            b1_names = ["halL", "halR", "wpw", "wpool", "diag", "acc0", "yb0", "yb1", "ysq0", "ysq1", "mean0", "mean1", "msq0",
                        "rstdc0", "rstdc1", "actb0", "actb1", "p1e", "w4e", "w8e", "WS", "ybp", "tmp8", "glu", "gluC", "pu", "puC"]

            def conv_pool_chunk(G, dG, U, dU, cat, ci, T0, n, T, fixcol):
                acc, d_acc = accr.next()
                yb, d_yb = ybr.next()
                ysq, d_ysq = ysqr.next()
                mean_sb, d_mean = meanr.next()
                msq, d_msq = msqr.next()
                rstdc, d_rstdc = rstdr.next()
                actb, d_actb = actr.next()
                pcs = []
                for vt in range(2):
                    pc, pdc = ps_main.next()
                    pcs.append((pc, pdc))
                    for j in range(31):
                        S.op("pe", lambda e, vt=vt, j=j, pc=pc: e.matmul(pc[:, :n], diag[:, vt * 31 + j, :], G[:, vt, T0 + j:T0 + j + n],
                                                                         start=(j == 0), stop=(j == 30)),
                             reads=dG[vt] + [d_diag], writes=[pdc])
                    S.op("act", lambda e, vt=vt, pc=pc: e.activation(out=yb[:, vt, :n], in_=pc[:, :n], func=AF.Identity, bias=params[:, 128 + vt:129 + vt]),
                         reads=[pdc, d_params], writes=[d_yb[vt]])
                    S.op("act", lambda e, vt=vt, pc=pc: e.activation(out=ysq[:, vt, :n], in_=pc[:, :n], func=AF.Square, bias=params[:, 128 + vt:129 + vt]),
                         reads=[pdc, d_params], writes=[d_ysq[vt]])
                pm, pdm = ps_aux.next()
                pq, pdq = ps_aux.next()
                for vt in range(2):
                    S.op("pe", lambda e, vt=vt: e.matmul(pm[:, :n], cmat[:, ONESLN, :], yb[:, vt, :n], start=(vt == 0), stop=(vt == 1)),
                         reads=[d_yb[vt], d_const], writes=[pdm])
                for vt in range(2):
                    S.op("pe", lambda e, vt=vt: e.matmul(pq[:, :n], cmat[:, ONESLN, :], ysq[:, vt, :n], start=(vt == 0), stop=(vt == 1)),
                         reads=[d_ysq[vt], d_const], writes=[pdq])
                S.op("act", lambda e: e.copy(out=mean_sb[:, :n], in_=pm[:, :n]), reads=[pdm], writes=[d_mean])
                S.op("dve", lambda e: e.tensor_tensor(out=msq[:, :n], in0=mean_sb[:, :n], in1=mean_sb[:, :n], op=ALU.mult),
                     reads=[d_mean], writes=[d_msq])
                S.op("dve", lambda e: e.tensor_tensor(out=rstdc[:, :n], in0=pq[:, :n], in1=msq[:, :n], op=ALU.subtract),
                     reads=[pdq, d_msq], writes=[d_rstdc])
                S.op("act", lambda e: e.activation(out=rstdc[:, :n], in_=rstdc[:, :n], func=AF.Ln, bias=cvec[:, 0:1]), reads=[d_rstdc, d_const],
                     writes=[d_rstdc])
                S.op("act", lambda e: e.activation(out=rstdc[:, :n], in_=rstdc[:, :n], func=AF.Exp, scale=-0.5), reads=[d_rstdc], writes=[d_rstdc])
                for vt in range(2):
                    pc, pdc = pcs[vt]
                    S.op("dve", lambda e, vt=vt, pc=pc: e.scalar_tensor_tensor(out=acc[:, vt, :n], in0=pc[:, :n], scalar=params[:, 128 + vt:129 + vt],
                                                                               in1=mean_sb[:, :n], op0=ALU.add, op1=ALU.subtract),
                         reads=[pdc, d_mean, d_params], writes=[d_acc[vt]])
                    S.op("dve", lambda e, vt=vt: e.tensor_tensor(out=acc[:, vt, :n], in0=acc[:, vt, :n], in1=rstdc[:, :n], op=ALU.mult),
                         reads=[d_rstdc, d_acc[vt]], writes=[d_acc[vt]])
                    S.op("act", lambda e, vt=vt: e.activation(out=actb[:, vt, :n], in_=acc[:, vt, :n], func=AF.Silu,
                                                              bias=params[:, 132 + vt:133 + vt], scale=params[:, 130 + vt:131 + vt]),
                         reads=[d_acc[vt], d_params], writes=[d_actb[vt]])
                for ot in range(2):
                    pb, pd = ps_main.next()
                    for vt in range(2):
                        S.op("pe", lambda e, vt=vt, ot=ot, pb=pb: e.matmul(pb[:, :n], wpw[:, vt, ot * 128:(ot + 1) * 128], actb[:, vt, :n],
                                                                            start=(vt == 0), stop=(vt == 1)),
                             reads=[d_actb[vt], d_wcp], writes=[pd])
                    S.op("act", lambda e, ot=ot, pb=pb: e.copy(out=cat[:, 2 + ot, T0:T0 + n], in_=pb[:, :n]), reads=[pd], writes=[d_cat[2 + ot][ci]])
                e_ = n + 16
                c0 = 8
                S.op("dve", lambda e: e.tensor_tensor(out=WS[0:64, 0, :n], in0=U[0:64, 0, T0 + c0 - 1:T0 + c0 - 1 + n], in1=U[0:64, 0, T0 + c0:T0 + c0 + n],
                                                       op=ALU.add),
                     reads=dU[0], writes=[d_WS[0]])
                S.op("dve", lambda e: e.tensor_tensor(out=p1e[64:128, 1:e_], in0=U[64:128, 0, T0:T0 + e_ - 1], in1=U[64:128, 0, T0 + 1:T0 + e_], op=ALU.add),
                     reads=dU[0], writes=[d_p1e])
                S.op("dve", lambda e: e.tensor_tensor(out=WS[64:128, 0, :n], in0=p1e[64:128, c0 - 1:c0 - 1 + n], in1=p1e[64:128, c0 + 1:c0 + 1 + n],
                                                       op=ALU.add),
                     reads=[d_p1e], writes=[d_WS[0]])
                S.op("dve", lambda e: e.tensor_tensor(out=p1e[:, 1:e_], in0=U[:, 1, T0:T0 + e_ - 1], in1=U[:, 1, T0 + 1:T0 + e_], op=ALU.add),
                     reads=dU[1], writes=[d_p1e])
                S.op("dve", lambda e: e.tensor_tensor(out=w4e[:, 2:e_ - 1], in0=p1e[:, 1:e_ - 2], in1=p1e[:, 3:e_], op=ALU.add),
                     reads=[d_p1e], writes=[d_w4e])
                S.op("dve", lambda e: e.tensor_tensor(out=WS[0:64, 1, :n], in0=w4e[0:64, c0 - 2:c0 - 2 + n], in1=w4e[0:64, c0 + 2:c0 + 2 + n], op=ALU.add),
                     reads=[d_w4e], writes=[d_WS[1]])
                S.op("dve", lambda e: e.tensor_tensor(out=w8e[64:128, 4:e_ - 3], in0=w4e[64:128, 2:e_ - 5], in1=w4e[64:128, 6:e_ - 1], op=ALU.add),
                     reads=[d_w4e], writes=[d_w8e])
                S.op("dve", lambda e: e.tensor_tensor(out=WS[64:128, 1, :n], in0=w8e[64:128, c0 - 4:c0 - 4 + n], in1=w8e[64:128, c0 + 4:c0 + 4 + n],
                                                       op=ALU.add),
                     reads=[d_w8e], writes=[d_WS[1]])
                for tl_ in range(2):
                    S.op("dve", lambda e, tl_=tl_: e.scalar_tensor_tensor(out=ybp[:, tl_, :n], in0=WS[:, tl_, :n], scalar=params[:, 136 + tl_:137 + tl_],
                                                                          in1=U[:, tl_, T0 + c0:T0 + c0 + n], op0=ALU.mult, op1=ALU.subtract),
                         reads=[d_WS[tl_], d_params] + dU[tl_], writes=[d_ybp[tl_]])
                    edges = []
                    if T0 == 0:
                        edges.append((0, fixcol + tl_ * 16))
                    if T0 + n == T:
                        edges.append((n - 8, fixcol + tl_ * 16 + 8))
                    for (e0, fc) in edges:
                        S.op("dve", lambda e, tl_=tl_, e0=e0, fc=fc: e.tensor_tensor(out=tmp8[:, :], in0=WS[:, tl_, e0:e0 + 8], in1=percore[:, fc:fc + 8],
                                                                                      op=ALU.mult),
                             reads=[d_WS[tl_], d_pc], writes=[d_tmp8])
                        S.op("dve", lambda e, tl_=tl_, e0=e0: e.tensor_tensor(out=ybp[:, tl_, e0:e0 + 8], in0=tmp8[:, :],
                                                                               in1=U[:, tl_, T0 + c0 + e0:T0 + c0 + e0 + 8], op=ALU.subtract),
                             reads=[d_tmp8] + dU[tl_], writes=[d_ybp[tl_]])
                    pb, pd = ps_main.next()
                    S.op("pe", lambda e, tl_=tl_, pb=pb: e.matmul(pb[:, :n], wpool[:, tl_, :], ybp[:, tl_, :n], start=True, stop=True),
                         reads=[d_ybp[tl_], d_wcp], writes=[pd])
                    S.op("act", lambda e, tl_=tl_, pb=pb: e.activation(out=cat[:, 4 + tl_, T0:T0 + n], in_=pb[:, :n], func=AF.Identity,
                                                                        scale=params[:, 134 + tl_:135 + tl_]),
                         reads=[pd, d_params], writes=[d_cat[4 + tl_][ci]])

            if not last:
                conv_pool_chunk(gluC, [[d] for d in d_gluC], puC, [[d] for d in d_puC], catC, 4, 0, CTX, CTX, 34)
            for c in (1, 2, 0, 3):
                edge = c in (0, 3)
                if c == 0:
                    halo_fill()
                conv_pool_chunk(glu, [[d] + ([d_haloG] if edge else []) for d in d_glu], pu, [[d] + ([d_haloP] if edge else []) for d in d_pu],
                                catT, c, c * 512, 512, TL, 2)
            tap("catconv" + L, catT[:, 2:6, :], [128, 4, TL], BF16, [d_cat[r][c] for r in range(2, 6) for c in range(4)])
            tap("catCconv" + L, catC[:, 2:6, :], [128, 4, CTX], BF16, [d_cat[r][4] for r in range(2, 6)])
            phase_end(b1_names)
            if stop_after == "B1" + L:
                return True

            KT = al("KT", [128, SEQ], BF16)
            d_KT = [gdep("KT%d" % r) for r in range(4)]
            X1 = al("X1", [128, 64, 128], BF16)
            d_X1 = gdep("X1")
            Gr = al("Gr", [128, 64, 64], BF16)
            Gi = al("Gi", [128, 64, 64], BF16)
            d_Gr, d_Gi = gdep("Gr"), gdep("Gi")
            Yout = al("Yout", [128, 64, 64], BF16)
            d_Yout = gdep("Yout")
            tw = al("tw", [128, 2, 512], F32)
            d_tw = gdep("tw")
            S.dma("sp", tw, tw_d.ap(), writes=[d_tw])
            ta = Ring([(al("twa%d" % i, [128, 512], F32), gdep("twa%d" % i)) for i in range(2)])
            tb_ = Ring([(al("twb%d" % i, [128, 512], F32), gdep("twb%d" % i)) for i in range(2)])
            b2_names = ["X1", "Gr", "Gi", "Yout", "tw", "twa0", "twa1", "twb0", "twb1"]
            for c in range(4):
                for ri in range(2):
                    for r in range(4):
                        src = rcvc[c].ap()[bass.ds(jr * 64 + (r * RC[c] + 320 + ri * 256), 64), :].rearrange("m (a t) -> a m t", t=128)
                        p0 = ri * 64 + r * 16 + c * 4
                        S.dma("sp", X1[p0:p0 + 4, :, :], src, reads=[d_rcvc[c]], writes=[d_X1])
            for r in range(4):
                for c in range(4):
                    S.dma("sp", KT[:, r * TL + c * 512:r * TL + (c + 1) * 512], rcvc[c].ap()[r * RC[c]:r * RC[c] + 128, :], reads=[d_rcvc[c]],
                          writes=[d_KT[r]])
            for mg in range(16):
                pb, pd = ps_main.next()
                for mi in range(4):
                    S.op("pe", lambda e, mi=mi, mg=mg, pb=pb: e.matmul(pb[:, mi * 128:(mi + 1) * 128], X1[:, mg * 4 + mi, :], cmat[:, R1M, :],
                                                                        start=True, stop=True),
                         reads=[d_X1, d_const], writes=[pd])
                a_, da = ta.next()
                b_, db = tb_.next()
                S.op("dve", lambda e, pb=pb, a_=a_: e.tensor_tensor(out=a_[:, :], in0=pb[:, :], in1=tw[:, 0, :], op=ALU.mult), reads=[pd, d_tw], writes=[da])
                S.op("dve", lambda e, pb=pb, b_=b_: e.tensor_tensor(out=b_[:, :], in0=pb[:, :], in1=tw[:, 1, :], op=ALU.mult), reads=[pd, d_tw], writes=[db])
                av = a_.rearrange("p (m r k) -> p m r k", r=2, k=64)
                bv = b_.rearrange("p (m r k) -> p m r k", r=2, k=64)
                S.op("dve", lambda e, av=av, bv=bv, mg=mg: e.tensor_tensor(out=Gr[:, mg * 4:(mg + 1) * 4, :], in0=av[:, :, 0, :], in1=bv[:, :, 1, :], op=ALU.add),
                     reads=[da, db], writes=[d_Gr])
                S.op("dve", lambda e, av=av, bv=bv, mg=mg: e.tensor_tensor(out=Gi[:, mg * 4:(mg + 1) * 4, :], in0=av[:, :, 1, :], in1=bv[:, :, 0, :],
                                                                             op=ALU.subtract),
                     reads=[da, db], writes=[d_Gi])
            Grf = Gr.rearrange("p m k -> p (m k)")
            Gif = Gi.rearrange("p m k -> p (m k)")
            Yf = Yout.rearrange("p m k -> p (m k)")
            for ch in range(8):
                pb, pd = ps_main.next()
                S.op("pe", lambda e, ch=ch, pb=pb: e.matmul(pb[:, :], cmat[:, C128S, :], Grf[:, ch * 512:(ch + 1) * 512], start=True, stop=False),
                     reads=[d_Gr, d_const], writes=[pd])
                S.op("pe", lambda e, ch=ch, pb=pb: e.matmul(pb[:, :], cmat[:, S128S, :], Gif[:, ch * 512:(ch + 1) * 512], start=False, stop=True),
                     reads=[d_Gi, d_const], writes=[pd])
                S.op("act", lambda e, ch=ch, pb=pb: e.copy(out=Yf[:, ch * 512:(ch + 1) * 512], in_=pb[:, :]), reads=[pd], writes=[d_Yout])
            o2 = S.dma("sp", snd2.ap().rearrange("m (k2 k1) -> k2 m k1", k1=64), Yout, reads=[d_Yout], writes=[d_snd2], out_side=True)
            tap("Yout" + L, Yout, [128, 64, 64], BF16, [d_Yout])
            S.collective(lambda e: e.collective_compute("AllGather", ALU.bypass, replica_groups=RG, ins=[snd2.ap()], outs=[rcv2.ap()]),
                         reads=[d_snd2], writes=[d_rcv2], extra=[o2], name="g2")
            if not last:
                for mt in range(2):
                    pb, pd = ps_main.next()
                    for tt in range(2):
                        S.op("pe", lambda e, tt=tt, mt=mt, pb=pb: e.matmul(pb[:, 0:CTX], zc_tm[:, tt, mt * 128:(mt + 1) * 128], cs256[:, 0, tt, :],
                                                                            start=(tt == 0), stop=False),
                             reads=[d_zc, d_const], writes=[pd])
                        S.op("pe", lambda e, tt=tt, mt=mt, pb=pb: e.matmul(pb[:, 0:CTX], zc_tm[:, tt, 256 + mt * 128:256 + (mt + 1) * 128], cs256[:, 1, tt, :],
                                                                            start=False, stop=(tt == 1)),
                             reads=[d_zc, d_const], writes=[pd])
                    S.op("act", lambda e, mt=mt, pb=pb: e.copy(out=ycT[:, mt, :], in_=pb[:, 0:CTX]), reads=[pd], writes=[d_ycT])
                tap("ycT" + L, ycT, [128, 2, CTX], BF16, [d_ycT])
            phase_end(b2_names)
            if stop_after == "B2" + L:
                return True

            attnT = al("attnT", [128, 2, TL], BF16, top=True)
            attnC = al("attnC", [128, 2, CTX], BF16, top=True)
            d_attn = [[gdep("attn%d_%d" % (h, c)) for c in range(5)] for h in range(4)]
            wo_att = al("wo_att", [128, 2, D], BF16)
            wo_rest = al("wo_rest", [128, 6, D], BF16)
            d_wo = gdep("wo")
            for tq in range(2):
                S.dma("pool", wo_att[0:64, tq, :], w_out_d.ap()[l, tq * 64:(tq + 1) * 64, :], writes=[d_wo])
                S.dma("pool", wo_att[64:128, tq, :], w_out_d.ap()[l, (2 + tq) * 64:(3 + tq) * 64, :], writes=[d_wo])
            S.dma("pool", wo_rest, w_out_d.ap()[l, 256:1024, :].rearrange("(r p) o -> p r o", p=128), writes=[d_wo])
            wf = al("wf", [128, 2, 256], BF16)
            d_wf = gdep("wf")
            S.dma("pool", wf, w_f_d.ap()[l].rearrange("(k p) o -> p k o", p=128), writes=[d_wf])
            Vx = al("Vx", [128, 64, 192], BF16)
            d_Vx = [gdep("Vx%d" % r) for r in range(4)]
            for r in range(4):
                for c in range(4):
                    rcv_ = rcvc[c].ap()
                    vsrc = rcv_[r * RC[c] + 128:r * RC[c] + 320, :].rearrange("r c -> (r c)").rearrange("(tt p c) -> p tt c", p=128, c=192)
                    S.dma("sp", Vx[:, r * 16 + c * 4:r * 16 + (c + 1) * 4, :], vsrc, reads=[d_rcvc[c]], writes=[d_Vx[r]])
            ering = Ring([(al("E%d" % i, [128, 512], BF16), gdep("E%d" % i)) for i in range(3)])
            b3_names = ["KT", "Vx", "E0", "E1", "E2", "ob0", "ob1", "rs0", "qT", "qC", "KTc", "Vxc", "zctm"]

            obr = Ring([(al("ob%d" % i, [128, 512], F32), gdep("ob%d" % i)) for i in range(2)])
            rsr = Ring([(al("rs%d" % i, [128, 512], F32), gdep("rsr%d" % i)) for i in range(1)])
            pending_fin = []

            def flush_fin():
                while pending_fin:
                    pending_fin.pop(0)()

            def attention(Q, dQ, ci, T0, n, key_tiles, dest):
                for tq in range(2):
                    for hf in range(2):
                        head = hf * 2 + tq
                        ps_ = slice(hf * 64, (hf + 1) * 64)
                        pO, pdO = ps_acc.next()
                        nk = len(key_tiles)
                        sbank = {}

                        def issue_S(kt, head=head, tq=tq, sbank=sbank):
                            Ksrc, dK, Vsrc, dV = key_tiles[kt]
                            pS, pdS = ps_main.next()
                            sbank[kt] = (pS, pdS)
                            S.op("pe", lambda e, pS=pS, Ksrc=Ksrc, head=head: e.matmul(pS[:, :n], Ksrc[:, :], Q[:, head, T0:T0 + n],
                                                                                      start=True, stop=True),
                                 reads=[dK, dQ[tq][ci]], writes=[pdS])

                        LA = 2
                        for k0 in range(min(LA, nk)):
                            issue_S(k0)
                        for kt in range(nk):
                            Ksrc, dK, Vsrc, dV = key_tiles[kt]
                            pS, pdS = sbank.pop(kt)
                            Eb, dE = ering.next()
                            S.op("act", lambda e, pS=pS, Eb=Eb: e.activation(out=Eb[:, :n], in_=pS[:, :n], func=AF.Exp, scale=0.125),
                                 reads=[pdS], writes=[dE])
                            S.op("pe", lambda e, Eb=Eb, Vsrc=Vsrc, kt=kt, pO=pO, hf=hf, nk=nk: e.matmul(pO[:, :n], Vsrc[:, hf * 64:hf * 64 + 128], Eb[:, :n],
                                                                                   start=(kt == 0), stop=(kt == nk - 1)),
                                 reads=[dE, dV], writes=[pdO])
                            if kt + LA < nk:
                                issue_S(kt + LA)
                            if kt == min(3, nk - 1):
                                flush_fin()
                        sr = (64 if hf == 0 else 0)
                        ob, dob = obr.next()
                        rs_, drs = rsr.next()
                        S.op("act", lambda e, pO=pO, ob=ob, ps_=ps_: e.copy(out=ob[ps_, :n], in_=pO[ps_, :n]), reads=[pdO], writes=[dob])
                        S.op("act", lambda e, pO=pO, rs_=rs_, sr=sr: e.activation(out=rs_[sr:sr + 1, :n], in_=pO[sr:sr + 1, :n], func=AF.Ln),
                             reads=[pdO], writes=[drs])
                        S.op("act", lambda e, rs_=rs_, sr=sr: e.activation(out=rs_[sr:sr + 1, :n], in_=rs_[sr:sr + 1, :n], func=AF.Exp, scale=-1.0),
                             reads=[drs], writes=[drs])

                        def fin(ob=ob, dob=dob, rs_=rs_, drs=drs, sr=sr, ps_=ps_, tq=tq, head=head):
                            pbc, pdbc = ps_aux.next()
                            S.op("pe", lambda e: e.matmul(pbc[:, :n], onesrow[sr:sr + 1, :], rs_[sr:sr + 1, :n], start=True, stop=True),
                                 reads=[drs, d_const], writes=[pdbc])
                            S.op("dve", lambda e: e.tensor_tensor(out=dest[ps_, tq, T0:T0 + n], in0=ob[ps_, :n], in1=pbc[ps_, :n], op=ALU.mult),
                                 reads=[dob, pdbc], writes=[d_attn[head][ci]])
                        pending_fin.append(fin)

            ctx_keys = [(KTc[:, tt * 128:(tt + 1) * 128], d_KTc, Vxc[:, tt, :], d_Vxc) for tt in range(2)]
            lat_keys = [(KT[:, tt * 128:(tt + 1) * 128], d_KT[tt // 16], Vx[:, tt, :], d_Vx[tt // 16]) for tt in range(64)]
            if not last:
                attention(qC, d_q, 4, 0, CTX, ctx_keys, attnC)
            for c in range(4):
                attention(qT, d_q, c, c * 512, 512, ctx_keys + lat_keys, attnT)
            flush_fin()
            tap("attnT" + L, attnT, [128, 2, TL], BF16, [d_attn[h][c] for h in range(4) for c in range(4)])
            tap("attnC" + L, attnC, [128, 2, CTX], BF16, [d_attn[h][4] for h in range(4)])
            phase_end(b3_names)
            if stop_after == "B3" + L:
                return True

            XN2 = al("XN2", [128, 8, TL], BF16, top=True)
            xnC = al("xnC", [128, 8, CTX], BF16, top=True)
            d_xn2 = [[gdep("xn2_%d_%d" % (c, k)) for k in range(8)] for c in range(5)]
            A_views["wfi0"] = al("wfi0", [128, 8, 1024], BF16, top=True)
            fisrc0 = w_fi_d.ap()[l].rearrange("(k p) o -> p k o", p=128)
            S.dma("pool", A_views["wfi0"][:, :, 0:512], fisrc0[:, :, 0:512], writes=[gdep("wfi0")])
            S.dma("pool", A_views["wfi0"][:, :, 512:1024], fisrc0[:, :, D_FF:D_FF + 512], writes=[gdep("wfi0")])
            yT = al("yT", [128, 2, 512], BF16)
            d_yT = gdep("yT")
            sqbC = al("sqbC", [128, 4, 512], BF16)
            d_sqC = [gdep("sqC%d" % k) for k in range(8)]
            rstdC = al("rstdC", [128, 512], F32)
            d_rstdC = gdep("rstdC")
            tmprC = Ring([(al("tmpC%d" % i, [128, 512], F32), gdep("tmpC%d" % i)) for i in range(3)])
            ntmpsC = (sqbC, d_sqC, rstdC, d_rstdC, tmprC)
            c1_names = ["wo_att", "wo_rest", "wf", "yT", "sqbC", "rstdC", "tmpC0", "tmpC1", "tmpC2", "catT", "catC", "attnT", "attnC", "ycT"]
            r2v = rcv2.ap()
            chunks = ([ctx_chunk] if not last else []) + lat_chunks
            def do_chunk_C1(chunk):
                s, X, dX, T0, n, ci = chunk
                isctx = (s == 1)
                cat = catC if isctx else catT
                att = attnC if isctx else attnT
                if isctx:
                    ysrc, dys = ycT, d_ycT
                else:
                    S.dma("sp", yT[:, :, :n], r2v[:, bass.ds(jr * TL + T0, n)].rearrange("(k p) t -> p k t", p=128), reads=[d_rcv2], writes=[d_yT])
                    ysrc, dys = yT, d_yT
                for ot in range(2):
                    pb, pd = ps_main.next()
                    for k2 in range(2):
                        S.op("pe", lambda e, k2=k2, ot=ot, pb=pb, ysrc=ysrc: e.matmul(pb[:, :n], wf[:, k2, ot * 128:(ot + 1) * 128], ysrc[:, k2, :n],
                                                                                       start=(k2 == 0), stop=(k2 == 1)),
                             reads=[dys, d_wf], writes=[pd])
                    S.op("act", lambda e, ot=ot, pb=pb, cat=cat: e.copy(out=cat[:, ot, T0:T0 + n], in_=pb[:, :n]), reads=[pd], writes=[d_cat[ot][ci]])
                for ot in range(8):
                    pb, pd = ps_main.next()
                    for h in range(2):
                        S.op("pe", lambda e, h=h, ot=ot, pb=pb, att=att: e.matmul(pb[:, :n], wo_att[:, h, ot * 128:(ot + 1) * 128], att[:, h, T0:T0 + n],
                                                                                   start=(h == 0), stop=False),
                             reads=[d_attn[h][ci], d_attn[2 + h][ci], d_wo], writes=[pd])
                    for r in range(6):
                        S.op("pe", lambda e, r=r, ot=ot, pb=pb, cat=cat: e.matmul(pb[:, :n], wo_rest[:, r, ot * 128:(ot + 1) * 128], cat[:, r, T0:T0 + n],
                                                                                   start=False, stop=(r == 5)),
                             reads=[d_cat[r][ci], d_wo], writes=[pd])
                    S.op("dve", lambda e, ot=ot, pb=pb: e.scalar_tensor_tensor(out=X[:, ot, T0:T0 + n], in0=pb[:, :n], scalar=modT[:, 16 + ot, s:s + 1],
                                                                               in1=X[:, ot, T0:T0 + n], op0=ALU.mult, op1=ALU.add),
                         reads=[pd, d_mod, dX[ot]], writes=[dX[ot]])
                xdst = xnC if isctx else XN2[:, :, T0:T0 + n]
                norm_mod(chunk, 1, xdst, d_xn2[ci], ntmpsC, mods)

            for chunk in chunks:
                do_chunk_C1(chunk)
            tap("x1_" + L, xT, [128, 8, TL], F32, [d_x[k][c] for k in range(8) for c in range(4)])
            tap("h1_" + L, hT, [128, 8, CTX], F32, d_h)
            tap("cat" + L, catT, [128, 6, TL], BF16, [d_cat[r][c] for r in range(6) for c in range(4)])
            phase_end(c1_names)
            if stop_after == "C1" + L:
                return True

            wfi = [A_views["wfi0"], al("wfi1", [128, 8, 1024], BF16)]
            wfo = [al("wfo%d" % i, [128, 4, D], BF16) for i in range(2)]
            d_wfi = [gdep("wfi%d" % i) for i in range(2)]
            d_wfo = [gdep("wfo%d" % i) for i in range(2)]
            hbr = Ring([(al("hb%d" % i, [128, 4, 512], BF16), gdep("hb%d" % i)) for i in range(2)])
            sar = Ring([(al("sa%d" % i, [128, 512], F32), gdep("sa%d" % i)) for i in range(2)])
            c2_names = ["wfi0", "wfi1", "wfo0", "wfo1", "hb0", "hb1", "sa0", "sa1", "XN2", "xnC"]
            if not last:
                A_views["wm0"] = al("wm0", [128, 8, 512], BF16)
                A_views["wm1"] = al("wm1", [128, 8, 512], BF16)
                c2_names = c2_names + ["wm0", "wm1"]
                stage_M(l + 1, "begin")
            fisrc = w_fi_d.ap()[l].rearrange("(k p) o -> p k o", p=128)
            groups = [(0, 4), (4, 4), (8, 4), (12, 4), (16, 4), (20, 2)]
            for gi, (h0, gw) in enumerate(groups):
                sl = gi % 2
                if gi > 0:
                    S.dma("pool", wfi[sl][:, :, 0:gw * 128], fisrc[:, :, h0 * 128:(h0 + gw) * 128], writes=[d_wfi[sl]])
                    S.dma("pool", wfi[sl][:, :, 512:512 + gw * 128], fisrc[:, :, D_FF + h0 * 128:D_FF + (h0 + gw) * 128], writes=[d_wfi[sl]])
                S.dma("pool", wfo[sl][:, 0:gw, :], w_fo_d.ap()[l, h0 * 128:(h0 + gw) * 128, :].rearrange("(c p) o -> p c o", p=128), writes=[d_wfo[sl]])
                def do_chunk_C2(chunk, sl=sl, gw=gw):
                    s, X, dX, T0, n, ci = chunk
                    isctx = (s == 1)
                    xsrc_ = xnC if isctx else XN2[:, :, T0:T0 + n]
                    hb, dhb = hbr.next()
                    for hc in range(gw):
                        pa, pda = ps_main.next()
                        pg, pdg = ps_main.next()
                        for k in range(8):
                            S.op("pe", lambda e, k=k, hc=hc, pa=pa, sl=sl, xsrc_=xsrc_: e.matmul(pa[:, :n], wfi[sl][:, k, hc * 128:(hc + 1) * 128], xsrc_[:, k, :n],
                                                                                                  start=(k == 0), stop=(k == 7)),
                                 reads=[d_wfi[sl], d_xn2[ci][k]], writes=[pda])
                        for k in range(8):
                            S.op("pe", lambda e, k=k, hc=hc, pg=pg, sl=sl, xsrc_=xsrc_: e.matmul(pg[:, :n], wfi[sl][:, k, 512 + hc * 128:512 + (hc + 1) * 128],
                                                                                                  xsrc_[:, k, :n], start=(k == 0), stop=(k == 7)),
                                 reads=[d_wfi[sl], d_xn2[ci][k]], writes=[pdg])
                        sa, dsa = sar.next()
                        S.op("act", lambda e, pa=pa, sa=sa: e.activation(out=sa[:, :n], in_=pa[:, :n], func=AF.Silu), reads=[pda], writes=[dsa])
                        S.op("dve", lambda e, pg=pg, sa=sa, hb=hb, hc=hc: e.tensor_tensor(out=hb[:, hc, :n], in0=pg[:, :n], in1=sa[:, :n], op=ALU.mult),
                             reads=[pdg, dsa], writes=[dhb])
                    for ot in range(8):
                        po, pdo = ps_acc.next()
                        for hc in range(gw):
                            S.op("pe", lambda e, hc=hc, ot=ot, po=po, sl=sl, hb=hb: e.matmul(po[:, :n], wfo[sl][:, hc, ot * 128:(ot + 1) * 128], hb[:, hc, :n],
                                                                                              start=(hc == 0), stop=(hc == gw - 1)),
                                 reads=[d_wfo[sl], dhb], writes=[pdo])
                        S.op("dve", lambda e, ot=ot, po=po, X=X, T0=T0, s=s: e.scalar_tensor_tensor(out=X[:, ot, T0:T0 + n], in0=po[:, :n],
                                                                                                    scalar=modT[:, 40 + ot, s:s + 1], in1=X[:, ot, T0:T0 + n],
                                                                                                    op0=ALU.mult, op1=ALU.add),
                             reads=[pdo, d_mod, dX[ot]], writes=[dX[ot]])

                for chunk in chunks:
                    do_chunk_C2(chunk)
                if not last:
                    stage_M(l + 1, 2 * gi)
                    stage_M(l + 1, 2 * gi + 1)
            if not last:
                stage_M(l + 1, "end")
            tap("x2_" + L, xT, [128, 8, TL], F32, [d_x[k][c] for k in range(8) for c in range(4)])
            if last and stop_after is None:
                osrc = outT_d.ap().rearrange("(k p) t -> p k t", p=128)
                d_osem = Dep("osem")
                for c in range(4):
                    o = S.dma("sp", osrc[:, :, c * 512:(c + 1) * 512], xT[:, :, c * 512:(c + 1) * 512], reads=[d_x[k][c] for k in range(8)],
                              out_side=True, sem_dep=d_osem)
                    final_ops.append(o)
                out_done[0] = True
            phase_end(c2_names)
            if stop_after == "C2" + L:
                return True
            return False

        for l in range(DEPTH):
            if do_layer(l):
                break

        if not out_done[0]:
            osrc = outT_d.ap().rearrange("(k p) t -> p k t", p=128)
            d_osem = Dep("osem")
            for k in range(8):
                o = S.dma("sp", osrc[:, k, :], xT[:, k, :], reads=d_x[k], out_side=True, sem_dep=d_osem)
                final_ops.append(o)
        block = st.enter_context(nc.Block())
        S.emit(block, final_waits=final_ops)
        build_program.peak_words = A.peak
    return nc, tap_out


def _consts():
    f = np.float32
    cm = np.zeros((128, 9, 128), f)
    cm[:, 0, :] = 1.0 / 1024
    for b in range(2):
        cm[b * 64:(b + 1) * 64, 1, b * 64:(b + 1) * 64] = 1.0 / 64
    for k in range(128):
        cm[k, 2, k ^ 1] = 1.0
    cm[:, 3, :] = 1.0 / 256
    cm[:, 7, :] = np.eye(128)
    t1 = np.arange(64)[:, None].astype(np.float64)
    k1 = np.arange(64)[None, :].astype(np.float64)
    C = np.cos(2 * np.pi * t1 * k1 / 64)
    Sn = np.sin(2 * np.pi * t1 * k1 / 64)
    R1m = np.zeros((128, 128))
    R1m[0:64, 0:64] = C
    R1m[64:128, 0:64] = Sn
    R1m[0:64, 64:128] = -Sn
    R1m[64:128, 64:128] = C
    cm[:, 4, :] = R1m
    t2 = np.arange(128)[:, None].astype(np.float64)
    k2 = np.arange(128)[None, :].astype(np.float64)
    nrm = 1.0 / np.sqrt(8192.0 * 64.0)
    cm[:, 5, :] = np.cos(2 * np.pi * t2 * k2 / 128) * nrm
    cm[:, 6, :] = np.sin(2 * np.pi * t2 * k2 / 128) * nrm
    cs = np.zeros((256, 512))
    cc = np.arange(64)[:, None].astype(np.float64)
    m = np.arange(64)[None, :].astype(np.float64)
    for h in range(4):
        cs[h * 64:(h + 1) * 64, h * 64:(h + 1) * 64] = np.cos(2 * np.pi * cc * m / 64)
        cs[h * 64:(h + 1) * 64, 256 + h * 64:256 + (h + 1) * 64] = -np.sin(2 * np.pi * cc * m / 64)
    csblk = np.ascontiguousarray(cs.reshape(2, 128, 512).transpose(1, 0, 2)).astype(f)
    k1r = np.arange(64)[None, :].astype(np.float64)
    twr = np.cos(2 * np.pi * t2 * k1r / 8192)
    twi = np.sin(2 * np.pi * t2 * k1r / 8192)
    tw = np.stack([np.tile(twr, (1, 8)), np.tile(twi, (1, 8))], axis=1).astype(f)
    t = np.arange(256)[:, None].astype(np.float64)
    k = np.arange(256)[None, :].astype(np.float64)
    n2 = 1.0 / np.sqrt(256.0 * 64.0)
    c256 = np.cos(2 * np.pi * t * k / 256) * n2
    s256 = np.sin(2 * np.pi * t * k / 256) * n2
    cs256 = np.stack([c256.reshape(2, 128, 256).transpose(1, 0, 2), s256.reshape(2, 128, 256).transpose(1, 0, 2)], axis=1).astype(f)
    return dict(cmat=cm, csblk=csblk, tw=tw, cs256=np.ascontiguousarray(cs256))


def _rope_tables(t0):
    tpos = np.arange(t0, t0 + TL)
    row = (tpos // 64).astype(np.float32)
    col = (tpos % 64).astype(np.float32)
    inv_freq = (np.float32(10000.0) ** (-np.arange(16, dtype=np.float32) / np.float32(16))).astype(np.float32)
    ang = np.concatenate([row[:, None] * inv_freq, col[:, None] * inv_freq], axis=-1).astype(np.float32)
    cos = np.cos(ang).astype(np.float32)
    sin = np.sin(ang).astype(np.float32)
    p = np.arange(128)
    d = p % 64
    i = d // 2
    sign = np.where(d % 2 == 0, -1.0, 1.0).astype(np.float32)
    rc = np.ascontiguousarray(cos[:, i].T)
    rs = np.ascontiguousarray((sin[:, i] * sign[None, :]).T)
    return rc, rs


def _invcnt(tglob, n, win):
    lo = np.clip(tglob - win // 2, 0, n)
    hi = np.clip(tglob - win // 2 + win, 0, n)
    return (1.0 / (hi - lo)).astype(np.float32)


def _percore(j):
    pc = np.zeros((128, 66), np.float32)
    pc[:, 0] = 0.0 if j == 0 else 1.0
    pc[:, 1] = 0.0 if j == 3 else 1.0
    wins = {(0, 0): 2, (0, 1): 4, (1, 0): 8, (1, 1): 16}
    for tile in range(2):
        for half in range(2):
            win = wins[(tile, half)]
            ps = slice(half * 64, (half + 1) * 64)
            tl = np.concatenate([np.arange(j * TL, j * TL + 8), np.arange((j + 1) * TL - 8, (j + 1) * TL)])
            pc[ps, 2 + tile * 16:2 + (tile + 1) * 16] = _invcnt(tl, SEQ, win)[None, :]
            tc = np.concatenate([np.arange(0, 8), np.arange(CTX - 8, CTX)])
            pc[ps, 34 + tile * 16:34 + (tile + 1) * 16] = _invcnt(tc, CTX, win)[None, :]
    return pc


def _params(inp):
    P = np.zeros((DEPTH, 128, NP_COLS), np.float32)
    for l in range(DEPTH):
        P[l, :, 0:8] = inp["g_norm1"][l].reshape(8, 128).T
        P[l, :, 8:16] = inp["g_norm2"][l].reshape(8, 128).T
        P[l, :, 16:64] = inp["b_mod"][l].reshape(48, 128).T
        P[l, :, 64] = np.tile(inp["q_norm_g"][l], 2)
        P[l, :, 65] = np.tile(inp["k_norm_g"][l], 2)
        cw = inp["conv_dw_w"][l]
        for vt in range(2):
            P[l, :, 66 + vt * 31:66 + (vt + 1) * 31] = cw[:, vt * 128:(vt + 1) * 128].T
        P[l, :, 128:130] = inp["conv_dw_b"][l].reshape(2, 128).T
        P[l, :, 130:132] = inp["conv_ln_g"][l].reshape(2, 128).T
        P[l, :, 132:134] = inp["conv_ln_b"][l].reshape(2, 128).T
        P[l, :, 134:136] = inp["pool_scale"][l].reshape(2, 128).T
        P[l, 0:64, 136] = 1.0 / 2
        P[l, 64:128, 136] = 1.0 / 4
        P[l, 0:64, 137] = 1.0 / 8
        P[l, 64:128, 137] = 1.0 / 16
    return P


_QPERM = np.concatenate([np.arange(0, 64), np.arange(128, 192), np.arange(64, 128), np.arange(192, 256), np.arange(256, D_IN)])


def prep_inputs(inp):
    inp = {k: np.asarray(v) for k, v in inp.items()}
    cst = _consts()
    params = _params(inp)
    w_in_p = np.ascontiguousarray(inp["w_in"][:, :, _QPERM])
    shared = dict(params=params, w_mod=inp["w_mod"], w_in=w_in_p, w_fourier=inp["w_fourier"], w_conv_pw=inp["w_conv_pw"],
                  w_pool=inp["w_pool"], w_out=inp["w_out"], w_ffn_in=inp["w_ffn_in"], w_ffn_out=inp["w_ffn_out"], **cst)
    maps = []
    for i in range(8):
        b, j = i // 4, i % 4
        m = dict(shared)
        m["xT"] = np.ascontiguousarray(inp["x"][b, j * TL:(j + 1) * TL, :].T)
        m["ctxT"] = np.ascontiguousarray(inp["ctx"][b].T)
        ccv = np.zeros((128, 16), np.float32)
        ccv[:, 0::2] = inp["c"][b].reshape(8, 128).T
        ccv[:, 1::2] = inp["c_ctx"].reshape(8, 128).T
        m["cc"] = ccv
        rc, rs = _rope_tables(j * TL)
        m["ropeC"] = rc
        m["ropeS"] = rs
        m["percore"] = _percore(j)
        maps.append(m)
    return maps


_NC_CACHE = {}


def kernel(**inputs):
    maps = prep_inputs(inputs)
    if "nc" not in _NC_CACHE:
        _NC_CACHE["nc"] = build_program()[0]
    nc = _NC_CACHE["nc"]
    res = run_bass_kernel_spmd(nc, maps, core_ids=list(range(8)))
    out = np.zeros((2, SEQ, D), np.float32)
    for i in range(8):
        b, j = i // 4, i % 4
        out[b, j * TL:(j + 1) * TL, :] = res.results[i]["outT"].T
    return out
```

```python
import contextlib
import numpy as np
import concourse.bass as bass
import concourse.mybir as mybir
from concourse.bass_utils import run_bass_kernel_spmd

F32 = mybir.dt.float32
BF16 = mybir.dt.bfloat16
AF = mybir.ActivationFunctionType
ALU = mybir.AluOpType

D = 1024
SEQ = 8192
TL = 2048
CTX = 256
DEPTH = 2
D_IN = 1536
D_FF = 2816
NHC = D_FF // 128
EPS = 1e-6
NKT = (CTX + SEQ) // 128
NP_COLS = 138
R1 = 832


class Dep:
    __slots__ = ("name", "w", "r", "sem_in", "cnt_in", "sem_out", "cnt_out")

    def __init__(self, name=""):
        self.name = name
        self.w = None
        self.r = []
        self.sem_in = None
        self.cnt_in = 0
        self.sem_out = None
        self.cnt_out = 0


class Op:
    __slots__ = ("eng", "fn", "deps", "alldeps", "signaled", "sigidx", "dsem", "dval", "name", "dinc", "seq", "cost", "lat", "seg", "eidx", "pend", "tmin")

    def __init__(self, eng, fn, name=""):
        self.eng = eng
        self.fn = fn
        self.deps = []
        self.signaled = False
        self.sigidx = None
        self.dsem = None
        self.dval = 0
        self.dinc = 16
        self.seq = 0
        self.cost = 0.0
        self.lat = 0.0
        self.seg = 0
        self.eidx = 0
        self.alldeps = []
        self.pend = None
        self.tmin = 0.0
        self.name = name


ENGS = ["pe", "act", "dve", "pool", "sp"]


class Sched:
    def __init__(self, nc, stack):
        self.nc = nc
        self.stack = stack
        self.ops = {e: [] for e in ENGS}
        self.esem = {e: stack.enter_context(nc.semaphore("es_" + e)) for e in ENGS}
        self.nsem = len(ENGS)
        self.pending_dma = []
        self.cc_sems = {}
        self.seg = 0
        self.ecount = 0
        self.reorder = True

    def new_sem(self, name):
        self.nsem += 1
        return self.stack.enter_context(self.nc.semaphore("%s_%d" % (name.replace(".", "_"), self.nsem)))

    def _collect(self, o, reads, writes, extra):
        deps = []
        seen = set()

        def add(d):
            if d is None or d is o or id(d) in seen:
                return
            seen.add(id(d))
            deps.append(d)

        for t in reads:
            add(t.w)
        for t in writes:
            add(t.w)
            for r in t.r:
                add(r)
        for d in extra:
            add(d)
        o.alldeps = deps
        o.deps = deps
        for t in reads:
            t.r.append(o)
        for t in writes:
            t.w = o
            t.r = []

    DEFCOST = {"pe": 0.25, "act": 0.6, "dve": 0.6, "pool": 1.2, "sp": 0.1}

    def _register(self, o):
        o.seg = self.seg
        o.eidx = self.ecount
        self.ecount += 1
        self.ops[o.eng].append(o)

    def op(self, eng, fn, reads=(), writes=(), extra=(), name="", cost=None):
        o = Op(eng, fn, name)
        o.cost = self.DEFCOST[eng] if cost is None else cost
        o.lat = o.cost
        self._collect(o, reads, writes, extra)
        self._register(o)
        return o

    def dma(self, q, out_ap, in_ap, reads=(), writes=(), sem_dep=None, out_side=False, extra=(), name="", tmin=0.0):
        if sem_dep is None:
            sem_dep = (reads[0] if out_side else writes[0])
        if out_side:
            if sem_dep.sem_out is None:
                sem_dep.sem_out = self.new_sem("do_" + sem_dep.name)
            sem_dep.cnt_out += 16
            dsem, dval = sem_dep.sem_out, sem_dep.cnt_out
        else:
            if sem_dep.sem_in is None:
                sem_dep.sem_in = self.new_sem("di_" + sem_dep.name)
            sem_dep.cnt_in += 16
            dsem, dval = sem_dep.sem_in, sem_dep.cnt_in

        def fn(eng, out_ap=out_ap, in_ap=in_ap):
            return eng.dma_start(out=out_ap, in_=in_ap)

        o = Op(q, fn, name)
        o.dsem, o.dval = dsem, dval
        o.cost, o.lat = 0.1, 20.0
        o.tmin = tmin
        self._collect(o, reads, writes, extra)
        o.alldeps = [d for d in o.alldeps if not (d.dsem is not None and d.dsem is dsem)]
        self._register(o)
        self.pending_dma.append(o)
        return o

    def collective(self, fn, reads=(), writes=(), extra=(), name="cc"):
        o = Op("pool", fn, name)
        if name not in self.cc_sems:
            self.cc_sems[name] = [self.new_sem("cc_" + name), 0]
        self.cc_sems[name][1] += 1
        o.dsem = self.cc_sems[name][0]
        o.dval = self.cc_sems[name][1]
        o.dinc = 1
        o.cost, o.lat = 0.5, 40.0
        self._collect(o, reads, writes, extra)
        self._register(o)
        return o

    def barrier(self):
        pend = list(self.pending_dma)
        self.pending_dma = []
        for e in ENGS:
            o = Op(e, None, "barrier")
            o.pend = pend
            o.seg = self.seg
            o.eidx = self.ecount
            self.ops[e].append(o)
        self.ecount += 1
        self.seg += 1

    def schedule(self):
        W = 64
        nseg = self.seg + 1
        segs = [{e: [] for e in ENGS} for _ in range(nseg)]
        bars = [{e: None for e in ENGS} for _ in range(nseg)]
        for e in ENGS:
            for o in self.ops[e]:
                if o.fn is None:
                    bars[o.seg][e] = o
                else:
                    segs[o.seg][e].append(o)
        new_ops = {e: [] for e in ENGS}
        for si in range(nseg):
            lists = segs[si]
            if self.reorder:
                finish = {}
                done = set()
                etime = {e: 0.0 for e in ENGS}
                remaining = {e: list(lists[e]) for e in ENGS}
                out = {e: [] for e in ENGS}
                total = sum(len(v) for v in remaining.values())
                while total:
                    best = None
                    for e in ENGS:
                        rem = remaining[e]
                        if not rem:
                            continue
                        seen_dma = False
                        for idx in range(min(W, len(rem))):
                            o = rem[idx]
                            if o.dsem is not None:
                                if seen_dma:
                                    continue
                                seen_dma = True
                            ready = o.tmin
                            ok = True
                            for d in o.alldeps:
                                if d.seg != si or d.fn is None:
                                    continue
                                if id(d) not in done:
                                    ok = False
                                    break
                                f = finish[id(d)] + (0.0 if d.eng == e else 0.5)
                                if f > ready:
                                    ready = f
                            if not ok:
                                continue
                            start = max(etime[e], ready)
                            key = (start, o.eidx)
                            if best is None or key < best[0]:
                                best = (key, e, idx, o)
                    if best is None:
                        raise RuntimeError("scheduler stuck")
                    (start, _), e, idx, o = best
                    remaining[e].pop(idx)
                    out[e].append(o)
                    done.add(id(o))
                    finish[id(o)] = start + o.lat
                    etime[e] = start + o.cost
                    total -= 1
                lists = out
            for e in ENGS:
                new_ops[e].extend(lists[e])
            if bars[si][ENGS[0]] is not None:
                lasts = [lists[e][-1] for e in ENGS if lists[e] and lists[e][-1].dsem is None]
                for e in ENGS:
                    b = bars[si][e]
                    b.alldeps = [d for d in lasts + b.pend if not (d.eng == e == "pe") or d.dsem is not None]
                    new_ops[e].append(b)
        self.ops = new_ops
        for e in ENGS:
            for i, o in enumerate(self.ops[e]):
                o.seq = i
        for e in ENGS:
            for o in self.ops[e]:
                bestd = {}
                for d in o.alldeps:
                    if d.dsem is not None:
                        key = ("s", id(d.dsem))
                        if key not in bestd or bestd[key].dval < d.dval:
                            bestd[key] = d
                    else:
                        key = ("e", d.eng)
                        if key not in bestd or bestd[key].seq < d.seq:
                            bestd[key] = d
                o.deps = list(bestd.values())

    def finalize(self, final_waits):
        for o in final_waits:
            if o.dsem is None:
                o.signaled = True
        for e in ENGS:
            for o in self.ops[e]:
                for d in o.deps:
                    if d.dsem is None:
                        if d.eng == "pe" and o.eng == "pe":
                            continue
                        d.signaled = True
        for e in ENGS:
            c = 0
            for o in self.ops[e]:
                if o.dsem is None and o.signaled:
                    c += 1
                    o.sigidx = c

    def emit(self, block, final_waits=()):
        self.schedule()
        self.finalize(final_waits)
        esem = self.esem

        def run(ename, eng):
            waited = {}
            for o in self.ops[ename]:
                for d in o.deps:
                    if d.dsem is not None:
                        key, val, sem = id(d.dsem), d.dval, d.dsem
                    else:
                        if d.eng == "pe" and ename == "pe":
                            continue
                        key, val, sem = d.eng, d.sigidx, esem[d.eng]
                    if waited.get(key, 0) >= val:
                        continue
                    waited[key] = val
                    eng.wait_ge(sem, val)
                if o.fn is None:
                    continue
                ins = o.fn(eng)
                if o.dsem is not None:
                    ins.then_inc(o.dsem, o.dinc)
                elif o.sigidx:
                    ins.then_inc(esem[ename], 1)
            if ename == "sp":
                fin = {}
                for o in final_waits:
                    if o.dsem is not None:
                        key, sem, val = id(o.dsem), o.dsem, o.dval
                    else:
                        key, sem, val = o.eng, esem[o.eng], o.sigidx
                    if key not in fin or fin[key][1] < val:
                        fin[key] = (sem, val)
                for sem, val in fin.values():
                    eng.wait_ge(sem, val)

        block.tensor(lambda eng: run("pe", eng))
        block.scalar(lambda eng: run("act", eng))
        block.vector(lambda eng: run("dve", eng))
        block.gpsimd(lambda eng: run("pool", eng))
        block.sync(lambda eng: run("sp", eng))


class Ring:
    def __init__(self, items):
        self.items = items
        self.i = 0

    def next(self):
        it = self.items[self.i % len(self.items)]
        self.i += 1
        return it


ARENA_WORDS = 52736


class Arena:
    def __init__(self, base_ap):
        self.base = base_ap
        self.free_list = [(0, ARENA_WORDS)]
        self.live = {}
        self.peak = 0

    def alloc(self, name, shape, dtype, top=False):
        elems = 1
        for d in shape[1:]:
            elems *= d
        esz = 4 if dtype == F32 else 2
        words = (elems * esz + 3) // 4
        words = (words + 15) // 16 * 16
        order = range(len(self.free_list) - 1, -1, -1) if top else range(len(self.free_list))
        for i in order:
            o, n = self.free_list[i]
            if n >= words:
                if top:
                    off = o + n - words
                    if n == words:
                        self.free_list.pop(i)
                    else:
                        self.free_list[i] = (o, n - words)
                else:
                    off = o
                    if n == words:
                        self.free_list.pop(i)
                    else:
                        self.free_list[i] = (o + words, n - words)
                break
        else:
            raise RuntimeError("arena full allocating %s (%d words); free=%s" % (name, words, self.free_list))
        v = self.base[0:shape[0], off:off + words]
        if dtype != F32:
            v = v.bitcast(dtype)
        v = v[:, 0:elems]
        if len(shape) == 3:
            v = v.rearrange("p (a b) -> p a b", b=shape[2])
        elif len(shape) == 4:
            v = v.rearrange("p (a b c) -> p a b c", b=shape[2], c=shape[3])
        elif len(shape) == 5:
            v = v.rearrange("p (a b c d) -> p a b c d", b=shape[2], c=shape[3], d=shape[4])
        self.live[name] = (off, words)
        used = ARENA_WORDS - sum(n for _, n in self.free_list)
        self.peak = max(self.peak, used)
        return v

    def free(self, *names):
        for name in names:
            off, words = self.live.pop(name)
            self.free_list.append((off, words))
        self.free_list.sort()
        merged = []
        for o, n in self.free_list:
            if merged and merged[-1][0] + merged[-1][1] == o:
                merged[-1] = (merged[-1][0], merged[-1][1] + n)
            else:
                merged.append((o, n))
        self.free_list = merged


def build_program(taps=(), stop_after=None):
    nc = bass.Bass("TRN2", target_bir_lowering=False)
    dt = nc.dram_tensor

    def ein(name, shape, dtype=F32):
        return dt(name, list(shape), dtype, kind="ExternalInput")

    xT_d = ein("xT", [D, TL])
    ctxT_d = ein("ctxT", [D, CTX])
    cc_d = ein("cc", [128, 16])
    ropeC_d = ein("ropeC", [128, TL])
    ropeS_d = ein("ropeS", [128, TL])
    percore_d = ein("percore", [128, 66])
    params_d = ein("params", [DEPTH, 128, NP_COLS])
    w_mod_d = ein("w_mod", [DEPTH, D, 6 * D])
    w_in_d = ein("w_in", [DEPTH, D, D_IN])
    w_f_d = ein("w_fourier", [DEPTH, 256, 256])
    w_pw_d = ein("w_conv_pw", [DEPTH, 256, 256])
    w_pool_d = ein("w_pool", [DEPTH, 4, 64, 64])
    w_out_d = ein("w_out", [DEPTH, D, D])
    w_fi_d = ein("w_ffn_in", [DEPTH, D, 2 * D_FF])
    w_fo_d = ein("w_ffn_out", [DEPTH, D_FF, D])
    cmat_d = ein("cmat", [128, 9, 128])
    csblk_d = ein("csblk", [128, 2, 512])
    tw_d = ein("tw", [128, 2, 512])
    c256_d = ein("cs256", [128, 2, 2, 256])
    outT_d = dt("outT", [D, TL], F32, kind="ExternalOutput")

    RC = [R1, R1, R1, R1]
    snde = dt("snde", [512, 32], BF16)
    rcve = dt("rcve", [4 * 512, 32], BF16)
    sndc = [dt("sndc%d" % c, [RC[c], 512], BF16) for c in range(4)]
    rcvc = [dt("rcvc%d" % c, [4 * RC[c], 512], BF16) for c in range(4)]
    snd2 = dt("snd2", [64, SEQ], BF16)
    rcv2 = dt("rcv2", [4 * 64, SEQ], BF16)
    RG = [[0, 1, 2, 3], [4, 5, 6, 7]]

    tap_out = {}
    final_ops = []
    stopped = [False]
    out_done = [False]

    with contextlib.ExitStack() as st:
        S = Sched(nc, st)
        pid = nc.partition_id()
        jr = pid % 4
        arena_t = st.enter_context(nc.sbuf_tensor("arena", [128, ARENA_WORDS], F32))
        A = Arena(arena_t[:, :])
        al = A.alloc

        psb = [st.enter_context(nc.psum_tensor("ps%d" % i, [128, 512], F32)) for i in range(8)]
        psd = [Dep("ps%d" % i) for i in range(8)]
        ps_main = Ring([(psb[i], psd[i]) for i in range(0, 4)])
        ps_aux = Ring([(psb[i], psd[i]) for i in range(4, 6)])
        ps_acc = Ring([(psb[i], psd[i]) for i in range(6, 8)])

        xT = al("xT", [128, 8, TL], F32)
        hT = al("hT", [128, 8, CTX], F32)
        d_x = [[Dep("x%d_%d" % (k, c)) for c in range(4)] for k in range(8)]
        d_h = [Dep("h%d" % k) for k in range(8)]
        cmat = al("cmat", [128, 9, 128], BF16)
        onesrow = al("onesrow", [128, 128], F32)
        csblk = al("csblk", [128, 2, 512], BF16)
        cs256 = al("cs256", [128, 2, 2, 256], BF16)
        percore = al("percore", [128, 66], F32)
        ccs = al("ccs", [128, 16], F32)
        scb = al("scb", [128, 8, 2], BF16)
        d_const = Dep("const")
        d_pc = Dep("percore")
        d_sc = Dep("sc")
        params = al("params", [128, NP_COLS], F32)
        d_params = Dep("params")
        modT2 = al("modT", [128, 2, 48, 2], F32)
        d_mod2 = [Dep("modT0"), Dep("modT1")]
        gm2 = al("gm", [128, 2, 2, 8, 2], F32)
        d_gm2 = [Dep("gm0"), Dep("gm1")]
        d_snde, d_rcve = Dep("snde"), Dep("rcve")
        d_sndc = [Dep("sndc%d" % i) for i in range(4)]
        d_rcvc = [Dep("rcvc%d" % i) for i in range(4)]
        d_snd2, d_rcv2 = Dep("snd2"), Dep("rcv2")

        ONES_MEAN, BLK64, SWAPP, ONESLN, R1M, C128S, S128S = range(7)

        xsrc = xT_d.ap().rearrange("(k p) t -> p k t", p=128)
        d_xload = [Dep("xload%d" % c) for c in range(4)]
        for c in range(4):
            S.dma("sp", xT[:, :, c * 512:(c + 1) * 512], xsrc[:, :, c * 512:(c + 1) * 512], writes=[d_x[k][c] for k in range(8)],
                  sem_dep=d_xload[c])
        hsrc = ctxT_d.ap().rearrange("(k p) t -> p k t", p=128)
        S.dma("sp", hT, hsrc, writes=d_h, sem_dep=Dep("hload"))
        S.dma("pool", cmat, cmat_d.ap(), writes=[d_const])
        S.dma("pool", csblk, csblk_d.ap(), writes=[d_const])
        S.dma("pool", cs256, c256_d.ap(), writes=[d_const])
        S.dma("sp", percore, percore_d.ap(), writes=[d_pc])
        S.dma("sp", ccs, cc_d.ap(), writes=[d_sc])
        cvec = al("cvec", [128, 4], F32)
        S.op("pool", lambda e: e.memset(cvec, EPS), writes=[d_const])
        S.op("pool", lambda e: e.memset(onesrow, 1.0), writes=[d_const])
        S.op("act", lambda e: e.activation(out=scb.rearrange("p k s -> p (k s)"), in_=ccs, func=AF.Silu),
             reads=[d_sc], writes=[d_sc])

        DEPS = {}

        def gdep(name):
            if name not in DEPS:
                DEPS[name] = Dep(name)
            return DEPS[name]

        def tap(name, ap_sb, shape, dtype, reads):
            if name not in taps:
                return
            t = dt("tap_" + name, list(shape), dtype, kind="ExternalOutput")
            o = S.dma("sp", t.ap(), ap_sb, reads=reads, out_side=True, sem_dep=Dep("tap" + name))
            final_ops.append(o)
            tap_out[name] = t

        def phase_end(names):
            S.barrier()
            A.free(*names)

        lat_chunks = [(0, xT, [d_x[k][c] for k in range(8)], c * 512, 512, c) for c in range(4)]
        ctx_chunk = (1, hT, d_h, 0, CTX, 4)

        def norm_mod(chunk, which, xn_ap, d_xn, tmps, mods):
            s, X, dX, T0, n, ci = chunk
            sqb, d_sq, rstd, d_rstd, tmpr = tmps
            modT, gm, d_mod, d_gm = mods
            pb, pd = ps_aux.next()
            for k in range(8):
                S.op("act", lambda e, k=k: e.activation(out=sqb[:, k % 4, :n], in_=X[:, k, T0:T0 + n], func=AF.Square),
                     reads=[dX[k]], writes=[d_sq[k % 4]])
                S.op("pe", lambda e, k=k: e.matmul(pb[:, :n], cmat[:, ONES_MEAN, :], sqb[:, k % 4, :n], start=(k == 0), stop=(k == 7)),
                     reads=[d_sq[k % 4], d_const], writes=[pd])
            S.op("act", lambda e: e.activation(out=rstd[:, :n], in_=pb[:, :n], func=AF.Ln, bias=cvec[:, 0:1]), reads=[pd, d_const], writes=[d_rstd])
            S.op("act", lambda e: e.activation(out=rstd[:, :n], in_=rstd[:, :n], func=AF.Exp, scale=-0.5), reads=[d_rstd], writes=[d_rstd])
            shift_chunk = 0 if which == 0 else 3
            for k in range(8):
                tb, td = tmpr.next()
                S.op("dve", lambda e, k=k, tb=tb: e.tensor_tensor(out=tb[:, :n], in0=X[:, k, T0:T0 + n], in1=rstd[:, :n], op=ALU.mult),
                     reads=[dX[k], d_rstd], writes=[td])
                S.op("act", lambda e, k=k, tb=tb: e.activation(out=xn_ap[:, k, :n], in_=tb[:, :n], func=AF.Identity,
                                                             bias=modT[:, shift_chunk * 8 + k, s:s + 1], scale=gm[:, which, k, s:s + 1]),
                     reads=[td, d_gm, d_mod], writes=[d_xn[k]])

        def stage_M(l, part):
            modT, gm, d_mod, d_gm = modT2[:, l % 2], gm2[:, l % 2], d_mod2[l % 2], d_gm2[l % 2]
            if part == "begin":
                S.dma("sp", params, params_d.ap()[l], writes=[d_params])
                return
            if part == "end":
                for which, sc_chunk, gcol in ((0, 1, 0), (1, 4, 8)):
                    for s_ in range(2):
                        S.op("dve", lambda e, which=which, sc_chunk=sc_chunk, gcol=gcol, s_=s_: e.scalar_tensor_tensor(
                            out=gm[:, which, :, s_], in0=modT[:, sc_chunk * 8:(sc_chunk + 1) * 8, s_], scalar=1.0,
                            in1=params[:, gcol:gcol + 8], op0=ALU.add, op1=ALU.mult),
                            reads=[d_mod, d_params], writes=[d_gm], cost=0.2)
                tap("modT%d" % l, modT, [128, 48, 2], F32, [d_mod])
                tap("gm%d" % l, gm, [128, 2, 8, 2], F32, [d_gm])
                return
            oc = part
            wm = [A_views["wm0"], A_views["wm1"]]
            d_wm = [gdep("wm%d" % i) for i in range(2)]
            wsrc = w_mod_d.ap()[l].rearrange("(k p) o -> p k o", p=128)
            sl = oc % 2
            S.dma("pool", wm[sl], wsrc[:, :, oc * 512:(oc + 1) * 512], writes=[d_wm[sl]])
            for o4 in range(4):
                o = oc * 4 + o4
                pb, pd = ps_aux.next()
                for k in range(8):
                    S.op("pe", lambda e, k=k, sl=sl, o4=o4, pb=pb: e.matmul(pb[:, 0:2], wm[sl][:, k, o4 * 128:(o4 + 1) * 128], scb[:, k, :],
                                                                          start=(k == 0), stop=(k == 7)),
                         reads=[d_wm[sl], d_sc], writes=[pd], cost=0.06)
                S.op("dve", lambda e, o=o, pb=pb: e.tensor_scalar(out=modT[:, o, :], in0=pb[:, 0:2], scalar1=params[:, 16 + o:17 + o], scalar2=None,
                                                                 op0=ALU.add),
                     reads=[pd, d_params], writes=[d_mod], cost=0.2)

        A_views = {}
        A_views["wm0"] = al("wm0", [128, 8, 512], BF16)
        A_views["wm1"] = al("wm1", [128, 8, 512], BF16)
        for part in ["begin"] + list(range(12)) + ["end"]:
            stage_M(0, part)
        phase_end(["wm0", "wm1"])

        def do_layer(l):
            last = (l == DEPTH - 1)
            L = "%d" % l
            modT, gm, d_mod, d_gm = modT2[:, l % 2], gm2[:, l % 2], d_mod2[l % 2], d_gm2[l % 2]
            mods = (modT, gm, d_mod, d_gm)
            ycT = al("ycT", [128, 2, CTX], BF16, top=True)
            d_ycT = gdep("ycT")
            qT = al("qT", [128, 4, TL], BF16, top=True)
            qC = al("qC", [128, 4, CTX], BF16, top=True)
            d_q = [[gdep("q%d_%d" % (t, c)) for c in range(5)] for t in range(2)]
            KTc = al("KTc", [128, CTX], BF16, top=True)
            d_KTc = gdep("KTc")
            Vxc = al("Vxc", [128, 2, 192], BF16, top=True)
            d_Vxc = gdep("Vxc")
            zc_tm = al("zctm", [128, 2, 512], BF16, top=True)
            d_zc = gdep("zc")
            glu = al("glu", [128, 2, TL + 30], BF16, top=True)
            gluC = al("gluC", [128, 2, CTX + 30], BF16, top=True)
            pu = al("pu", [128, 2, TL + 16], BF16, top=True)
            puC = al("puC", [128, 2, CTX + 16], BF16, top=True)
            d_glu = [gdep("glu%d" % i) for i in range(2)]
            d_gluC = [gdep("gluC%d" % i) for i in range(2)]
            d_pu = [gdep("pu%d" % i) for i in range(2)]
            d_puC = [gdep("puC%d" % i) for i in range(2)]
            S.op("pool", lambda e: e.memset(qT, 0.0), writes=[gdep("q%d_%d" % (t, c)) for t in range(2) for c in range(4)])
            if not last:
                S.op("pool", lambda e: e.memset(qC, 0.0), writes=[gdep("q%d_4" % t) for t in range(2)])
            S.op("pool", lambda e: e.memset(Vxc, 1.0), writes=[d_Vxc])
            if not last:
                S.op("pool", lambda e: e.memset(gluC, 0.0), writes=d_gluC)
                S.op("pool", lambda e: e.memset(puC, 0.0), writes=d_puC)

            diag = al("diag", [128, 62, 128], BF16, top=True)
            d_diag = gdep("diag")
            for idx in range(62):
                S.op("dve", lambda e, idx=idx: e.tensor_scalar(out=diag[:, idx, :], in0=cmat[:, 7, :], scalar1=params[:, 66 + idx:67 + idx], scalar2=None,
                                                               op0=ALU.mult),
                     reads=[d_const, d_params], writes=[d_diag])
            wpw = al("wpw", [128, 2, 256], BF16, top=True)
            wpool = al("wpool", [128, 2, 128], BF16, top=True)
            d_wcp = gdep("wcp")
            S.dma("pool", wpw, w_pw_d.ap()[l].rearrange("(k p) o -> p k o", p=128), writes=[d_wcp])
            S.op("pool", lambda e: e.memset(wpool, 0.0), writes=[d_wcp])
            for g in range(4):
                tile_, half = g // 2, g % 2
                S.dma("pool", wpool[half * 64:(half + 1) * 64, tile_, half * 64:(half + 1) * 64], w_pool_d.ap()[l, g], writes=[d_wcp],
                      reads=[])
            w_in = al("w_in", [128, 8, D_IN], BF16)
            d_win = gdep("w_in")
            wisrc = w_in_d.ap()[l].rearrange("(k p) o -> p k o", p=128)
            for c3 in range(3):
                S.dma("pool", w_in[:, :, c3 * 512:(c3 + 1) * 512], wisrc[:, :, c3 * 512:(c3 + 1) * 512], writes=[d_win])
            xn = al("xnA", [128, 8, 512], BF16)
            d_xn = [gdep("xnA%d" % k) for k in range(8)]
            sqb = al("sqbA", [128, 4, 512], BF16)
            d_sq = [gdep("sqA%d" % k) for k in range(8)]
            rstd = al("rstdA", [128, 512], F32)
            d_rstd = gdep("rstdA")
            tmpr = Ring([(al("tmpA%d" % i, [128, 512], F32), gdep("tmpA%d" % i)) for i in range(3)])
            ntmps = (sqb, d_sq, rstd, d_rstd, tmpr)
            ropeC = al("ropeC", [128, 512], F32)
            ropeS = al("ropeS", [128, 512], F32)
            d_rope = gdep("rope")
            sqq = al("sqq", [128, 512], BF16)
            d_sqq = gdep("sqq")
            rs2 = al("rs2", [128, 512], F32)
            d_rs2 = gdep("rs2")
            qg = al("qg", [128, 512], BF16)
            d_qg = gdep("qg")
            kloc = al("kloc", [128, 512], BF16)
            d_kloc = gdep("kloc")
            vsb = al("vsb", [128, 4, 192], BF16)
            d_vsb = gdep("vsb")
            S.op("pool", lambda e: e.memset(vsb, 1.0), writes=[d_vsb])
            ub = al("ub", [128, 2, 512], BF16)
            d_ub = [gdep("ub0"), gdep("ub1")]
            zsb = al("zsb", [128, 4, 512], BF16)
            d_zsb = gdep("zsb")
            a_names = ["w_in", "xnA", "sqbA", "rstdA", "tmpA0", "tmpA1", "tmpA2", "ropeC", "ropeS", "sqq", "rs2", "qg", "kloc", "vsb",
                       "ub", "zsb"]

            def qk_tile(chunk, ctile, dest_ap, d_dest, gcol, rope):
                s, X, dX, T0, n, ci = chunk
                pb, pd = ps_main.next()
                for k in range(8):
                    S.op("pe", lambda e, k=k: e.matmul(pb[:, :n], w_in[:, k, ctile * 128:(ctile + 1) * 128], xn[:, k, :n],
                                                       start=(k == 0), stop=(k == 7)),
                         reads=[d_win, d_xn[k]], writes=[pd])
                S.op("act", lambda e: e.activation(out=sqq[:, :n], in_=pb[:, :n], func=AF.Square), reads=[pd], writes=[d_sqq])
                p2, pd2 = ps_aux.next()
                S.op("pe", lambda e: e.matmul(p2[:, :n], cmat[:, BLK64, :], sqq[:, :n], start=True, stop=True),
                     reads=[d_sqq, d_const], writes=[pd2])
                S.op("act", lambda e: e.activation(out=rs2[:, :n], in_=p2[:, :n], func=AF.Ln, bias=cvec[:, 0:1]), reads=[pd2, d_const], writes=[d_rs2])
                S.op("act", lambda e: e.activation(out=rs2[:, :n], in_=rs2[:, :n], func=AF.Exp, scale=-0.5), reads=[d_rs2], writes=[d_rs2])
                if not rope:
                    for (psl, dap) in dest_ap:
                        S.op("dve", lambda e, psl=psl, dap=dap: e.scalar_tensor_tensor(out=dap, in0=pb[psl, :n], scalar=params[psl, gcol:gcol + 1],
                                                                                       in1=rs2[psl, :n], op0=ALU.mult, op1=ALU.mult),
                             reads=[pd, d_rs2, d_params], writes=[d_dest])
                    return
                S.op("dve", lambda e: e.scalar_tensor_tensor(out=qg[:, :n], in0=pb[:, :n], scalar=params[:, gcol:gcol + 1], in1=rs2[:, :n],
                                                             op0=ALU.mult, op1=ALU.mult),
                     reads=[pd, d_rs2, d_params], writes=[d_qg])
                p3, pd3 = ps_aux.next()
                S.op("pe", lambda e: e.matmul(p3[:, :n], cmat[:, SWAPP, :], qg[:, :n], start=True, stop=True),
                     reads=[d_qg, d_const], writes=[pd3])
                t1, td1 = tmpr.next()
                t2, td2 = tmpr.next()
                S.op("dve", lambda e: e.tensor_tensor(out=t1[:, :n], in0=qg[:, :n], in1=ropeC[:, :n], op=ALU.mult),
                     reads=[d_qg, d_rope], writes=[td1])
                S.op("dve", lambda e: e.tensor_tensor(out=t2[:, :n], in0=p3[:, :n], in1=ropeS[:, :n], op=ALU.mult),
                     reads=[pd3, d_rope], writes=[td2])
                for (psl, dap) in dest_ap:
                    S.op("dve", lambda e, psl=psl, dap=dap: e.tensor_tensor(out=dap, in0=t1[psl, :n], in1=t2[psl, :n], op=ALU.add),
                         reads=[td1, td2], writes=[d_dest])

            def v_tiles(chunk, dest_fn, d_dest):
                s, X, dX, T0, n, ci = chunk
                for tt in range(n // 128):
                    pb, pd = ps_main.next()
                    for k in range(8):
                        S.op("pe", lambda e, k=k, tt=tt, pb=pb: e.matmul(pb[:, 0:128], xn[:, k, tt * 128:(tt + 1) * 128], w_in[:, k, 384:512],
                                                                          start=(k == 0), stop=(k == 7)),
                             reads=[d_win, d_xn[k]], writes=[pd])
                    S.op("act", lambda e, tt=tt, pb=pb: e.copy(out=dest_fn(tt)[:, 0:64], in_=pb[:, 0:64]), reads=[pd], writes=[d_dest])
                    S.op("act", lambda e, tt=tt, pb=pb: e.copy(out=dest_fn(tt)[:, 128:192], in_=pb[:, 64:128]), reads=[pd], writes=[d_dest])

            def dest_in(pb):
                return pb[:, 0:128]

            def proj_tile(chunk, ctile):
                s, X, dX, T0, n, ci = chunk
                pb, pd = ps_main.next()
                for k in range(8):
                    S.op("pe", lambda e, k=k: e.matmul(pb[:, :n], w_in[:, k, ctile * 128:(ctile + 1) * 128], xn[:, k, :n],
                                                       start=(k == 0), stop=(k == 7)),
                         reads=[d_win, d_xn[k]], writes=[pd])
                return pb, pd

            chunks = [ctx_chunk] + lat_chunks
            def do_chunk_A(chunk):
                s, X, dX, T0, n, ci = chunk
                isctx = (s == 1)
                so_box = []
                norm_mod(chunk, 0, xn, d_xn, ntmps, mods)
                if not isctx:
                    S.dma("sp", ropeC[:, :n], ropeC_d.ap()[:, T0:T0 + n], writes=[d_rope])
                    S.dma("sp", ropeS[:, :n], ropeS_d.ap()[:, T0:T0 + n], writes=[d_rope])
                full = not (isctx and last)
                if full:
                    H0, H1, ALLP = slice(0, 64), slice(64, 128), slice(0, 128)
                    for tq in range(2):
                        if isctx:
                            dest = [(H0, qC[0:64, tq, :]), (H1, qC[64:128, 2 + tq, :])]
                        else:
                            dest = [(H0, qT[0:64, tq, T0:T0 + n]), (H1, qT[64:128, 2 + tq, T0:T0 + n])]
                        qk_tile(chunk, tq, dest, d_q[tq][ci], 64, rope=not isctx)
                ALLP = slice(0, 128)
                if isctx:
                    qk_tile(chunk, 2, [(ALLP, KTc[:, :])], d_KTc, 65, rope=False)
                else:
                    qk_tile(chunk, 2, [(ALLP, kloc[:, :n])], d_kloc, 65, rope=True)
                if isctx:
                    for tt in range(2):
                        pb, pd = ps_main.next()
                        for k in range(8):
                            S.op("pe", lambda e, k=k, tt=tt, pb=pb: e.matmul(pb[:, 0:128], xn[:, k, tt * 128:(tt + 1) * 128], w_in[:, k, 384:512],
                                                                              start=(k == 0), stop=(k == 7)),
                                 reads=[d_win, d_xn[k]], writes=[pd])
                        S.op("act", lambda e, tt=tt, pb=pb: e.copy(out=Vxc[:, tt, 0:64], in_=pb[:, 0:64]), reads=[pd], writes=[d_Vxc])
                        S.op("act", lambda e, tt=tt, pb=pb: e.copy(out=Vxc[:, tt, 128:192], in_=pb[:, 64:128]), reads=[pd], writes=[d_Vxc])
                else:
                    v_tiles(chunk, lambda tt: vsb[:, tt, :], d_vsb)
                if not full:
                    return
                for ut in range(2):
                    pb, pd = proj_tile(chunk, 4 + ut)
                    S.op("act", lambda e, ut=ut, pb=pb: e.copy(out=ub[:, ut, :n], in_=pb[:, :n]), reads=[pd], writes=[d_ub[ut]])
                if isctx:
                    for tt in range(2):
                        pb, pd = ps_main.next()
                        for k2 in range(2):
                            S.op("pe", lambda e, k2=k2, tt=tt, pb=pb: e.matmul(pb[:, :], ub[:, k2, tt * 128:(tt + 1) * 128], csblk[:, k2, :],
                                                                                start=(k2 == 0), stop=(k2 == 1)),
                                 reads=[d_ub[k2], d_const], writes=[pd])
                        S.op("act", lambda e, tt=tt, pb=pb: e.copy(out=zc_tm[:, tt, :], in_=pb[:, :]), reads=[pd], writes=[d_zc])
                else:
                    for zt in range(4):
                        pb, pd = ps_main.next()
                        for k2 in range(2):
                            S.op("pe", lambda e, k2=k2, zt=zt, pb=pb: e.matmul(pb[:, :n], csblk[:, k2, zt * 128:(zt + 1) * 128], ub[:, k2, :n],
                                                                                start=(k2 == 0), stop=(k2 == 1)),
                                 reads=[d_ub[k2], d_const], writes=[pd])
                        S.op("dve", lambda e, zt=zt, pb=pb: e.tensor_copy(out=zsb[:, zt, :n], in_=pb[:, :n]), reads=[pd], writes=[d_zsb])
                    sc_ = sndc[ci].ap()
                    so = [S.dma("sp", sc_[320:832, :].rearrange("(z p) t -> p z t", p=128), zsb[:, :, :n], reads=[d_zsb], writes=[d_sndc[ci]],
                                out_side=True),
                          S.dma("sp", sc_[0:128, :], kloc[:, :n], reads=[d_kloc], writes=[d_sndc[ci]], out_side=True),
                          S.dma("sp", sc_[128:320, :].rearrange("r c -> (r c)").rearrange("(tt p c) -> p tt c", p=128, c=192), vsb,
                                reads=[d_vsb], writes=[d_sndc[ci]], out_side=True)]
                    so_box.append(so)
                    if ci == 0:
                        tap("zsb" + L, zsb, [128, 4, 512], BF16, [d_zsb])
                for vt in range(2):
                    pv, pdv = proj_tile(chunk, 6 + vt)
                    pg, pdg = proj_tile(chunk, 8 + vt)
                    sig, d_sig = tmpr.next()
                    S.op("act", lambda e, pg=pg, sig=sig: e.activation(out=sig[:, :n], in_=pg[:, :n], func=AF.Sigmoid), reads=[pdg], writes=[d_sig])
                    gdst = (gluC[:, vt, 15:15 + n] if isctx else glu[:, vt, 15 + T0:15 + T0 + n])
                    S.op("dve", lambda e, pv=pv, gdst=gdst, sig=sig: e.tensor_tensor(out=gdst, in0=pv[:, :n], in1=sig[:, :n], op=ALU.mult),
                         reads=[pdv, d_sig], writes=[(d_gluC if isctx else d_glu)[vt]])
                for pt in range(2):
                    pb, pd = proj_tile(chunk, 10 + pt)
                    pdst = (puC[:, pt, 8:8 + n] if isctx else pu[:, pt, 8 + T0:8 + T0 + n])
                    S.op("act", lambda e, pb=pb, pdst=pdst: e.copy(out=pdst, in_=pb[:, :n]), reads=[pd],
                         writes=[(d_puC if isctx else d_pu)[pt]])

                if so_box:
                    so = so_box[0]
                    S.collective(lambda e: e.collective_compute("AllGather", ALU.bypass, replica_groups=RG, ins=[sndc[ci].ap()],
                                                                outs=[rcvc[ci].ap()]),
                                 reads=[d_sndc[ci]], writes=[d_rcvc[ci]], extra=so, name="gc%d" % ci)

            for chunk in chunks:
                do_chunk_A(chunk)
            sev = snde.ap().rearrange("(a p) c -> p a c", p=128)
            e_ops = []
            e_ops.append(S.dma("sp", sev[:, 0:2, 0:16], glu[:, :, 15:31], reads=d_glu, writes=[d_snde], out_side=True, sem_dep=d_glu[0]))
            e_ops.append(S.dma("sp", sev[:, 0:2, 16:32], glu[:, :, TL - 1:TL + 15], reads=d_glu, writes=[d_snde], out_side=True, sem_dep=d_glu[0]))
            e_ops.append(S.dma("sp", sev[:, 2:4, 0:16], pu[:, :, 8:24], reads=d_pu, writes=[d_snde], out_side=True, sem_dep=d_pu[0]))
            e_ops.append(S.dma("sp", sev[:, 2:4, 16:32], pu[:, :, TL - 8:TL + 8], reads=d_pu, writes=[d_snde], out_side=True, sem_dep=d_pu[0]))
            S.collective(lambda e: e.collective_compute("AllGather", ALU.bypass, replica_groups=RG, ins=[snde.ap()], outs=[rcve.ap()]),
                         reads=[d_snde], writes=[d_rcve], extra=e_ops, name="ge")

            tap("q" + L, qT, [128, 4, TL], BF16, [d_q[0][3], d_q[1][3], d_q[0][0], d_q[1][0]])
            tap("glu" + L, glu, [128, 2, TL + 30], BF16, d_glu)
            tap("pu" + L, pu, [128, 2, TL + 16], BF16, d_pu)
            tap("KTc" + L, KTc, [128, CTX], BF16, [d_KTc])
            tap("Vxc" + L, Vxc, [128, 2, 192], BF16, [d_Vxc])
            phase_end(a_names)
            if stop_after == "A" + L:
                return True

            catT = al("catT", [128, 6, TL], BF16)
            catC = al("catC", [128, 6, CTX], BF16)
            d_cat = [[gdep("cat%d_%d" % (r, c)) for c in range(5)] for r in range(6)]
            halL = al("halL", [128, 4, 32], BF16)
            halR = al("halR", [128, 4, 32], BF16)
            d_hal = gdep("hal")
            d_haloG, d_haloP = gdep("haloG"), gdep("haloP")
            def halo_fill():
                jl = (jr + 3) % 4
                jrr = (jr + 1) % 4
                rv = rcve.ap()
                S.dma("sp", halL, rv[bass.ds(jl * 512, 512), :].rearrange("(a p) c -> p a c", p=128), reads=[d_rcve], writes=[d_hal], tmin=150.0)
                S.dma("sp", halR, rv[bass.ds(jrr * 512, 512), :].rearrange("(a p) c -> p a c", p=128), reads=[d_rcve], writes=[d_hal], tmin=150.0)
                S.op("dve", lambda e: e.tensor_scalar(out=glu[:, :, 0:15], in0=halL[:, 0:2, 17:32], scalar1=percore[:, 0:1], scalar2=None, op0=ALU.mult),
                     reads=[d_hal, d_pc], writes=[d_haloG])
                S.op("dve", lambda e: e.tensor_scalar(out=glu[:, :, TL + 15:TL + 30], in0=halR[:, 0:2, 0:15], scalar1=percore[:, 1:2], scalar2=None,
                                                      op0=ALU.mult),
                     reads=[d_hal, d_pc], writes=[d_haloG])
                S.op("dve", lambda e: e.tensor_scalar(out=pu[:, :, 0:8], in0=halL[:, 2:4, 24:32], scalar1=percore[:, 0:1], scalar2=None, op0=ALU.mult),
                     reads=[d_hal, d_pc], writes=[d_haloP])
                S.op("dve", lambda e: e.tensor_scalar(out=pu[:, :, TL + 8:TL + 16], in0=halR[:, 2:4, 0:8], scalar1=percore[:, 1:2], scalar2=None,
                                                      op0=ALU.mult),
                     reads=[d_hal, d_pc], writes=[d_haloP])

            accr = Ring([(al("acc%d" % i, [128, 2, 512], F32), [gdep("acc%d_0" % i), gdep("acc%d_1" % i)]) for i in range(1)])
            ybr = Ring([(al("yb%d" % i, [128, 2, 512], BF16), [gdep("yb%d_0" % i), gdep("yb%d_1" % i)]) for i in range(2)])
            ysqr = Ring([(al("ysq%d" % i, [128, 2, 512], BF16), [gdep("ysq%d_0" % i), gdep("ysq%d_1" % i)]) for i in range(2)])
            meanr = Ring([(al("mean%d" % i, [128, 512], F32), gdep("mean%d" % i)) for i in range(2)])
            msqr = Ring([(al("msq%d" % i, [128, 512], F32), gdep("msq%d" % i)) for i in range(1)])
            rstdr = Ring([(al("rstdc%d" % i, [128, 512], F32), gdep("rstdc%d" % i)) for i in range(2)])
            actr = Ring([(al("actb%d" % i, [128, 2, 512], BF16), [gdep("actb%d_0" % i), gdep("actb%d_1" % i)]) for i in range(2)])
            p1e = al("p1e", [128, 528], F32)
            w4e = al("w4e", [128, 528], F32)
            w8e = al("w8e", [128, 528], F32)
            d_p1e, d_w4e, d_w8e = gdep("p1e"), gdep("w4e"), gdep("w8e")
            WS = al("WS", [128, 2, 512], F32)
            d_WS = [gdep("WS0"), gdep("WS1")]
            ybp = al("ybp", [128, 2, 512], BF16)
            d_ybp = [gdep("ybp0"), gdep("ybp1")]
            tmp8 = al("tmp8", [128, 8], F32)
            d_tmp8 = gdep("tmp8")
            b1_names = ["halL", "halR", "wpw", "wpool", "diag", "acc0", "yb0", "yb1", "ysq0", "ysq1", "mean0", "mean1", "msq0",
                        "rstdc0", "rstdc1", "actb0", "actb1", "p1e", "w4e", "w8e", "WS", "ybp", "tmp8", "glu", "gluC", "pu", "puC"]

            def conv_pool_chunk(G, dG, U, dU, cat, ci, T0, n, T, fixcol):
                acc, d_acc = accr.next()
                yb, d_yb = ybr.next()
                ysq, d_ysq = ysqr.next()
                mean_sb, d_mean = meanr.next()
                msq, d_msq = msqr.next()
                rstdc, d_rstdc = rstdr.next()
                actb, d_actb = actr.next()
                pcs = []
                for vt in range(2):
                    pc, pdc = ps_main.next()
                    pcs.append((pc, pdc))
                    for j in range(31):
                        S.op("pe", lambda e, vt=vt, j=j, pc=pc: e.matmul(pc[:, :n], diag[:, vt * 31 + j, :], G[:, vt, T0 + j:T0 + j + n],
                                                                         start=(j == 0), stop=(j == 30)),
                             reads=dG[vt] + [d_diag], writes=[pdc])
                    S.op("act", lambda e, vt=vt, pc=pc: e.activation(out=yb[:, vt, :n], in_=pc[:, :n], func=AF.Identity, bias=params[:, 128 + vt:129 + vt]),
                         reads=[pdc, d_params], writes=[d_yb[vt]])
                    S.op("act", lambda e, vt=vt, pc=pc: e.activation(out=ysq[:, vt, :n], in_=pc[:, :n], func=AF.Square, bias=params[:, 128 + vt:129 + vt]),
                         reads=[pdc, d_params], writes=[d_ysq[vt]])
                pm, pdm = ps_aux.next()
                pq, pdq = ps_aux.next()
                for vt in range(2):
                    S.op("pe", lambda e, vt=vt: e.matmul(pm[:, :n], cmat[:, ONESLN, :], yb[:, vt, :n], start=(vt == 0), stop=(vt == 1)),
                         reads=[d_yb[vt], d_const], writes=[pdm])
                for vt in range(2):
                    S.op("pe", lambda e, vt=vt: e.matmul(pq[:, :n], cmat[:, ONESLN, :], ysq[:, vt, :n], start=(vt == 0), stop=(vt == 1)),
                         reads=[d_ysq[vt], d_const], writes=[pdq])
                S.op("act", lambda e: e.copy(out=mean_sb[:, :n], in_=pm[:, :n]), reads=[pdm], writes=[d_mean])
                S.op("dve", lambda e: e.tensor_tensor(out=msq[:, :n], in0=mean_sb[:, :n], in1=mean_sb[:, :n], op=ALU.mult),
                     reads=[d_mean], writes=[d_msq])
                S.op("dve", lambda e: e.tensor_tensor(out=rstdc[:, :n], in0=pq[:, :n], in1=msq[:, :n], op=ALU.subtract),
                     reads=[pdq, d_msq], writes=[d_rstdc])
                S.op("act", lambda e: e.activation(out=rstdc[:, :n], in_=rstdc[:, :n], func=AF.Ln, bias=cvec[:, 0:1]), reads=[d_rstdc, d_const],
                     writes=[d_rstdc])
                S.op("act", lambda e: e.activation(out=rstdc[:, :n], in_=rstdc[:, :n], func=AF.Exp, scale=-0.5), reads=[d_rstdc], writes=[d_rstdc])
                for vt in range(2):
                    pc, pdc = pcs[vt]
                    S.op("dve", lambda e, vt=vt, pc=pc: e.scalar_tensor_tensor(out=acc[:, vt, :n], in0=pc[:, :n], scalar=params[:, 128 + vt:129 + vt],
                                                                               in1=mean_sb[:, :n], op0=ALU.add, op1=ALU.subtract),
                         reads=[pdc, d_mean, d_params], writes=[d_acc[vt]])
                    S.op("dve", lambda e, vt=vt: e.tensor_tensor(out=acc[:, vt, :n], in0=acc[:, vt, :n], in1=rstdc[:, :n], op=ALU.mult),
                         reads=[d_rstdc, d_acc[vt]], writes=[d_acc[vt]])
                    S.op("act", lambda e, vt=vt: e.activation(out=actb[:, vt, :n], in_=acc[:, vt, :n], func=AF.Silu,
                                                              bias=params[:, 132 + vt:133 + vt], scale=params[:, 130 + vt:131 + vt]),
                         reads=[d_acc[vt], d_params], writes=[d_actb[vt]])
                for ot in range(2):
                    pb, pd = ps_main.next()
                    for vt in range(2):
                        S.op("pe", lambda e, vt=vt, ot=ot, pb=pb: e.matmul(pb[:, :n], wpw[:, vt, ot * 128:(ot + 1) * 128], actb[:, vt, :n],
                                                                            start=(vt == 0), stop=(vt == 1)),
                             reads=[d_actb[vt], d_wcp], writes=[pd])
                    S.op("act", lambda e, ot=ot, pb=pb: e.copy(out=cat[:, 2 + ot, T0:T0 + n], in_=pb[:, :n]), reads=[pd], writes=[d_cat[2 + ot][ci]])
                e_ = n + 16
                c0 = 8
                S.op("dve", lambda e: e.tensor_tensor(out=WS[0:64, 0, :n], in0=U[0:64, 0, T0 + c0 - 1:T0 + c0 - 1 + n], in1=U[0:64, 0, T0 + c0:T0 + c0 + n],
                                                       op=ALU.add),
                     reads=dU[0], writes=[d_WS[0]])
                S.op("dve", lambda e: e.tensor_tensor(out=p1e[64:128, 1:e_], in0=U[64:128, 0, T0:T0 + e_ - 1], in1=U[64:128, 0, T0 + 1:T0 + e_], op=ALU.add),
                     reads=dU[0], writes=[d_p1e])
                S.op("dve", lambda e: e.tensor_tensor(out=WS[64:128, 0, :n], in0=p1e[64:128, c0 - 1:c0 - 1 + n], in1=p1e[64:128, c0 + 1:c0 + 1 + n],
                                                       op=ALU.add),
                     reads=[d_p1e], writes=[d_WS[0]])
                S.op("dve", lambda e: e.tensor_tensor(out=p1e[:, 1:e_], in0=U[:, 1, T0:T0 + e_ - 1], in1=U[:, 1, T0 + 1:T0 + e_], op=ALU.add),
                     reads=dU[1], writes=[d_p1e])
                S.op("dve", lambda e: e.tensor_tensor(out=w4e[:, 2:e_ - 1], in0=p1e[:, 1:e_ - 2], in1=p1e[:, 3:e_], op=ALU.add),
                     reads=[d_p1e], writes=[d_w4e])
                S.op("dve", lambda e: e.tensor_tensor(out=WS[0:64, 1, :n], in0=w4e[0:64, c0 - 2:c0 - 2 + n], in1=w4e[0:64, c0 + 2:c0 + 2 + n], op=ALU.add),
                     reads=[d_w4e], writes=[d_WS[1]])
                S.op("dve", lambda e: e.tensor_tensor(out=w8e[64:128, 4:e_ - 3], in0=w4e[64:128, 2:e_ - 5], in1=w4e[64:128, 6:e_ - 1], op=ALU.add),
                     reads=[d_w4e], writes=[d_w8e])
                S.op("dve", lambda e: e.tensor_tensor(out=WS[64:128, 1, :n], in0=w8e[64:128, c0 - 4:c0 - 4 + n], in1=w8e[64:128, c0 + 4:c0 + 4 + n],
                                                       op=ALU.add),
                     reads=[d_w8e], writes=[d_WS[1]])
                for tl_ in range(2):
                    S.op("dve", lambda e, tl_=tl_: e.scalar_tensor_tensor(out=ybp[:, tl_, :n], in0=WS[:, tl_, :n], scalar=params[:, 136 + tl_:137 + tl_],
                                                                          in1=U[:, tl_, T0 + c0:T0 + c0 + n], op0=ALU.mult, op1=ALU.subtract),
                         reads=[d_WS[tl_], d_params] + dU[tl_], writes=[d_ybp[tl_]])
                    edges = []
                    if T0 == 0:
                        edges.append((0, fixcol + tl_ * 16))
                    if T0 + n == T:
                        edges.append((n - 8, fixcol + tl_ * 16 + 8))
                    for (e0, fc) in edges:
                        S.op("dve", lambda e, tl_=tl_, e0=e0, fc=fc: e.tensor_tensor(out=tmp8[:, :], in0=WS[:, tl_, e0:e0 + 8], in1=percore[:, fc:fc + 8],
                                                                                      op=ALU.mult),
                             reads=[d_WS[tl_], d_pc], writes=[d_tmp8])
                        S.op("dve", lambda e, tl_=tl_, e0=e0: e.tensor_tensor(out=ybp[:, tl_, e0:e0 + 8], in0=tmp8[:, :],
                                                                               in1=U[:, tl_, T0 + c0 + e0:T0 + c0 + e0 + 8], op=ALU.subtract),
                             reads=[d_tmp8] + dU[tl_], writes=[d_ybp[tl_]])
                    pb, pd = ps_main.next()
                    S.op("pe", lambda e, tl_=tl_, pb=pb: e.matmul(pb[:, :n], wpool[:, tl_, :], ybp[:, tl_, :n], start=True, stop=True),
                         reads=[d_ybp[tl_], d_wcp], writes=[pd])
                    S.op("act", lambda e, tl_=tl_, pb=pb: e.activation(out=cat[:, 4 + tl_, T0:T0 + n], in_=pb[:, :n], func=AF.Identity,
                                                                        scale=params[:, 134 + tl_:135 + tl_]),
                         reads=[pd, d_params], writes=[d_cat[4 + tl_][ci]])

            if not last:
                conv_pool_chunk(gluC, [[d] for d in d_gluC], puC, [[d] for d in d_puC], catC, 4, 0, CTX, CTX, 34)
            for c in (1, 2, 0, 3):
                edge = c in (0, 3)
                if c == 0:
                    halo_fill()
                conv_pool_chunk(glu, [[d] + ([d_haloG] if edge else []) for d in d_glu], pu, [[d] + ([d_haloP] if edge else []) for d in d_pu],
                                catT, c, c * 512, 512, TL, 2)
            tap("catconv" + L, catT[:, 2:6, :], [128, 4, TL], BF16, [d_cat[r][c] for r in range(2, 6) for c in range(4)])
            tap("catCconv" + L, catC[:, 2:6, :], [128, 4, CTX], BF16, [d_cat[r][4] for r in range(2, 6)])
            phase_end(b1_names)
            if stop_after == "B1" + L:
                return True

            KT = al("KT", [128, SEQ], BF16)
            d_KT = [gdep("KT%d" % r) for r in range(4)]
            X1 = al("X1", [128, 64, 128], BF16)
            d_X1 = gdep("X1")
            Gr = al("Gr", [128, 64, 64], BF16)
            Gi = al("Gi", [128, 64, 64], BF16)
            d_Gr, d_Gi = gdep("Gr"), gdep("Gi")
            Yout = al("Yout", [128, 64, 64], BF16)
            d_Yout = gdep("Yout")
            tw = al("tw", [128, 2, 512], F32)
            d_tw = gdep("tw")
            S.dma("sp", tw, tw_d.ap(), writes=[d_tw])
            ta = Ring([(al("twa%d" % i, [128, 512], F32), gdep("twa%d" % i)) for i in range(2)])
            tb_ = Ring([(al("twb%d" % i, [128, 512], F32), gdep("twb%d" % i)) for i in range(2)])
            b2_names = ["X1", "Gr", "Gi", "Yout", "tw", "twa0", "twa1", "twb0", "twb1"]
            for c in range(4):
                for ri in range(2):
                    for r in range(4):
                        src = rcvc[c].ap()[bass.ds(jr * 64 + (r * RC[c] + 320 + ri * 256), 64), :].rearrange("m (a t) -> a m t", t=128)
                        p0 = ri * 64 + r * 16 + c * 4
                        S.dma("sp", X1[p0:p0 + 4, :, :], src, reads=[d_rcvc[c]], writes=[d_X1])
            for r in range(4):
                for c in range(4):
                    S.dma("sp", KT[:, r * TL + c * 512:r * TL + (c + 1) * 512], rcvc[c].ap()[r * RC[c]:r * RC[c] + 128, :], reads=[d_rcvc[c]],
                          writes=[d_KT[r]])
            for mg in range(16):
                pb, pd = ps_main.next()
                for mi in range(4):
                    S.op("pe", lambda e, mi=mi, mg=mg, pb=pb: e.matmul(pb[:, mi * 128:(mi + 1) * 128], X1[:, mg * 4 + mi, :], cmat[:, R1M, :],
                                                                        start=True, stop=True),
                         reads=[d_X1, d_const], writes=[pd])
                a_, da = ta.next()
                b_, db = tb_.next()
                S.op("dve", lambda e, pb=pb, a_=a_: e.tensor_tensor(out=a_[:, :], in0=pb[:, :], in1=tw[:, 0, :], op=ALU.mult), reads=[pd, d_tw], writes=[da])
                S.op("dve", lambda e, pb=pb, b_=b_: e.tensor_tensor(out=b_[:, :], in0=pb[:, :], in1=tw[:, 1, :], op=ALU.mult), reads=[pd, d_tw], writes=[db])
                av = a_.rearrange("p (m r k) -> p m r k", r=2, k=64)
                bv = b_.rearrange("p (m r k) -> p m r k", r=2, k=64)
                S.op("dve", lambda e, av=av, bv=bv, mg=mg: e.tensor_tensor(out=Gr[:, mg * 4:(mg + 1) * 4, :], in0=av[:, :, 0, :], in1=bv[:, :, 1, :], op=ALU.add),
                     reads=[da, db], writes=[d_Gr])
                S.op("dve", lambda e, av=av, bv=bv, mg=mg: e.tensor_tensor(out=Gi[:, mg * 4:(mg + 1) * 4, :], in0=av[:, :, 1, :], in1=bv[:, :, 0, :],
                                                                             op=ALU.subtract),
                     reads=[da, db], writes=[d_Gi])
            Grf = Gr.rearrange("p m k -> p (m k)")
            Gif = Gi.rearrange("p m k -> p (m k)")
            Yf = Yout.rearrange("p m k -> p (m k)")
            for ch in range(8):
                pb, pd = ps_main.next()
                S.op("pe", lambda e, ch=ch, pb=pb: e.matmul(pb[:, :], cmat[:, C128S, :], Grf[:, ch * 512:(ch + 1) * 512], start=True, stop=False),
                     reads=[d_Gr, d_const], writes=[pd])
                S.op("pe", lambda e, ch=ch, pb=pb: e.matmul(pb[:, :], cmat[:, S128S, :], Gif[:, ch * 512:(ch + 1) * 512], start=False, stop=True),
                     reads=[d_Gi, d_const], writes=[pd])
                S.op("act", lambda e, ch=ch, pb=pb: e.copy(out=Yf[:, ch * 512:(ch + 1) * 512], in_=pb[:, :]), reads=[pd], writes=[d_Yout])
            o2 = S.dma("sp", snd2.ap().rearrange("m (k2 k1) -> k2 m k1", k1=64), Yout, reads=[d_Yout], writes=[d_snd2], out_side=True)
            tap("Yout" + L, Yout, [128, 64, 64], BF16, [d_Yout])
            S.collective(lambda e: e.collective_compute("AllGather", ALU.bypass, replica_groups=RG, ins=[snd2.ap()], outs=[rcv2.ap()]),
                         reads=[d_snd2], writes=[d_rcv2], extra=[o2], name="g2")
            if not last:
                for mt in range(2):
                    pb, pd = ps_main.next()
                    for tt in range(2):
                        S.op("pe", lambda e, tt=tt, mt=mt, pb=pb: e.matmul(pb[:, 0:CTX], zc_tm[:, tt, mt * 128:(mt + 1) * 128], cs256[:, 0, tt, :],
                                                                            start=(tt == 0), stop=False),
                             reads=[d_zc, d_const], writes=[pd])
                        S.op("pe", lambda e, tt=tt, mt=mt, pb=pb: e.matmul(pb[:, 0:CTX], zc_tm[:, tt, 256 + mt * 128:256 + (mt + 1) * 128], cs256[:, 1, tt, :],
                                                                            start=False, stop=(tt == 1)),
                             reads=[d_zc, d_const], writes=[pd])
                    S.op("act", lambda e, mt=mt, pb=pb: e.copy(out=ycT[:, mt, :], in_=pb[:, 0:CTX]), reads=[pd], writes=[d_ycT])
                tap("ycT" + L, ycT, [128, 2, CTX], BF16, [d_ycT])
            phase_end(b2_names)
            if stop_after == "B2" + L:
                return True

            attnT = al("attnT", [128, 2, TL], BF16, top=True)
            attnC = al("attnC", [128, 2, CTX], BF16, top=True)
            d_attn = [[gdep("attn%d_%d" % (h, c)) for c in range(5)] for h in range(4)]
            wo_att = al("wo_att", [128, 2, D], BF16)
            wo_rest = al("wo_rest", [128, 6, D], BF16)
            d_wo = gdep("wo")
            for tq in range(2):
                S.dma("pool", wo_att[0:64, tq, :], w_out_d.ap()[l, tq * 64:(tq + 1) * 64, :], writes=[d_wo])
                S.dma("pool", wo_att[64:128, tq, :], w_out_d.ap()[l, (2 + tq) * 64:(3 + tq) * 64, :], writes=[d_wo])
            S.dma("pool", wo_rest, w_out_d.ap()[l, 256:1024, :].rearrange("(r p) o -> p r o", p=128), writes=[d_wo])
            wf = al("wf", [128, 2, 256], BF16)
            d_wf = gdep("wf")
            S.dma("pool", wf, w_f_d.ap()[l].rearrange("(k p) o -> p k o", p=128), writes=[d_wf])
            Vx = al("Vx", [128, 64, 192], BF16)
            d_Vx = [gdep("Vx%d" % r) for r in range(4)]
            for r in range(4):
                for c in range(4):
                    rcv_ = rcvc[c].ap()
                    vsrc = rcv_[r * RC[c] + 128:r * RC[c] + 320, :].rearrange("r c -> (r c)").rearrange("(tt p c) -> p tt c", p=128, c=192)
                    S.dma("sp", Vx[:, r * 16 + c * 4:r * 16 + (c + 1) * 4, :], vsrc, reads=[d_rcvc[c]], writes=[d_Vx[r]])
            ering = Ring([(al("E%d" % i, [128, 512], BF16), gdep("E%d" % i)) for i in range(3)])
            b3_names = ["KT", "Vx", "E0", "E1", "E2", "ob0", "ob1", "rs0", "qT", "qC", "KTc", "Vxc", "zctm"]

            obr = Ring([(al("ob%d" % i, [128, 512], F32), gdep("ob%d" % i)) for i in range(2)])
            rsr = Ring([(al("rs%d" % i, [128, 512], F32), gdep("rsr%d" % i)) for i in range(1)])
            pending_fin = []

            def flush_fin():
                while pending_fin:
                    pending_fin.pop(0)()

            def attention(Q, dQ, ci, T0, n, key_tiles, dest):
                for tq in range(2):
                    for hf in range(2):
                        head = hf * 2 + tq
                        ps_ = slice(hf * 64, (hf + 1) * 64)
                        pO, pdO = ps_acc.next()
                        nk = len(key_tiles)
                        sbank = {}

                        def issue_S(kt, head=head, tq=tq, sbank=sbank):
                            Ksrc, dK, Vsrc, dV = key_tiles[kt]
                            pS, pdS = ps_main.next()
                            sbank[kt] = (pS, pdS)
                            S.op("pe", lambda e, pS=pS, Ksrc=Ksrc, head=head: e.matmul(pS[:, :n], Ksrc[:, :], Q[:, head, T0:T0 + n],
                                                                                      start=True, stop=True),
                                 reads=[dK, dQ[tq][ci]], writes=[pdS])

                        LA = 2
                        for k0 in range(min(LA, nk)):
                            issue_S(k0)
                        for kt in range(nk):
                            Ksrc, dK, Vsrc, dV = key_tiles[kt]
                            pS, pdS = sbank.pop(kt)
                            Eb, dE = ering.next()
                            S.op("act", lambda e, pS=pS, Eb=Eb: e.activation(out=Eb[:, :n], in_=pS[:, :n], func=AF.Exp, scale=0.125),
                                 reads=[pdS], writes=[dE])
                            S.op("pe", lambda e, Eb=Eb, Vsrc=Vsrc, kt=kt, pO=pO, hf=hf, nk=nk: e.matmul(pO[:, :n], Vsrc[:, hf * 64:hf * 64 + 128], Eb[:, :n],
                                                                                   start=(kt == 0), stop=(kt == nk - 1)),
                                 reads=[dE, dV], writes=[pdO])
                            if kt + LA < nk:
                                issue_S(kt + LA)
                            if kt == min(3, nk - 1):
                                flush_fin()
                        sr = (64 if hf == 0 else 0)
                        ob, dob = obr.next()
                        rs_, drs = rsr.next()
                        S.op("act", lambda e, pO=pO, ob=ob, ps_=ps_: e.copy(out=ob[ps_, :n], in_=pO[ps_, :n]), reads=[pdO], writes=[dob])
                        S.op("act", lambda e, pO=pO, rs_=rs_, sr=sr: e.activation(out=rs_[sr:sr + 1, :n], in_=pO[sr:sr + 1, :n], func=AF.Ln),
                             reads=[pdO], writes=[drs])
                        S.op("act", lambda e, rs_=rs_, sr=sr: e.activation(out=rs_[sr:sr + 1, :n], in_=rs_[sr:sr + 1, :n], func=AF.Exp, scale=-1.0),
                             reads=[drs], writes=[drs])

                        def fin(ob=ob, dob=dob, rs_=rs_, drs=drs, sr=sr, ps_=ps_, tq=tq, head=head):
                            pbc, pdbc = ps_aux.next()
                            S.op("pe", lambda e: e.matmul(pbc[:, :n], onesrow[sr:sr + 1, :], rs_[sr:sr + 1, :n], start=True, stop=True),
                                 reads=[drs, d_const], writes=[pdbc])
                            S.op("dve", lambda e: e.tensor_tensor(out=dest[ps_, tq, T0:T0 + n], in0=ob[ps_, :n], in1=pbc[ps_, :n], op=ALU.mult),
                                 reads=[dob, pdbc], writes=[d_attn[head][ci]])
                        pending_fin.append(fin)

            ctx_keys = [(KTc[:, tt * 128:(tt + 1) * 128], d_KTc, Vxc[:, tt, :], d_Vxc) for tt in range(2)]
            lat_keys = [(KT[:, tt * 128:(tt + 1) * 128], d_KT[tt // 16], Vx[:, tt, :], d_Vx[tt // 16]) for tt in range(64)]
            if not last:
                attention(qC, d_q, 4, 0, CTX, ctx_keys, attnC)
            for c in range(4):
                attention(qT, d_q, c, c * 512, 512, ctx_keys + lat_keys, attnT)
            flush_fin()
            tap("attnT" + L, attnT, [128, 2, TL], BF16, [d_attn[h][c] for h in range(4) for c in range(4)])
            tap("attnC" + L, attnC, [128, 2, CTX], BF16, [d_attn[h][4] for h in range(4)])
            phase_end(b3_names)
            if stop_after == "B3" + L:
                return True

            XN2 = al("XN2", [128, 8, TL], BF16, top=True)
            xnC = al("xnC", [128, 8, CTX], BF16, top=True)
            d_xn2 = [[gdep("xn2_%d_%d" % (c, k)) for k in range(8)] for c in range(5)]
            A_views["wfi0"] = al("wfi0", [128, 8, 1024], BF16, top=True)
            fisrc0 = w_fi_d.ap()[l].rearrange("(k p) o -> p k o", p=128)
            S.dma("pool", A_views["wfi0"][:, :, 0:512], fisrc0[:, :, 0:512], writes=[gdep("wfi0")])
            S.dma("pool", A_views["wfi0"][:, :, 512:1024], fisrc0[:, :, D_FF:D_FF + 512], writes=[gdep("wfi0")])
            yT = al("yT", [128, 2, 512], BF16)
            d_yT = gdep("yT")
            sqbC = al("sqbC", [128, 4, 512], BF16)
            d_sqC = [gdep("sqC%d" % k) for k in range(8)]
            rstdC = al("rstdC", [128, 512], F32)
            d_rstdC = gdep("rstdC")
            tmprC = Ring([(al("tmpC%d" % i, [128, 512], F32), gdep("tmpC%d" % i)) for i in range(3)])
            ntmpsC = (sqbC, d_sqC, rstdC, d_rstdC, tmprC)
            c1_names = ["wo_att", "wo_rest", "wf", "yT", "sqbC", "rstdC", "tmpC0", "tmpC1", "tmpC2", "catT", "catC", "attnT", "attnC", "ycT"]
            r2v = rcv2.ap()
            chunks = ([ctx_chunk] if not last else []) + lat_chunks
            def do_chunk_C1(chunk):
                s, X, dX, T0, n, ci = chunk
                isctx = (s == 1)
                cat = catC if isctx else catT
                att = attnC if isctx else attnT
                if isctx:
                    ysrc, dys = ycT, d_ycT
                else:
                    S.dma("sp", yT[:, :, :n], r2v[:, bass.ds(jr * TL + T0, n)].rearrange("(k p) t -> p k t", p=128), reads=[d_rcv2], writes=[d_yT])
                    ysrc, dys = yT, d_yT
                for ot in range(2):
                    pb, pd = ps_main.next()
                    for k2 in range(2):
                        S.op("pe", lambda e, k2=k2, ot=ot, pb=pb, ysrc=ysrc: e.matmul(pb[:, :n], wf[:, k2, ot * 128:(ot + 1) * 128], ysrc[:, k2, :n],
                                                                                       start=(k2 == 0), stop=(k2 == 1)),
                             reads=[dys, d_wf], writes=[pd])
                    S.op("act", lambda e, ot=ot, pb=pb, cat=cat: e.copy(out=cat[:, ot, T0:T0 + n], in_=pb[:, :n]), reads=[pd], writes=[d_cat[ot][ci]])
                for ot in range(8):
                    pb, pd = ps_main.next()
                    for h in range(2):
                        S.op("pe", lambda e, h=h, ot=ot, pb=pb, att=att: e.matmul(pb[:, :n], wo_att[:, h, ot * 128:(ot + 1) * 128], att[:, h, T0:T0 + n],
                                                                                   start=(h == 0), stop=False),
                             reads=[d_attn[h][ci], d_attn[2 + h][ci], d_wo], writes=[pd])
                    for r in range(6):
                        S.op("pe", lambda e, r=r, ot=ot, pb=pb, cat=cat: e.matmul(pb[:, :n], wo_rest[:, r, ot * 128:(ot + 1) * 128], cat[:, r, T0:T0 + n],
                                                                                   start=False, stop=(r == 5)),
                             reads=[d_cat[r][ci], d_wo], writes=[pd])
                    S.op("dve", lambda e, ot=ot, pb=pb: e.scalar_tensor_tensor(out=X[:, ot, T0:T0 + n], in0=pb[:, :n], scalar=modT[:, 16 + ot, s:s + 1],
                                                                               in1=X[:, ot, T0:T0 + n], op0=ALU.mult, op1=ALU.add),
                         reads=[pd, d_mod, dX[ot]], writes=[dX[ot]])
                xdst = xnC if isctx else XN2[:, :, T0:T0 + n]
                norm_mod(chunk, 1, xdst, d_xn2[ci], ntmpsC, mods)

            for chunk in chunks:
                do_chunk_C1(chunk)
            tap("x1_" + L, xT, [128, 8, TL], F32, [d_x[k][c] for k in range(8) for c in range(4)])
            tap("h1_" + L, hT, [128, 8, CTX], F32, d_h)
            tap("cat" + L, catT, [128, 6, TL], BF16, [d_cat[r][c] for r in range(6) for c in range(4)])
            phase_end(c1_names)
            if stop_after == "C1" + L:
                return True

            wfi = [A_views["wfi0"], al("wfi1", [128, 8, 1024], BF16)]
            wfo = [al("wfo%d" % i, [128, 4, D], BF16) for i in range(2)]
            d_wfi = [gdep("wfi%d" % i) for i in range(2)]
            d_wfo = [gdep("wfo%d" % i) for i in range(2)]
            hbr = Ring([(al("hb%d" % i, [128, 4, 512], BF16), gdep("hb%d" % i)) for i in range(2)])
            sar = Ring([(al("sa%d" % i, [128, 512], F32), gdep("sa%d" % i)) for i in range(2)])
            c2_names = ["wfi0", "wfi1", "wfo0", "wfo1", "hb0", "hb1", "sa0", "sa1", "XN2", "xnC"]
            if not last:
                A_views["wm0"] = al("wm0", [128, 8, 512], BF16)
                A_views["wm1"] = al("wm1", [128, 8, 512], BF16)
                c2_names = c2_names + ["wm0", "wm1"]
                stage_M(l + 1, "begin")
            fisrc = w_fi_d.ap()[l].rearrange("(k p) o -> p k o", p=128)
            groups = [(0, 4), (4, 4), (8, 4), (12, 4), (16, 4), (20, 2)]
            for gi, (h0, gw) in enumerate(groups):
                sl = gi % 2
                if gi > 0:
                    S.dma("pool", wfi[sl][:, :, 0:gw * 128], fisrc[:, :, h0 * 128:(h0 + gw) * 128], writes=[d_wfi[sl]])
                    S.dma("pool", wfi[sl][:, :, 512:512 + gw * 128], fisrc[:, :, D_FF + h0 * 128:D_FF + (h0 + gw) * 128], writes=[d_wfi[sl]])
                S.dma("pool", wfo[sl][:, 0:gw, :], w_fo_d.ap()[l, h0 * 128:(h0 + gw) * 128, :].rearrange("(c p) o -> p c o", p=128), writes=[d_wfo[sl]])
                def do_chunk_C2(chunk, sl=sl, gw=gw):
                    s, X, dX, T0, n, ci = chunk
                    isctx = (s == 1)
                    xsrc_ = xnC if isctx else XN2[:, :, T0:T0 + n]
                    hb, dhb = hbr.next()
                    for hc in range(gw):
                        pa, pda = ps_main.next()
                        pg, pdg = ps_main.next()
                        for k in range(8):
                            S.op("pe", lambda e, k=k, hc=hc, pa=pa, sl=sl, xsrc_=xsrc_: e.matmul(pa[:, :n], wfi[sl][:, k, hc * 128:(hc + 1) * 128], xsrc_[:, k, :n],
                                                                                                  start=(k == 0), stop=(k == 7)),
                                 reads=[d_wfi[sl], d_xn2[ci][k]], writes=[pda])
                        for k in range(8):
                            S.op("pe", lambda e, k=k, hc=hc, pg=pg, sl=sl, xsrc_=xsrc_: e.matmul(pg[:, :n], wfi[sl][:, k, 512 + hc * 128:512 + (hc + 1) * 128],
                                                                                                  xsrc_[:, k, :n], start=(k == 0), stop=(k == 7)),
                                 reads=[d_wfi[sl], d_xn2[ci][k]], writes=[pdg])
                        sa, dsa = sar.next()
                        S.op("act", lambda e, pa=pa, sa=sa: e.activation(out=sa[:, :n], in_=pa[:, :n], func=AF.Silu), reads=[pda], writes=[dsa])
                        S.op("dve", lambda e, pg=pg, sa=sa, hb=hb, hc=hc: e.tensor_tensor(out=hb[:, hc, :n], in0=pg[:, :n], in1=sa[:, :n], op=ALU.mult),
                             reads=[pdg, dsa], writes=[dhb])
                    for ot in range(8):
                        po, pdo = ps_acc.next()
                        for hc in range(gw):
                            S.op("pe", lambda e, hc=hc, ot=ot, po=po, sl=sl, hb=hb: e.matmul(po[:, :n], wfo[sl][:, hc, ot * 128:(ot + 1) * 128], hb[:, hc, :n],
                                                                                              start=(hc == 0), stop=(hc == gw - 1)),
                                 reads=[d_wfo[sl], dhb], writes=[pdo])
                        S.op("dve", lambda e, ot=ot, po=po, X=X, T0=T0, s=s: e.scalar_tensor_tensor(out=X[:, ot, T0:T0 + n], in0=po[:, :n],
                                                                                                    scalar=modT[:, 40 + ot, s:s + 1], in1=X[:, ot, T0:T0 + n],
                                                                                                    op0=ALU.mult, op1=ALU.add),
                             reads=[pdo, d_mod, dX[ot]], writes=[dX[ot]])

                for chunk in chunks:
                    do_chunk_C2(chunk)
                if not last:
                    stage_M(l + 1, 2 * gi)
                    stage_M(l + 1, 2 * gi + 1)
            if not last:
                stage_M(l + 1, "end")
            tap("x2_" + L, xT, [128, 8, TL], F32, [d_x[k][c] for k in range(8) for c in range(4)])
            if last and stop_after is None:
                osrc = outT_d.ap().rearrange("(k p) t -> p k t", p=128)
                d_osem = Dep("osem")
                for c in range(4):
                    o = S.dma("sp", osrc[:, :, c * 512:(c + 1) * 512], xT[:, :, c * 512:(c + 1) * 512], reads=[d_x[k][c] for k in range(8)],
                              out_side=True, sem_dep=d_osem)
                    final_ops.append(o)
                out_done[0] = True
            phase_end(c2_names)
            if stop_after == "C2" + L:
                return True
            return False

        for l in range(DEPTH):
            if do_layer(l):
                break

        if not out_done[0]:
            osrc = outT_d.ap().rearrange("(k p) t -> p k t", p=128)
            d_osem = Dep("osem")
            for k in range(8):
                o = S.dma("sp", osrc[:, k, :], xT[:, k, :], reads=d_x[k], out_side=True, sem_dep=d_osem)
                final_ops.append(o)
        block = st.enter_context(nc.Block())
        S.emit(block, final_waits=final_ops)
        build_program.peak_words = A.peak
    return nc, tap_out


def _consts():
    f = np.float32
    cm = np.zeros((128, 9, 128), f)
    cm[:, 0, :] = 1.0 / 1024
    for b in range(2):
        cm[b * 64:(b + 1) * 64, 1, b * 64:(b + 1) * 64] = 1.0 / 64
    for k in range(128):
        cm[k, 2, k ^ 1] = 1.0
    cm[:, 3, :] = 1.0 / 256
    cm[:, 7, :] = np.eye(128)
    t1 = np.arange(64)[:, None].astype(np.float64)
    k1 = np.arange(64)[None, :].astype(np.float64)
    C = np.cos(2 * np.pi * t1 * k1 / 64)
    Sn = np.sin(2 * np.pi * t1 * k1 / 64)
    R1m = np.zeros((128, 128))
    R1m[0:64, 0:64] = C
    R1m[64:128, 0:64] = Sn
    R1m[0:64, 64:128] = -Sn
    R1m[64:128, 64:128] = C
    cm[:, 4, :] = R1m
    t2 = np.arange(128)[:, None].astype(np.float64)
    k2 = np.arange(128)[None, :].astype(np.float64)
    nrm = 1.0 / np.sqrt(8192.0 * 64.0)
    cm[:, 5, :] = np.cos(2 * np.pi * t2 * k2 / 128) * nrm
    cm[:, 6, :] = np.sin(2 * np.pi * t2 * k2 / 128) * nrm
    cs = np.zeros((256, 512))
    cc = np.arange(64)[:, None].astype(np.float64)
    m = np.arange(64)[None, :].astype(np.float64)
    for h in range(4):
        cs[h * 64:(h + 1) * 64, h * 64:(h + 1) * 64] = np.cos(2 * np.pi * cc * m / 64)
        cs[h * 64:(h + 1) * 64, 256 + h * 64:256 + (h + 1) * 64] = -np.sin(2 * np.pi * cc * m / 64)
    csblk = np.ascontiguousarray(cs.reshape(2, 128, 512).transpose(1, 0, 2)).astype(f)
    k1r = np.arange(64)[None, :].astype(np.float64)
    twr = np.cos(2 * np.pi * t2 * k1r / 8192)
    twi = np.sin(2 * np.pi * t2 * k1r / 8192)
    tw = np.stack([np.tile(twr, (1, 8)), np.tile(twi, (1, 8))], axis=1).astype(f)
    t = np.arange(256)[:, None].astype(np.float64)
    k = np.arange(256)[None, :].astype(np.float64)
    n2 = 1.0 / np.sqrt(256.0 * 64.0)
    c256 = np.cos(2 * np.pi * t * k / 256) * n2
    s256 = np.sin(2 * np.pi * t * k / 256) * n2
    cs256 = np.stack([c256.reshape(2, 128, 256).transpose(1, 0, 2), s256.reshape(2, 128, 256).transpose(1, 0, 2)], axis=1).astype(f)
    return dict(cmat=cm, csblk=csblk, tw=tw, cs256=np.ascontiguousarray(cs256))


def _rope_tables(t0):
    tpos = np.arange(t0, t0 + TL)
    row = (tpos // 64).astype(np.float32)
    col = (tpos % 64).astype(np.float32)
    inv_freq = (np.float32(10000.0) ** (-np.arange(16, dtype=np.float32) / np.float32(16))).astype(np.float32)
    ang = np.concatenate([row[:, None] * inv_freq, col[:, None] * inv_freq], axis=-1).astype(np.float32)
    cos = np.cos(ang).astype(np.float32)
    sin = np.sin(ang).astype(np.float32)
    p = np.arange(128)
    d = p % 64
    i = d // 2
    sign = np.where(d % 2 == 0, -1.0, 1.0).astype(np.float32)
    rc = np.ascontiguousarray(cos[:, i].T)
    rs = np.ascontiguousarray((sin[:, i] * sign[None, :]).T)
    return rc, rs


def _invcnt(tglob, n, win):
    lo = np.clip(tglob - win // 2, 0, n)
    hi = np.clip(tglob - win // 2 + win, 0, n)
    return (1.0 / (hi - lo)).astype(np.float32)


def _percore(j):
    pc = np.zeros((128, 66), np.float32)
    pc[:, 0] = 0.0 if j == 0 else 1.0
    pc[:, 1] = 0.0 if j == 3 else 1.0
    wins = {(0, 0): 2, (0, 1): 4, (1, 0): 8, (1, 1): 16}
    for tile in range(2):
        for half in range(2):
            win = wins[(tile, half)]
            ps = slice(half * 64, (half + 1) * 64)
            tl = np.concatenate([np.arange(j * TL, j * TL + 8), np.arange((j + 1) * TL - 8, (j + 1) * TL)])
            pc[ps, 2 + tile * 16:2 + (tile + 1) * 16] = _invcnt(tl, SEQ, win)[None, :]
            tc = np.concatenate([np.arange(0, 8), np.arange(CTX - 8, CTX)])
            pc[ps, 34 + tile * 16:34 + (tile + 1) * 16] = _invcnt(tc, CTX, win)[None, :]
    return pc


def _params(inp):
    P = np.zeros((DEPTH, 128, NP_COLS), np.float32)
    for l in range(DEPTH):
        P[l, :, 0:8] = inp["g_norm1"][l].reshape(8, 128).T
        P[l, :, 8:16] = inp["g_norm2"][l].reshape(8, 128).T
        P[l, :, 16:64] = inp["b_mod"][l].reshape(48, 128).T
        P[l, :, 64] = np.tile(inp["q_norm_g"][l], 2)
        P[l, :, 65] = np.tile(inp["k_norm_g"][l], 2)
        cw = inp["conv_dw_w"][l]
        for vt in range(2):
            P[l, :, 66 + vt * 31:66 + (vt + 1) * 31] = cw[:, vt * 128:(vt + 1) * 128].T
        P[l, :, 128:130] = inp["conv_dw_b"][l].reshape(2, 128).T
        P[l, :, 130:132] = inp["conv_ln_g"][l].reshape(2, 128).T
        P[l, :, 132:134] = inp["conv_ln_b"][l].reshape(2, 128).T
        P[l, :, 134:136] = inp["pool_scale"][l].reshape(2, 128).T
        P[l, 0:64, 136] = 1.0 / 2
        P[l, 64:128, 136] = 1.0 / 4
        P[l, 0:64, 137] = 1.0 / 8
        P[l, 64:128, 137] = 1.0 / 16
    return P


_QPERM = np.concatenate([np.arange(0, 64), np.arange(128, 192), np.arange(64, 128), np.arange(192, 256), np.arange(256, D_IN)])


def prep_inputs(inp):
    inp = {k: np.asarray(v) for k, v in inp.items()}
    cst = _consts()
    params = _params(inp)
    w_in_p = np.ascontiguousarray(inp["w_in"][:, :, _QPERM])
    shared = dict(params=params, w_mod=inp["w_mod"], w_in=w_in_p, w_fourier=inp["w_fourier"], w_conv_pw=inp["w_conv_pw"],
                  w_pool=inp["w_pool"], w_out=inp["w_out"], w_ffn_in=inp["w_ffn_in"], w_ffn_out=inp["w_ffn_out"], **cst)
    maps = []
    for i in range(8):
        b, j = i // 4, i % 4
        m = dict(shared)
        m["xT"] = np.ascontiguousarray(inp["x"][b, j * TL:(j + 1) * TL, :].T)
        m["ctxT"] = np.ascontiguousarray(inp["ctx"][b].T)
        ccv = np.zeros((128, 16), np.float32)
        ccv[:, 0::2] = inp["c"][b].reshape(8, 128).T
        ccv[:, 1::2] = inp["c_ctx"].reshape(8, 128).T
        m["cc"] = ccv
        rc, rs = _rope_tables(j * TL)
        m["ropeC"] = rc
        m["ropeS"] = rs
        m["percore"] = _percore(j)
        maps.append(m)
    return maps


_NC_CACHE = {}


def kernel(**inputs):
    maps = prep_inputs(inputs)
    if "nc" not in _NC_CACHE:
        _NC_CACHE["nc"] = build_program()[0]
    nc = _NC_CACHE["nc"]
    res = run_bass_kernel_spmd(nc, maps, core_ids=list(range(8)))
    out = np.zeros((2, SEQ, D), np.float32)
    for i in range(8):
        b, j = i // 4, i % 4
        out[b, j * TL:(j + 1) * TL, :] = res.results[i]["outT"].T
    return out
```
